# Optimizing a Trainium2 kernel written in Bass

```python
import jax, jax.numpy as jnp
from jax import lax
import numpy as np

D_MODEL = 1024
BATCH = 4
SEQ = 4096
DEPTH = 2

CHUNK = 128
RET_HEADS = 4
RET_HEAD_DIM = 128
RET_W = RET_HEADS * RET_HEAD_DIM
SB_HEADS = 8
SB_HEAD_DIM = 64
SB_W = SB_HEADS * SB_HEAD_DIM
SGU_GROUPS = 4
SGU_GROUP_DIM = 128
SGU_W = SGU_GROUPS * SGU_GROUP_DIM
D_FF = 4 * D_MODEL
ROPE_BASE = 10000.0
LN_EPS = 1e-5
DEEPNORM_ALPHA = (2 * DEPTH) ** 0.25
DEEPNORM_BETA = (8 * DEPTH) ** -0.25
SPLITS = (RET_W, RET_W, RET_W, RET_W, SB_W, SB_W, SB_W, SGU_W, SGU_W, D_MODEL, D_MODEL, D_MODEL)
N_IN = sum(SPLITS)

kernel_name = "hybrid_ret_sb_sgu_deepnorm"


def layer_norm(x, g, b):
    xf = x.astype(jnp.float32)
    mu = jnp.mean(xf, axis=-1, keepdims=True)
    var = jnp.mean(jnp.square(xf - mu), axis=-1, keepdims=True)
    y = (xf - mu) * lax.rsqrt(var + LN_EPS)
    return (y * g.astype(jnp.float32) + b.astype(jnp.float32)).astype(x.dtype)


def head_group_norm(o, g, b):
    B, S, H, D = o.shape
    of = o.astype(jnp.float32)
    mu = jnp.mean(of, axis=-1, keepdims=True)
    var = jnp.mean(jnp.square(of - mu), axis=-1, keepdims=True)
    y = ((of - mu) * lax.rsqrt(var + LN_EPS)).reshape(B, S, H * D)
    return (y * g.astype(jnp.float32) + b.astype(jnp.float32)).astype(o.dtype)


def rotary(x, pos):
    half = x.shape[-1] // 2
    inv_freq = ROPE_BASE ** (-jnp.arange(half, dtype=jnp.float32) / half)
    ang = pos.astype(jnp.float32)[:, None] * inv_freq[None, :]
    cos = jnp.cos(ang)[None, :, None, :].astype(x.dtype)
    sin = jnp.sin(ang)[None, :, None, :].astype(x.dtype)
    x1, x2 = x[..., :half], x[..., half:]
    return jnp.concatenate([x1 * cos - x2 * sin, x2 * cos + x1 * sin], axis=-1)


def retention(q, k, v):
    B, S, H, D = q.shape
    N = S // CHUNK
    dt = q.dtype
    log_g = jnp.log(1.0 - 2.0 ** (-5.0 - jnp.arange(H, dtype=jnp.float32)))
    idx = jnp.arange(CHUNK, dtype=jnp.float32)
    diff = idx[:, None] - idx[None, :]
    intra_decay = jnp.where(diff[None] >= 0, jnp.exp(log_g[:, None, None] * diff[None]), 0.0).astype(dt)
    k_decay = jnp.exp(log_g[:, None] * (CHUNK - 1 - idx)[None, :]).T.astype(dt)
    q_decay = jnp.exp(log_g[:, None] * (idx + 1.0)[None, :]).T.astype(dt)
    chunk_decay = jnp.exp(log_g * CHUNK).astype(dt)

    qc = q.reshape(B, N, CHUNK, H, D)
    kc = k.reshape(B, N, CHUNK, H, D)
    vc = v.reshape(B, N, CHUNK, H, D)

    scores = jnp.einsum('bnihd,bnjhd->bnhij', qc, kc) * intra_decay
    intra = jnp.einsum('bnhij,bnjhe->bnihe', scores, vc)

    kv = jnp.einsum('bnjhd,bnjhe->nbhde', kc * k_decay[:, :, None], vc)

    def step(state, kv_n):
        new_state = state * chunk_decay[None, :, None, None] + kv_n
        return new_state, state

    state0 = jnp.zeros((B, H, D, D), dt)
    _, prev_states = lax.scan(step, state0, kv)
    inter = jnp.einsum('bnihd,nbhde->bnihe', qc * q_decay[:, :, None], prev_states)
    return (intra + inter).reshape(B, S, H, D)


def stick_breaking(q, k, v):
    B, S, H, D = q.shape
    NB = S // CHUNK
    scale = D ** -0.5
    qb = q.reshape(B, NB, CHUNK, H, D).transpose(1, 0, 2, 3, 4)
    s_pos = jnp.arange(S)

    def block(args):
        qn, n = args
        z = jnp.einsum('bihd,bshd->bhis', qn, k).astype(jnp.float32) * scale
        t_pos = n * CHUNK + jnp.arange(CHUNK)
        mask = s_pos[None, :] < t_pos[:, None]
        log_1m_beta = jnp.where(mask, jax.nn.log_sigmoid(-z), 0.0)
        later = lax.cumsum(log_1m_beta, axis=3, reverse=True) - log_1m_beta
        a = jnp.where(mask, jnp.exp(jax.nn.log_sigmoid(z) + later), 0.0)
        return jnp.einsum('bhis,bshd->bihd', a.astype(v.dtype), v)

    out = lax.map(block, (qb, jnp.arange(NB)))
    return out.transpose(1, 0, 2, 3, 4).reshape(B, S, H * D)


def chunked_sgu(u, v, ln_g, ln_b, w_s, b_s):
    B, S, _ = v.shape
    N = S // CHUNK
    v = layer_norm(v, ln_g, ln_b)
    vg = v.reshape(B, N, CHUNK, SGU_GROUPS, SGU_GROUP_DIM)
    causal = jnp.tril(jnp.ones((CHUNK, CHUNK), dtype=w_s.dtype))
    w = w_s * causal[None]
    sv = jnp.einsum('gij,bnjgc->bnigc', w, vg) + b_s.T[None, None, :, :, None]
    return u * sv.reshape(B, S, SGU_W)


def mixer(x, w_in, ret_gn_g, ret_gn_b, sgu_ln_g, sgu_ln_b, sgu_w, sgu_b, p_ret, p_sb, p_sgu, w_out):
    B, S, _ = x.shape
    proj = x @ w_in
    points = [int(p) for p in np.cumsum(SPLITS)[:-1]]
    (rq, rk, rv, rg, sq, sk, sv, gu, gv, gate_ret, gate_sb, gate_sgu) = jnp.split(proj, points, axis=-1)
    pos = jnp.arange(S, dtype=jnp.int32)

    rq = rotary(rq.reshape(B, S, RET_HEADS, RET_HEAD_DIM), pos)
    rk = rotary(rk.reshape(B, S, RET_HEADS, RET_HEAD_DIM), pos) * (RET_HEAD_DIM ** -0.5)
    ret = retention(rq, rk, rv.reshape(B, S, RET_HEADS, RET_HEAD_DIM))
    ret = jax.nn.silu(rg) * head_group_norm(ret, ret_gn_g, ret_gn_b)

    sb = stick_breaking(sq.reshape(B, S, SB_HEADS, SB_HEAD_DIM),
                        sk.reshape(B, S, SB_HEADS, SB_HEAD_DIM),
                        sv.reshape(B, S, SB_HEADS, SB_HEAD_DIM))

    sg = chunked_sgu(jax.nn.gelu(gu), jax.nn.gelu(gv), sgu_ln_g, sgu_ln_b, sgu_w, sgu_b)

    merged = (jax.nn.sigmoid(gate_ret) * (ret @ p_ret)
              + jax.nn.sigmoid(gate_sb) * (sb @ p_sb)
              + jax.nn.sigmoid(gate_sgu) * (sg @ p_sgu))
    return merged @ w_out


def setup_inputs(seed: int = 0) -> dict:
    key = jax.random.key(seed)
    ks = jax.random.split(key, 20)
    L = DEPTH
    f32 = jnp.float32

    def nrm(k, shape, scale):
        return jax.random.normal(k, shape, f32) * scale

    return {
        "x": jax.random.normal(ks[0], (BATCH, SEQ, D_MODEL), f32),
        "w_in": nrm(ks[1], (L, D_MODEL, N_IN), D_MODEL ** -0.5),
        "ret_gn_g": 1.0 + nrm(ks[2], (L, RET_W), 0.02),
        "ret_gn_b": nrm(ks[3], (L, RET_W), 0.02),
        "sgu_ln_g": 1.0 + nrm(ks[4], (L, SGU_W), 0.02),
        "sgu_ln_b": nrm(ks[5], (L, SGU_W), 0.02),
        "sgu_w": nrm(ks[6], (L, SGU_GROUPS, CHUNK, CHUNK), CHUNK ** -0.5),
        "sgu_b": 1.0 + nrm(ks[7], (L, SGU_GROUPS, CHUNK), 0.01),
        "p_ret": nrm(ks[8], (L, RET_W, D_MODEL), RET_W ** -0.5 * DEEPNORM_BETA),
        "p_sb": nrm(ks[9], (L, SB_W, D_MODEL), SB_W ** -0.5 * DEEPNORM_BETA),
        "p_sgu": nrm(ks[10], (L, SGU_W, D_MODEL), SGU_W ** -0.5 * DEEPNORM_BETA),
        "w_out": nrm(ks[11], (L, D_MODEL, D_MODEL), D_MODEL ** -0.5 * DEEPNORM_BETA),
        "ln1_g": 1.0 + nrm(ks[12], (L, D_MODEL), 0.02),
        "ln1_b": nrm(ks[13], (L, D_MODEL), 0.02),
        "w_up": nrm(ks[14], (L, D_MODEL, D_FF), D_MODEL ** -0.5 * DEEPNORM_BETA),
        "w_down": nrm(ks[15], (L, D_FF, D_MODEL), D_FF ** -0.5 * DEEPNORM_BETA),
        "ln2_g": 1.0 + nrm(ks[16], (L, D_MODEL), 0.02),
        "ln2_b": nrm(ks[17], (L, D_MODEL), 0.02),
    }


def reference(x, w_in, ret_gn_g, ret_gn_b, sgu_ln_g, sgu_ln_b, sgu_w, sgu_b, p_ret, p_sb, p_sgu,
              w_out, ln1_g, ln1_b, w_up, w_down, ln2_g, ln2_b):
    for l in range(DEPTH):
        y = mixer(x, w_in[l], ret_gn_g[l], ret_gn_b[l], sgu_ln_g[l], sgu_ln_b[l], sgu_w[l], sgu_b[l],
                  p_ret[l], p_sb[l], p_sgu[l], w_out[l])
        x = layer_norm(DEEPNORM_ALPHA * x + y, ln1_g[l], ln1_b[l])
        h = jnp.square(jax.nn.relu(x @ w_up[l])) @ w_down[l]
        x = layer_norm(DEEPNORM_ALPHA * x + h, ln2_g[l], ln2_b[l])
    return x
```

```python
import contextlib
import numpy as np
import ml_dtypes
import concourse.bass as bass
import concourse.mybir as mybir
from concourse.bass_utils import run_bass_kernel_spmd

F32 = mybir.dt.float32
BF16 = mybir.dt.bfloat16
AF = mybir.ActivationFunctionType
ALU = mybir.AluOpType

D = 1024
S = 4096
NB = 4
DEPTH = 2
TH = 2048
DFF = 4096
ALPHA = (2 * DEPTH) ** 0.25
EPS = 1e-5
FUSED = False
import os as _os
PARTS = _os.environ.get("KPARTS", "proj,ret,sb").split(",")
SUB = _os.environ.get("KSUB", "fm,tm,rot,tr").split(",")

ENGS = ("pe", "act", "dve", "pool", "sp")
SAME_ENGINE_SYNC = True
SCHEDULE = True


class _Op:
    __slots__ = ("eng", "fn", "chan", "deps", "tick", "needs_inc", "kind", "waits_extra", "seg", "cost", "fin")

    def __init__(self, eng, fn, chan, kind):
        self.seg = 0
        self.cost = None
        self.fin = 0.0
        self.eng = eng
        self.fn = fn
        self.chan = chan
        self.deps = set()
        self.tick = None
        self.needs_inc = False
        self.kind = kind
        self.waits_extra = None


class Prog:
    def __init__(self, nc):
        self.nc = nc
        self.streams = {e: [] for e in ENGS}
        self.res = {}
        self.chan_count = {}
        self.last_op = {e: None for e in ENGS}
        self.seg = 0

    def _add(self, op, reads, writes):
        deps = set()
        for k in reads:
            st = self.res.get(k)
            if st is not None and st[0] is not None:
                deps.add(st[0])
        for k in writes:
            st = self.res.get(k)
            if st is not None:
                if st[0] is not None:
                    deps.add(st[0])
                deps.update(st[1])
        for k in writes:
            self.res[k] = [op, []]
        for k in reads:
            st = self.res.get(k)
            if st is None:
                st = self.res[k] = [None, []]
            if k not in writes:
                st[1].append(op)
        deps.discard(op)
        op.deps = deps
        op.seg = self.seg
        self.streams[op.eng].append(op)
        self.last_op[op.eng] = op
        return op

    def op(self, eng, fn, reads=(), writes=()):
        return self._add(_Op(eng, fn, None, "c"), tuple(reads), tuple(writes))

    def dma(self, queue, out, in_, chan, reads=(), writes=(), **kw):
        k = self.chan_count.get(chan, 0)
        self.chan_count[chan] = k + 1
        o = _Op(queue, (lambda e: e.dma_start(out=out, in_=in_, **kw)), (chan, k), "d")
        return self._add(o, tuple(reads), tuple(writes))

    def xop(self, queue, fn, reads=()):
        o = _Op(queue, fn, None, "x")
        o.cost = 2.0
        return self._add(o, tuple(reads), ())

    def custom_dma(self, queue, fn, chan, reads=(), writes=()):
        k = self.chan_count.get(chan, 0)
        self.chan_count[chan] = k + 1
        o = _Op(queue, fn, (chan, k), "d")
        return self._add(o, tuple(reads), tuple(writes))

    def barrier(self):
        lasts = [self.last_op[e] for e in ENGS
                 if self.last_op[e] is not None and self.last_op[e].kind == "c"]
        lasts = []
        for e in ENGS:
            for o in reversed(self.streams[e]):
                if o.kind == "c":
                    lasts.append(o)
                    break
        chans = dict(self.chan_count)
        for e in ENGS:
            o = _Op(e, None, None, "b")
            o.deps = set(lasts)
            o.waits_extra = chans
            o.seg = self.seg
            self.streams[e].append(o)
        self.res = {}
        self.seg += 1

    COST = {"pe": 0.22, "act": 0.5, "dve": 0.6, "pool": 1.0, "sp": 0.1}
    DMA_LAT = 3.0
    XLAT = 0.5
    SLAT = 0.35
    WINDOW = 32

    def schedule(self):
        nseg = self.seg + 1
        per = {e: [[] for _ in range(nseg + 1)] for e in ENGS}
        bar = {e: [None] * (nseg + 1) for e in ENGS}
        for e in ENGS:
            for o in self.streams[e]:
                if o.kind == "b":
                    bar[e][o.seg] = o
                else:
                    per[e][o.seg].append(o)
        new = {e: [] for e in ENGS}
        for sg in range(nseg + 1):
            lists = {e: per[e][sg] for e in ENGS}
            if any(lists[e] for e in ENGS):
                inseg = set()
                for e in ENGS:
                    inseg.update(lists[e])
                ptr = {e: 0 for e in ENGS}
                done = set()
                tfree = {e: 0.0 for e in ENGS}
                out = {e: [] for e in ENGS}
                pend = {e: list(lists[e]) for e in ENGS}
                t = 0.0
                remaining = sum(len(v) for v in pend.values())
                while remaining:
                    progressed = False
                    nxt = None
                    for e in ENGS:
                        if not pend[e]:
                            continue
                        if tfree[e] > t + 1e-9:
                            nxt = tfree[e] if nxt is None else min(nxt, tfree[e])
                            continue
                        win = pend[e][:1] if e == "sp" else pend[e][:self.WINDOW]
                        best = None
                        for o in win:
                            rdy = 0.0
                            ok = True
                            for d in o.deps:
                                if d not in inseg:
                                    continue
                                if d not in done:
                                    ok = False
                                    break
                                lat = 0.0 if (d.eng == e and e == "pe") else (self.SLAT if d.eng == e else self.XLAT)
                                rdy = max(rdy, d.fin + lat)
                            if not ok:
                                continue
                            if rdy <= t + 1e-9:
                                best = o
                                break
                            nxt = rdy if nxt is None else min(nxt, rdy)
                        if best is not None:
                            c = best.cost if best.cost is not None else self.COST[e]
                            if best.kind == "d":
                                best.fin = t + self.DMA_LAT
                                tfree[e] = t + c
                            else:
                                best.fin = t + c
                                tfree[e] = t + c
                            done.add(best)
                            pend[e].remove(best)
                            out[e].append(best)
                            remaining -= 1
                            progressed = True
                            nxt = tfree[e] if nxt is None else min(nxt, tfree[e])
                    if not progressed:
                        if nxt is None or nxt <= t + 1e-9:
                            for e in ENGS:
                                out[e].extend(pend[e])
                                pend[e] = []
                            break
                        t = nxt
                    else:
                        t = t if nxt is None else min(t + 0.05, nxt) if False else t
                for e in ENGS:
                    new[e].extend(out[e])
            for e in ENGS:
                if bar[e][sg] is not None:
                    new[e].append(bar[e][sg])
        for e in ENGS:
            assert len(new[e]) == len(self.streams[e]), (e, len(new[e]), len(self.streams[e]))
        self.streams = new

    def emit(self):
        nc = self.nc
        if SCHEDULE:
            self.schedule()
        for e in ENGS:
            for o in self.streams[e]:
                for d in o.deps:
                    if d.kind == "c":
                        if d.eng == o.eng and (d.eng == "pe" or not SAME_ENGINE_SYNC) and o.kind != "b":
                            continue
                        d.needs_inc = True
        for e in ENGS:
            t = 0
            for o in self.streams[e]:
                if o.kind == "c" and o.needs_inc:
                    t += 1
                    o.tick = t
        with contextlib.ExitStack() as es:
            esem = {e: es.enter_context(nc.semaphore("s_" + e)) for e in ENGS}
            csem = {c: es.enter_context(nc.semaphore("c_" + str(c))) for c in self.chan_count}
            self.flag_sem = es.enter_context(nc.semaphore("flag_sem"))
            block = es.enter_context(nc.Block())

            def run(e):
                def body(eng):
                    known = {}

                    def wait(sem, val):
                        if known.get(sem.name, 0) >= val:
                            return
                        known[sem.name] = val
                        eng.wait_ge(sem, val)

                    for o in self.streams[e]:
                        for d in o.deps:
                            if d.kind == "c":
                                if d.tick is None:
                                    continue
                                if d.eng == e and (e == "pe" or not SAME_ENGINE_SYNC) and o.kind != "b":
                                    continue
                                wait(esem[d.eng], d.tick)
                            elif d.kind == "d":
                                c, k = d.chan
                                wait(csem[c], 16 * (k + 1))
                        if o.kind == "b":
                            for c, n in o.waits_extra.items():
                                wait(csem[c], 16 * n)
                            continue
                        ins = o.fn(eng)
                        if o.kind == "x":
                            continue
                        if o.kind == "d":
                            ins.then_inc(csem[o.chan[0]], 16)
                        elif o.needs_inc:
                            ins.then_inc(esem[e], 1)
                    if e == "sp":
                        for c, n in self.chan_count.items():
                            wait(csem[c], 16 * n)
                        for e2 in ENGS:
                            lt = max([o.tick for o in self.streams[e2] if o.tick is not None] or [0])
                            if lt:
                                wait(esem[e2], lt)
                return body

            block.tensor(run("pe"))
            block.scalar(run("act"))
            block.vector(run("dve"))
            block.gpsimd(run("pool"))
            block.sync(run("sp"))


class Arena:
    def __init__(self, t32, nbytes):
        self.t = t32
        self.n = nbytes
        self.top = 0
        self.marks = []

    def alloc(self, nelem, dtype, parts=128):
        esz = 4 if dtype == F32 else 2
        nb = (nelem * esz + 63) // 64 * 64
        assert self.top + nb <= self.n, f"SBUF arena overflow {self.top}+{nb}>{self.n}"
        o = self.top // 4
        self.top += nb
        v = self.t[0:parts, o:o + nb // 4]
        if dtype != F32:
            v = v.bitcast(dtype)
        return v[:, 0:nelem]

    def mark(self):
        self.marks.append(self.top)

    def release(self):
        self.top = self.marks.pop()


CA_IDENT, CA_CAUS, CA_U, CA_L, CA_SBM, CA_DEC, CA_CH, CA_CD, CA_N = (
    0, 128, 256, 384, 512, 2560, 2564, 2566, 2568)
CR_COS, CR_SIN, CR_N = 0, 2048, 4096
CB_CAUS, CB_ONES, CB_N = 0, 128, 256
LB_LNG, LB_LNB, LB_WT, LB_SB, LB_N = 0, 512, 1024, 1536, 2048
LB_L1G, LB_L1B, LB_L2G, LB_L2B, LV_N = 0, 8, 16, 24, 32


def _consts_A(hh):
    c = np.zeros((128, CA_N), np.float32)
    p = np.arange(128)
    c[:, CA_IDENT:CA_IDENT + 128] = np.eye(128)
    c[:, CA_CAUS:CA_CAUS + 128] = (p[:, None] <= p[None, :])
    c[:, CA_U:CA_U + 128] = (p[:, None] >= p[None, :])
    c[:, CA_L:CA_L + 128] = (p[:, None] < p[None, :])
    t = np.arange(512)
    for r in range(4):
        c[:, CA_SBM + r * 512:CA_SBM + (r + 1) * 512] = ((r * 128 + p)[:, None] < t[None, :])
    for h in range(2):
        hg = hh * 2 + h
        g = 1.0 - 2.0 ** (-5.0 - hg)
        lg = np.log(g)
        c[:, CA_DEC + h] = (128.0 ** -0.5) * np.exp(lg * (p + 1.0))
        c[:, CA_DEC + 2 + h] = np.exp(lg * (127.0 - p))
        c[:, CA_CH + h] = np.exp(-lg * 128.0)
        c[:, CA_CD + h] = np.exp(lg * 128.0)
    return c


def _consts_R(shift=0):
    c = np.zeros((128, CR_N), np.float32)
    p = np.arange(128)
    half = 64
    inv_freq = (10000.0 ** (-np.arange(half, dtype=np.float32) / half)).astype(np.float32)
    pos = np.abs((np.arange(32)[None, :] - shift) * 128 + p[:, None]).astype(np.float32)
    ang = (pos[:, :, None] * inv_freq[None, None, :]).astype(np.float32)
    c[:, CR_COS:CR_COS + 2048] = np.cos(ang).astype(np.float32).reshape(128, 2048)
    c[:, CR_SIN:CR_SIN + 2048] = np.sin(ang).astype(np.float32).reshape(128, 2048)
    return c


def _consts_B():
    c = np.zeros((128, CB_N), np.float32)
    p = np.arange(128)
    c[:, CB_CAUS:CB_CAUS + 128] = (p[:, None] <= p[None, :])
    c[:, CB_ONES:CB_ONES + 128] = 1.0 / D
    return c


def _blk_lhsT(w, cw=128):
    K, N = w.shape
    return np.ascontiguousarray(w.reshape(K // 128, 128, N // cw, cw).transpose(2, 1, 0, 3))


def _host_inputs(inp):
    x = np.asarray(inp["x"], np.float32)
    maps = []
    for core in range(8):
        b, hh = core // 2, core % 2
        m = {}
        m["xT"] = np.ascontiguousarray(x[b, hh * TH:(hh + 1) * TH, :].T)
        m["cA"] = _consts_A(hh)
        m["cB"] = _consts_B()
        m["cR"] = _consts_R()
        m["cRL"] = _consts_R(16 if hh == 0 else 0)
        for l in range(DEPTH):
            w_in = np.asarray(inp["w_in"][l], np.float32)
            hs = slice(hh * 256, (hh + 1) * 256)
            blk = lambda i: w_in[:, i * 512:(i + 1) * 512]
            rq, rk, rv, rg, sq, sk, sv = [blk(i)[:, hs] for i in range(7)]
            m[f"wAf{l}"] = _blk_lhsT(np.concatenate([rg, sq, sk], axis=1))
            m[f"wAt{l}"] = _blk_lhsT(np.concatenate([rq, rk, rv, sv], axis=1), cw=512)
            gu, gv = w_in[:, 3584:4096], w_in[:, 4096:4608]
            gates = w_in[:, 4608:7680]
            m[f"wGu{l}"] = _blk_lhsT(gu)
            m[f"wGv{l}"] = _blk_lhsT(gv, cw=512)
            m[f"wGt{l}"] = _blk_lhsT(gates)
            m[f"pR{l}"] = _blk_lhsT(np.asarray(inp["p_ret"][l], np.float32))
            m[f"pS{l}"] = _blk_lhsT(np.asarray(inp["p_sb"][l], np.float32))
            m[f"pG{l}"] = _blk_lhsT(np.asarray(inp["p_sgu"][l], np.float32))
            m[f"wO{l}"] = _blk_lhsT(np.asarray(inp["w_out"][l], np.float32))
            m[f"wU{l}"] = _blk_lhsT(np.asarray(inp["w_up"][l], np.float32))
            m[f"wD{l}"] = _blk_lhsT(np.asarray(inp["w_down"][l], np.float32))
            la = np.zeros((128, 4), np.float32)
            la[:, 0:2] = np.asarray(inp["ret_gn_g"][l], np.float32)[hs].reshape(2, 128).T
            la[:, 2:4] = np.asarray(inp["ret_gn_b"][l], np.float32)[hs].reshape(2, 128).T
            m[f"lA{l}"] = la
            lb = np.zeros((128, LB_N), np.float32)
            lb[:, LB_LNG:LB_LNG + 512] = np.asarray(inp["sgu_ln_g"][l], np.float32)[None, :]
            lb[:, LB_LNB:LB_LNB + 512] = np.asarray(inp["sgu_ln_b"][l], np.float32)[None, :]
            sw = np.asarray(inp["sgu_w"][l], np.float32)
            lb[:, LB_WT:LB_WT + 512] = sw.transpose(2, 0, 1).reshape(128, 512)
            lb[:, LB_SB:LB_SB + 512] = np.asarray(inp["sgu_b"][l], np.float32).reshape(1, 512)
            lv = np.zeros((128, LV_N), np.float32)
            for nm, off in (("ln1_g", LB_L1G), ("ln1_b", LB_L1B), ("ln2_g", LB_L2G), ("ln2_b", LB_L2B)):
                lv[:, off:off + 8] = np.asarray(inp[nm][l], np.float32).reshape(8, 128).T
            m[f"lB{l}"] = lb
            m[f"lV{l}"] = lv
        maps.append(m)
    return maps


IN_SHAPES = {"xT": [D, TH], "cA": [128, CA_N], "cB": [128, CB_N], "cR": [128, CR_N], "cRL": [128, CR_N]}
for _l in range(DEPTH):
    IN_SHAPES.update({
        f"wAf{_l}": [6, 128, 8, 128], f"wAt{_l}": [2, 128, 8, 512],
        f"wGu{_l}": [4, 128, 8, 128], f"wGv{_l}": [1, 128, 8, 512], f"wGt{_l}": [24, 128, 8, 128],
        f"pR{_l}": [8, 128, 4, 128], f"pS{_l}": [8, 128, 4, 128], f"pG{_l}": [8, 128, 4, 128],
        f"wO{_l}": [8, 128, 8, 128], f"wU{_l}": [32, 128, 8, 128], f"wD{_l}": [8, 128, 32, 128],
        f"lA{_l}": [128, 4], f"lB{_l}": [128, LB_N], f"lV{_l}": [128, LV_N]})


class Builder:
    def __init__(self, stages):
        self.stages = stages
        self.nc = bass.Bass("TRN2", target_bir_lowering=False)
        self.dram = {}
        self.ext_in = []
        self.ext_out = []

    def dt(self, name, shape, dtype, kind):
        if name not in self.dram:
            self.dram[name] = self.nc.dram_tensor(name, list(shape), dtype, kind=kind).ap()
            if kind == "ExternalInput":
                self.ext_in.append(name)
            elif kind == "ExternalOutput":
                self.ext_out.append(name)
        return self.dram[name]

    def win(self, name):
        return self.dt(name, IN_SHAPES[name], F32, "ExternalInput")

    def winl(self, base, l):
        return self.dt(base + (str(l) if FUSED else ""), IN_SHAPES[base + str(l)], F32, "ExternalInput")

    def build(self):
        nc = self.nc
        with contextlib.ExitStack() as es:
            at = es.enter_context(nc.sbuf_tensor("arena", [128, 53200], F32))
            self.ar = Arena(at, 53200 * 4)
            self.ps = [es.enter_context(nc.psum_tensor(f"ps{i}", [128, 512], F32)) for i in range(6)]
            self.psb = es.enter_context(nc.psum_tensor("psb", [128, 1024], BF16))
            self.psb2 = es.enter_context(nc.psum_tensor("psb2", [128, 1024], BF16))
            self.P = Prog(nc)
            if self.stages == ["FX"]:
                self.wire_fx()
            elif self.stages == ["FUSED"]:
                self.wire_fused()
            else:
                for s in self.stages:
                    self.wire_unfused(s)
                    self.P.barrier()
            self.P.emit()
        return nc

    def wire_unfused(self, s):
        EI, EO = "ExternalInput", "ExternalOutput"
        w = lambda base: self.dt(base, IN_SHAPES[base + "0"] if base + "0" in IN_SHAPES else IN_SHAPES[base], F32, EI)
        if s == "P0":
            self.stage_p0(dict(xT=w("xT"), xres_o=self.dt("xres_o", [D, TH], F32, EO), xb_o=self.dt("xb_o", [D, TH], BF16, EO)))
        elif s[0] == "A":
            self.stage_a(dict(xall=self.dt("xball", [2, D, TH], BF16, EI), rs=self.dt("rs", [512, S], BF16, EO),
                              cA=w("cA"), cR=w("cR"), lA=w("lA"), wAf=w("wAf"), wAt=w("wAt")))
        elif s[0] == "B":
            io = dict(xres_i=self.dt("xres_i", [D, TH], F32, EI), xb_i=self.dt("xb_i", [D, TH], BF16, EI),
                      rsall=self.dt("rsall", [2, 512, TH], BF16, EI),
                      xres_o=self.dt("xres_o", [D, TH], F32, EO), xb_o=self.dt("xb_o", [D, TH], BF16, EO),
                      wU16=self.dt("wU16", [32, 128, 1024], BF16, "Internal"), wD16=self.dt("wD16", [8, 128, 4096], BF16, "Internal"),
                      make_cache=True)
            for nm in ("cB", "lB", "lV", "wGu", "wGv", "wGt", "pR", "pS", "pG", "wO", "wU", "wD"):
                io[nm] = w(nm)
            self.stage_b(io)

    def wire_fx(self):
        EI = "ExternalInput"
        P = self.P
        w = lambda base: self.dt(base, IN_SHAPES[base], F32, EI)
        wl = lambda base, l: self.dt(f"{base}{l}", IN_SHAPES[base + "0"], F32, EI)
        I32 = mybir.dt.int32
        nonce = self.dt("nonce", [1, 128], I32, EI)
        sh = lambda nm, shape, dtp: self.dram.setdefault(nm, self.nc.dram_tensor(nm, shape, dtp, kind="Internal", addr_space="Shared").ap())
        XB = [sh("EX0", [2, D, TH], BF16)] * DEPTH
        RS = [sh("EX1", [2, 512, S], BF16)] * DEPTH
        FL = sh("FL", [2, 16], I32)
        xres = [self.dt(f"xres_p{l}", [D, TH], F32, "Internal") for l in range(DEPTH)]
        xbp = [self.dt(f"xb_p{l}", [D, TH], BF16, "Internal") for l in range(DEPTH)]
        rsp = [self.dt(f"rs_p{l}", [512, S], BF16, "Internal") for l in range(DEPTH)]
        rsall = [self.dt(f"rsall_p{l}", [2, 512, TH], BF16, "Internal") for l in range(DEPTH)]
        outT = self.dt("outT", [D, TH], F32, "ExternalOutput")
        self.ar.mark()
        ntile = self.ar.alloc(128, F32, parts=1).bitcast(I32)
        P.dma("sp", ntile, nonce, "nonce", writes=["ntile"])
        phase = [0]

        def publish(dst_fn, src):
            phase[0] += 1
            k = phase[0]
            P.custom_dma("sp", (lambda e: e.dma_start(out=dst_fn(self.parity(e)), in_=src)), "xch", writes=[("xch", k)])

            def fn(e, k=k):
                par = self.parity(e)
                e.dma_start(out=FL[bass.ds(par, 1)], in_=ntile[0:1, k * 16:(k + 1) * 16]).then_inc(self.P.flag_sem, 16)
                e.wait_ge(self.P.flag_sem, 16 * k)
            P.xop("sp", fn, reads=[("xch", k), "ntile"])
            return k

        def wait_partner(k):
            def fn(e, k=k):
                par = self.parity(e)
                if getattr(self, "_nbase", None) is None:
                    self._nbase = e.alloc_register("nonce_base")
                    e.reg_load(self._nbase, nonce[0:1, 0:1])
                with e.register(f"want{k}") as want, e.register(f"got{k}") as got, e.register(f"r{k}") as r:
                    e.reg_add(want, self._nbase, k)
                    e.reg_mov(r, 1)
                    with e.While(r):
                        e.reg_load(got, FL[bass.ds(1 - par, 1), 0:1])
                        e.reg_sub(r, got, want)
                        e.reg_alu(r, r, -4, ALU.bitwise_and)
            P.xop("sp", fn, reads=["ntile"])

        def publish_and_wait(dst_fn, src, key):
            wait_partner(publish(dst_fn, src))

        self.stage_p0(dict(xT=w("xT"), xres_o=None, xb_o=xbp[0]))
        xres[0] = w("xT")
        P.barrier()
        kx = publish(lambda par: XB[0][bass.ds(par, 1)].rearrange("o d t -> (o d) t"), xbp[0])
        for l in range(DEPTH):
            wU16 = self.dt(f"wU16_{l}", [32, 128, 1024], BF16, "Internal")
            wD16 = self.dt(f"wD16_{l}", [8, 128, 4096], BF16, "Internal")
            cio = dict(wU=wl("wU", l), wD=wl("wD", l), wU16=wU16, wD16=wD16)
            self.stage_a(dict(xall=XB[l], rs=rsp[l], cA=w("cA"), cR=w("cR"), lA=wl("lA", l), wAf=wl("wAf", l), wAt=wl("wAt", l), cache_io=cio,
                              pre_x=(lambda kx=kx: wait_partner(kx))))
            P.barrier()
            kk = publish(lambda par, l=l: RS[l][bass.ds(par, 1)].rearrange("o r t -> (o r) t"), rsp[l])

            def pre_rs(l=l, kk=kk):
                wait_partner(kk)
                P.custom_dma("sp", (lambda e: e.dma_start(out=rsall[l], in_=RS[l].rearrange("h r (two t) -> h r two t", two=2)[:, :, bass.ds(self.parity(e), 1), :]
                                                          .rearrange("h r o t -> h r (o t)"))), "xch2", writes=["rsall_d"])
            lastl = (l == DEPTH - 1)
            io = dict(xres_i=xres[l], xb_i=xbp[l], rsall=rsall[l], wU16=wU16, wD16=wD16, make_cache=False, pre_rs=pre_rs,
                      xres_o=outT if lastl else xres[l + 1], xb_o=None if lastl else xbp[l + 1], cB=w("cB"))
            for nm in ("lB", "lV", "wGu", "wGv", "wGt", "pR", "pS", "pG", "wO", "wU", "wD"):
                io[nm] = wl(nm, l)
            self.stage_b(io)
            P.barrier()
            if not lastl:
                kx = publish(lambda par, l=l: XB[l + 1][bass.ds(par, 1)].rearrange("o d t -> (o d) t"), xbp[l + 1])
        self.ar.release()

    def wire_fused(self):
        EI = "ExternalInput"
        xT = self.dt("xT2", [2, D, TH], F32, EI)
        xres = [self.dt(f"xres_s{l}", [2, D, TH], F32, "Internal") for l in range(DEPTH)]
        xb = [self.dt(f"xb_s{l}", [3 if l == DEPTH - 1 else 2, D, TH], BF16, "Internal") for l in range(DEPTH)]
        rs = [self.dt(f"rs_s{l}", [2, 512, TH if l == DEPTH - 1 else S], BF16, "Internal") for l in range(DEPTH)]
        self.ar.mark()
        zt = self.ar.alloc(TH, BF16)
        self.P.op("pool", lambda e: e.memset(zt, 0.0), writes=["zt"])
        for dc in range(8):
            self.P.dma("sp", xb[DEPTH - 1][0, dc * 128:(dc + 1) * 128, :], zt, "zst", reads=["zt"])
        self.ar.release()
        self.P.barrier()
        outT = self.dt("outT", [D, TH], F32, "ExternalOutput")
        wl = lambda base, l, sfx="": self.dt(f"{base}{l}{sfx}", IN_SHAPES[base + "0"], F32, EI)
        for th in range(2):
            self.stage_p0(dict(xT=xT[th], xres_o=xres[0][th], xb_o=xb[0][th]))
            self.P.barrier()
        for l in range(DEPTH):
            wU16 = self.dt(f"wU16_{l}", [32, 128, 1024], BF16, "Internal")
            wD16 = self.dt(f"wD16_{l}", [8, 128, 4096], BF16, "Internal")
            lastl = (l == DEPTH - 1)
            xin = xb[l]
            if lastl:
                xin = self.dt("xb_shift", [2, D, TH], BF16, "Internal")
                for r in range(2):
                    self.P.custom_dma("sp", (lambda e, r=r: e.dma_start(out=xin[r], in_=xb[l][bass.ds(self.parity(e) + r, 1)].rearrange("o d t -> (o d) t"))),
                                      "xsh")
                self.P.barrier()
            for hh in range(2):
                cio = dict(wU=wl("wU", l), wD=wl("wD", l), wU16=wU16, wD16=wD16) if hh == 0 else None
                self.stage_a(dict(xall=xin, rs=rs[l][hh], cA=self.dt(f"cA_{hh}", IN_SHAPES["cA"], F32, EI),
                                  cR=self.win("cRL" if lastl else "cR"), last=lastl,
                                  lA=wl("lA", l, f"_{hh}"), wAf=wl("wAf", l, f"_{hh}"), wAt=wl("wAt", l, f"_{hh}"), cache_io=cio))
                self.P.barrier()
            for th in range(1 if lastl else 2):
                if lastl:
                    io = dict(xres_i=xres[l], xb_i=xin[1], rsall=rs[l], dyn=True, wU16=wU16, wD16=wD16, make_cache=False)
                    io["xres_o"], io["xb_o"] = outT, None
                else:
                    io = dict(xres_i=xres[l][th], xb_i=xb[l][th], rsall=rs[l][:, :, th * TH:(th + 1) * TH],
                              wU16=wU16, wD16=wD16, make_cache=False)
                    io["xres_o"], io["xb_o"] = xres[l + 1][th], xb[l + 1][(1 + th) if l + 1 == DEPTH - 1 else th]
                io["cB"] = self.win("cB")
                for nm in ("lB", "lV", "wGu", "wGv", "wGt", "pR", "pS", "pG", "wO", "wU", "wD"):
                    io[nm] = wl(nm, l)
                self.stage_b(io)
                self.P.barrier()

    def parity(self, e):
        if getattr(self, "_par", None) is None:
            self._par = e.snap(e.partition_id() % 2, min_val=0, max_val=1)
        return self._par

    def load_cast(self, dst16, src, n, tag, stg, nbuf=2):
        P = self.P
        CH = stg[0].shape[1]
        cnt = getattr(self, "_lc_cnt", 0)
        for o in range(0, n, CH):
            w = min(CH, n - o)
            bi = cnt % nbuf
            cnt += 1
            sb = stg[bi]
            P.dma("sp", sb[:, 0:w], src[:, o:o + w], f"stg{bi}", writes=[("stg", bi)])
            P.op("dve", (lambda e, a=dst16[:, o:o + w], b=sb[:, 0:w]: e.tensor_copy(out=a, in_=b)),
                 reads=[("stg", bi)], writes=[tag])
        self._lc_cnt = cnt

    def cache_chunks(self, cio, stg, c16):
        P = self.P
        k = 0
        for src, dst, nblk, per in ((cio["wU"], cio["wU16"], 32, 1024), (cio["wD"], cio["wD16"], 8, 4096)):
            for blk in range(nblk):
                sflat = src[blk].rearrange("p a b -> p (a b)")
                for o in range(0, per, 1024):
                    def emit(bi=k % len(stg), sflat=sflat, dst=dst, blk=blk, o=o):
                        P.dma("sp", stg[bi], sflat[:, o:o + 1024], f"cstg{bi}", writes=[("cstg", bi)])
                        P.op("dve", (lambda e, a=c16[bi], b=stg[bi]: e.tensor_copy(out=a, in_=b)),
                             reads=[("cstg", bi)], writes=[("cc16", bi)])
                        P.dma("sp", dst[blk][:, o:o + 1024], c16[bi], f"cwc{bi}", reads=[("cc16", bi)])
                    yield emit
                    k += 1

    def stage_p0(self, io):
        P, ar = self.P, self.ar
        xT, xres_d, xb_d = io["xT"], io["xres_o"], io["xb_o"]
        ar.mark()
        x32 = ar.alloc(8 * TH, F32)
        x16 = ar.alloc(8 * TH, BF16)
        for dc in range(8):
            sl = slice(dc * TH, (dc + 1) * TH)
            P.dma("sp", x32[:, sl], xT[dc * 128:(dc + 1) * 128, :], "p0l", writes=[("x32", dc)])
            P.op("pool" if dc % 2 else "dve", (lambda e, a=x16[:, sl], b=x32[:, sl]: e.tensor_copy(out=a, in_=b)),
                 reads=[("x32", dc)], writes=[("x16", dc)])
            if xres_d is not None:
                P.dma("sp", xres_d[dc * 128:(dc + 1) * 128, :], x32[:, sl], "p0s", reads=[("x32", dc)])
            P.dma("sp", xb_d[dc * 128:(dc + 1) * 128, :], x16[:, sl], "p0s", reads=[("x16", dc)])
        ar.release()

    def stage_a(self, io):
        P, ar, ps, psb, psb2 = self.P, self.ar, self.ps, self.psb, self.psb2
        xall, rs_d = io["xall"], io["rs"]
        lastm = io.get("last", False)
        cA_d, lA_d = io["cA"], io["lA"]
        wAf_d, wAt_d = io["wAf"], io["wAt"]
        ar.mark()
        cA = ar.alloc(CA_N, F32)
        lA = ar.alloc(4, F32)
        c16 = ar.alloc(384, BF16)
        P.dma("sp", cA, cA_d, "cA", writes=["cA"])
        P.dma("sp", lA, lA_d, "lA", writes=["lA"])
        P.op("dve", lambda e: e.tensor_copy(out=c16[:, 0:128], in_=cA[:, CA_IDENT:CA_IDENT + 128]), reads=["cA"], writes=["c16a"])
        P.op("dve", lambda e: e.tensor_copy(out=c16[:, 128:384], in_=cA[:, CA_U:CA_U + 256]), reads=["cA"], writes=["c16b"])
        ident, U16, L16 = c16[:, 0:128], c16[:, 128:256], c16[:, 256:384]
        caus = cA[:, CA_CAUS:CA_CAUS + 128]
        rgT = ar.alloc(2 * S, BF16)
        sqT = ar.alloc(2 * S, BF16)
        skT = ar.alloc(2 * S, BF16)
        qdT = ar.alloc(2 * S, BF16)
        kdT = ar.alloc(2 * S, BF16)
        kdk = ar.alloc(32 * 256, BF16)
        vtk = ar.alloc(32 * 512, BF16)
        ar.mark()
        wf = ar.alloc(6 * 1024, BF16)
        wt = ar.alloc(2 * 4096, BF16)
        stg = [ar.alloc(1024, F32) for _ in range(2)]
        cR_d = io["cR"]
        crt = [ar.alloc(512, F32) for _ in range(2)]
        xt = [ar.alloc(8 * 512, BF16) for _ in range(2)]
        qk32 = [ar.alloc(512, F32)] * 2
        qk16 = [ar.alloc(512, BF16) for _ in range(2)]
        tmpr = [ar.alloc(512, F32)] * 2
        for cb in range(6):
            self.load_cast(wf[:, cb * 1024:(cb + 1) * 1024], wAf_d[cb].rearrange("p a b -> p (a b)"), 1024, ("wf", cb), stg)
        for g in range(2):
            self.load_cast(wt[:, g * 4096:(g + 1) * 4096], wAt_d[g].rearrange("p a b -> p (a b)"), 4096, ("wt", g), stg)

        if io.get("pre_x") is not None:
            io["pre_x"]()
        for T in range(8):
            xb_ = xt[T % 2]
            r, t0 = T // 4, (T % 4) * 512
            P.dma("sp", xb_.rearrange("p (dc t) -> p dc t", dc=8),
                  xall[r].rearrange("(dc p) t -> p dc t", p=128)[:, :, t0:t0 + 512],
                  f"xt{T % 2}", writes=[("xt", T % 2)])
            P.dma("sp", crt[T % 2][:, 0:256], cR_d[:, CR_COS + T * 256: CR_COS + (T + 1) * 256], f"cr{T % 2}", writes=[("crt", T % 2)])
            P.dma("sp", crt[T % 2][:, 256:512], cR_d[:, CR_SIN + T * 256: CR_SIN + (T + 1) * 256], f"cr{T % 2}", writes=[("crt", T % 2)])
            for cb in (range(6) if "fm" in SUB else []):
                bank = ps[cb % 2]
                for dc in range(8):
                    P.op("pe", (lambda e, o=bank[:, :], a=wf[:, cb * 1024 + dc * 128: cb * 1024 + (dc + 1) * 128],
                                b=xb_[:, dc * 512:(dc + 1) * 512], st=(dc == 0), sp=(dc == 7):
                                e.matmul(o, lhsT=a, rhs=b, start=st, stop=sp)),
                         reads=[("wf", cb), ("xt", T % 2)], writes=[("ps", cb % 2)])
                if cb < 2:
                    dst = rgT[:, cb * S + T * 512: cb * S + (T + 1) * 512]
                    P.op("act", (lambda e, o=dst, i=bank[:, :]: e.activation(out=o, in_=i, func=AF.Silu)),
                         reads=[("ps", cb % 2)], writes=[("rgT", cb, T)])
                elif cb < 4:
                    dst = sqT[:, (cb - 2) * S + T * 512: (cb - 2) * S + (T + 1) * 512]
                    P.op("act", (lambda e, o=dst, i=bank[:, :]: e.activation(out=o, in_=i, func=AF.Copy, scale=0.125)),
                         reads=[("ps", cb % 2)], writes=[("sqT", cb - 2, T)])
                else:
                    dst = skT[:, (cb - 4) * S + T * 512: (cb - 4) * S + (T + 1) * 512]
                    P.op("dve", (lambda e, o=dst, i=bank[:, :]: e.tensor_copy(out=o, in_=i)),
                         reads=[("ps", cb % 2)], writes=[("skT", cb - 4, T)])
            for q in (range(4) if "tm" in SUB else []):
                n = T * 4 + q
                pq, pv = ps[2 + (n % 2)], ps[4 + (n % 2)]
                for g, bank in ((0, pq), (1, pv)):
                    for dc in range(8):
                        P.op("pe", (lambda e, o=bank[:, :], a=xb_[:, dc * 512 + q * 128: dc * 512 + (q + 1) * 128],
                                    b=wt[:, g * 4096 + dc * 512: g * 4096 + (dc + 1) * 512], st=(dc == 0), sp=(dc == 7):
                                    e.matmul(o, lhsT=a, rhs=b, start=st, stop=sp)),
                             reads=[("wt", g), ("xt", T % 2)], writes=[("ps", 2 + 2 * g + (n % 2))])
                P.op("act", (lambda e, o=vtk[:, n * 512:(n + 1) * 512], i=pv[:, :]: e.copy(out=o, in_=i)),
                     reads=[("ps", 4 + (n % 2))], writes=[("vtk", n)])
                if "rot" not in SUB:
                    continue
                A32, T32, O16 = qk32[n % 2], tmpr[n % 2], qk16[n % 2]
                X = pq[:, :].rearrange("p (g two f) -> p g two f", g=4, two=2)
                A4 = A32.rearrange("p (g two f) -> p g two f", g=4, two=2)
                T4 = T32.rearrange("p (g two f) -> p g two f", g=4, two=2)
                cosb = crt[T % 2][:, q * 64:(q + 1) * 64].unsqueeze(1).to_broadcast([128, 4, 64])
                sinb = crt[T % 2][:, 256 + q * 64: 256 + (q + 1) * 64].unsqueeze(1).to_broadcast([128, 4, 64])
                rk_ = [("ps", 2 + (n % 2)), ("crt", T % 2)]
                P.op("dve", (lambda e, o=A4[:, :, 0, :], a=X[:, :, 0, :], b=cosb: e.tensor_tensor(out=o, in0=a, in1=b, op=ALU.mult)),
                     reads=rk_, writes=[("A32a", 0)])
                P.op("dve", (lambda e, o=A4[:, :, 1, :], a=X[:, :, 1, :], b=cosb: e.tensor_tensor(out=o, in0=a, in1=b, op=ALU.mult)),
                     reads=rk_, writes=[("A32b", 0)])
                P.op("dve", (lambda e, o=T4[:, :, 0, :], a=X[:, :, 1, :], b=sinb: e.tensor_tensor(out=o, in0=a, in1=b, op=ALU.mult)),
                     reads=rk_, writes=[("T32a", 0)])
                P.op("dve", (lambda e, o=T4[:, :, 1, :], a=X[:, :, 0, :], b=sinb: e.tensor_tensor(out=o, in0=a, in1=b, op=ALU.mult)),
                     reads=rk_, writes=[("T32b", 0)])
                P.op("pool", (lambda e, o=A4[:, :, 0, :], a=A4[:, :, 0, :], b=T4[:, :, 0, :]: e.tensor_tensor(out=o, in0=a, in1=b, op=ALU.subtract)),
                     reads=[("A32a", 0), ("T32a", 0)], writes=[("A32a", 0)])
                P.op("pool", (lambda e, o=A4[:, :, 1, :], a=A4[:, :, 1, :], b=T4[:, :, 1, :]: e.tensor_tensor(out=o, in0=a, in1=b, op=ALU.add)),
                     reads=[("A32b", 0), ("T32b", 0)], writes=[("A32b", 0)])
                decb = cA[:, CA_DEC:CA_DEC + 4].unsqueeze(2).to_broadcast([128, 4, 128])
                P.op("pool", (lambda e, o=O16.rearrange("p (g f) -> p g f", g=4), a=A32.rearrange("p (g f) -> p g f", g=4), b=decb:
                              e.tensor_tensor(out=o, in0=a, in1=b, op=ALU.mult)),
                     reads=[("A32a", 0), ("A32b", 0), "cA"], writes=[("qk16", n % 2)])
                P.op("pool", (lambda e, o=kdk[:, n * 256:(n + 1) * 256], i=O16[:, 256:512]: e.tensor_copy(out=o, in_=i)),
                     reads=[("qk16", n % 2)], writes=[("kdk", n)])
                if "tr" not in SUB:
                    continue
                for g in range(4):
                    pT = psb if g < 2 else psb2
                    P.op("pe", (lambda e, o=pT[:, (g % 2) * 128:(g % 2 + 1) * 128], i=O16[:, g * 128:(g + 1) * 128]:
                                e.transpose(out=o, in_=i, identity=ident)),
                         reads=[("qk16", n % 2), "c16a"], writes=["psb" if g < 2 else "psb2"])
                for h in range(2):
                    P.op("act", (lambda e, o=qdT[:, h * S + n * 128: h * S + (n + 1) * 128], i=psb[:, h * 128:(h + 1) * 128]: e.copy(out=o, in_=i)),
                         reads=["psb"], writes=[("qdT", h, n)])
                    P.op("dve", (lambda e, o=kdT[:, h * S + n * 128: h * S + (n + 1) * 128], i=psb2[:, h * 128:(h + 1) * 128]: e.tensor_copy(out=o, in_=i)),
                         reads=["psb2"], writes=[("kdT", h, n)])
        ar.release()
        P.barrier()

        ar.mark()
        if "ret" not in PARTS:
            ar.release(); ar.release(); return
        rso = ar.alloc(2 * S, BF16)
        st32 = ar.alloc(256, F32)
        st16 = ar.alloc(256, BF16)
        std = [ar.alloc(256, BF16) for _ in range(2)]
        nrm = [ar.alloc(256, BF16) for _ in range(2)]
        stt = [ar.alloc(32, F32) for _ in range(2)]
        tmpg = [ar.alloc(256, F32) for _ in range(2)]
        for n in range(32):
            pb = n % 2
            pS, pO, pK = ps[0 + pb], ps[2 + pb], ps[4 + pb]
            H = [(h, slice(h * S + n * 128, h * S + (n + 1) * 128), slice(h * 128, (h + 1) * 128)) for h in range(2)]
            qry = not (lastm and n < 16)
            for h, csl, hs in (H if qry else []):
                P.op("pe", (lambda e, o=pS[:, hs], a=kdT[:, csl], b=qdT[:, csl]: e.matmul(o, lhsT=a, rhs=b, start=True, stop=True)),
                     reads=[("kdT", h, n), ("qdT", h, n)], writes=[("pS", pb)])
            for h, csl, hs in (H if qry else []):
                P.op("dve", (lambda e, o=std[pb][:, hs], a=pS[:, hs], s_=cA[:, CA_CH + h:CA_CH + h + 1], m=caus:
                             e.scalar_tensor_tensor(out=o, in0=a, scalar=s_, in1=m, op0=ALU.mult, op1=ALU.mult)),
                     reads=[("pS", pb), "cA"], writes=[("std", pb, h)])
            for h, csl, hs in (H if qry else []):
                vsl = vtk[:, n * 512 + h * 128: n * 512 + (h + 1) * 128]
                P.op("pe", (lambda e, o=pO[:, hs], a=std[pb][:, hs], b=vsl, sp=(n == 0): e.matmul(o, lhsT=a, rhs=b, start=True, stop=sp)),
                     reads=[("std", pb, h), ("vtk", n)], writes=[("pO", pb)])
                if n > 0:
                    P.op("pe", (lambda e, o=pO[:, hs], a=qdT[:, csl], b=st16[:, hs]: e.matmul(o, lhsT=a, rhs=b, start=False, stop=True)),
                         reads=[("qdT", h, n), ("st16", h)], writes=[("pO", pb)])
            for h, csl, hs in H:
                vsl = vtk[:, n * 512 + h * 128: n * 512 + (h + 1) * 128]
                P.op("pe", (lambda e, o=pK[:, hs], a=kdk[:, n * 256 + h * 128: n * 256 + (h + 1) * 128], b=vsl: e.matmul(o, lhsT=a, rhs=b, start=True, stop=True)),
                     reads=[("kdk", n), ("vtk", n)], writes=[("pK", pb)])
            for h, csl, hs in H:
                if n == 0:
                    P.op("dve", (lambda e, o=st32[:, hs], i=pK[:, hs]: e.tensor_copy(out=o, in_=i)),
                         reads=[("pK", pb)], writes=[("st32", h)])
                else:
                    P.op("dve", (lambda e, o=st32[:, hs], a=st32[:, hs], s_=cA[:, CA_CD + h:CA_CD + h + 1], b=pK[:, hs]:
                                 e.scalar_tensor_tensor(out=o, in0=a, scalar=s_, in1=b, op0=ALU.mult, op1=ALU.add)),
                         reads=[("pK", pb), ("st32", h), "cA"], writes=[("st32", h)])
                P.op("pool", (lambda e, o=st16[:, hs], i=st32[:, hs]: e.tensor_copy(out=o, in_=i)),
                     reads=[("st32", h)], writes=[("st16", h)])
            if not qry:
                continue
            sv = stt[pb]
            for h, csl, hs in H:
                b0 = h * 16
                P.op("dve", (lambda e, o=sv[:, b0:b0 + 6], i=pO[:, hs]: e.bn_stats(out=o, in_=i)),
                     reads=[("pO", pb)], writes=[("stt", pb, h)])
                P.op("dve", (lambda e, o=sv[:, b0 + 8:b0 + 10], i=sv[:, b0:b0 + 6]: e.bn_aggr(out=o, in_=i)),
                     reads=[("stt", pb, h)], writes=[("stt", pb, h)])
            for h, csl, hs in H:
                b0 = h * 16
                P.op("act", (lambda e, o=sv[:, b0 + 11:b0 + 12], i=sv[:, b0 + 9:b0 + 10]: e.activation(out=o, in_=i, func=AF.Ln, bias=EPS)),
                     reads=[("stt", pb, h)], writes=[("stt", pb, h)])
            for h, csl, hs in H:
                b0 = h * 16
                P.op("act", (lambda e, o=sv[:, b0 + 10:b0 + 11], i=sv[:, b0 + 11:b0 + 12]: e.activation(out=o, in_=i, func=AF.Exp, scale=-0.5)),
                     reads=[("stt", pb, h)], writes=[("stt", pb, h)])
            for h, csl, hs in H:
                b0 = h * 16
                P.op("dve", (lambda e, o=nrm[pb][:, hs], a=pO[:, hs], m=sv[:, b0 + 8:b0 + 9], r=sv[:, b0 + 10:b0 + 11]:
                             e.tensor_scalar(out=o, in0=a, scalar1=m, scalar2=r, op0=ALU.subtract, op1=ALU.mult)),
                     reads=[("pO", pb), ("stt", pb, h)], writes=[("nrm", pb, h)])
            pT = psb if pb == 0 else psb2
            for h, csl, hs in H:
                P.op("pe", (lambda e, o=pT[:, hs], i=nrm[pb][:, hs]: e.transpose(out=o, in_=i, identity=ident)),
                     reads=[("nrm", pb, h), "c16a"], writes=[("psbr", pb)])
            for h, csl, hs in H:
                P.op("dve", (lambda e, o=tmpg[pb][:, hs], a=pT[:, hs], g=lA[:, h:h + 1], b=lA[:, 2 + h:3 + h]:
                             e.tensor_scalar(out=o, in0=a, scalar1=g, scalar2=b, op0=ALU.mult, op1=ALU.add)),
                     reads=[("psbr", pb), "lA"], writes=[("tmpg", pb, h)])
                P.op("pool", (lambda e, o=rso[:, csl], a=tmpg[pb][:, hs], b=rgT[:, csl]: e.tensor_tensor(out=o, in0=a, in1=b, op=ALU.mult)),
                     reads=[("tmpg", pb, h), ("rgT", h, n // 4)], writes=[("rso", h)])
        for h in range(2):
            P.dma("sp", rs_d[h * 128:(h + 1) * 128, :], rso[:, h * S + (TH if lastm else 0):(h + 1) * S], "rsst", reads=[("rso", h)])
        P.barrier()
        ar.release()

        ar.mark()
        if "sb" not in PARTS:
            ar.release(); ar.release(); return
        sbo = ar.alloc(4 * S, BF16, parts=64)
        e32 = [ar.alloc(512, F32) for _ in range(4)]
        sp16 = [ar.alloc(512, BF16) for _ in range(4)]
        w32 = [ar.alloc(512, F32) for _ in range(2)]
        a16 = [ar.alloc(512, BF16) for _ in range(4)]
        sbm = cA[:, CA_SBM:CA_SBM + 2048]
        cgen = None
        if io.get("cache_io") is not None:
            cstg = [ar.alloc(1024, F32) for _ in range(2)]
            cc16 = [ar.alloc(1024, BF16) for _ in range(2)]
            cgen = self.cache_chunks(io["cache_io"], cstg, cc16)
        step_i = 0

        def emit_z(s, hd, T, kb):
            base, pr = (hd % 2) * 64, hd // 2
            c0 = max(0, kb - 4 * T) * 128
            P.op("pe", (lambda e, o=ps[s][:, c0:], a=skT[base:base + 64, pr * S + kb * 128: pr * S + (kb + 1) * 128],
                        b=sqT[base:base + 64, pr * S + T * 512 + c0: pr * S + (T + 1) * 512]: e.matmul(o, lhsT=a, rhs=b, start=True, stop=True)),
                 reads=[("skT", pr, kb // 4), ("sqT", pr, T)], writes=[("pz", s)])

        for pr in range(2):
            for T in (range(4, 8) if lastm else range(8)):
                kbs = list(range(4 * T + 3, -1, -1))
                for s in range(2):
                    emit_z(s, pr * 2 + s, T, kbs[0])
                for ki, kb in enumerate(kbs):
                    first, last = (ki == 0), (ki == len(kbs) - 1)
                    c0 = max(0, kb - 4 * T) * 128
                    step_i += 1
                    pj = step_i % 2
                    if cgen is not None and step_i % 3 == 0:
                        em = next(cgen, None)
                        if em is not None:
                            em()
                    for s in range(2):
                        P.op("act", (lambda e, o=e32[s + 2 * pj][:, c0:], i=ps[s][:, c0:]: e.activation(out=o, in_=i, func=AF.Exp)),
                             reads=[("pz", s)], writes=[("e32", s, pj)])
                    if kb >= 4 * T:
                        r = kb - 4 * T
                        for s in range(2):
                            P.op("dve", (lambda e, o=e32[s + 2 * pj][:, c0:], a=e32[s + 2 * pj][:, c0:], m=sbm[:, r * 512 + c0:(r + 1) * 512]: e.tensor_tensor(out=o, in0=a, in1=m, op=ALU.mult)),
                                 reads=[("e32", s, pj), "cA"], writes=[("e32", s, pj)])
                    for s in range(2):
                        P.op("act", (lambda e, o=sp16[s + 2 * pj][:, c0:], i=e32[s + 2 * pj][:, c0:]: e.activation(out=o, in_=i, func=AF.Ln, bias=1.0)),
                             reads=[("e32", s, pj)], writes=[("sp16", s, pj)])
                    for s in range(2):
                        P.op("pe", (lambda e, o=ps[2 + s][:, c0:], b=sp16[s + 2 * pj][:, c0:], st=first: e.matmul(o, lhsT=U16, rhs=b, start=st, stop=False, skip_group_check=True)),
                             reads=[("sp16", s, pj), "c16b"], writes=[("pR", s)])
                    if not last:
                        for s in range(2):
                            emit_z(s, pr * 2 + s, T, kbs[ki + 1])
                    for s in range(2):
                        P.op("act", (lambda e, o=w32[s][:, c0:], i=ps[2 + s][:, c0:]: e.activation(out=o, in_=i, func=AF.Exp, scale=-1.0)),
                             reads=[("pR", s)], writes=[("w32", s)])
                    for s in range(2):
                        P.op("pe", (lambda e, o=ps[2 + s][:, c0:], b=sp16[s + 2 * pj][:, c0:], sp_=last: e.matmul(o, lhsT=L16, rhs=b, start=False, stop=True if sp_ else False, skip_group_check=True)),
                             reads=[("sp16", s, pj), "c16b"], writes=[("pR", s)])
                    for s in range(2):
                        P.op("dve", (lambda e, o=a16[s + 2 * pj][:, c0:], a=e32[s + 2 * pj][:, c0:], b=w32[s][:, c0:]: e.tensor_tensor(out=o, in0=a, in1=b, op=ALU.mult)),
                             reads=[("e32", s, pj), ("w32", s)], writes=[("a16", s, pj)])
                    for s in range(2):
                        hd = pr * 2 + s
                        P.op("pe", (lambda e, o=ps[4 + s][0:64, c0:], a=vtk[:, kb * 512 + 256 + hd * 64: kb * 512 + 256 + (hd + 1) * 64], b=a16[s + 2 * pj][:, c0:], st=first, sp_=last:
                                    e.matmul(o, lhsT=a, rhs=b, start=st, stop=sp_, skip_group_check=True)),
                             reads=[("a16", s, pj), ("vtk", kb)], writes=[("po", s)])
                for s in range(2):
                    hd = pr * 2 + s
                    P.op("act" if s else "dve",
                         (lambda e, o=sbo[:, hd * S + T * 512: hd * S + (T + 1) * 512], i=ps[4 + s][0:64, :], s_=s:
                          (e.copy(out=o, in_=i) if s_ else e.tensor_copy(out=o, in_=i))),
                         reads=[("po", s)], writes=[("sbo", hd)])
        if cgen is not None:
            for em in cgen:
                em()
        for hd in range(4):
            P.dma("sp", rs_d[256 + hd * 64: 256 + (hd + 1) * 64, :], sbo[:, hd * S + (TH if lastm else 0):(hd + 1) * S], "rsst", reads=[("sbo", hd)])
        P.barrier()
        ar.release()
        ar.release()

    def stage_b(self, io):
        P, ar, ps, psb = self.P, self.ar, self.ps, self.psb
        xres_d, xb_d, rsa_d = io["xres_i"], io["xb_i"], io["rsall"]
        xres_o, xb_o = io["xres_o"], io["xb_o"]
        cB_d, lB_d = io["cB"], io["lB"]
        wGu_d, wGv_d, wGt_d = io["wGu"], io["wGv"], io["wGt"]
        pR_d, pS_d, pG_d = io["pR"], io["pS"], io["pG"]
        wO_d, wU_d, wD_d = io["wO"], io["wU"], io["wD"]
        wU16, wD16 = io["wU16"], io["wD16"]

        ar.mark()
        cB = ar.alloc(CB_N, F32)
        lV = ar.alloc(LV_N, F32)
        P.dma("sp", cB, cB_d, "cB", writes=["cB"])
        P.dma("sp", lV, io["lV"], "lV", writes=["lV"])
        onesF = cB[:, CB_ONES:CB_ONES + 128]
        xres = ar.alloc(8 * TH, F32)
        xb = ar.alloc(8 * TH, BF16)
        stg = [ar.alloc(1024, F32) for _ in range(2)]
        XR8 = [("xres", dc) for dc in range(8)]
        for dc in range(8):
            P.dma("sp", xb[:, dc * TH:(dc + 1) * TH], xb_d[dc * 128:(dc + 1) * 128, :], "bld", writes=[("xb", dc)])

        def load_xres():
            if io.get("dyn"):
                P.custom_dma("sp", (lambda e, o=xres.rearrange("p (dc t) -> p dc t", dc=8):
                                    e.dma_start(out=o, in_=xres_d[bass.ds(self.parity(e), 1)].rearrange("o (dc p) t -> p (o dc) t", p=128))),
                             "bld", writes=XR8)
            else:
                for dc in range(8):
                    P.dma("sp", xres[:, dc * TH:(dc + 1) * TH], xres_d[dc * 128:(dc + 1) * 128, :], "bld", writes=[("xres", dc)])
        XR = [("xres", dc) for dc in range(8)]
        XB = [("xb", dc) for dc in range(8)]

        ar.mark()
        c16 = [ar.alloc(1024, BF16) for _ in range(2)]
        k = 0
        for src, dst, nblk, per, wk in (((wU_d, wU16, 32, 1024, "wcU"), (wD_d, wD16, 8, 4096, "wcD")) if io["make_cache"] else ()):
            for blk in range(nblk):
                sflat = src[blk].rearrange("p a b -> p (a b)")
                for o in range(0, per, 1024):
                    w = min(1024, per - o)
                    bi = k % 2
                    k += 1
                    P.dma("sp", stg[bi][:, 0:w], sflat[:, o:o + w], f"stg{bi}", writes=[("stg", bi)])
                    P.op("pool" if bi else "dve", (lambda e, a=c16[bi][:, 0:w], b=stg[bi][:, 0:w]: e.tensor_copy(out=a, in_=b)),
                         reads=[("stg", bi)], writes=[("c16", bi)])
                    P.dma("sp", dst[blk][:, o:o + w], c16[bi][:, 0:w], f"wc{bi}", reads=[("c16", bi)], writes=[(wk, blk, o)])
        ar.release()
        if io["make_cache"]:
            P.barrier()

        ar.mark()
        rsf = ar.alloc(8 * TH, BF16)

        def load_rsf():
          for hh in range(2):
            for c4 in range(2):
                for base, slot in ((0, hh * 2 + c4), (256, 4 + hh * 2 + c4)):
                    dst = rsf[:, slot * TH:(slot + 1) * TH]
                    if io.get("dyn_rs"):
                        P.custom_dma("sp", (lambda e, o=dst, hh=hh, r0=base + c4 * 128:
                                            e.dma_start(out=o, in_=rsa_d.rearrange("h r (two t) -> h r two t", two=2)[hh, r0:r0 + 128, bass.ds(self.parity(e), 1), :]
                                                        .rearrange("p o t -> p (o t)"))),
                                     "bld", writes=[("rsf", slot)])
                    else:
                        P.dma("sp", dst, rsa_d[hh, base + c4 * 128: base + (c4 + 1) * 128, :], "bld", reads=["rsall_d"], writes=[("rsf", slot)])
        sgT = ar.alloc(4 * TH, BF16)
        ar.mark()
        lB = ar.alloc(LB_N, F32)
        P.dma("sp", lB, lB_d, "lB", writes=["lB"])
        wgu = ar.alloc(4 * 1024, BF16)
        wgv = ar.alloc(4096, BF16)
        wsg = ar.alloc(512, BF16)
        guT = [ar.alloc(4 * 512, BF16) for _ in range(2)]
        g32 = [ar.alloc(512, F32) for _ in range(2)]
        vn = [ar.alloc(512, BF16) for _ in range(2)]
        stt = [ar.alloc(32, F32) for _ in range(2)]
        t32 = [ar.alloc(512, F32) for _ in range(2)]
        for cb in range(4):
            self.load_cast(wgu[:, cb * 1024:(cb + 1) * 1024], wGu_d[cb].rearrange("p a b -> p (a b)"), 1024, ("wgu", cb), stg)
        self.load_cast(wgv, wGv_d[0].rearrange("p a b -> p (a b)"), 4096, "wgv", stg)
        if io.get("pre_rs") is not None:
            io["pre_rs"]()
        load_rsf()
        load_xres()
        causb = cB[:, CB_CAUS:CB_CAUS + 128].unsqueeze(1).to_broadcast([128, 4, 128])
        P.op("dve", (lambda e: e.tensor_tensor(out=wsg.rearrange("p (g i) -> p g i", g=4), in0=lB[:, LB_WT:LB_WT + 512].rearrange("p (g i) -> p g i", g=4),
                                               in1=causb, op=ALU.mult)), reads=["lB", "cB"], writes=["wsg"])
        for T in range(4):
            gb = guT[T % 2]
            for cb in range(4):
                bank = ps[cb % 2]
                for dc in range(8):
                    P.op("pe", (lambda e, o=bank[:, :], a=wgu[:, cb * 1024 + dc * 128: cb * 1024 + (dc + 1) * 128],
                                b=xb[:, dc * TH + T * 512: dc * TH + (T + 1) * 512], st=(dc == 0), sp=(dc == 7): e.matmul(o, lhsT=a, rhs=b, start=st, stop=sp)),
                         reads=[("wgu", cb), ("xb", dc)], writes=[("ps", cb % 2)])
                P.op("act", (lambda e, o=gb[:, cb * 512:(cb + 1) * 512], i=bank[:, :]: e.activation(out=o, in_=i, func=AF.Gelu_apprx_tanh)),
                     reads=[("ps", cb % 2)], writes=[("guT", T % 2, cb)])
            for q in range(4):
                n = T * 4 + q
                pb = n % 2
                pv, psv = ps[2 + pb], ps[4 + pb]
                for dc in range(8):
                    P.op("pe", (lambda e, o=pv[:, :], a=xb[:, dc * TH + n * 128: dc * TH + (n + 1) * 128], b=wgv[:, dc * 512:(dc + 1) * 512], st=(dc == 0), sp=(dc == 7):
                                e.matmul(o, lhsT=a, rhs=b, start=st, stop=sp)),
                         reads=["wgv", ("xb", dc)], writes=[("ps", 2 + pb)])
                P.op("act", (lambda e, o=g32[pb], i=pv[:, :]: e.activation(out=o, in_=i, func=AF.Gelu_apprx_tanh)),
                     reads=[("ps", 2 + pb)], writes=[("g32", pb)])
                sv = stt[pb]
                P.op("dve", (lambda e, o=sv[:, 0:6], i=g32[pb]: e.bn_stats(out=o, in_=i)), reads=[("g32", pb)], writes=[("stt", pb)])
                P.op("dve", (lambda e, o=sv[:, 8:10], i=sv[:, 0:6]: e.bn_aggr(out=o, in_=i)), reads=[("stt", pb)], writes=[("stt", pb)])
                P.op("act", (lambda e, o=sv[:, 11:12], i=sv[:, 9:10]: e.activation(out=o, in_=i, func=AF.Ln, bias=EPS)), reads=[("stt", pb)], writes=[("stt", pb)])
                P.op("act", (lambda e, o=sv[:, 10:11], i=sv[:, 11:12]: e.activation(out=o, in_=i, func=AF.Exp, scale=-0.5)), reads=[("stt", pb)], writes=[("stt", pb)])
                P.op("dve", (lambda e, o=t32[pb], a=g32[pb], m=sv[:, 8:9], r=sv[:, 10:11]: e.tensor_scalar(out=o, in0=a, scalar1=m, scalar2=r, op0=ALU.subtract, op1=ALU.mult)),
                     reads=[("g32", pb), ("stt", pb)], writes=[("t32", pb)])
                P.op("pool", (lambda e, o=t32[pb], a=t32[pb], b=lB[:, LB_LNG:LB_LNG + 512]: e.tensor_tensor(out=o, in0=a, in1=b, op=ALU.mult)),
                     reads=[("t32", pb), "lB"], writes=[("t32", pb)])
                P.op("pool", (lambda e, o=vn[pb], a=t32[pb], b=lB[:, LB_LNB:LB_LNB + 512]: e.tensor_tensor(out=o, in0=a, in1=b, op=ALU.add)),
                     reads=[("t32", pb), "lB"], writes=[("vn", pb)])
                for g in range(4):
                    P.op("pe", (lambda e, o=psv[:, g * 128:(g + 1) * 128], a=vn[pb][:, g * 128:(g + 1) * 128], b=wsg[:, g * 128:(g + 1) * 128]:
                                e.matmul(o, lhsT=a, rhs=b, start=True, stop=True)),
                         reads=[("vn", pb), "wsg"], writes=[("ps", 4 + pb)])
                P.op("dve", (lambda e, o=t32[pb], a=psv[:, :], b=lB[:, LB_SB:LB_SB + 512]: e.tensor_tensor(out=o, in0=a, in1=b, op=ALU.add)),
                     reads=[("ps", 4 + pb), ("t32", pb), "lB"], writes=[("t32", pb)])
                gview = gb.rearrange("p (g t) -> p g t", g=4)[:, :, q * 128:(q + 1) * 128]
                oview = sgT.rearrange("p (g t) -> p g t", g=4)[:, :, n * 128:(n + 1) * 128]
                P.op("pool", (lambda e, o=oview, a=t32[pb].rearrange("p (g i) -> p g i", g=4), b=gview: e.tensor_tensor(out=o, in0=a, in1=b, op=ALU.mult)),
                     reads=[("t32", pb)] + [("guT", T % 2, cb) for cb in range(4)], writes=[("sgT", n)])
        ar.release()
        P.barrier()
        SG = [("sgT", n) for n in range(16)]
        RS = [("rsf", i) for i in range(8)]

        mg = ar.alloc(8 * TH, BF16)
        ar.mark()
        wg = [ar.alloc(3 * 1024, BF16)] * 2
        wp = [ar.alloc(3 * 512, BF16)] * 2
        sg32 = [ar.alloc(512, F32) for _ in range(2)]
        m32 = [ar.alloc(512, F32) for _ in range(2)]
        srcs = [(pR_d, 0, "ret"), (pS_d, 4, "sb"), (pG_d, None, "sgu")]
        for cb in range(8):
            wb = 0
            for br in range(3):
                self.load_cast(wg[wb][:, br * 1024:(br + 1) * 1024], wGt_d[br * 8 + cb].rearrange("p a b -> p (a b)"), 1024, ("wg", wb, br), stg)
                self.load_cast(wp[wb][:, br * 512:(br + 1) * 512], srcs[br][0][cb].rearrange("p a b -> p (a b)"), 512, ("wp", wb, br), stg)
            for T in range(4):
                for br in range(3):
                    j = (T * 3 + br) % 2
                    pg, pp = ps[j], ps[2 + j]
                    for dc in range(8):
                        P.op("pe", (lambda e, o=pg[:, :], a=wg[wb][:, br * 1024 + dc * 128: br * 1024 + (dc + 1) * 128],
                                    b=xb[:, dc * TH + T * 512: dc * TH + (T + 1) * 512], st=(dc == 0), sp=(dc == 7): e.matmul(o, lhsT=a, rhs=b, start=st, stop=sp)),
                             reads=[("wg", wb, br), ("xb", dc)], writes=[("ps", j)])
                    P.op("act", (lambda e, o=sg32[j], i=pg[:, :]: e.activation(out=o, in_=i, func=AF.Sigmoid)),
                         reads=[("ps", j)], writes=[("sg32", j)])
                    for kc in range(4):
                        if br < 2:
                            rhs = rsf[:, (srcs[br][1] + kc) * TH + T * 512: (srcs[br][1] + kc) * TH + (T + 1) * 512]
                            rk = [("rsf", srcs[br][1] + kc)]
                        else:
                            rhs = sgT[:, kc * TH + T * 512: kc * TH + (T + 1) * 512]
                            rk = SG[T * 4:(T + 1) * 4]
                        P.op("pe", (lambda e, o=pp[:, :], a=wp[wb][:, br * 512 + kc * 128: br * 512 + (kc + 1) * 128], b=rhs, st=(kc == 0), sp=(kc == 3):
                                    e.matmul(o, lhsT=a, rhs=b, start=st, stop=sp)),
                             reads=[("wp", wb, br)] + rk, writes=[("ps", 2 + j)])
                    mt = m32[T % 2]
                    if br == 0:
                        P.op("dve", (lambda e, o=mt, a=pp[:, :], b=sg32[j]: e.tensor_tensor(out=o, in0=a, in1=b, op=ALU.mult)),
                             reads=[("ps", 2 + j), ("sg32", j)], writes=[("m32", T % 2)])
                    else:
                        P.op("dve", (lambda e, o=sg32[j], a=pp[:, :], b=sg32[j]: e.tensor_tensor(out=o, in0=a, in1=b, op=ALU.mult)),
                             reads=[("ps", 2 + j), ("sg32", j)], writes=[("sg32", j)])
                        dst = mt if br == 1 else mg[:, cb * TH + T * 512: cb * TH + (T + 1) * 512]
                        wk = [("m32", T % 2)] if br == 1 else [("mg", cb)]
                        P.op("pool", (lambda e, o=dst, a=mt, b=sg32[j]: e.tensor_tensor(out=o, in0=a, in1=b, op=ALU.add)),
                             reads=[("m32", T % 2), ("sg32", j)], writes=wk)
        ar.release()
        P.barrier()

        ar.mark()
        wo = [ar.alloc(1024, BF16) for _ in range(2)]
        for cb in range(8):
            wb = cb % 2
            self.load_cast(wo[wb], wO_d[cb].rearrange("p a b -> p (a b)"), 1024, ("wo", wb), stg)
            for T in range(4):
                bank = ps[T % 2]
                for kc in range(8):
                    P.op("pe", (lambda e, o=bank[:, :], a=wo[wb][:, kc * 128:(kc + 1) * 128], b=mg[:, kc * TH + T * 512: kc * TH + (T + 1) * 512], st=(kc == 0), sp=(kc == 7):
                                e.matmul(o, lhsT=a, rhs=b, start=st, stop=sp)),
                         reads=[("wo", wb), ("mg", kc)], writes=[("ps", T % 2)])
                xs = xres[:, cb * TH + T * 512: cb * TH + (T + 1) * 512]
                P.op("dve", (lambda e, o=xs, a=xs, b=bank[:, :]: e.scalar_tensor_tensor(out=o, in0=a, scalar=ALPHA, in1=b, op0=ALU.mult, op1=ALU.add)),
                     reads=[("ps", T % 2), ("xres", cb)], writes=[("xres", cb)])
        ar.release()
        ar.release()
        self.layer_norm_fm(xres, xb, lV, LB_L1G, LB_L1B, onesF)
        P.barrier()

        ar.mark()
        hT = ar.alloc(32 * 512, BF16)
        wu = [ar.alloc(4096, BF16) for _ in range(2)]
        wd = [ar.alloc(4096, BF16) for _ in range(2)]
        r32 = [ar.alloc(512, F32) for _ in range(2)]
        for T in range(4):
            for f4 in range(8):
                wb = f4 % 2
                for i in range(4):
                    P.dma("sp", wu[wb][:, i * 1024:(i + 1) * 1024], wU16[f4 * 4 + i], f"wu{wb}", writes=[("wu", wb)])
                for i in range(4):
                    fb = f4 * 4 + i
                    bank = ps[fb % 2]
                    for dc in range(8):
                        P.op("pe", (lambda e, o=bank[:, :], a=wu[wb][:, i * 1024 + dc * 128: i * 1024 + (dc + 1) * 128],
                                    b=xb[:, dc * TH + T * 512: dc * TH + (T + 1) * 512], st=(dc == 0), sp=(dc == 7): e.matmul(o, lhsT=a, rhs=b, start=st, stop=sp)),
                             reads=[("wu", wb), ("xb", dc)], writes=[("ps", fb % 2)])
                    P.op("act", (lambda e, o=r32[fb % 2], i_=bank[:, :]: e.activation(out=o, in_=i_, func=AF.Relu)),
                         reads=[("ps", fb % 2)], writes=[("r32", fb % 2)])
                    P.op("dve", (lambda e, o=hT[:, fb * 512:(fb + 1) * 512], a=r32[fb % 2]: e.tensor_tensor(out=o, in0=a, in1=a, op=ALU.mult)),
                         reads=[("r32", fb % 2)], writes=[("hT", fb)])
            for cb in range(8):
                wb = cb % 2
                P.dma("sp", wd[wb], wD16[cb], f"wd{wb}", writes=[("wd", wb)])
                bank = ps[2 + cb % 2]
                for fc in range(32):
                    P.op("pe", (lambda e, o=bank[:, :], a=wd[wb][:, fc * 128:(fc + 1) * 128], b=hT[:, fc * 512:(fc + 1) * 512], st=(fc == 0), sp=(fc == 31):
                                e.matmul(o, lhsT=a, rhs=b, start=st, stop=sp)),
                         reads=[("wd", wb), ("hT", fc)], writes=[("ps", 2 + cb % 2)])
                xs = xres[:, cb * TH + T * 512: cb * TH + (T + 1) * 512]
                P.op("dve", (lambda e, o=xs, a=xs, b=bank[:, :]: e.scalar_tensor_tensor(out=o, in0=a, scalar=ALPHA, in1=b, op0=ALU.mult, op1=ALU.add)),
                     reads=[("ps", 2 + cb % 2), ("xres", cb)], writes=[("xres", cb)])
        ar.release()
        P.barrier()
        self.layer_norm_fm(xres, xb, lV, LB_L2G, LB_L2B, onesF, store=(xres_o, xb_o))
        P.barrier()
        ar.release()

    def layer_norm_fm(self, xres, xb, lB, og, ob, onesF, store=None):
        P, ar, ps = self.P, self.ar, self.ps
        P.barrier()
        ar.mark()
        usq = ar.alloc(8 * 512, F32)
        mean = [ar.alloc(512, F32) for _ in range(2)]
        rstd = [ar.alloc(512, F32) for _ in range(2)]
        vtm = [ar.alloc(512, F32) for _ in range(2)]
        tmp = [ar.alloc(512, F32) for _ in range(3)]
        banks = [(ps[4], ps[5]), (ps[2], ps[3])]

        def xs_(cb, T):
            return xres[:, cb * TH + T * 512: cb * TH + (T + 1) * 512]

        def stats(T):
            j = T % 2
            p1, p2 = banks[j]
            for cb in range(8):
                P.op("act", (lambda e, o=usq[:, cb * 512:(cb + 1) * 512], i=xs_(cb, T): e.activation(out=o, in_=i, func=AF.Square)),
                     reads=[("xr", cb, T)], writes=[("usq", cb)])
                P.op("pe", (lambda e, o=p1[:, :], b=xs_(cb, T), st=(cb == 0), sp=(cb == 7): e.matmul(o, lhsT=onesF, rhs=b, start=st, stop=sp)),
                     reads=[("xr", cb, T), "cB"], writes=[("lnp1", j)])
                P.op("pe", (lambda e, o=p2[:, :], b=usq[:, cb * 512:(cb + 1) * 512], st=(cb == 0), sp=(cb == 7): e.matmul(o, lhsT=onesF, rhs=b, start=st, stop=sp)),
                     reads=[("usq", cb), "cB"], writes=[("lnp2", j)])
            P.op("act", (lambda e: e.copy(out=mean[j], in_=p1[:, :])), reads=[("lnp1", j)], writes=[("mean", j)])
            P.op("dve", (lambda e: e.tensor_tensor(out=vtm[j], in0=mean[j], in1=mean[j], op=ALU.mult)), reads=[("mean", j)], writes=[("vt", j)])
            P.op("dve", (lambda e: e.tensor_tensor(out=vtm[j], in0=p2[:, :], in1=vtm[j], op=ALU.subtract)), reads=[("lnp2", j), ("vt", j)], writes=[("vt", j)])
            P.op("act", (lambda e: e.activation(out=vtm[j], in_=vtm[j], func=AF.Ln, bias=EPS)), reads=[("vt", j)], writes=[("vt", j)])
            P.op("act", (lambda e: e.activation(out=rstd[j], in_=vtm[j], func=AF.Exp, scale=-0.5)), reads=[("vt", j)], writes=[("rstd", j)])

        def norm(T):
            j = T % 2
            for cb in range(8):
                xs = xs_(cb, T)
                tb = tmp[cb % 3]
                P.op("dve", (lambda e, o=tb, a=xs: e.tensor_tensor(out=o, in0=a, in1=mean[j], op=ALU.subtract)),
                     reads=[("xr", cb, T), ("mean", j)], writes=[("lt", cb % 3)])
                P.op("dve", (lambda e, o=tb: e.tensor_tensor(out=o, in0=o, in1=rstd[j], op=ALU.mult)),
                     reads=[("lt", cb % 3), ("rstd", j)], writes=[("lt", cb % 3)])
                P.op("act", (lambda e, o=xs, i=tb, g=lB[:, og + cb:og + cb + 1], b=lB[:, ob + cb:ob + cb + 1]: e.activation(out=o, in_=i, func=AF.Identity, scale=g, bias=b)),
                     reads=[("lt", cb % 3), "lV"], writes=[("xr", cb, T)])
                P.op("dve", (lambda e, o=xb[:, cb * TH + T * 512: cb * TH + (T + 1) * 512], i=xs: e.tensor_copy(out=o, in_=i)),
                     reads=[("xr", cb, T)], writes=[("xbk", cb, T)])
            if store is not None:
                xo, bo = store
                tsl = slice(T * 512, (T + 1) * 512)
                P.dma("sp", xo.rearrange("(dc p) t -> p dc t", p=128)[:, :, tsl], xres.rearrange("p (dc t) -> p dc t", dc=8)[:, :, tsl],
                      "bst", reads=[("xr", cb, T) for cb in range(8)])
                if bo is not None:
                    P.dma("sp", bo.rearrange("(dc p) t -> p dc t", p=128)[:, :, tsl], xb.rearrange("p (dc t) -> p dc t", dc=8)[:, :, tsl],
                          "bst", reads=[("xbk", cb, T) for cb in range(8)])

        stats(0)
        for T in range(4):
            if T + 1 < 4:
                stats(T + 1)
            norm(T)
        ar.release()


_CACHE = {}


def _prog(stages):
    key = tuple(stages)
    if key not in _CACHE:
        b = Builder(list(stages))
        b.build()
        _CACHE[key] = b
    return _CACHE[key]


def _run(stage, l, maps_all, state):
    b = _prog([stage])
    in_maps = []
    for c in range(8):
        m = {}
        for nm in b.ext_in:
            if nm + str(l) in maps_all[c]:
                m[nm] = maps_all[c][nm + str(l)]
            elif nm in maps_all[c]:
                m[nm] = maps_all[c][nm]
            else:
                m[nm] = state[c][nm]
        in_maps.append(m)
    res = run_bass_kernel_spmd(b.nc, in_maps, core_ids=list(range(8)))
    for c in range(8):
        for nm in b.ext_out:
            state[c][nm] = np.asarray(res.results[c][nm])


def kernel_unfused(**inputs):
    maps = _host_inputs(inputs)
    state = [dict() for _ in range(8)]
    _run("P0", 0, maps, state)
    for l in range(DEPTH):
        for c in range(8):
            pr = c // 2 * 2
            state[c]["xball"] = np.stack([state[pr]["xb_o"], state[pr + 1]["xb_o"]])
            state[c]["xres_i"] = state[c]["xres_o"]
            state[c]["xb_i"] = state[c]["xb_o"]
        _run("A0", l, maps, state)
        for c in range(8):
            pr, hh = c // 2 * 2, c % 2
            state[c]["rsall"] = np.ascontiguousarray(
                np.stack([state[pr]["rs"][:, hh * TH:(hh + 1) * TH], state[pr + 1]["rs"][:, hh * TH:(hh + 1) * TH]]))
        _run("B0", l, maps, state)
    out = np.empty((NB, S, D), np.float32)
    for c in range(8):
        b, hh = c // 2, c % 2
        out[b, hh * TH:(hh + 1) * TH, :] = state[c]["xres_o"].T
    return out


def kernel(**inputs):
    maps = _host_inputs(inputs)
    b = _prog(["FX"])
    nv = int(np.random.randint(1 << 10, 1 << 26))
    nonce = (nv * 8 + np.repeat(np.arange(8, dtype=np.int64), 16)[None, :]).astype(np.int32)
    in_maps = []
    for c in range(8):
        m = {}
        for nm in b.ext_in:
            m[nm] = nonce if nm == "nonce" else maps[c][nm]
        in_maps.append(m)
    res = run_bass_kernel_spmd(b.nc, in_maps, core_ids=list(range(8)))
    out = np.empty((NB, S, D), np.float32)
    for c in range(8):
        bb, hh = c // 2, c % 2
        out[bb, hh * TH:(hh + 1) * TH, :] = np.asarray(res.results[c]["outT"]).T
    return out


def kernel_dup(**inputs):
    maps = _host_inputs(inputs)
    b = _prog(["FUSED"])
    in_maps = []
    for c in range(8):
        pr = c // 2 * 2
        m = {}
        for nm in b.ext_in:
            if nm == "xT2":
                m[nm] = np.stack([maps[pr]["xT"], maps[pr + 1]["xT"]])
            elif nm.startswith("cA_"):
                m[nm] = maps[pr + int(nm[-1])]["cA"]
            elif nm[-2] == "_" and nm[:-2] in maps[c]:
                m[nm] = maps[pr + int(nm[-1])][nm[:-2]]
            else:
                m[nm] = maps[c][nm]
        in_maps.append(m)
    res = run_bass_kernel_spmd(b.nc, in_maps, core_ids=list(range(8)))
    out = np.empty((NB, S, D), np.float32)
    for c in range(8):
        bb, hh = c // 2, c % 2
        out[bb, hh * TH:(hh + 1) * TH, :] = np.asarray(res.results[c]["outT"]).T
    return out
```

```python
import contextlib
import numpy as np
import ml_dtypes
import concourse.bass as bass
import concourse.mybir as mybir
from concourse.bass_utils import run_bass_kernel_spmd

F32 = mybir.dt.float32
BF16 = mybir.dt.bfloat16
AF = mybir.ActivationFunctionType
ALU = mybir.AluOpType

D = 1024
S = 4096
NB = 4
DEPTH = 2
TH = 2048
DFF = 4096
ALPHA = (2 * DEPTH) ** 0.25
EPS = 1e-5
FUSED = False
import os as _os
PARTS = _os.environ.get("KPARTS", "proj,ret,sb").split(",")
SUB = _os.environ.get("KSUB", "fm,tm,rot,tr").split(",")

ENGS = ("pe", "act", "dve", "pool", "sp")
SAME_ENGINE_SYNC = True
SCHEDULE = True


class _Op:
    __slots__ = ("eng", "fn", "chan", "deps", "tick", "needs_inc", "kind", "waits_extra", "seg", "cost", "fin")

    def __init__(self, eng, fn, chan, kind):
        self.seg = 0
        self.cost = None
        self.fin = 0.0
        self.eng = eng
        self.fn = fn
        self.chan = chan
        self.deps = set()
        self.tick = None
        self.needs_inc = False
        self.kind = kind
        self.waits_extra = None


class Prog:
    def __init__(self, nc):
        self.nc = nc
        self.streams = {e: [] for e in ENGS}
        self.res = {}
        self.chan_count = {}
        self.last_op = {e: None for e in ENGS}
        self.seg = 0

    def _add(self, op, reads, writes):
        deps = set()
        for k in reads:
            st = self.res.get(k)
            if st is not None and st[0] is not None:
                deps.add(st[0])
        for k in writes:
            st = self.res.get(k)
            if st is not None:
                if st[0] is not None:
                    deps.add(st[0])
                deps.update(st[1])
        for k in writes:
            self.res[k] = [op, []]
        for k in reads:
            st = self.res.get(k)
            if st is None:
                st = self.res[k] = [None, []]
            if k not in writes:
                st[1].append(op)
        deps.discard(op)
        op.deps = deps
        op.seg = self.seg
        self.streams[op.eng].append(op)
        self.last_op[op.eng] = op
        return op

    def op(self, eng, fn, reads=(), writes=()):
        return self._add(_Op(eng, fn, None, "c"), tuple(reads), tuple(writes))

    def dma(self, queue, out, in_, chan, reads=(), writes=(), **kw):
        k = self.chan_count.get(chan, 0)
        self.chan_count[chan] = k + 1
        o = _Op(queue, (lambda e: e.dma_start(out=out, in_=in_, **kw)), (chan, k), "d")
        return self._add(o, tuple(reads), tuple(writes))

    def xop(self, queue, fn, reads=()):
        o = _Op(queue, fn, None, "x")
        o.cost = 2.0
        return self._add(o, tuple(reads), ())

    def custom_dma(self, queue, fn, chan, reads=(), writes=()):
        k = self.chan_count.get(chan, 0)
        self.chan_count[chan] = k + 1
        o = _Op(queue, fn, (chan, k), "d")
        return self._add(o, tuple(reads), tuple(writes))

    def barrier(self):
        lasts = [self.last_op[e] for e in ENGS
                 if self.last_op[e] is not None and self.last_op[e].kind == "c"]
        lasts = []
        for e in ENGS:
            for o in reversed(self.streams[e]):
                if o.kind == "c":
                    lasts.append(o)
                    break
        chans = dict(self.chan_count)
        for e in ENGS:
            o = _Op(e, None, None, "b")
            o.deps = set(lasts)
            o.waits_extra = chans
            o.seg = self.seg
            self.streams[e].append(o)
        self.res = {}
        self.seg += 1

    COST = {"pe": 0.22, "act": 0.5, "dve": 0.6, "pool": 1.0, "sp": 0.1}
    DMA_LAT = 3.0
    XLAT = 0.5
    SLAT = 0.35
    WINDOW = 32

    def schedule(self):
        nseg = self.seg + 1
        per = {e: [[] for _ in range(nseg + 1)] for e in ENGS}
        bar = {e: [None] * (nseg + 1) for e in ENGS}
        for e in ENGS:
            for o in self.streams[e]:
                if o.kind == "b":
                    bar[e][o.seg] = o
                else:
                    per[e][o.seg].append(o)
        new = {e: [] for e in ENGS}
        for sg in range(nseg + 1):
            lists = {e: per[e][sg] for e in ENGS}
            if any(lists[e] for e in ENGS):
                inseg = set()
                for e in ENGS:
                    inseg.update(lists[e])
                ptr = {e: 0 for e in ENGS}
                done = set()
                tfree = {e: 0.0 for e in ENGS}
                out = {e: [] for e in ENGS}
                pend = {e: list(lists[e]) for e in ENGS}
                t = 0.0
                remaining = sum(len(v) for v in pend.values())
                while remaining:
                    progressed = False
                    nxt = None
                    for e in ENGS:
                        if not pend[e]:
                            continue
                        if tfree[e] > t + 1e-9:
                            nxt = tfree[e] if nxt is None else min(nxt, tfree[e])
                            continue
                        win = pend[e][:1] if e == "sp" else pend[e][:self.WINDOW]
                        best = None
                        for o in win:
                            rdy = 0.0
                            ok = True
                            for d in o.deps:
                                if d not in inseg:
                                    continue
                                if d not in done:
                                    ok = False
                                    break
                                lat = 0.0 if (d.eng == e and e == "pe") else (self.SLAT if d.eng == e else self.XLAT)
                                rdy = max(rdy, d.fin + lat)
                            if not ok:
                                continue
                            if rdy <= t + 1e-9:
                                best = o
                                break
                            nxt = rdy if nxt is None else min(nxt, rdy)
                        if best is not None:
                            c = best.cost if best.cost is not None else self.COST[e]
                            if best.kind == "d":
                                best.fin = t + self.DMA_LAT
                                tfree[e] = t + c
                            else:
                                best.fin = t + c
                                tfree[e] = t + c
                            done.add(best)
                            pend[e].remove(best)
                            out[e].append(best)
                            remaining -= 1
                            progressed = True
                            nxt = tfree[e] if nxt is None else min(nxt, tfree[e])
                    if not progressed:
                        if nxt is None or nxt <= t + 1e-9:
                            for e in ENGS:
                                out[e].extend(pend[e])
                                pend[e] = []
                            break
                        t = nxt
                    else:
                        t = t if nxt is None else min(t + 0.05, nxt) if False else t
                for e in ENGS:
                    new[e].extend(out[e])
            for e in ENGS:
                if bar[e][sg] is not None:
                    new[e].append(bar[e][sg])
        for e in ENGS:
            assert len(new[e]) == len(self.streams[e]), (e, len(new[e]), len(self.streams[e]))
        self.streams = new

    def emit(self):
        nc = self.nc
        if SCHEDULE:
            self.schedule()
        for e in ENGS:
            for o in self.streams[e]:
                for d in o.deps:
                    if d.kind == "c":
                        if d.eng == o.eng and (d.eng == "pe" or not SAME_ENGINE_SYNC) and o.kind != "b":
                            continue
                        d.needs_inc = True
        for e in ENGS:
            t = 0
            for o in self.streams[e]:
                if o.kind == "c" and o.needs_inc:
                    t += 1
                    o.tick = t
        with contextlib.ExitStack() as es:
            esem = {e: es.enter_context(nc.semaphore("s_" + e)) for e in ENGS}
            csem = {c: es.enter_context(nc.semaphore("c_" + str(c))) for c in self.chan_count}
            self.flag_sem = es.enter_context(nc.semaphore("flag_sem"))
            block = es.enter_context(nc.Block())

            def run(e):
                def body(eng):
                    known = {}

                    def wait(sem, val):
                        if known.get(sem.name, 0) >= val:
                            return
                        known[sem.name] = val
                        eng.wait_ge(sem, val)

                    for o in self.streams[e]:
                        for d in o.deps:
                            if d.kind == "c":
                                if d.tick is None:
                                    continue
                                if d.eng == e and (e == "pe" or not SAME_ENGINE_SYNC) and o.kind != "b":
                                    continue
                                wait(esem[d.eng], d.tick)
                            elif d.kind == "d":
                                c, k = d.chan
                                wait(csem[c], 16 * (k + 1))
                        if o.kind == "b":
                            for c, n in o.waits_extra.items():
                                wait(csem[c], 16 * n)
                            continue
                        ins = o.fn(eng)
                        if o.kind == "x":
                            continue
                        if o.kind == "d":
                            ins.then_inc(csem[o.chan[0]], 16)
                        elif o.needs_inc:
                            ins.then_inc(esem[e], 1)
                    if e == "sp":
                        for c, n in self.chan_count.items():
                            wait(csem[c], 16 * n)
                        for e2 in ENGS:
                            lt = max([o.tick for o in self.streams[e2] if o.tick is not None] or [0])
                            if lt:
                                wait(esem[e2], lt)
                return body

            block.tensor(run("pe"))
            block.scalar(run("act"))
            block.vector(run("dve"))
            block.gpsimd(run("pool"))
            block.sync(run("sp"))


class Arena:
    def __init__(self, t32, nbytes):
        self.t = t32
        self.n = nbytes
        self.top = 0
        self.marks = []

    def alloc(self, nelem, dtype, parts=128):
        esz = 4 if dtype == F32 else 2
        nb = (nelem * esz + 63) // 64 * 64
        assert self.top + nb <= self.n, f"SBUF arena overflow {self.top}+{nb}>{self.n}"
        o = self.top // 4
        self.top += nb
        v = self.t[0:parts, o:o + nb // 4]
        if dtype != F32:
            v = v.bitcast(dtype)
        return v[:, 0:nelem]

    def mark(self):
        self.marks.append(self.top)

    def release(self):
        self.top = self.marks.pop()


CA_IDENT, CA_CAUS, CA_U, CA_L, CA_SBM, CA_DEC, CA_CH, CA_CD, CA_N = (
    0, 128, 256, 384, 512, 2560, 2564, 2566, 2568)
CR_COS, CR_SIN, CR_N = 0, 2048, 4096
CB_CAUS, CB_ONES, CB_N = 0, 128, 256
LB_LNG, LB_LNB, LB_WT, LB_SB, LB_N = 0, 512, 1024, 1536, 2048
LB_L1G, LB_L1B, LB_L2G, LB_L2B, LV_N = 0, 8, 16, 24, 32


def _consts_A(hh):
    c = np.zeros((128, CA_N), np.float32)
    p = np.arange(128)
    c[:, CA_IDENT:CA_IDENT + 128] = np.eye(128)
    c[:, CA_CAUS:CA_CAUS + 128] = (p[:, None] <= p[None, :])
    c[:, CA_U:CA_U + 128] = (p[:, None] >= p[None, :])
    c[:, CA_L:CA_L + 128] = (p[:, None] < p[None, :])
    t = np.arange(512)
    for r in range(4):
        c[:, CA_SBM + r * 512:CA_SBM + (r + 1) * 512] = ((r * 128 + p)[:, None] < t[None, :])
    for h in range(2):
        hg = hh * 2 + h
        g = 1.0 - 2.0 ** (-5.0 - hg)
        lg = np.log(g)
        c[:, CA_DEC + h] = (128.0 ** -0.5) * np.exp(lg * (p + 1.0))
        c[:, CA_DEC + 2 + h] = np.exp(lg * (127.0 - p))
        c[:, CA_CH + h] = np.exp(-lg * 128.0)
        c[:, CA_CD + h] = np.exp(lg * 128.0)
    return c


def _consts_R(shift=0):
    c = np.zeros((128, CR_N), np.float32)
    p = np.arange(128)
    half = 64
    inv_freq = (10000.0 ** (-np.arange(half, dtype=np.float32) / half)).astype(np.float32)
    pos = np.abs((np.arange(32)[None, :] - shift) * 128 + p[:, None]).astype(np.float32)
    ang = (pos[:, :, None] * inv_freq[None, None, :]).astype(np.float32)
    c[:, CR_COS:CR_COS + 2048] = np.cos(ang).astype(np.float32).reshape(128, 2048)
    c[:, CR_SIN:CR_SIN + 2048] = np.sin(ang).astype(np.float32).reshape(128, 2048)
    return c


def _consts_B():
    c = np.zeros((128, CB_N), np.float32)
    p = np.arange(128)
    c[:, CB_CAUS:CB_CAUS + 128] = (p[:, None] <= p[None, :])
    c[:, CB_ONES:CB_ONES + 128] = 1.0 / D
    return c


def _blk_lhsT(w, cw=128):
    K, N = w.shape
    return np.ascontiguousarray(w.reshape(K // 128, 128, N // cw, cw).transpose(2, 1, 0, 3))


def _host_inputs(inp):
    x = np.asarray(inp["x"], np.float32)
    maps = []
    for core in range(8):
        b, hh = core // 2, core % 2
        m = {}
        m["xT"] = np.ascontiguousarray(x[b, hh * TH:(hh + 1) * TH, :].T)
        m["cA"] = _consts_A(hh)
        m["cB"] = _consts_B()
        m["cR"] = _consts_R()
        m["cRL"] = _consts_R(16 if hh == 0 else 0)
        for l in range(DEPTH):
            w_in = np.asarray(inp["w_in"][l], np.float32)
            hs = slice(hh * 256, (hh + 1) * 256)
            blk = lambda i: w_in[:, i * 512:(i + 1) * 512]
            rq, rk, rv, rg, sq, sk, sv = [blk(i)[:, hs] for i in range(7)]
            m[f"wAf{l}"] = _blk_lhsT(np.concatenate([rg, sq, sk], axis=1))
            m[f"wAt{l}"] = _blk_lhsT(np.concatenate([rq, rk, rv, sv], axis=1), cw=512)
            gu, gv = w_in[:, 3584:4096], w_in[:, 4096:4608]
            gates = w_in[:, 4608:7680]
            m[f"wGu{l}"] = _blk_lhsT(gu)
            m[f"wGv{l}"] = _blk_lhsT(gv, cw=512)
            m[f"wGt{l}"] = _blk_lhsT(gates)
            m[f"pR{l}"] = _blk_lhsT(np.asarray(inp["p_ret"][l], np.float32))
            m[f"pS{l}"] = _blk_lhsT(np.asarray(inp["p_sb"][l], np.float32))
            m[f"pG{l}"] = _blk_lhsT(np.asarray(inp["p_sgu"][l], np.float32))
            m[f"wO{l}"] = _blk_lhsT(np.asarray(inp["w_out"][l], np.float32))
            m[f"wU{l}"] = _blk_lhsT(np.asarray(inp["w_up"][l], np.float32))
            m[f"wD{l}"] = _blk_lhsT(np.asarray(inp["w_down"][l], np.float32))
            la = np.zeros((128, 4), np.float32)
            la[:, 0:2] = np.asarray(inp["ret_gn_g"][l], np.float32)[hs].reshape(2, 128).T
            la[:, 2:4] = np.asarray(inp["ret_gn_b"][l], np.float32)[hs].reshape(2, 128).T
            m[f"lA{l}"] = la
            lb = np.zeros((128, LB_N), np.float32)
            lb[:, LB_LNG:LB_LNG + 512] = np.asarray(inp["sgu_ln_g"][l], np.float32)[None, :]
            lb[:, LB_LNB:LB_LNB + 512] = np.asarray(inp["sgu_ln_b"][l], np.float32)[None, :]
            sw = np.asarray(inp["sgu_w"][l], np.float32)
            lb[:, LB_WT:LB_WT + 512] = sw.transpose(2, 0, 1).reshape(128, 512)
            lb[:, LB_SB:LB_SB + 512] = np.asarray(inp["sgu_b"][l], np.float32).reshape(1, 512)
            lv = np.zeros((128, LV_N), np.float32)
            for nm, off in (("ln1_g", LB_L1G), ("ln1_b", LB_L1B), ("ln2_g", LB_L2G), ("ln2_b", LB_L2B)):
                lv[:, off:off + 8] = np.asarray(inp[nm][l], np.float32).reshape(8, 128).T
            m[f"lB{l}"] = lb
            m[f"lV{l}"] = lv
        maps.append(m)
    return maps


IN_SHAPES = {"xT": [D, TH], "cA": [128, CA_N], "cB": [128, CB_N], "cR": [128, CR_N], "cRL": [128, CR_N]}
for _l in range(DEPTH):
    IN_SHAPES.update({
        f"wAf{_l}": [6, 128, 8, 128], f"wAt{_l}": [2, 128, 8, 512],
        f"wGu{_l}": [4, 128, 8, 128], f"wGv{_l}": [1, 128, 8, 512], f"wGt{_l}": [24, 128, 8, 128],
        f"pR{_l}": [8, 128, 4, 128], f"pS{_l}": [8, 128, 4, 128], f"pG{_l}": [8, 128, 4, 128],
        f"wO{_l}": [8, 128, 8, 128], f"wU{_l}": [32, 128, 8, 128], f"wD{_l}": [8, 128, 32, 128],
        f"lA{_l}": [128, 4], f"lB{_l}": [128, LB_N], f"lV{_l}": [128, LV_N]})


class Builder:
    def __init__(self, stages):
        self.stages = stages
        self.nc = bass.Bass("TRN2", target_bir_lowering=False)
        self.dram = {}
        self.ext_in = []
        self.ext_out = []

    def dt(self, name, shape, dtype, kind):
        if name not in self.dram:
            self.dram[name] = self.nc.dram_tensor(name, list(shape), dtype, kind=kind).ap()
            if kind == "ExternalInput":
                self.ext_in.append(name)
            elif kind == "ExternalOutput":
                self.ext_out.append(name)
        return self.dram[name]

    def win(self, name):
        return self.dt(name, IN_SHAPES[name], F32, "ExternalInput")

    def winl(self, base, l):
        return self.dt(base + (str(l) if FUSED else ""), IN_SHAPES[base + str(l)], F32, "ExternalInput")

    def build(self):
        nc = self.nc
        with contextlib.ExitStack() as es:
            at = es.enter_context(nc.sbuf_tensor("arena", [128, 53200], F32))
            self.ar = Arena(at, 53200 * 4)
            self.ps = [es.enter_context(nc.psum_tensor(f"ps{i}", [128, 512], F32)) for i in range(6)]
            self.psb = es.enter_context(nc.psum_tensor("psb", [128, 1024], BF16))
            self.psb2 = es.enter_context(nc.psum_tensor("psb2", [128, 1024], BF16))
            self.P = Prog(nc)
            if self.stages == ["FX"]:
                self.wire_fx()
            elif self.stages == ["FUSED"]:
                self.wire_fused()
            else:
                for s in self.stages:
                    self.wire_unfused(s)
                    self.P.barrier()
            self.P.emit()
        return nc

    def wire_unfused(self, s):
        EI, EO = "ExternalInput", "ExternalOutput"
        w = lambda base: self.dt(base, IN_SHAPES[base + "0"] if base + "0" in IN_SHAPES else IN_SHAPES[base], F32, EI)
        if s == "P0":
            self.stage_p0(dict(xT=w("xT"), xres_o=self.dt("xres_o", [D, TH], F32, EO), xb_o=self.dt("xb_o", [D, TH], BF16, EO)))
        elif s[0] == "A":
            self.stage_a(dict(xall=self.dt("xball", [2, D, TH], BF16, EI), rs=self.dt("rs", [512, S], BF16, EO),
                              cA=w("cA"), cR=w("cR"), lA=w("lA"), wAf=w("wAf"), wAt=w("wAt")))
        elif s[0] == "B":
            io = dict(xres_i=self.dt("xres_i", [D, TH], F32, EI), xb_i=self.dt("xb_i", [D, TH], BF16, EI),
                      rsall=self.dt("rsall", [2, 512, TH], BF16, EI),
                      xres_o=self.dt("xres_o", [D, TH], F32, EO), xb_o=self.dt("xb_o", [D, TH], BF16, EO),
                      wU16=self.dt("wU16", [32, 128, 1024], BF16, "Internal"), wD16=self.dt("wD16", [8, 128, 4096], BF16, "Internal"),
                      make_cache=True)
            for nm in ("cB", "lB", "lV", "wGu", "wGv", "wGt", "pR", "pS", "pG", "wO", "wU", "wD"):
                io[nm] = w(nm)
            self.stage_b(io)

    def wire_fx(self):
        EI = "ExternalInput"
        P = self.P
        w = lambda base: self.dt(base, IN_SHAPES[base], F32, EI)
        wl = lambda base, l: self.dt(f"{base}{l}", IN_SHAPES[base + "0"], F32, EI)
        I32 = mybir.dt.int32
        nonce = self.dt("nonce", [1, 128], I32, EI)
        sh = lambda nm, shape, dtp: self.dram.setdefault(nm, self.nc.dram_tensor(nm, shape, dtp, kind="Internal", addr_space="Shared").ap())
        XB = [sh("EX0", [2, D, TH], BF16)] * DEPTH
        RS = [sh("EX1", [2, 512, S], BF16)] * DEPTH
        FL = sh("FL", [2, 16], I32)
        xres = [self.dt(f"xres_p{l}", [D, TH], F32, "Internal") for l in range(DEPTH)]
        xbp = [self.dt(f"xb_p{l}", [D, TH], BF16, "Internal") for l in range(DEPTH)]
        rsp = [self.dt(f"rs_p{l}", [512, S], BF16, "Internal") for l in range(DEPTH)]
        rsall = [self.dt(f"rsall_p{l}", [2, 512, TH], BF16, "Internal") for l in range(DEPTH)]
        outT = self.dt("outT", [D, TH], F32, "ExternalOutput")
        self.ar.mark()
        ntile = self.ar.alloc(128, F32, parts=1).bitcast(I32)
        P.dma("sp", ntile, nonce, "nonce", writes=["ntile"])
        phase = [0]

        def publish(dst_fn, src):
            phase[0] += 1
            k = phase[0]
            P.custom_dma("sp", (lambda e: e.dma_start(out=dst_fn(self.parity(e)), in_=src)), "xch", writes=[("xch", k)])

            def fn(e, k=k):
                par = self.parity(e)
                e.dma_start(out=FL[bass.ds(par, 1)], in_=ntile[0:1, k * 16:(k + 1) * 16]).then_inc(self.P.flag_sem, 16)
                e.wait_ge(self.P.flag_sem, 16 * k)
            P.xop("sp", fn, reads=[("xch", k), "ntile"])
            return k

        def wait_partner(k):
            def fn(e, k=k):
                par = self.parity(e)
                if getattr(self, "_nbase", None) is None:
                    self._nbase = e.alloc_register("nonce_base")
                    e.reg_load(self._nbase, nonce[0:1, 0:1])
                with e.register(f"want{k}") as want, e.register(f"got{k}") as got, e.register(f"r{k}") as r:
                    e.reg_add(want, self._nbase, k)
                    e.reg_mov(r, 1)
                    with e.While(r):
                        e.reg_load(got, FL[bass.ds(1 - par, 1), 0:1])
                        e.reg_sub(r, got, want)
                        e.reg_alu(r, r, -4, ALU.bitwise_and)
            P.xop("sp", fn, reads=["ntile"])

        def publish_and_wait(dst_fn, src, key):
            wait_partner(publish(dst_fn, src))

        self.stage_p0(dict(xT=w("xT"), xres_o=None, xb_o=xbp[0]))
        xres[0] = w("xT")
        P.barrier()
        kx = publish(lambda par: XB[0][bass.ds(par, 1)].rearrange("o d t -> (o d) t"), xbp[0])
        for l in range(DEPTH):
            wU16 = self.dt(f"wU16_{l}", [32, 128, 1024], BF16, "Internal")
            wD16 = self.dt(f"wD16_{l}", [8, 128, 4096], BF16, "Internal")
            cio = dict(wU=wl("wU", l), wD=wl("wD", l), wU16=wU16, wD16=wD16)
            self.stage_a(dict(xall=XB[l], rs=rsp[l], cA=w("cA"), cR=w("cR"), lA=wl("lA", l), wAf=wl("wAf", l), wAt=wl("wAt", l), cache_io=cio,
                              pre_x=(lambda kx=kx: wait_partner(kx))))
            P.barrier()
            kk = publish(lambda par, l=l: RS[l][bass.ds(par, 1)].rearrange("o r t -> (o r) t"), rsp[l])

            def pre_rs(l=l, kk=kk):
                wait_partner(kk)
                P.custom_dma("sp", (lambda e: e.dma_start(out=rsall[l], in_=RS[l].rearrange("h r (two t) -> h r two t", two=2)[:, :, bass.ds(self.parity(e), 1), :]
                                                          .rearrange("h r o t -> h r (o t)"))), "xch2", writes=["rsall_d"])
            lastl = (l == DEPTH - 1)
            io = dict(xres_i=xres[l], xb_i=xbp[l], rsall=rsall[l], wU16=wU16, wD16=wD16, make_cache=False, pre_rs=pre_rs,
                      xres_o=outT if lastl else xres[l + 1], xb_o=None if lastl else xbp[l + 1], cB=w("cB"))
            for nm in ("lB", "lV", "wGu", "wGv", "wGt", "pR", "pS", "pG", "wO", "wU", "wD"):
                io[nm] = wl(nm, l)
            self.stage_b(io)
            P.barrier()
            if not lastl:
                kx = publish(lambda par, l=l: XB[l + 1][bass.ds(par, 1)].rearrange("o d t -> (o d) t"), xbp[l + 1])
        self.ar.release()

    def wire_fused(self):
        EI = "ExternalInput"
        xT = self.dt("xT2", [2, D, TH], F32, EI)
        xres = [self.dt(f"xres_s{l}", [2, D, TH], F32, "Internal") for l in range(DEPTH)]
        xb = [self.dt(f"xb_s{l}", [3 if l == DEPTH - 1 else 2, D, TH], BF16, "Internal") for l in range(DEPTH)]
        rs = [self.dt(f"rs_s{l}", [2, 512, TH if l == DEPTH - 1 else S], BF16, "Internal") for l in range(DEPTH)]
        self.ar.mark()
        zt = self.ar.alloc(TH, BF16)
        self.P.op("pool", lambda e: e.memset(zt, 0.0), writes=["zt"])
        for dc in range(8):
            self.P.dma("sp", xb[DEPTH - 1][0, dc * 128:(dc + 1) * 128, :], zt, "zst", reads=["zt"])
        self.ar.release()
        self.P.barrier()
        outT = self.dt("outT", [D, TH], F32, "ExternalOutput")
        wl = lambda base, l, sfx="": self.dt(f"{base}{l}{sfx}", IN_SHAPES[base + "0"], F32, EI)
        for th in range(2):
            self.stage_p0(dict(xT=xT[th], xres_o=xres[0][th], xb_o=xb[0][th]))
            self.P.barrier()
        for l in range(DEPTH):
            wU16 = self.dt(f"wU16_{l}", [32, 128, 1024], BF16, "Internal")
            wD16 = self.dt(f"wD16_{l}", [8, 128, 4096], BF16, "Internal")
            lastl = (l == DEPTH - 1)
            xin = xb[l]
            if lastl:
                xin = self.dt("xb_shift", [2, D, TH], BF16, "Internal")
                for r in range(2):
                    self.P.custom_dma("sp", (lambda e, r=r: e.dma_start(out=xin[r], in_=xb[l][bass.ds(self.parity(e) + r, 1)].rearrange("o d t -> (o d) t"))),
                                      "xsh")
                self.P.barrier()
            for hh in range(2):
                cio = dict(wU=wl("wU", l), wD=wl("wD", l), wU16=wU16, wD16=wD16) if hh == 0 else None
                self.stage_a(dict(xall=xin, rs=rs[l][hh], cA=self.dt(f"cA_{hh}", IN_SHAPES["cA"], F32, EI),
                                  cR=self.win("cRL" if lastl else "cR"), last=lastl,
                                  lA=wl("lA", l, f"_{hh}"), wAf=wl("wAf", l, f"_{hh}"), wAt=wl("wAt", l, f"_{hh}"), cache_io=cio))
                self.P.barrier()
            for th in range(1 if lastl else 2):
                if lastl:
                    io = dict(xres_i=xres[l], xb_i=xin[1], rsall=rs[l], dyn=True, wU16=wU16, wD16=wD16, make_cache=False)
                    io["xres_o"], io["xb_o"] = outT, None
                else:
                    io = dict(xres_i=xres[l][th], xb_i=xb[l][th], rsall=rs[l][:, :, th * TH:(th + 1) * TH],
                              wU16=wU16, wD16=wD16, make_cache=False)
                    io["xres_o"], io["xb_o"] = xres[l + 1][th], xb[l + 1][(1 + th) if l + 1 == DEPTH - 1 else th]
                io["cB"] = self.win("cB")
                for nm in ("lB", "lV", "wGu", "wGv", "wGt", "pR", "pS", "pG", "wO", "wU", "wD"):
                    io[nm] = wl(nm, l)
                self.stage_b(io)
                self.P.barrier()

    def parity(self, e):
        if getattr(self, "_par", None) is None:
            self._par = e.snap(e.partition_id() % 2, min_val=0, max_val=1)
        return self._par

    def load_cast(self, dst16, src, n, tag, stg, nbuf=2):
        P = self.P
        CH = stg[0].shape[1]
        cnt = getattr(self, "_lc_cnt", 0)
        for o in range(0, n, CH):
            w = min(CH, n - o)
            bi = cnt % nbuf
            cnt += 1
            sb = stg[bi]
            P.dma("sp", sb[:, 0:w], src[:, o:o + w], f"stg{bi}", writes=[("stg", bi)])
            P.op("dve", (lambda e, a=dst16[:, o:o + w], b=sb[:, 0:w]: e.tensor_copy(out=a, in_=b)),
                 reads=[("stg", bi)], writes=[tag])
        self._lc_cnt = cnt

    def cache_chunks(self, cio, stg, c16):
        P = self.P
        k = 0
        for src, dst, nblk, per in ((cio["wU"], cio["wU16"], 32, 1024), (cio["wD"], cio["wD16"], 8, 4096)):
            for blk in range(nblk):
                sflat = src[blk].rearrange("p a b -> p (a b)")
                for o in range(0, per, 1024):
                    def emit(bi=k % len(stg), sflat=sflat, dst=dst, blk=blk, o=o):
                        P.dma("sp", stg[bi], sflat[:, o:o + 1024], f"cstg{bi}", writes=[("cstg", bi)])
                        P.op("dve", (lambda e, a=c16[bi], b=stg[bi]: e.tensor_copy(out=a, in_=b)),
                             reads=[("cstg", bi)], writes=[("cc16", bi)])
                        P.dma("sp", dst[blk][:, o:o + 1024], c16[bi], f"cwc{bi}", reads=[("cc16", bi)])
                    yield emit
                    k += 1

    def stage_p0(self, io):
        P, ar = self.P, self.ar
        xT, xres_d, xb_d = io["xT"], io["xres_o"], io["xb_o"]
        ar.mark()
        x32 = ar.alloc(8 * TH, F32)
        x16 = ar.alloc(8 * TH, BF16)
        for dc in range(8):
            sl = slice(dc * TH, (dc + 1) * TH)
            P.dma("sp", x32[:, sl], xT[dc * 128:(dc + 1) * 128, :], "p0l", writes=[("x32", dc)])
            P.op("pool" if dc % 2 else "dve", (lambda e, a=x16[:, sl], b=x32[:, sl]: e.tensor_copy(out=a, in_=b)),
                 reads=[("x32", dc)], writes=[("x16", dc)])
            if xres_d is not None:
                P.dma("sp", xres_d[dc * 128:(dc + 1) * 128, :], x32[:, sl], "p0s", reads=[("x32", dc)])
            P.dma("sp", xb_d[dc * 128:(dc + 1) * 128, :], x16[:, sl], "p0s", reads=[("x16", dc)])
        ar.release()

    def stage_a(self, io):
        P, ar, ps, psb, psb2 = self.P, self.ar, self.ps, self.psb, self.psb2
        xall, rs_d = io["xall"], io["rs"]
        lastm = io.get("last", False)
        cA_d, lA_d = io["cA"], io["lA"]
        wAf_d, wAt_d = io["wAf"], io["wAt"]
        ar.mark()
        cA = ar.alloc(CA_N, F32)
        lA = ar.alloc(4, F32)
        c16 = ar.alloc(384, BF16)
        P.dma("sp", cA, cA_d, "cA", writes=["cA"])
        P.dma("sp", lA, lA_d, "lA", writes=["lA"])
        P.op("dve", lambda e: e.tensor_copy(out=c16[:, 0:128], in_=cA[:, CA_IDENT:CA_IDENT + 128]), reads=["cA"], writes=["c16a"])
        P.op("dve", lambda e: e.tensor_copy(out=c16[:, 128:384], in_=cA[:, CA_U:CA_U + 256]), reads=["cA"], writes=["c16b"])
        ident, U16, L16 = c16[:, 0:128], c16[:, 128:256], c16[:, 256:384]
        caus = cA[:, CA_CAUS:CA_CAUS + 128]
        rgT = ar.alloc(2 * S, BF16)
        sqT = ar.alloc(2 * S, BF16)
        skT = ar.alloc(2 * S, BF16)
        qdT = ar.alloc(2 * S, BF16)
        kdT = ar.alloc(2 * S, BF16)
        kdk = ar.alloc(32 * 256, BF16)
        vtk = ar.alloc(32 * 512, BF16)
        ar.mark()
        wf = ar.alloc(6 * 1024, BF16)
        wt = ar.alloc(2 * 4096, BF16)
        stg = [ar.alloc(1024, F32) for _ in range(2)]
        cR_d = io["cR"]
        crt = [ar.alloc(512, F32) for _ in range(2)]
        xt = [ar.alloc(8 * 512, BF16) for _ in range(2)]
        qk32 = [ar.alloc(512, F32)] * 2
        qk16 = [ar.alloc(512, BF16) for _ in range(2)]
        tmpr = [ar.alloc(512, F32)] * 2
        for cb in range(6):
            self.load_cast(wf[:, cb * 1024:(cb + 1) * 1024], wAf_d[cb].rearrange("p a b -> p (a b)"), 1024, ("wf", cb), stg)
        for g in range(2):
            self.load_cast(wt[:, g * 4096:(g + 1) * 4096], wAt_d[g].rearrange("p a b -> p (a b)"), 4096, ("wt", g), stg)

        if io.get("pre_x") is not None:
            io["pre_x"]()
        for T in range(8):
            xb_ = xt[T % 2]
            r, t0 = T // 4, (T % 4) * 512
            P.dma("sp", xb_.rearrange("p (dc t) -> p dc t", dc=8),
                  xall[r].rearrange("(dc p) t -> p dc t", p=128)[:, :, t0:t0 + 512],
                  f"xt{T % 2}", writes=[("xt", T % 2)])
            P.dma("sp", crt[T % 2][:, 0:256], cR_d[:, CR_COS + T * 256: CR_COS + (T + 1) * 256], f"cr{T % 2}", writes=[("crt", T % 2)])
            P.dma("sp", crt[T % 2][:, 256:512], cR_d[:, CR_SIN + T * 256: CR_SIN + (T + 1) * 256], f"cr{T % 2}", writes=[("crt", T % 2)])
            for cb in (range(6) if "fm" in SUB else []):
                bank = ps[cb % 2]
                for dc in range(8):
                    P.op("pe", (lambda e, o=bank[:, :], a=wf[:, cb * 1024 + dc * 128: cb * 1024 + (dc + 1) * 128],
                                b=xb_[:, dc * 512:(dc + 1) * 512], st=(dc == 0), sp=(dc == 7):
                                e.matmul(o, lhsT=a, rhs=b, start=st, stop=sp)),
                         reads=[("wf", cb), ("xt", T % 2)], writes=[("ps", cb % 2)])
                if cb < 2:
                    dst = rgT[:, cb * S + T * 512: cb * S + (T + 1) * 512]
                    P.op("act", (lambda e, o=dst, i=bank[:, :]: e.activation(out=o, in_=i, func=AF.Silu)),
                         reads=[("ps", cb % 2)], writes=[("rgT", cb, T)])
                elif cb < 4:
                    dst = sqT[:, (cb - 2) * S + T * 512: (cb - 2) * S + (T + 1) * 512]
                    P.op("act", (lambda e, o=dst, i=bank[:, :]: e.activation(out=o, in_=i, func=AF.Copy, scale=0.125)),
                         reads=[("ps", cb % 2)], writes=[("sqT", cb - 2, T)])
                else:
                    dst = skT[:, (cb - 4) * S + T * 512: (cb - 4) * S + (T + 1) * 512]
                    P.op("dve", (lambda e, o=dst, i=bank[:, :]: e.tensor_copy(out=o, in_=i)),
                         reads=[("ps", cb % 2)], writes=[("skT", cb - 4, T)])
            for q in (range(4) if "tm" in SUB else []):
                n = T * 4 + q
                pq, pv = ps[2 + (n % 2)], ps[4 + (n % 2)]
                for g, bank in ((0, pq), (1, pv)):
                    for dc in range(8):
                        P.op("pe", (lambda e, o=bank[:, :], a=xb_[:, dc * 512 + q * 128: dc * 512 + (q + 1) * 128],
                                    b=wt[:, g * 4096 + dc * 512: g * 4096 + (dc + 1) * 512], st=(dc == 0), sp=(dc == 7):
                                    e.matmul(o, lhsT=a, rhs=b, start=st, stop=sp)),
                             reads=[("wt", g), ("xt", T % 2)], writes=[("ps", 2 + 2 * g + (n % 2))])
                P.op("act", (lambda e, o=vtk[:, n * 512:(n + 1) * 512], i=pv[:, :]: e.copy(out=o, in_=i)),
                     reads=[("ps", 4 + (n % 2))], writes=[("vtk", n)])
                if "rot" not in SUB:
                    continue
                A32, T32, O16 = qk32[n % 2], tmpr[n % 2], qk16[n % 2]
                X = pq[:, :].rearrange("p (g two f) -> p g two f", g=4, two=2)
                A4 = A32.rearrange("p (g two f) -> p g two f", g=4, two=2)
                T4 = T32.rearrange("p (g two f) -> p g two f", g=4, two=2)
                cosb = crt[T % 2][:, q * 64:(q + 1) * 64].unsqueeze(1).to_broadcast([128, 4, 64])
                sinb = crt[T % 2][:, 256 + q * 64: 256 + (q + 1) * 64].unsqueeze(1).to_broadcast([128, 4, 64])
                rk_ = [("ps", 2 + (n % 2)), ("crt", T % 2)]
                P.op("dve", (lambda e, o=A4[:, :, 0, :], a=X[:, :, 0, :], b=cosb: e.tensor_tensor(out=o, in0=a, in1=b, op=ALU.mult)),
                     reads=rk_, writes=[("A32a", 0)])
                P.op("dve", (lambda e, o=A4[:, :, 1, :], a=X[:, :, 1, :], b=cosb: e.tensor_tensor(out=o, in0=a, in1=b, op=ALU.mult)),
                     reads=rk_, writes=[("A32b", 0)])
                P.op("dve", (lambda e, o=T4[:, :, 0, :], a=X[:, :, 1, :], b=sinb: e.tensor_tensor(out=o, in0=a, in1=b, op=ALU.mult)),
                     reads=rk_, writes=[("T32a", 0)])
                P.op("dve", (lambda e, o=T4[:, :, 1, :], a=X[:, :, 0, :], b=sinb: e.tensor_tensor(out=o, in0=a, in1=b, op=ALU.mult)),
                     reads=rk_, writes=[("T32b", 0)])
                P.op("pool", (lambda e, o=A4[:, :, 0, :], a=A4[:, :, 0, :], b=T4[:, :, 0, :]: e.tensor_tensor(out=o, in0=a, in1=b, op=ALU.subtract)),
                     reads=[("A32a", 0), ("T32a", 0)], writes=[("A32a", 0)])
                P.op("pool", (lambda e, o=A4[:, :, 1, :], a=A4[:, :, 1, :], b=T4[:, :, 1, :]: e.tensor_tensor(out=o, in0=a, in1=b, op=ALU.add)),
                     reads=[("A32b", 0), ("T32b", 0)], writes=[("A32b", 0)])
                decb = cA[:, CA_DEC:CA_DEC + 4].unsqueeze(2).to_broadcast([128, 4, 128])
                P.op("pool", (lambda e, o=O16.rearrange("p (g f) -> p g f", g=4), a=A32.rearrange("p (g f) -> p g f", g=4), b=decb:
                              e.tensor_tensor(out=o, in0=a, in1=b, op=ALU.mult)),
                     reads=[("A32a", 0), ("A32b", 0), "cA"], writes=[("qk16", n % 2)])
                P.op("pool", (lambda e, o=kdk[:, n * 256:(n + 1) * 256], i=O16[:, 256:512]: e.tensor_copy(out=o, in_=i)),
                     reads=[("qk16", n % 2)], writes=[("kdk", n)])
                if "tr" not in SUB:
                    continue
                for g in range(4):
                    pT = psb if g < 2 else psb2
                    P.op("pe", (lambda e, o=pT[:, (g % 2) * 128:(g % 2 + 1) * 128], i=O16[:, g * 128:(g + 1) * 128]:
                                e.transpose(out=o, in_=i, identity=ident)),
                         reads=[("qk16", n % 2), "c16a"], writes=["psb" if g < 2 else "psb2"])
                for h in range(2):
                    P.op("act", (lambda e, o=qdT[:, h * S + n * 128: h * S + (n + 1) * 128], i=psb[:, h * 128:(h + 1) * 128]: e.copy(out=o, in_=i)),
                         reads=["psb"], writes=[("qdT", h, n)])
                    P.op("dve", (lambda e, o=kdT[:, h * S + n * 128: h * S + (n + 1) * 128], i=psb2[:, h * 128:(h + 1) * 128]: e.tensor_copy(out=o, in_=i)),
                         reads=["psb2"], writes=[("kdT", h, n)])
        ar.release()
        P.barrier()

        ar.mark()
        if "ret" not in PARTS:
            ar.release(); ar.release(); return
        rso = ar.alloc(2 * S, BF16)
        st32 = ar.alloc(256, F32)
        st16 = ar.alloc(256, BF16)
        std = [ar.alloc(256, BF16) for _ in range(2)]
        nrm = [ar.alloc(256, BF16) for _ in range(2)]
        stt = [ar.alloc(32, F32) for _ in range(2)]
        tmpg = [ar.alloc(256, F32) for _ in range(2)]
        for n in range(32):
            pb = n % 2
            pS, pO, pK = ps[0 + pb], ps[2 + pb], ps[4 + pb]
            H = [(h, slice(h * S + n * 128, h * S + (n + 1) * 128), slice(h * 128, (h + 1) * 128)) for h in range(2)]
            qry = not (lastm and n < 16)
            for h, csl, hs in (H if qry else []):
                P.op("pe", (lambda e, o=pS[:, hs], a=kdT[:, csl], b=qdT[:, csl]: e.matmul(o, lhsT=a, rhs=b, start=True, stop=True)),
                     reads=[("kdT", h, n), ("qdT", h, n)], writes=[("pS", pb)])
            for h, csl, hs in (H if qry else []):
                P.op("dve", (lambda e, o=std[pb][:, hs], a=pS[:, hs], s_=cA[:, CA_CH + h:CA_CH + h + 1], m=caus:
                             e.scalar_tensor_tensor(out=o, in0=a, scalar=s_, in1=m, op0=ALU.mult, op1=ALU.mult)),
                     reads=[("pS", pb), "cA"], writes=[("std", pb, h)])
            for h, csl, hs in (H if qry else []):
                vsl = vtk[:, n * 512 + h * 128: n * 512 + (h + 1) * 128]
                P.op("pe", (lambda e, o=pO[:, hs], a=std[pb][:, hs], b=vsl, sp=(n == 0): e.matmul(o, lhsT=a, rhs=b, start=True, stop=sp)),
                     reads=[("std", pb, h), ("vtk", n)], writes=[("pO", pb)])
                if n > 0:
                    P.op("pe", (lambda e, o=pO[:, hs], a=qdT[:, csl], b=st16[:, hs]: e.matmul(o, lhsT=a, rhs=b, start=False, stop=True)),
                         reads=[("qdT", h, n), ("st16", h)], writes=[("pO", pb)])
            for h, csl, hs in H:
                vsl = vtk[:, n * 512 + h * 128: n * 512 + (h + 1) * 128]
                P.op("pe", (lambda e, o=pK[:, hs], a=kdk[:, n * 256 + h * 128: n * 256 + (h + 1) * 128], b=vsl: e.matmul(o, lhsT=a, rhs=b, start=True, stop=True)),
                     reads=[("kdk", n), ("vtk", n)], writes=[("pK", pb)])
            for h, csl, hs in H:
                if n == 0:
                    P.op("dve", (lambda e, o=st32[:, hs], i=pK[:, hs]: e.tensor_copy(out=o, in_=i)),
                         reads=[("pK", pb)], writes=[("st32", h)])
                else:
                    P.op("dve", (lambda e, o=st32[:, hs], a=st32[:, hs], s_=cA[:, CA_CD + h:CA_CD + h + 1], b=pK[:, hs]:
                                 e.scalar_tensor_tensor(out=o, in0=a, scalar=s_, in1=b, op0=ALU.mult, op1=ALU.add)),
                         reads=[("pK", pb), ("st32", h), "cA"], writes=[("st32", h)])
                P.op("pool", (lambda e, o=st16[:, hs], i=st32[:, hs]: e.tensor_copy(out=o, in_=i)),
                     reads=[("st32", h)], writes=[("st16", h)])
            if not qry:
                continue
            sv = stt[pb]
            for h, csl, hs in H:
                b0 = h * 16
                P.op("dve", (lambda e, o=sv[:, b0:b0 + 6], i=pO[:, hs]: e.bn_stats(out=o, in_=i)),
                     reads=[("pO", pb)], writes=[("stt", pb, h)])
                P.op("dve", (lambda e, o=sv[:, b0 + 8:b0 + 10], i=sv[:, b0:b0 + 6]: e.bn_aggr(out=o, in_=i)),
                     reads=[("stt", pb, h)], writes=[("stt", pb, h)])
            for h, csl, hs in H:
                b0 = h * 16
                P.op("act", (lambda e, o=sv[:, b0 + 11:b0 + 12], i=sv[:, b0 + 9:b0 + 10]: e.activation(out=o, in_=i, func=AF.Ln, bias=EPS)),
                     reads=[("stt", pb, h)], writes=[("stt", pb, h)])
            for h, csl, hs in H:
                b0 = h * 16
                P.op("act", (lambda e, o=sv[:, b0 + 10:b0 + 11], i=sv[:, b0 + 11:b0 + 12]: e.activation(out=o, in_=i, func=AF.Exp, scale=-0.5)),
                     reads=[("stt", pb, h)], writes=[("stt", pb, h)])
            for h, csl, hs in H:
                b0 = h * 16
                P.op("dve", (lambda e, o=nrm[pb][:, hs], a=pO[:, hs], m=sv[:, b0 + 8:b0 + 9], r=sv[:, b0 + 10:b0 + 11]:
                             e.tensor_scalar(out=o, in0=a, scalar1=m, scalar2=r, op0=ALU.subtract, op1=ALU.mult)),
                     reads=[("pO", pb), ("stt", pb, h)], writes=[("nrm", pb, h)])
            pT = psb if pb == 0 else psb2
            for h, csl, hs in H:
                P.op("pe", (lambda e, o=pT[:, hs], i=nrm[pb][:, hs]: e.transpose(out=o, in_=i, identity=ident)),
                     reads=[("nrm", pb, h), "c16a"], writes=[("psbr", pb)])
            for h, csl, hs in H:
                P.op("dve", (lambda e, o=tmpg[pb][:, hs], a=pT[:, hs], g=lA[:, h:h + 1], b=lA[:, 2 + h:3 + h]:
                             e.tensor_scalar(out=o, in0=a, scalar1=g, scalar2=b, op0=ALU.mult, op1=ALU.add)),
                     reads=[("psbr", pb), "lA"], writes=[("tmpg", pb, h)])
                P.op("pool", (lambda e, o=rso[:, csl], a=tmpg[pb][:, hs], b=rgT[:, csl]: e.tensor_tensor(out=o, in0=a, in1=b, op=ALU.mult)),
                     reads=[("tmpg", pb, h), ("rgT", h, n // 4)], writes=[("rso", h)])
        for h in range(2):
            P.dma("sp", rs_d[h * 128:(h + 1) * 128, :], rso[:, h * S + (TH if lastm else 0):(h + 1) * S], "rsst", reads=[("rso", h)])
        P.barrier()
        ar.release()

        ar.mark()
        if "sb" not in PARTS:
            ar.release(); ar.release(); return
        sbo = ar.alloc(4 * S, BF16, parts=64)
        e32 = [ar.alloc(512, F32) for _ in range(4)]
        sp16 = [ar.alloc(512, BF16) for _ in range(4)]
        w32 = [ar.alloc(512, F32) for _ in range(4)]
        a16 = [ar.alloc(512, BF16) for _ in range(4)]
        sbm = cA[:, CA_SBM:CA_SBM + 2048]
        cgen = None
        if io.get("cache_io") is not None:
            cstg = [ar.alloc(1024, F32) for _ in range(2)]
            cc16 = [ar.alloc(1024, BF16) for _ in range(2)]
            cgen = self.cache_chunks(io["cache_io"], cstg, cc16)
        step_i = 0

        def emit_z(s, hd, T, kb):
            base, pr = (hd % 2) * 64, hd // 2
            c0 = max(0, kb - 4 * T) * 128
            P.op("pe", (lambda e, o=ps[s][:, c0:], a=skT[base:base + 64, pr * S + kb * 128: pr * S + (kb + 1) * 128],
                        b=sqT[base:base + 64, pr * S + T * 512 + c0: pr * S + (T + 1) * 512]: e.matmul(o, lhsT=a, rhs=b, start=True, stop=True)),
                 reads=[("skT", pr, kb // 4), ("sqT", pr, T)], writes=[("pz", s)])

        for pr in range(2):
            for T in (range(4, 8) if lastm else range(8)):
                kbs = list(range(4 * T + 3, -1, -1))
                for s in range(2):
                    emit_z(s, pr * 2 + s, T, kbs[0])
                for ki, kb in enumerate(kbs):
                    first, last = (ki == 0), (ki == len(kbs) - 1)
                    c0 = max(0, kb - 4 * T) * 128
                    step_i += 1
                    pj = step_i % 2
                    if cgen is not None and step_i % 3 == 0:
                        em = next(cgen, None)
                        if em is not None:
                            em()
                    for s in range(2):
                        P.op("act", (lambda e, o=e32[s + 2 * pj][:, c0:], i=ps[s][:, c0:]: e.activation(out=o, in_=i, func=AF.Exp)),
                             reads=[("pz", s)], writes=[("e32", s, pj)])
                    if kb >= 4 * T:
                        r = kb - 4 * T
                        for s in range(2):
                            P.op("dve", (lambda e, o=e32[s + 2 * pj][:, c0:], a=e32[s + 2 * pj][:, c0:], m=sbm[:, r * 512 + c0:(r + 1) * 512]: e.tensor_tensor(out=o, in0=a, in1=m, op=ALU.mult)),
                                 reads=[("e32", s, pj), "cA"], writes=[("e32", s, pj)])
                    for s in range(2):
                        P.op("act", (lambda e, o=sp16[s + 2 * pj][:, c0:], i=e32[s + 2 * pj][:, c0:]: e.activation(out=o, in_=i, func=AF.Ln, bias=1.0)),
                             reads=[("e32", s, pj)], writes=[("sp16", s, pj)])
                    for s in range(2):
                        P.op("pe", (lambda e, o=ps[2 + s][:, c0:], b=sp16[s + 2 * pj][:, c0:], st=first: e.matmul(o, lhsT=U16, rhs=b, start=st, stop=False, skip_group_check=True)),
                             reads=[("sp16", s, pj), "c16b"], writes=[("pR", s)])
                    if not last:
                        for s in range(2):
                            emit_z(s, pr * 2 + s, T, kbs[ki + 1])
                    for s in range(2):
                        P.op("act", (lambda e, o=w32[s + 2 * pj][:, c0:], i=ps[2 + s][:, c0:]: e.activation(out=o, in_=i, func=AF.Exp, scale=-1.0)),
                             reads=[("pR", s)], writes=[("w32", s, pj)])
                    for s in range(2):
                        P.op("pe", (lambda e, o=ps[2 + s][:, c0:], b=sp16[s + 2 * pj][:, c0:], sp_=last: e.matmul(o, lhsT=L16, rhs=b, start=False, stop=True if sp_ else False, skip_group_check=True)),
                             reads=[("sp16", s, pj), "c16b"], writes=[("pR", s)])
                    for s in range(2):
                        P.op("dve", (lambda e, o=a16[s + 2 * pj][:, c0:], a=e32[s + 2 * pj][:, c0:], b=w32[s + 2 * pj][:, c0:]: e.tensor_tensor(out=o, in0=a, in1=b, op=ALU.mult)),
                             reads=[("e32", s, pj), ("w32", s, pj)], writes=[("a16", s, pj)])
                    for s in range(2):
                        hd = pr * 2 + s
                        P.op("pe", (lambda e, o=ps[4 + s][0:64, c0:], a=vtk[:, kb * 512 + 256 + hd * 64: kb * 512 + 256 + (hd + 1) * 64], b=a16[s + 2 * pj][:, c0:], st=first, sp_=last:
                                    e.matmul(o, lhsT=a, rhs=b, start=st, stop=sp_, skip_group_check=True)),
                             reads=[("a16", s, pj), ("vtk", kb)], writes=[("po", s)])
                for s in range(2):
                    hd = pr * 2 + s
                    P.op("act" if s else "dve",
                         (lambda e, o=sbo[:, hd * S + T * 512: hd * S + (T + 1) * 512], i=ps[4 + s][0:64, :], s_=s:
                          (e.copy(out=o, in_=i) if s_ else e.tensor_copy(out=o, in_=i))),
                         reads=[("po", s)], writes=[("sbo", hd)])
        if cgen is not None:
            for em in cgen:
                em()
        for hd in range(4):
            P.dma("sp", rs_d[256 + hd * 64: 256 + (hd + 1) * 64, :], sbo[:, hd * S + (TH if lastm else 0):(hd + 1) * S], "rsst", reads=[("sbo", hd)])
        P.barrier()
        ar.release()
        ar.release()

    def stage_b(self, io):
        P, ar, ps, psb = self.P, self.ar, self.ps, self.psb
        xres_d, xb_d, rsa_d = io["xres_i"], io["xb_i"], io["rsall"]
        xres_o, xb_o = io["xres_o"], io["xb_o"]
        cB_d, lB_d = io["cB"], io["lB"]
        wGu_d, wGv_d, wGt_d = io["wGu"], io["wGv"], io["wGt"]
        pR_d, pS_d, pG_d = io["pR"], io["pS"], io["pG"]
        wO_d, wU_d, wD_d = io["wO"], io["wU"], io["wD"]
        wU16, wD16 = io["wU16"], io["wD16"]

        ar.mark()
        cB = ar.alloc(CB_N, F32)
        lV = ar.alloc(LV_N, F32)
        P.dma("sp", cB, cB_d, "cB", writes=["cB"])
        P.dma("sp", lV, io["lV"], "lV", writes=["lV"])
        onesF = cB[:, CB_ONES:CB_ONES + 128]
        xres = ar.alloc(8 * TH, F32)
        xb = ar.alloc(8 * TH, BF16)
        stg = [ar.alloc(1024, F32) for _ in range(2)]
        XR8 = [("xres", dc) for dc in range(8)]
        for dc in range(8):
            P.dma("sp", xb[:, dc * TH:(dc + 1) * TH], xb_d[dc * 128:(dc + 1) * 128, :], "bld", writes=[("xb", dc)])

        def load_xres():
            if io.get("dyn"):
                P.custom_dma("sp", (lambda e, o=xres.rearrange("p (dc t) -> p dc t", dc=8):
                                    e.dma_start(out=o, in_=xres_d[bass.ds(self.parity(e), 1)].rearrange("o (dc p) t -> p (o dc) t", p=128))),
                             "bld", writes=XR8)
            else:
                for dc in range(8):
                    P.dma("sp", xres[:, dc * TH:(dc + 1) * TH], xres_d[dc * 128:(dc + 1) * 128, :], "bld", writes=[("xres", dc)])
        XR = [("xres", dc) for dc in range(8)]
        XB = [("xb", dc) for dc in range(8)]

        ar.mark()
        c16 = [ar.alloc(1024, BF16) for _ in range(2)]
        k = 0
        for src, dst, nblk, per, wk in (((wU_d, wU16, 32, 1024, "wcU"), (wD_d, wD16, 8, 4096, "wcD")) if io["make_cache"] else ()):
            for blk in range(nblk):
                sflat = src[blk].rearrange("p a b -> p (a b)")
                for o in range(0, per, 1024):
                    w = min(1024, per - o)
                    bi = k % 2
                    k += 1
                    P.dma("sp", stg[bi][:, 0:w], sflat[:, o:o + w], f"stg{bi}", writes=[("stg", bi)])
                    P.op("pool" if bi else "dve", (lambda e, a=c16[bi][:, 0:w], b=stg[bi][:, 0:w]: e.tensor_copy(out=a, in_=b)),
                         reads=[("stg", bi)], writes=[("c16", bi)])
                    P.dma("sp", dst[blk][:, o:o + w], c16[bi][:, 0:w], f"wc{bi}", reads=[("c16", bi)], writes=[(wk, blk, o)])
        ar.release()
        if io["make_cache"]:
            P.barrier()

        ar.mark()
        rsf = ar.alloc(8 * TH, BF16)

        def load_rsf():
          for hh in range(2):
            for c4 in range(2):
                for base, slot in ((0, hh * 2 + c4), (256, 4 + hh * 2 + c4)):
                    dst = rsf[:, slot * TH:(slot + 1) * TH]
                    if io.get("dyn_rs"):
                        P.custom_dma("sp", (lambda e, o=dst, hh=hh, r0=base + c4 * 128:
                                            e.dma_start(out=o, in_=rsa_d.rearrange("h r (two t) -> h r two t", two=2)[hh, r0:r0 + 128, bass.ds(self.parity(e), 1), :]
                                                        .rearrange("p o t -> p (o t)"))),
                                     "bld", writes=[("rsf", slot)])
                    else:
                        P.dma("sp", dst, rsa_d[hh, base + c4 * 128: base + (c4 + 1) * 128, :], "bld", reads=["rsall_d"], writes=[("rsf", slot)])
        sgT = ar.alloc(4 * TH, BF16)
        ar.mark()
        lB = ar.alloc(LB_N, F32)
        P.dma("sp", lB, lB_d, "lB", writes=["lB"])
        wgu = ar.alloc(4 * 1024, BF16)
        wgv = ar.alloc(4096, BF16)
        wsg = ar.alloc(512, BF16)
        guT = [ar.alloc(4 * 512, BF16) for _ in range(2)]
        g32 = [ar.alloc(512, F32) for _ in range(2)]
        vn = [ar.alloc(512, BF16) for _ in range(2)]
        stt = [ar.alloc(32, F32) for _ in range(2)]
        t32 = [ar.alloc(512, F32) for _ in range(2)]
        for cb in range(4):
            self.load_cast(wgu[:, cb * 1024:(cb + 1) * 1024], wGu_d[cb].rearrange("p a b -> p (a b)"), 1024, ("wgu", cb), stg)
        self.load_cast(wgv, wGv_d[0].rearrange("p a b -> p (a b)"), 4096, "wgv", stg)
        if io.get("pre_rs") is not None:
            io["pre_rs"]()
        load_rsf()
        load_xres()
        causb = cB[:, CB_CAUS:CB_CAUS + 128].unsqueeze(1).to_broadcast([128, 4, 128])
        P.op("dve", (lambda e: e.tensor_tensor(out=wsg.rearrange("p (g i) -> p g i", g=4), in0=lB[:, LB_WT:LB_WT + 512].rearrange("p (g i) -> p g i", g=4),
                                               in1=causb, op=ALU.mult)), reads=["lB", "cB"], writes=["wsg"])
        for T in range(4):
            gb = guT[T % 2]
            for cb in range(4):
                bank = ps[cb % 2]
                for dc in range(8):
                    P.op("pe", (lambda e, o=bank[:, :], a=wgu[:, cb * 1024 + dc * 128: cb * 1024 + (dc + 1) * 128],
                                b=xb[:, dc * TH + T * 512: dc * TH + (T + 1) * 512], st=(dc == 0), sp=(dc == 7): e.matmul(o, lhsT=a, rhs=b, start=st, stop=sp)),
                         reads=[("wgu", cb), ("xb", dc)], writes=[("ps", cb % 2)])
                P.op("act", (lambda e, o=gb[:, cb * 512:(cb + 1) * 512], i=bank[:, :]: e.activation(out=o, in_=i, func=AF.Gelu_apprx_tanh)),
                     reads=[("ps", cb % 2)], writes=[("guT", T % 2, cb)])
            for q in range(4):
                n = T * 4 + q
                pb = n % 2
                pv, psv = ps[2 + pb], ps[4 + pb]
                for dc in range(8):
                    P.op("pe", (lambda e, o=pv[:, :], a=xb[:, dc * TH + n * 128: dc * TH + (n + 1) * 128], b=wgv[:, dc * 512:(dc + 1) * 512], st=(dc == 0), sp=(dc == 7):
                                e.matmul(o, lhsT=a, rhs=b, start=st, stop=sp)),
                         reads=["wgv", ("xb", dc)], writes=[("ps", 2 + pb)])
                P.op("act", (lambda e, o=g32[pb], i=pv[:, :]: e.activation(out=o, in_=i, func=AF.Gelu_apprx_tanh)),
                     reads=[("ps", 2 + pb)], writes=[("g32", pb)])
                sv = stt[pb]
                P.op("dve", (lambda e, o=sv[:, 0:6], i=g32[pb]: e.bn_stats(out=o, in_=i)), reads=[("g32", pb)], writes=[("stt", pb)])
                P.op("dve", (lambda e, o=sv[:, 8:10], i=sv[:, 0:6]: e.bn_aggr(out=o, in_=i)), reads=[("stt", pb)], writes=[("stt", pb)])
                P.op("act", (lambda e, o=sv[:, 11:12], i=sv[:, 9:10]: e.activation(out=o, in_=i, func=AF.Ln, bias=EPS)), reads=[("stt", pb)], writes=[("stt", pb)])
                P.op("act", (lambda e, o=sv[:, 10:11], i=sv[:, 11:12]: e.activation(out=o, in_=i, func=AF.Exp, scale=-0.5)), reads=[("stt", pb)], writes=[("stt", pb)])
                P.op("dve", (lambda e, o=t32[pb], a=g32[pb], m=sv[:, 8:9], r=sv[:, 10:11]: e.tensor_scalar(out=o, in0=a, scalar1=m, scalar2=r, op0=ALU.subtract, op1=ALU.mult)),
                     reads=[("g32", pb), ("stt", pb)], writes=[("t32", pb)])
                P.op("pool", (lambda e, o=t32[pb], a=t32[pb], b=lB[:, LB_LNG:LB_LNG + 512]: e.tensor_tensor(out=o, in0=a, in1=b, op=ALU.mult)),
                     reads=[("t32", pb), "lB"], writes=[("t32", pb)])
                P.op("pool", (lambda e, o=vn[pb], a=t32[pb], b=lB[:, LB_LNB:LB_LNB + 512]: e.tensor_tensor(out=o, in0=a, in1=b, op=ALU.add)),
                     reads=[("t32", pb), "lB"], writes=[("vn", pb)])
                for g in range(4):
                    P.op("pe", (lambda e, o=psv[:, g * 128:(g + 1) * 128], a=vn[pb][:, g * 128:(g + 1) * 128], b=wsg[:, g * 128:(g + 1) * 128]:
                                e.matmul(o, lhsT=a, rhs=b, start=True, stop=True)),
                         reads=[("vn", pb), "wsg"], writes=[("ps", 4 + pb)])
                P.op("dve", (lambda e, o=t32[pb], a=psv[:, :], b=lB[:, LB_SB:LB_SB + 512]: e.tensor_tensor(out=o, in0=a, in1=b, op=ALU.add)),
                     reads=[("ps", 4 + pb), ("t32", pb), "lB"], writes=[("t32", pb)])
                gview = gb.rearrange("p (g t) -> p g t", g=4)[:, :, q * 128:(q + 1) * 128]
                oview = sgT.rearrange("p (g t) -> p g t", g=4)[:, :, n * 128:(n + 1) * 128]
                P.op("pool", (lambda e, o=oview, a=t32[pb].rearrange("p (g i) -> p g i", g=4), b=gview: e.tensor_tensor(out=o, in0=a, in1=b, op=ALU.mult)),
                     reads=[("t32", pb)] + [("guT", T % 2, cb) for cb in range(4)], writes=[("sgT", n)])
        ar.release()
        P.barrier()
        SG = [("sgT", n) for n in range(16)]
        RS = [("rsf", i) for i in range(8)]

        mg = ar.alloc(8 * TH, BF16)
        ar.mark()
        wg = [ar.alloc(3 * 1024, BF16)] * 2
        wp = [ar.alloc(3 * 512, BF16)] * 2
        sg32 = [ar.alloc(512, F32) for _ in range(2)]
        m32 = [ar.alloc(512, F32) for _ in range(2)]
        srcs = [(pR_d, 0, "ret"), (pS_d, 4, "sb"), (pG_d, None, "sgu")]
        for cb in range(8):
            wb = 0
            for br in range(3):
                self.load_cast(wg[wb][:, br * 1024:(br + 1) * 1024], wGt_d[br * 8 + cb].rearrange("p a b -> p (a b)"), 1024, ("wg", wb, br), stg)
                self.load_cast(wp[wb][:, br * 512:(br + 1) * 512], srcs[br][0][cb].rearrange("p a b -> p (a b)"), 512, ("wp", wb, br), stg)
            for T in range(4):
                for br in range(3):
                    j = (T * 3 + br) % 2
                    pg, pp = ps[j], ps[2 + j]
                    for dc in range(8):
                        P.op("pe", (lambda e, o=pg[:, :], a=wg[wb][:, br * 1024 + dc * 128: br * 1024 + (dc + 1) * 128],
                                    b=xb[:, dc * TH + T * 512: dc * TH + (T + 1) * 512], st=(dc == 0), sp=(dc == 7): e.matmul(o, lhsT=a, rhs=b, start=st, stop=sp)),
                             reads=[("wg", wb, br), ("xb", dc)], writes=[("ps", j)])
                    P.op("act", (lambda e, o=sg32[j], i=pg[:, :]: e.activation(out=o, in_=i, func=AF.Sigmoid)),
                         reads=[("ps", j)], writes=[("sg32", j)])
                    for kc in range(4):
                        if br < 2:
                            rhs = rsf[:, (srcs[br][1] + kc) * TH + T * 512: (srcs[br][1] + kc) * TH + (T + 1) * 512]
                            rk = [("rsf", srcs[br][1] + kc)]
                        else:
                            rhs = sgT[:, kc * TH + T * 512: kc * TH + (T + 1) * 512]
                            rk = SG[T * 4:(T + 1) * 4]
                        P.op("pe", (lambda e, o=pp[:, :], a=wp[wb][:, br * 512 + kc * 128: br * 512 + (kc + 1) * 128], b=rhs, st=(kc == 0), sp=(kc == 3):
                                    e.matmul(o, lhsT=a, rhs=b, start=st, stop=sp)),
                             reads=[("wp", wb, br)] + rk, writes=[("ps", 2 + j)])
                    mt = m32[T % 2]
                    if br == 0:
                        P.op("dve", (lambda e, o=mt, a=pp[:, :], b=sg32[j]: e.tensor_tensor(out=o, in0=a, in1=b, op=ALU.mult)),
                             reads=[("ps", 2 + j), ("sg32", j)], writes=[("m32", T % 2)])
                    else:
                        P.op("dve", (lambda e, o=sg32[j], a=pp[:, :], b=sg32[j]: e.tensor_tensor(out=o, in0=a, in1=b, op=ALU.mult)),
                             reads=[("ps", 2 + j), ("sg32", j)], writes=[("sg32", j)])
                        dst = mt if br == 1 else mg[:, cb * TH + T * 512: cb * TH + (T + 1) * 512]
                        wk = [("m32", T % 2)] if br == 1 else [("mg", cb)]
                        P.op("pool", (lambda e, o=dst, a=mt, b=sg32[j]: e.tensor_tensor(out=o, in0=a, in1=b, op=ALU.add)),
                             reads=[("m32", T % 2), ("sg32", j)], writes=wk)
        ar.release()
        P.barrier()

        ar.mark()
        wo = [ar.alloc(1024, BF16) for _ in range(2)]
        for cb in range(8):
            wb = cb % 2
            self.load_cast(wo[wb], wO_d[cb].rearrange("p a b -> p (a b)"), 1024, ("wo", wb), stg)
            for T in range(4):
                bank = ps[T % 2]
                for kc in range(8):
                    P.op("pe", (lambda e, o=bank[:, :], a=wo[wb][:, kc * 128:(kc + 1) * 128], b=mg[:, kc * TH + T * 512: kc * TH + (T + 1) * 512], st=(kc == 0), sp=(kc == 7):
                                e.matmul(o, lhsT=a, rhs=b, start=st, stop=sp)),
                         reads=[("wo", wb), ("mg", kc)], writes=[("ps", T % 2)])
                xs = xres[:, cb * TH + T * 512: cb * TH + (T + 1) * 512]
                P.op("dve", (lambda e, o=xs, a=xs, b=bank[:, :]: e.scalar_tensor_tensor(out=o, in0=a, scalar=ALPHA, in1=b, op0=ALU.mult, op1=ALU.add)),
                     reads=[("ps", T % 2), ("xres", cb)], writes=[("xres", cb)])
        ar.release()
        ar.release()
        self.layer_norm_fm(xres, xb, lV, LB_L1G, LB_L1B, onesF)
        P.barrier()

        ar.mark()
        hT = ar.alloc(32 * 512, BF16)
        wu = [ar.alloc(4096, BF16) for _ in range(2)]
        wd = [ar.alloc(4096, BF16) for _ in range(2)]
        r32 = [ar.alloc(512, F32) for _ in range(2)]
        for T in range(4):
            for f4 in range(8):
                wb = f4 % 2
                for i in range(4):
                    P.dma("sp", wu[wb][:, i * 1024:(i + 1) * 1024], wU16[f4 * 4 + i], f"wu{wb}", writes=[("wu", wb)])
                for i in range(4):
                    fb = f4 * 4 + i
                    bank = ps[fb % 2]
                    for dc in range(8):
                        P.op("pe", (lambda e, o=bank[:, :], a=wu[wb][:, i * 1024 + dc * 128: i * 1024 + (dc + 1) * 128],
                                    b=xb[:, dc * TH + T * 512: dc * TH + (T + 1) * 512], st=(dc == 0), sp=(dc == 7): e.matmul(o, lhsT=a, rhs=b, start=st, stop=sp)),
                             reads=[("wu", wb), ("xb", dc)], writes=[("ps", fb % 2)])
                    P.op("act", (lambda e, o=r32[fb % 2], i_=bank[:, :]: e.activation(out=o, in_=i_, func=AF.Relu)),
                         reads=[("ps", fb % 2)], writes=[("r32", fb % 2)])
                    P.op("dve", (lambda e, o=hT[:, fb * 512:(fb + 1) * 512], a=r32[fb % 2]: e.tensor_tensor(out=o, in0=a, in1=a, op=ALU.mult)),
                         reads=[("r32", fb % 2)], writes=[("hT", fb)])
            for cb in range(8):
                wb = cb % 2
                P.dma("sp", wd[wb], wD16[cb], f"wd{wb}", writes=[("wd", wb)])
                bank = ps[2 + cb % 2]
                for fc in range(32):
                    P.op("pe", (lambda e, o=bank[:, :], a=wd[wb][:, fc * 128:(fc + 1) * 128], b=hT[:, fc * 512:(fc + 1) * 512], st=(fc == 0), sp=(fc == 31):
                                e.matmul(o, lhsT=a, rhs=b, start=st, stop=sp)),
                         reads=[("wd", wb), ("hT", fc)], writes=[("ps", 2 + cb % 2)])
                xs = xres[:, cb * TH + T * 512: cb * TH + (T + 1) * 512]
                P.op("dve", (lambda e, o=xs, a=xs, b=bank[:, :]: e.scalar_tensor_tensor(out=o, in0=a, scalar=ALPHA, in1=b, op0=ALU.mult, op1=ALU.add)),
                     reads=[("ps", 2 + cb % 2), ("xres", cb)], writes=[("xres", cb)])
        ar.release()
        P.barrier()
        self.layer_norm_fm(xres, xb, lV, LB_L2G, LB_L2B, onesF, store=(xres_o, xb_o))
        P.barrier()
        ar.release()

    def layer_norm_fm(self, xres, xb, lB, og, ob, onesF, store=None):
        P, ar, ps = self.P, self.ar, self.ps
        P.barrier()
        ar.mark()
        usq = ar.alloc(8 * 512, F32)
        mean = [ar.alloc(512, F32) for _ in range(2)]
        rstd = [ar.alloc(512, F32) for _ in range(2)]
        vtm = [ar.alloc(512, F32) for _ in range(2)]
        tmp = [ar.alloc(512, F32) for _ in range(3)]
        banks = [(ps[4], ps[5]), (ps[2], ps[3])]

        def xs_(cb, T):
            return xres[:, cb * TH + T * 512: cb * TH + (T + 1) * 512]

        def stats(T):
            j = T % 2
            p1, p2 = banks[j]
            for cb in range(8):
                P.op("act", (lambda e, o=usq[:, cb * 512:(cb + 1) * 512], i=xs_(cb, T): e.activation(out=o, in_=i, func=AF.Square)),
                     reads=[("xr", cb, T)], writes=[("usq", cb)])
                P.op("pe", (lambda e, o=p1[:, :], b=xs_(cb, T), st=(cb == 0), sp=(cb == 7): e.matmul(o, lhsT=onesF, rhs=b, start=st, stop=sp)),
                     reads=[("xr", cb, T), "cB"], writes=[("lnp1", j)])
                P.op("pe", (lambda e, o=p2[:, :], b=usq[:, cb * 512:(cb + 1) * 512], st=(cb == 0), sp=(cb == 7): e.matmul(o, lhsT=onesF, rhs=b, start=st, stop=sp)),
                     reads=[("usq", cb), "cB"], writes=[("lnp2", j)])
            P.op("act", (lambda e: e.copy(out=mean[j], in_=p1[:, :])), reads=[("lnp1", j)], writes=[("mean", j)])
            P.op("dve", (lambda e: e.tensor_tensor(out=vtm[j], in0=mean[j], in1=mean[j], op=ALU.mult)), reads=[("mean", j)], writes=[("vt", j)])
            P.op("dve", (lambda e: e.tensor_tensor(out=vtm[j], in0=p2[:, :], in1=vtm[j], op=ALU.subtract)), reads=[("lnp2", j), ("vt", j)], writes=[("vt", j)])
            P.op("act", (lambda e: e.activation(out=vtm[j], in_=vtm[j], func=AF.Ln, bias=EPS)), reads=[("vt", j)], writes=[("vt", j)])
            P.op("act", (lambda e: e.activation(out=rstd[j], in_=vtm[j], func=AF.Exp, scale=-0.5)), reads=[("vt", j)], writes=[("rstd", j)])

        def norm(T):
            j = T % 2
            for cb in range(8):
                xs = xs_(cb, T)
                tb = tmp[cb % 3]
                P.op("dve", (lambda e, o=tb, a=xs: e.tensor_tensor(out=o, in0=a, in1=mean[j], op=ALU.subtract)),
                     reads=[("xr", cb, T), ("mean", j)], writes=[("lt", cb % 3)])
                P.op("dve", (lambda e, o=tb: e.tensor_tensor(out=o, in0=o, in1=rstd[j], op=ALU.mult)),
                     reads=[("lt", cb % 3), ("rstd", j)], writes=[("lt", cb % 3)])
                P.op("act", (lambda e, o=xs, i=tb, g=lB[:, og + cb:og + cb + 1], b=lB[:, ob + cb:ob + cb + 1]: e.activation(out=o, in_=i, func=AF.Identity, scale=g, bias=b)),
                     reads=[("lt", cb % 3), "lV"], writes=[("xr", cb, T)])
                P.op("dve", (lambda e, o=xb[:, cb * TH + T * 512: cb * TH + (T + 1) * 512], i=xs: e.tensor_copy(out=o, in_=i)),
                     reads=[("xr", cb, T)], writes=[("xbk", cb, T)])
            if store is not None:
                xo, bo = store
                tsl = slice(T * 512, (T + 1) * 512)
                P.dma("sp", xo.rearrange("(dc p) t -> p dc t", p=128)[:, :, tsl], xres.rearrange("p (dc t) -> p dc t", dc=8)[:, :, tsl],
                      "bst", reads=[("xr", cb, T) for cb in range(8)])
                if bo is not None:
                    P.dma("sp", bo.rearrange("(dc p) t -> p dc t", p=128)[:, :, tsl], xb.rearrange("p (dc t) -> p dc t", dc=8)[:, :, tsl],
                          "bst", reads=[("xbk", cb, T) for cb in range(8)])

        stats(0)
        for T in range(4):
            if T + 1 < 4:
                stats(T + 1)
            norm(T)
        ar.release()


_CACHE = {}


def _prog(stages):
    key = tuple(stages)
    if key not in _CACHE:
        b = Builder(list(stages))
        b.build()
        _CACHE[key] = b
    return _CACHE[key]


def _run(stage, l, maps_all, state):
    b = _prog([stage])
    in_maps = []
    for c in range(8):
        m = {}
        for nm in b.ext_in:
            if nm + str(l) in maps_all[c]:
                m[nm] = maps_all[c][nm + str(l)]
            elif nm in maps_all[c]:
                m[nm] = maps_all[c][nm]
            else:
                m[nm] = state[c][nm]
        in_maps.append(m)
    res = run_bass_kernel_spmd(b.nc, in_maps, core_ids=list(range(8)))
    for c in range(8):
        for nm in b.ext_out:
            state[c][nm] = np.asarray(res.results[c][nm])


def kernel_unfused(**inputs):
    maps = _host_inputs(inputs)
    state = [dict() for _ in range(8)]
    _run("P0", 0, maps, state)
    for l in range(DEPTH):
        for c in range(8):
            pr = c // 2 * 2
            state[c]["xball"] = np.stack([state[pr]["xb_o"], state[pr + 1]["xb_o"]])
            state[c]["xres_i"] = state[c]["xres_o"]
            state[c]["xb_i"] = state[c]["xb_o"]
        _run("A0", l, maps, state)
        for c in range(8):
            pr, hh = c // 2 * 2, c % 2
            state[c]["rsall"] = np.ascontiguousarray(
                np.stack([state[pr]["rs"][:, hh * TH:(hh + 1) * TH], state[pr + 1]["rs"][:, hh * TH:(hh + 1) * TH]]))
        _run("B0", l, maps, state)
    out = np.empty((NB, S, D), np.float32)
    for c in range(8):
        b, hh = c // 2, c % 2
        out[b, hh * TH:(hh + 1) * TH, :] = state[c]["xres_o"].T
    return out


def kernel(**inputs):
    maps = _host_inputs(inputs)
    b = _prog(["FX"])
    nv = int(np.random.randint(1 << 10, 1 << 26))
    nonce = (nv * 8 + np.repeat(np.arange(8, dtype=np.int64), 16)[None, :]).astype(np.int32)
    in_maps = []
    for c in range(8):
        m = {}
        for nm in b.ext_in:
            m[nm] = nonce if nm == "nonce" else maps[c][nm]
        in_maps.append(m)
    res = run_bass_kernel_spmd(b.nc, in_maps, core_ids=list(range(8)))
    out = np.empty((NB, S, D), np.float32)
    for c in range(8):
        bb, hh = c // 2, c % 2
        out[bb, hh * TH:(hh + 1) * TH, :] = np.asarray(res.results[c]["outT"]).T
    return out


def kernel_dup(**inputs):
    maps = _host_inputs(inputs)
    b = _prog(["FUSED"])
    in_maps = []
    for c in range(8):
        pr = c // 2 * 2
        m = {}
        for nm in b.ext_in:
            if nm == "xT2":
                m[nm] = np.stack([maps[pr]["xT"], maps[pr + 1]["xT"]])
            elif nm.startswith("cA_"):
                m[nm] = maps[pr + int(nm[-1])]["cA"]
            elif nm[-2] == "_" and nm[:-2] in maps[c]:
                m[nm] = maps[pr + int(nm[-1])][nm[:-2]]
            else:
                m[nm] = maps[c][nm]
        in_maps.append(m)
    res = run_bass_kernel_spmd(b.nc, in_maps, core_ids=list(range(8)))
    out = np.empty((NB, S, D), np.float32)
    for c in range(8):
        bb, hh = c // 2, c % 2
        out[bb, hh * TH:(hh + 1) * TH, :] = np.asarray(res.results[c]["outT"]).T
    return out
```

```python
import contextlib
import numpy as np
import ml_dtypes
import concourse.bass as bass
import concourse.mybir as mybir
from concourse.bass_utils import run_bass_kernel_spmd

F32 = mybir.dt.float32
BF16 = mybir.dt.bfloat16
AF = mybir.ActivationFunctionType
ALU = mybir.AluOpType

D = 1024
S = 4096
NB = 4
DEPTH = 2
TH = 2048
DFF = 4096
ALPHA = (2 * DEPTH) ** 0.25
EPS = 1e-5
FUSED = False
import os as _os
PARTS = _os.environ.get("KPARTS", "proj,ret,sb").split(",")
SUB = _os.environ.get("KSUB", "fm,tm,rot,tr").split(",")

ENGS = ("pe", "act", "dve", "pool", "sp")
SAME_ENGINE_SYNC = True
SCHEDULE = True


class _Op:
    __slots__ = ("eng", "fn", "chan", "deps", "tick", "needs_inc", "kind", "waits_extra", "seg", "cost", "fin")

    def __init__(self, eng, fn, chan, kind):
        self.seg = 0
        self.cost = None
        self.fin = 0.0
        self.eng = eng
        self.fn = fn
        self.chan = chan
        self.deps = set()
        self.tick = None
        self.needs_inc = False
        self.kind = kind
        self.waits_extra = None


class Prog:
    def __init__(self, nc):
        self.nc = nc
        self.streams = {e: [] for e in ENGS}
        self.res = {}
        self.chan_count = {}
        self.last_op = {e: None for e in ENGS}
        self.seg = 0

    def _add(self, op, reads, writes):
        deps = set()
        for k in reads:
            st = self.res.get(k)
            if st is not None and st[0] is not None:
                deps.add(st[0])
        for k in writes:
            st = self.res.get(k)
            if st is not None:
                if st[0] is not None:
                    deps.add(st[0])
                deps.update(st[1])
        for k in writes:
            self.res[k] = [op, []]
        for k in reads:
            st = self.res.get(k)
            if st is None:
                st = self.res[k] = [None, []]
            if k not in writes:
                st[1].append(op)
        deps.discard(op)
        op.deps = deps
        op.seg = self.seg
        self.streams[op.eng].append(op)
        self.last_op[op.eng] = op
        return op

    def op(self, eng, fn, reads=(), writes=()):
        return self._add(_Op(eng, fn, None, "c"), tuple(reads), tuple(writes))

    def dma(self, queue, out, in_, chan, reads=(), writes=(), **kw):
        k = self.chan_count.get(chan, 0)
        self.chan_count[chan] = k + 1
        o = _Op(queue, (lambda e: e.dma_start(out=out, in_=in_, **kw)), (chan, k), "d")
        return self._add(o, tuple(reads), tuple(writes))

    def xop(self, queue, fn, reads=()):
        o = _Op(queue, fn, None, "x")
        o.cost = 2.0
        return self._add(o, tuple(reads), ())

    def custom_dma(self, queue, fn, chan, reads=(), writes=()):
        k = self.chan_count.get(chan, 0)
        self.chan_count[chan] = k + 1
        o = _Op(queue, fn, (chan, k), "d")
        return self._add(o, tuple(reads), tuple(writes))

    def barrier(self):
        lasts = [self.last_op[e] for e in ENGS
                 if self.last_op[e] is not None and self.last_op[e].kind == "c"]
        lasts = []
        for e in ENGS:
            for o in reversed(self.streams[e]):
                if o.kind == "c":
                    lasts.append(o)
                    break
        chans = dict(self.chan_count)
        for e in ENGS:
            o = _Op(e, None, None, "b")
            o.deps = set(lasts)
            o.waits_extra = chans
            o.seg = self.seg
            self.streams[e].append(o)
        self.res = {}
        self.seg += 1

    COST = {"pe": 0.22, "act": 0.5, "dve": 0.6, "pool": 1.0, "sp": 0.1}
    DMA_LAT = 3.0
    XLAT = 0.5
    SLAT = 0.35
    WINDOW = 32

    def schedule(self):
        nseg = self.seg + 1
        per = {e: [[] for _ in range(nseg + 1)] for e in ENGS}
        bar = {e: [None] * (nseg + 1) for e in ENGS}
        for e in ENGS:
            for o in self.streams[e]:
                if o.kind == "b":
                    bar[e][o.seg] = o
                else:
                    per[e][o.seg].append(o)
        new = {e: [] for e in ENGS}
        for sg in range(nseg + 1):
            lists = {e: per[e][sg] for e in ENGS}
            if any(lists[e] for e in ENGS):
                inseg = set()
                for e in ENGS:
                    inseg.update(lists[e])
                ptr = {e: 0 for e in ENGS}
                done = set()
                tfree = {e: 0.0 for e in ENGS}
                out = {e: [] for e in ENGS}
                pend = {e: list(lists[e]) for e in ENGS}
                t = 0.0
                remaining = sum(len(v) for v in pend.values())
                while remaining:
                    progressed = False
                    nxt = None
                    for e in ENGS:
                        if not pend[e]:
                            continue
                        if tfree[e] > t + 1e-9:
                            nxt = tfree[e] if nxt is None else min(nxt, tfree[e])
                            continue
                        win = pend[e][:1] if e == "sp" else pend[e][:self.WINDOW]
                        best = None
                        for o in win:
                            rdy = 0.0
                            ok = True
                            for d in o.deps:
                                if d not in inseg:
                                    continue
                                if d not in done:
                                    ok = False
                                    break
                                lat = 0.0 if (d.eng == e and e == "pe") else (self.SLAT if d.eng == e else self.XLAT)
                                rdy = max(rdy, d.fin + lat)
                            if not ok:
                                continue
                            if rdy <= t + 1e-9:
                                best = o
                                break
                            nxt = rdy if nxt is None else min(nxt, rdy)
                        if best is not None:
                            c = best.cost if best.cost is not None else self.COST[e]
                            if best.kind == "d":
                                best.fin = t + self.DMA_LAT
                                tfree[e] = t + c
                            else:
                                best.fin = t + c
                                tfree[e] = t + c
                            done.add(best)
                            pend[e].remove(best)
                            out[e].append(best)
                            remaining -= 1
                            progressed = True
                            nxt = tfree[e] if nxt is None else min(nxt, tfree[e])
                    if not progressed:
                        if nxt is None or nxt <= t + 1e-9:
                            for e in ENGS:
                                out[e].extend(pend[e])
                                pend[e] = []
                            break
                        t = nxt
                    else:
                        t = t if nxt is None else min(t + 0.05, nxt) if False else t
                for e in ENGS:
                    new[e].extend(out[e])
            for e in ENGS:
                if bar[e][sg] is not None:
                    new[e].append(bar[e][sg])
        for e in ENGS:
            assert len(new[e]) == len(self.streams[e]), (e, len(new[e]), len(self.streams[e]))
        self.streams = new

    def emit(self):
        nc = self.nc
        if SCHEDULE:
            self.schedule()
        for e in ENGS:
            for o in self.streams[e]:
                for d in o.deps:
                    if d.kind == "c":
                        if d.eng == o.eng and (d.eng == "pe" or not SAME_ENGINE_SYNC) and o.kind != "b":
                            continue
                        d.needs_inc = True
        for e in ENGS:
            t = 0
            for o in self.streams[e]:
                if o.kind == "c" and o.needs_inc:
                    t += 1
                    o.tick = t
        with contextlib.ExitStack() as es:
            esem = {e: es.enter_context(nc.semaphore("s_" + e)) for e in ENGS}
            csem = {c: es.enter_context(nc.semaphore("c_" + str(c))) for c in self.chan_count}
            self.flag_sem = es.enter_context(nc.semaphore("flag_sem"))
            block = es.enter_context(nc.Block())

            def run(e):
                def body(eng):
                    known = {}

                    def wait(sem, val):
                        if known.get(sem.name, 0) >= val:
                            return
                        known[sem.name] = val
                        eng.wait_ge(sem, val)

                    for o in self.streams[e]:
                        for d in o.deps:
                            if d.kind == "c":
                                if d.tick is None:
                                    continue
                                if d.eng == e and (e == "pe" or not SAME_ENGINE_SYNC) and o.kind != "b":
                                    continue
                                wait(esem[d.eng], d.tick)
                            elif d.kind == "d":
                                c, k = d.chan
                                wait(csem[c], 16 * (k + 1))
                        if o.kind == "b":
                            for c, n in o.waits_extra.items():
                                wait(csem[c], 16 * n)
                            continue
                        ins = o.fn(eng)
                        if o.kind == "x":
                            continue
                        if o.kind == "d":
                            ins.then_inc(csem[o.chan[0]], 16)
                        elif o.needs_inc:
                            ins.then_inc(esem[e], 1)
                    if e == "sp":
                        for c, n in self.chan_count.items():
                            wait(csem[c], 16 * n)
                        for e2 in ENGS:
                            lt = max([o.tick for o in self.streams[e2] if o.tick is not None] or [0])
                            if lt:
                                wait(esem[e2], lt)
                return body

            block.tensor(run("pe"))
            block.scalar(run("act"))
            block.vector(run("dve"))
            block.gpsimd(run("pool"))
            block.sync(run("sp"))


class Arena:
    def __init__(self, t32, nbytes):
        self.t = t32
        self.n = nbytes
        self.top = 0
        self.marks = []

    def alloc(self, nelem, dtype, parts=128):
        esz = 4 if dtype == F32 else 2
        nb = (nelem * esz + 63) // 64 * 64
        assert self.top + nb <= self.n, f"SBUF arena overflow {self.top}+{nb}>{self.n}"
        o = self.top // 4
        self.top += nb
        v = self.t[0:parts, o:o + nb // 4]
        if dtype != F32:
            v = v.bitcast(dtype)
        return v[:, 0:nelem]

    def mark(self):
        self.marks.append(self.top)

    def release(self):
        self.top = self.marks.pop()


CA_IDENT, CA_CAUS, CA_U, CA_L, CA_SBM, CA_DEC, CA_CH, CA_CD, CA_N = (
    0, 128, 256, 384, 512, 2560, 2564, 2566, 2568)
CR_COS, CR_SIN, CR_N = 0, 2048, 4096
CB_CAUS, CB_ONES, CB_N = 0, 128, 256
LB_LNG, LB_LNB, LB_WT, LB_SB, LB_N = 0, 512, 1024, 1536, 2048
LB_L1G, LB_L1B, LB_L2G, LB_L2B, LV_N = 0, 8, 16, 24, 32


def _consts_A(hh):
    c = np.zeros((128, CA_N), np.float32)
    p = np.arange(128)
    c[:, CA_IDENT:CA_IDENT + 128] = np.eye(128)
    c[:, CA_CAUS:CA_CAUS + 128] = (p[:, None] <= p[None, :])
    c[:, CA_U:CA_U + 128] = (p[:, None] >= p[None, :])
    c[:, CA_L:CA_L + 128] = (p[:, None] < p[None, :])
    t = np.arange(512)
    for r in range(4):
        c[:, CA_SBM + r * 512:CA_SBM + (r + 1) * 512] = ((r * 128 + p)[:, None] < t[None, :])
    for h in range(2):
        hg = hh * 2 + h
        g = 1.0 - 2.0 ** (-5.0 - hg)
        lg = np.log(g)
        c[:, CA_DEC + h] = (128.0 ** -0.5) * np.exp(lg * (p + 1.0))
        c[:, CA_DEC + 2 + h] = np.exp(lg * (127.0 - p))
        c[:, CA_CH + h] = np.exp(-lg * 128.0)
        c[:, CA_CD + h] = np.exp(lg * 128.0)
    return c


def _consts_R(shift=0):
    c = np.zeros((128, CR_N), np.float32)
    p = np.arange(128)
    half = 64
    inv_freq = (10000.0 ** (-np.arange(half, dtype=np.float32) / half)).astype(np.float32)
    pos = np.abs((np.arange(32)[None, :] - shift) * 128 + p[:, None]).astype(np.float32)
    ang = (pos[:, :, None] * inv_freq[None, None, :]).astype(np.float32)
    c[:, CR_COS:CR_COS + 2048] = np.cos(ang).astype(np.float32).reshape(128, 2048)
    c[:, CR_SIN:CR_SIN + 2048] = np.sin(ang).astype(np.float32).reshape(128, 2048)
    return c


def _consts_B():
    c = np.zeros((128, CB_N), np.float32)
    p = np.arange(128)
    c[:, CB_CAUS:CB_CAUS + 128] = (p[:, None] <= p[None, :])
    c[:, CB_ONES:CB_ONES + 128] = 1.0 / D
    return c


def _blk_lhsT(w, cw=128):
    K, N = w.shape
    return np.ascontiguousarray(w.reshape(K // 128, 128, N // cw, cw).transpose(2, 1, 0, 3))


def _host_inputs(inp):
    x = np.asarray(inp["x"], np.float32)
    maps = []
    for core in range(8):
        b, hh = core // 2, core % 2
        m = {}
        m["xT"] = np.ascontiguousarray(x[b, hh * TH:(hh + 1) * TH, :].T)
        m["cA"] = _consts_A(hh)
        m["cB"] = _consts_B()
        m["cR"] = _consts_R()
        m["cRL"] = _consts_R(16 if hh == 0 else 0)
        for l in range(DEPTH):
            w_in = np.asarray(inp["w_in"][l], np.float32)
            hs = slice(hh * 256, (hh + 1) * 256)
            blk = lambda i: w_in[:, i * 512:(i + 1) * 512]
            rq, rk, rv, rg, sq, sk, sv = [blk(i)[:, hs] for i in range(7)]
            m[f"wAf{l}"] = _blk_lhsT(np.concatenate([rg, sq, sk], axis=1))
            m[f"wAt{l}"] = _blk_lhsT(np.concatenate([rq, rk, rv, sv], axis=1), cw=512)
            gu, gv = w_in[:, 3584:4096], w_in[:, 4096:4608]
            gates = w_in[:, 4608:7680]
            m[f"wGu{l}"] = _blk_lhsT(gu)
            m[f"wGv{l}"] = _blk_lhsT(gv, cw=512)
            m[f"wGt{l}"] = _blk_lhsT(gates)
            m[f"pR{l}"] = _blk_lhsT(np.asarray(inp["p_ret"][l], np.float32))
            m[f"pS{l}"] = _blk_lhsT(np.asarray(inp["p_sb"][l], np.float32))
            m[f"pG{l}"] = _blk_lhsT(np.asarray(inp["p_sgu"][l], np.float32))
            m[f"wO{l}"] = _blk_lhsT(np.asarray(inp["w_out"][l], np.float32))
            m[f"wU{l}"] = _blk_lhsT(np.asarray(inp["w_up"][l], np.float32))
            m[f"wD{l}"] = _blk_lhsT(np.asarray(inp["w_down"][l], np.float32))
            la = np.zeros((128, 4), np.float32)
            la[:, 0:2] = np.asarray(inp["ret_gn_g"][l], np.float32)[hs].reshape(2, 128).T
            la[:, 2:4] = np.asarray(inp["ret_gn_b"][l], np.float32)[hs].reshape(2, 128).T
            m[f"lA{l}"] = la
            lb = np.zeros((128, LB_N), np.float32)
            lb[:, LB_LNG:LB_LNG + 512] = np.asarray(inp["sgu_ln_g"][l], np.float32)[None, :]
            lb[:, LB_LNB:LB_LNB + 512] = np.asarray(inp["sgu_ln_b"][l], np.float32)[None, :]
            sw = np.asarray(inp["sgu_w"][l], np.float32)
            lb[:, LB_WT:LB_WT + 512] = sw.transpose(2, 0, 1).reshape(128, 512)
            lb[:, LB_SB:LB_SB + 512] = np.asarray(inp["sgu_b"][l], np.float32).reshape(1, 512)
            lv = np.zeros((128, LV_N), np.float32)
            for nm, off in (("ln1_g", LB_L1G), ("ln1_b", LB_L1B), ("ln2_g", LB_L2G), ("ln2_b", LB_L2B)):
                lv[:, off:off + 8] = np.asarray(inp[nm][l], np.float32).reshape(8, 128).T
            m[f"lB{l}"] = lb
            m[f"lV{l}"] = lv
        maps.append(m)
    return maps


IN_SHAPES = {"xT": [D, TH], "cA": [128, CA_N], "cB": [128, CB_N], "cR": [128, CR_N], "cRL": [128, CR_N]}
for _l in range(DEPTH):
    IN_SHAPES.update({
        f"wAf{_l}": [6, 128, 8, 128], f"wAt{_l}": [2, 128, 8, 512],
        f"wGu{_l}": [4, 128, 8, 128], f"wGv{_l}": [1, 128, 8, 512], f"wGt{_l}": [24, 128, 8, 128],
        f"pR{_l}": [8, 128, 4, 128], f"pS{_l}": [8, 128, 4, 128], f"pG{_l}": [8, 128, 4, 128],
        f"wO{_l}": [8, 128, 8, 128], f"wU{_l}": [32, 128, 8, 128], f"wD{_l}": [8, 128, 32, 128],
        f"lA{_l}": [128, 4], f"lB{_l}": [128, LB_N], f"lV{_l}": [128, LV_N]})


class Builder:
    def __init__(self, stages):
        self.stages = stages
        self.nc = bass.Bass("TRN2", target_bir_lowering=False)
        self.dram = {}
        self.ext_in = []
        self.ext_out = []

    def dt(self, name, shape, dtype, kind):
        if name not in self.dram:
            self.dram[name] = self.nc.dram_tensor(name, list(shape), dtype, kind=kind).ap()
            if kind == "ExternalInput":
                self.ext_in.append(name)
            elif kind == "ExternalOutput":
                self.ext_out.append(name)
        return self.dram[name]

    def win(self, name):
        return self.dt(name, IN_SHAPES[name], F32, "ExternalInput")

    def winl(self, base, l):
        return self.dt(base + (str(l) if FUSED else ""), IN_SHAPES[base + str(l)], F32, "ExternalInput")

    def build(self):
        nc = self.nc
        with contextlib.ExitStack() as es:
            at = es.enter_context(nc.sbuf_tensor("arena", [128, 53200], F32))
            self.ar = Arena(at, 53200 * 4)
            self.ps = [es.enter_context(nc.psum_tensor(f"ps{i}", [128, 512], F32)) for i in range(6)]
            self.psb = es.enter_context(nc.psum_tensor("psb", [128, 1024], BF16))
            self.psb2 = es.enter_context(nc.psum_tensor("psb2", [128, 1024], BF16))
            self.P = Prog(nc)
            if self.stages == ["FX"]:
                self.wire_fx()
            elif self.stages == ["FUSED"]:
                self.wire_fused()
            else:
                for s in self.stages:
                    self.wire_unfused(s)
                    self.P.barrier()
            self.P.emit()
        return nc

    def wire_unfused(self, s):
        EI, EO = "ExternalInput", "ExternalOutput"
        w = lambda base: self.dt(base, IN_SHAPES[base + "0"] if base + "0" in IN_SHAPES else IN_SHAPES[base], F32, EI)
        if s == "P0":
            self.stage_p0(dict(xT=w("xT"), xres_o=self.dt("xres_o", [D, TH], F32, EO), xb_o=self.dt("xb_o", [D, TH], BF16, EO)))
        elif s[0] == "A":
            self.stage_a(dict(xall=self.dt("xball", [2, D, TH], BF16, EI), rs=self.dt("rs", [512, S], BF16, EO),
                              cA=w("cA"), cR=w("cR"), lA=w("lA"), wAf=w("wAf"), wAt=w("wAt")))
        elif s[0] == "B":
            io = dict(xres_i=self.dt("xres_i", [D, TH], F32, EI), xb_i=self.dt("xb_i", [D, TH], BF16, EI),
                      rsall=self.dt("rsall", [2, 512, TH], BF16, EI),
                      xres_o=self.dt("xres_o", [D, TH], F32, EO), xb_o=self.dt("xb_o", [D, TH], BF16, EO),
                      wU16=self.dt("wU16", [32, 128, 1024], BF16, "Internal"), wD16=self.dt("wD16", [8, 128, 4096], BF16, "Internal"),
                      make_cache=True)
            for nm in ("cB", "lB", "lV", "wGu", "wGv", "wGt", "pR", "pS", "pG", "wO", "wU", "wD"):
                io[nm] = w(nm)
            self.stage_b(io)

    def wire_fx(self):
        EI = "ExternalInput"
        P = self.P
        w = lambda base: self.dt(base, IN_SHAPES[base], F32, EI)
        wl = lambda base, l: self.dt(f"{base}{l}", IN_SHAPES[base + "0"], F32, EI)
        I32 = mybir.dt.int32
        nonce = self.dt("nonce", [1, 128], I32, EI)
        sh = lambda nm, shape, dtp: self.dram.setdefault(nm, self.nc.dram_tensor(nm, shape, dtp, kind="Internal", addr_space="Shared").ap())
        XB = [sh("EX0", [2, D, TH], BF16)] * DEPTH
        RS = [sh("EX1", [2, 512, S], BF16)] * DEPTH
        FL = sh("FL", [2, 16], I32)
        xres = [self.dt(f"xres_p{l}", [D, TH], F32, "Internal") for l in range(DEPTH)]
        xbp = [self.dt(f"xb_p{l}", [D, TH], BF16, "Internal") for l in range(DEPTH)]
        rsp = [self.dt(f"rs_p{l}", [512, S], BF16, "Internal") for l in range(DEPTH)]
        rsall = [self.dt(f"rsall_p{l}", [2, 512, TH], BF16, "Internal") for l in range(DEPTH)]
        outT = self.dt("outT", [D, TH], F32, "ExternalOutput")
        self.ar.mark()
        ntile = self.ar.alloc(128, F32, parts=1).bitcast(I32)
        P.dma("sp", ntile, nonce, "nonce", writes=["ntile"])
        phase = [0]

        def publish(dst_fn, src):
            phase[0] += 1
            k = phase[0]
            P.custom_dma("sp", (lambda e: e.dma_start(out=dst_fn(self.parity(e)), in_=src)), "xch", writes=[("xch", k)])

            def fn(e, k=k):
                par = self.parity(e)
                e.dma_start(out=FL[bass.ds(par, 1)], in_=ntile[0:1, k * 16:(k + 1) * 16]).then_inc(self.P.flag_sem, 16)
                e.wait_ge(self.P.flag_sem, 16 * k)
            P.xop("sp", fn, reads=[("xch", k), "ntile"])
            return k

        def wait_partner(k):
            def fn(e, k=k):
                par = self.parity(e)
                if getattr(self, "_nbase", None) is None:
                    self._nbase = e.alloc_register("nonce_base")
                    e.reg_load(self._nbase, nonce[0:1, 0:1])
                with e.register(f"want{k}") as want, e.register(f"got{k}") as got, e.register(f"r{k}") as r:
                    e.reg_add(want, self._nbase, k)
                    e.reg_mov(r, 1)
                    with e.While(r):
                        e.reg_load(got, FL[bass.ds(1 - par, 1), 0:1])
                        e.reg_sub(r, got, want)
                        e.reg_alu(r, r, -4, ALU.bitwise_and)
            P.xop("sp", fn, reads=["ntile"])

        def publish_and_wait(dst_fn, src, key):
            wait_partner(publish(dst_fn, src))

        self.stage_p0(dict(xT=w("xT"), xres_o=None, xb_o=xbp[0]))
        xres[0] = w("xT")
        P.barrier()
        kx = publish(lambda par: XB[0][bass.ds(par, 1)].rearrange("o d t -> (o d) t"), xbp[0])
        for l in range(DEPTH):
            wU16 = self.dt(f"wU16_{l}", [32, 128, 1024], BF16, "Internal")
            wD16 = self.dt(f"wD16_{l}", [8, 128, 4096], BF16, "Internal")
            cio = dict(wU=wl("wU", l), wD=wl("wD", l), wU16=wU16, wD16=wD16)
            self.stage_a(dict(xall=XB[l], rs=rsp[l], cA=w("cA"), cR=w("cR"), lA=wl("lA", l), wAf=wl("wAf", l), wAt=wl("wAt", l), cache_io=cio,
                              pre_x=(lambda kx=kx: wait_partner(kx))))
            P.barrier()
            kk = publish(lambda par, l=l: RS[l][bass.ds(par, 1)].rearrange("o r t -> (o r) t"), rsp[l])

            def pre_rs(l=l, kk=kk):
                wait_partner(kk)
                P.custom_dma("sp", (lambda e: e.dma_start(out=rsall[l], in_=RS[l].rearrange("h r (two t) -> h r two t", two=2)[:, :, bass.ds(self.parity(e), 1), :]
                                                          .rearrange("h r o t -> h r (o t)"))), "xch2", writes=["rsall_d"])
            lastl = (l == DEPTH - 1)
            io = dict(xres_i=xres[l], xb_i=xbp[l], rsall=rsall[l], wU16=wU16, wD16=wD16, make_cache=False, pre_rs=pre_rs,
                      xres_o=outT if lastl else xres[l + 1], xb_o=None if lastl else xbp[l + 1], cB=w("cB"))
            for nm in ("lB", "lV", "wGu", "wGv", "wGt", "pR", "pS", "pG", "wO", "wU", "wD"):
                io[nm] = wl(nm, l)
            self.stage_b(io)
            P.barrier()
            if not lastl:
                kx = publish(lambda par, l=l: XB[l + 1][bass.ds(par, 1)].rearrange("o d t -> (o d) t"), xbp[l + 1])
        self.ar.release()

    def wire_fused(self):
        EI = "ExternalInput"
        xT = self.dt("xT2", [2, D, TH], F32, EI)
        xres = [self.dt(f"xres_s{l}", [2, D, TH], F32, "Internal") for l in range(DEPTH)]
        xb = [self.dt(f"xb_s{l}", [3 if l == DEPTH - 1 else 2, D, TH], BF16, "Internal") for l in range(DEPTH)]
        rs = [self.dt(f"rs_s{l}", [2, 512, TH if l == DEPTH - 1 else S], BF16, "Internal") for l in range(DEPTH)]
        self.ar.mark()
        zt = self.ar.alloc(TH, BF16)
        self.P.op("pool", lambda e: e.memset(zt, 0.0), writes=["zt"])
        for dc in range(8):
            self.P.dma("sp", xb[DEPTH - 1][0, dc * 128:(dc + 1) * 128, :], zt, "zst", reads=["zt"])
        self.ar.release()
        self.P.barrier()
        outT = self.dt("outT", [D, TH], F32, "ExternalOutput")
        wl = lambda base, l, sfx="": self.dt(f"{base}{l}{sfx}", IN_SHAPES[base + "0"], F32, EI)
        for th in range(2):
            self.stage_p0(dict(xT=xT[th], xres_o=xres[0][th], xb_o=xb[0][th]))
            self.P.barrier()
        for l in range(DEPTH):
            wU16 = self.dt(f"wU16_{l}", [32, 128, 1024], BF16, "Internal")
            wD16 = self.dt(f"wD16_{l}", [8, 128, 4096], BF16, "Internal")
            lastl = (l == DEPTH - 1)
            xin = xb[l]
            if lastl:
                xin = self.dt("xb_shift", [2, D, TH], BF16, "Internal")
                for r in range(2):
                    self.P.custom_dma("sp", (lambda e, r=r: e.dma_start(out=xin[r], in_=xb[l][bass.ds(self.parity(e) + r, 1)].rearrange("o d t -> (o d) t"))),
                                      "xsh")
                self.P.barrier()
            for hh in range(2):
                cio = dict(wU=wl("wU", l), wD=wl("wD", l), wU16=wU16, wD16=wD16) if hh == 0 else None
                self.stage_a(dict(xall=xin, rs=rs[l][hh], cA=self.dt(f"cA_{hh}", IN_SHAPES["cA"], F32, EI),
                                  cR=self.win("cRL" if lastl else "cR"), last=lastl,
                                  lA=wl("lA", l, f"_{hh}"), wAf=wl("wAf", l, f"_{hh}"), wAt=wl("wAt", l, f"_{hh}"), cache_io=cio))
                self.P.barrier()
            for th in range(1 if lastl else 2):
                if lastl:
                    io = dict(xres_i=xres[l], xb_i=xin[1], rsall=rs[l], dyn=True, wU16=wU16, wD16=wD16, make_cache=False)
                    io["xres_o"], io["xb_o"] = outT, None
                else:
                    io = dict(xres_i=xres[l][th], xb_i=xb[l][th], rsall=rs[l][:, :, th * TH:(th + 1) * TH],
                              wU16=wU16, wD16=wD16, make_cache=False)
                    io["xres_o"], io["xb_o"] = xres[l + 1][th], xb[l + 1][(1 + th) if l + 1 == DEPTH - 1 else th]
                io["cB"] = self.win("cB")
                for nm in ("lB", "lV", "wGu", "wGv", "wGt", "pR", "pS", "pG", "wO", "wU", "wD"):
                    io[nm] = wl(nm, l)
                self.stage_b(io)
                self.P.barrier()

    def parity(self, e):
        if getattr(self, "_par", None) is None:
            self._par = e.snap(e.partition_id() % 2, min_val=0, max_val=1)
        return self._par

    def load_cast(self, dst16, src, n, tag, stg, nbuf=2):
        P = self.P
        CH = stg[0].shape[1]
        cnt = getattr(self, "_lc_cnt", 0)
        for o in range(0, n, CH):
            w = min(CH, n - o)
            bi = cnt % nbuf
            cnt += 1
            sb = stg[bi]
            P.dma("sp", sb[:, 0:w], src[:, o:o + w], f"stg{bi}", writes=[("stg", bi)])
            P.op("dve", (lambda e, a=dst16[:, o:o + w], b=sb[:, 0:w]: e.tensor_copy(out=a, in_=b)),
                 reads=[("stg", bi)], writes=[tag])
        self._lc_cnt = cnt

    def cache_chunks(self, cio, stg, c16):
        P = self.P
        k = 0
        for src, dst, nblk, per in ((cio["wU"], cio["wU16"], 32, 1024), (cio["wD"], cio["wD16"], 8, 4096)):
            for blk in range(nblk):
                sflat = src[blk].rearrange("p a b -> p (a b)")
                for o in range(0, per, 1024):
                    def emit(bi=k % len(stg), sflat=sflat, dst=dst, blk=blk, o=o):
                        P.dma("sp", stg[bi], sflat[:, o:o + 1024], f"cstg{bi}", writes=[("cstg", bi)])
                        P.op("dve", (lambda e, a=c16[bi], b=stg[bi]: e.tensor_copy(out=a, in_=b)),
                             reads=[("cstg", bi)], writes=[("cc16", bi)])
                        P.dma("sp", dst[blk][:, o:o + 1024], c16[bi], f"cwc{bi}", reads=[("cc16", bi)])
                    yield emit
                    k += 1

    def stage_p0(self, io):
        P, ar = self.P, self.ar
        xT, xres_d, xb_d = io["xT"], io["xres_o"], io["xb_o"]
        ar.mark()
        x32 = ar.alloc(8 * TH, F32)
        x16 = ar.alloc(8 * TH, BF16)
        for dc in range(8):
            sl = slice(dc * TH, (dc + 1) * TH)
            P.dma("sp", x32[:, sl], xT[dc * 128:(dc + 1) * 128, :], "p0l", writes=[("x32", dc)])
            P.op("dve", (lambda e, a=x16[:, sl], b=x32[:, sl]: e.tensor_copy(out=a, in_=b)),
                 reads=[("x32", dc)], writes=[("x16", dc)])
            if xres_d is not None:
                P.dma("sp", xres_d[dc * 128:(dc + 1) * 128, :], x32[:, sl], "p0s", reads=[("x32", dc)])
            P.dma("sp", xb_d[dc * 128:(dc + 1) * 128, :], x16[:, sl], "p0s", reads=[("x16", dc)])
        ar.release()

    def stage_a(self, io):
        P, ar, ps, psb, psb2 = self.P, self.ar, self.ps, self.psb, self.psb2
        xall, rs_d = io["xall"], io["rs"]
        lastm = io.get("last", False)
        cA_d, lA_d = io["cA"], io["lA"]
        wAf_d, wAt_d = io["wAf"], io["wAt"]
        ar.mark()
        cA = ar.alloc(CA_N, F32)
        lA = ar.alloc(4, F32)
        c16 = ar.alloc(384, BF16)
        P.dma("sp", cA, cA_d, "cA", writes=["cA"])
        P.dma("sp", lA, lA_d, "lA", writes=["lA"])
        P.op("dve", lambda e: e.tensor_copy(out=c16[:, 0:128], in_=cA[:, CA_IDENT:CA_IDENT + 128]), reads=["cA"], writes=["c16a"])
        P.op("dve", lambda e: e.tensor_copy(out=c16[:, 128:384], in_=cA[:, CA_U:CA_U + 256]), reads=["cA"], writes=["c16b"])
        ident, U16, L16 = c16[:, 0:128], c16[:, 128:256], c16[:, 256:384]
        caus = cA[:, CA_CAUS:CA_CAUS + 128]
        rgT = ar.alloc(2 * S, BF16)
        sqT = ar.alloc(2 * S, BF16)
        skT = ar.alloc(2 * S, BF16)
        qdT = ar.alloc(2 * S, BF16)
        kdT = ar.alloc(2 * S, BF16)
        kdk = ar.alloc(32 * 256, BF16)
        vtk = ar.alloc(32 * 512, BF16)
        ar.mark()
        wf = ar.alloc(6 * 1024, BF16)
        wt = ar.alloc(2 * 4096, BF16)
        stg = [ar.alloc(1024, F32) for _ in range(2)]
        cR_d = io["cR"]
        crt = [ar.alloc(512, F32) for _ in range(2)]
        xt = [ar.alloc(8 * 512, BF16) for _ in range(2)]
        qk32 = [ar.alloc(512, F32)] * 2
        qk16 = [ar.alloc(512, BF16) for _ in range(2)]
        tmpr = [ar.alloc(512, F32)] * 2
        for cb in range(6):
            self.load_cast(wf[:, cb * 1024:(cb + 1) * 1024], wAf_d[cb].rearrange("p a b -> p (a b)"), 1024, ("wf", cb), stg)
        for g in range(2):
            self.load_cast(wt[:, g * 4096:(g + 1) * 4096], wAt_d[g].rearrange("p a b -> p (a b)"), 4096, ("wt", g), stg)

        if io.get("pre_x") is not None:
            io["pre_x"]()
        for T in range(8):
            xb_ = xt[T % 2]
            r, t0 = T // 4, (T % 4) * 512
            P.dma("sp", xb_.rearrange("p (dc t) -> p dc t", dc=8),
                  xall[r].rearrange("(dc p) t -> p dc t", p=128)[:, :, t0:t0 + 512],
                  f"xt{T % 2}", writes=[("xt", T % 2)])
            P.dma("sp", crt[T % 2][:, 0:256], cR_d[:, CR_COS + T * 256: CR_COS + (T + 1) * 256], f"cr{T % 2}", writes=[("crt", T % 2)])
            P.dma("sp", crt[T % 2][:, 256:512], cR_d[:, CR_SIN + T * 256: CR_SIN + (T + 1) * 256], f"cr{T % 2}", writes=[("crt", T % 2)])
            for cb in (range(6) if "fm" in SUB else []):
                bank = ps[cb % 2]
                for dc in range(8):
                    P.op("pe", (lambda e, o=bank[:, :], a=wf[:, cb * 1024 + dc * 128: cb * 1024 + (dc + 1) * 128],
                                b=xb_[:, dc * 512:(dc + 1) * 512], st=(dc == 0), sp=(dc == 7):
                                e.matmul(o, lhsT=a, rhs=b, start=st, stop=sp)),
                         reads=[("wf", cb), ("xt", T % 2)], writes=[("ps", cb % 2)])
                if cb < 2:
                    dst = rgT[:, cb * S + T * 512: cb * S + (T + 1) * 512]
                    P.op("act", (lambda e, o=dst, i=bank[:, :]: e.activation(out=o, in_=i, func=AF.Silu)),
                         reads=[("ps", cb % 2)], writes=[("rgT", cb, T)])
                elif cb < 4:
                    dst = sqT[:, (cb - 2) * S + T * 512: (cb - 2) * S + (T + 1) * 512]
                    P.op("act", (lambda e, o=dst, i=bank[:, :]: e.activation(out=o, in_=i, func=AF.Copy, scale=0.125)),
                         reads=[("ps", cb % 2)], writes=[("sqT", cb - 2, T)])
                else:
                    dst = skT[:, (cb - 4) * S + T * 512: (cb - 4) * S + (T + 1) * 512]
                    P.op("dve", (lambda e, o=dst, i=bank[:, :]: e.tensor_copy(out=o, in_=i)),
                         reads=[("ps", cb % 2)], writes=[("skT", cb - 4, T)])
            for q in (range(4) if "tm" in SUB else []):
                n = T * 4 + q
                pq, pv = ps[2 + (n % 2)], ps[4 + (n % 2)]
                for g, bank in ((0, pq), (1, pv)):
                    for dc in range(8):
                        P.op("pe", (lambda e, o=bank[:, :], a=xb_[:, dc * 512 + q * 128: dc * 512 + (q + 1) * 128],
                                    b=wt[:, g * 4096 + dc * 512: g * 4096 + (dc + 1) * 512], st=(dc == 0), sp=(dc == 7):
                                    e.matmul(o, lhsT=a, rhs=b, start=st, stop=sp)),
                             reads=[("wt", g), ("xt", T % 2)], writes=[("ps", 2 + 2 * g + (n % 2))])
                P.op("act", (lambda e, o=vtk[:, n * 512:(n + 1) * 512], i=pv[:, :]: e.copy(out=o, in_=i)),
                     reads=[("ps", 4 + (n % 2))], writes=[("vtk", n)])
                if "rot" not in SUB:
                    continue
                A32, T32, O16 = qk32[n % 2], tmpr[n % 2], qk16[n % 2]
                X = pq[:, :].rearrange("p (g two f) -> p g two f", g=4, two=2)
                A4 = A32.rearrange("p (g two f) -> p g two f", g=4, two=2)
                T4 = T32.rearrange("p (g two f) -> p g two f", g=4, two=2)
                cosb = crt[T % 2][:, q * 64:(q + 1) * 64].unsqueeze(1).to_broadcast([128, 4, 64])
                sinb = crt[T % 2][:, 256 + q * 64: 256 + (q + 1) * 64].unsqueeze(1).to_broadcast([128, 4, 64])
                rk_ = [("ps", 2 + (n % 2)), ("crt", T % 2)]
                P.op("dve", (lambda e, o=A4[:, :, 0, :], a=X[:, :, 0, :], b=cosb: e.tensor_tensor(out=o, in0=a, in1=b, op=ALU.mult)),
                     reads=rk_, writes=[("A32a", 0)])
                P.op("dve", (lambda e, o=A4[:, :, 1, :], a=X[:, :, 1, :], b=cosb: e.tensor_tensor(out=o, in0=a, in1=b, op=ALU.mult)),
                     reads=rk_, writes=[("A32b", 0)])
                P.op("dve", (lambda e, o=T4[:, :, 0, :], a=X[:, :, 1, :], b=sinb: e.tensor_tensor(out=o, in0=a, in1=b, op=ALU.mult)),
                     reads=rk_, writes=[("T32a", 0)])
                P.op("dve", (lambda e, o=T4[:, :, 1, :], a=X[:, :, 0, :], b=sinb: e.tensor_tensor(out=o, in0=a, in1=b, op=ALU.mult)),
                     reads=rk_, writes=[("T32b", 0)])
                P.op("pool", (lambda e, o=A4[:, :, 0, :], a=A4[:, :, 0, :], b=T4[:, :, 0, :]: e.tensor_tensor(out=o, in0=a, in1=b, op=ALU.subtract)),
                     reads=[("A32a", 0), ("T32a", 0)], writes=[("A32a", 0)])
                P.op("pool", (lambda e, o=A4[:, :, 1, :], a=A4[:, :, 1, :], b=T4[:, :, 1, :]: e.tensor_tensor(out=o, in0=a, in1=b, op=ALU.add)),
                     reads=[("A32b", 0), ("T32b", 0)], writes=[("A32b", 0)])
                decb = cA[:, CA_DEC:CA_DEC + 4].unsqueeze(2).to_broadcast([128, 4, 128])
                P.op("pool", (lambda e, o=O16.rearrange("p (g f) -> p g f", g=4), a=A32.rearrange("p (g f) -> p g f", g=4), b=decb:
                              e.tensor_tensor(out=o, in0=a, in1=b, op=ALU.mult)),
                     reads=[("A32a", 0), ("A32b", 0), "cA"], writes=[("qk16", n % 2)])
                P.op("pool", (lambda e, o=kdk[:, n * 256:(n + 1) * 256], i=O16[:, 256:512]: e.tensor_copy(out=o, in_=i)),
                     reads=[("qk16", n % 2)], writes=[("kdk", n)])
                if "tr" not in SUB:
                    continue
                for g in range(4):
                    pT = psb if g < 2 else psb2
                    P.op("pe", (lambda e, o=pT[:, (g % 2) * 128:(g % 2 + 1) * 128], i=O16[:, g * 128:(g + 1) * 128]:
                                e.transpose(out=o, in_=i, identity=ident)),
                         reads=[("qk16", n % 2), "c16a"], writes=["psb" if g < 2 else "psb2"])
                for h in range(2):
                    P.op("act", (lambda e, o=qdT[:, h * S + n * 128: h * S + (n + 1) * 128], i=psb[:, h * 128:(h + 1) * 128]: e.copy(out=o, in_=i)),
                         reads=["psb"], writes=[("qdT", h, n)])
                    P.op("dve", (lambda e, o=kdT[:, h * S + n * 128: h * S + (n + 1) * 128], i=psb2[:, h * 128:(h + 1) * 128]: e.tensor_copy(out=o, in_=i)),
                         reads=["psb2"], writes=[("kdT", h, n)])
        ar.release()
        P.barrier()

        ar.mark()
        if "ret" not in PARTS:
            ar.release(); ar.release(); return
        rso = ar.alloc(2 * S, BF16)
        st32 = ar.alloc(256, F32)
        st16 = ar.alloc(256, BF16)
        std = [ar.alloc(256, BF16) for _ in range(2)]
        nrm = [ar.alloc(256, BF16) for _ in range(2)]
        stt = [ar.alloc(32, F32) for _ in range(2)]
        tmpg = [ar.alloc(256, F32) for _ in range(2)]
        for n in range(32):
            pb = n % 2
            pS, pO, pK = ps[0 + pb], ps[2 + pb], ps[4 + pb]
            H = [(h, slice(h * S + n * 128, h * S + (n + 1) * 128), slice(h * 128, (h + 1) * 128)) for h in range(2)]
            qry = not (lastm and n < 16)
            for h, csl, hs in (H if qry else []):
                P.op("pe", (lambda e, o=pS[:, hs], a=kdT[:, csl], b=qdT[:, csl]: e.matmul(o, lhsT=a, rhs=b, start=True, stop=True)),
                     reads=[("kdT", h, n), ("qdT", h, n)], writes=[("pS", pb)])
            for h, csl, hs in (H if qry else []):
                P.op("dve", (lambda e, o=std[pb][:, hs], a=pS[:, hs], s_=cA[:, CA_CH + h:CA_CH + h + 1], m=caus:
                             e.scalar_tensor_tensor(out=o, in0=a, scalar=s_, in1=m, op0=ALU.mult, op1=ALU.mult)),
                     reads=[("pS", pb), "cA"], writes=[("std", pb, h)])
            for h, csl, hs in (H if qry else []):
                vsl = vtk[:, n * 512 + h * 128: n * 512 + (h + 1) * 128]
                P.op("pe", (lambda e, o=pO[:, hs], a=std[pb][:, hs], b=vsl, sp=(n == 0): e.matmul(o, lhsT=a, rhs=b, start=True, stop=sp)),
                     reads=[("std", pb, h), ("vtk", n)], writes=[("pO", pb)])
                if n > 0:
                    P.op("pe", (lambda e, o=pO[:, hs], a=qdT[:, csl], b=st16[:, hs]: e.matmul(o, lhsT=a, rhs=b, start=False, stop=True)),
                         reads=[("qdT", h, n), ("st16", h)], writes=[("pO", pb)])
            for h, csl, hs in H:
                vsl = vtk[:, n * 512 + h * 128: n * 512 + (h + 1) * 128]
                P.op("pe", (lambda e, o=pK[:, hs], a=kdk[:, n * 256 + h * 128: n * 256 + (h + 1) * 128], b=vsl: e.matmul(o, lhsT=a, rhs=b, start=True, stop=True)),
                     reads=[("kdk", n), ("vtk", n)], writes=[("pK", pb)])
            for h, csl, hs in H:
                if n == 0:
                    P.op("dve", (lambda e, o=st32[:, hs], i=pK[:, hs]: e.tensor_copy(out=o, in_=i)),
                         reads=[("pK", pb)], writes=[("st32", h)])
                else:
                    P.op("dve", (lambda e, o=st32[:, hs], a=st32[:, hs], s_=cA[:, CA_CD + h:CA_CD + h + 1], b=pK[:, hs]:
                                 e.scalar_tensor_tensor(out=o, in0=a, scalar=s_, in1=b, op0=ALU.mult, op1=ALU.add)),
                         reads=[("pK", pb), ("st32", h), "cA"], writes=[("st32", h)])
                P.op("pool", (lambda e, o=st16[:, hs], i=st32[:, hs]: e.tensor_copy(out=o, in_=i)),
                     reads=[("st32", h)], writes=[("st16", h)])
            if not qry:
                continue
            sv = stt[pb]
            for h, csl, hs in H:
                b0 = h * 16
                P.op("dve", (lambda e, o=sv[:, b0:b0 + 6], i=pO[:, hs]: e.bn_stats(out=o, in_=i)),
                     reads=[("pO", pb)], writes=[("stt", pb, h)])
                P.op("dve", (lambda e, o=sv[:, b0 + 8:b0 + 10], i=sv[:, b0:b0 + 6]: e.bn_aggr(out=o, in_=i)),
                     reads=[("stt", pb, h)], writes=[("stt", pb, h)])
            for h, csl, hs in H:
                b0 = h * 16
                P.op("act", (lambda e, o=sv[:, b0 + 11:b0 + 12], i=sv[:, b0 + 9:b0 + 10]: e.activation(out=o, in_=i, func=AF.Ln, bias=EPS)),
                     reads=[("stt", pb, h)], writes=[("stt", pb, h)])
            for h, csl, hs in H:
                b0 = h * 16
                P.op("act", (lambda e, o=sv[:, b0 + 10:b0 + 11], i=sv[:, b0 + 11:b0 + 12]: e.activation(out=o, in_=i, func=AF.Exp, scale=-0.5)),
                     reads=[("stt", pb, h)], writes=[("stt", pb, h)])
            for h, csl, hs in H:
                b0 = h * 16
                P.op("dve", (lambda e, o=nrm[pb][:, hs], a=pO[:, hs], m=sv[:, b0 + 8:b0 + 9], r=sv[:, b0 + 10:b0 + 11]:
                             e.tensor_scalar(out=o, in0=a, scalar1=m, scalar2=r, op0=ALU.subtract, op1=ALU.mult)),
                     reads=[("pO", pb), ("stt", pb, h)], writes=[("nrm", pb, h)])
            pT = psb if pb == 0 else psb2
            for h, csl, hs in H:
                P.op("pe", (lambda e, o=pT[:, hs], i=nrm[pb][:, hs]: e.transpose(out=o, in_=i, identity=ident)),
                     reads=[("nrm", pb, h), "c16a"], writes=[("psbr", pb)])
            for h, csl, hs in H:
                P.op("dve", (lambda e, o=tmpg[pb][:, hs], a=pT[:, hs], g=lA[:, h:h + 1], b=lA[:, 2 + h:3 + h]:
                             e.tensor_scalar(out=o, in0=a, scalar1=g, scalar2=b, op0=ALU.mult, op1=ALU.add)),
                     reads=[("psbr", pb), "lA"], writes=[("tmpg", pb, h)])
                P.op("pool", (lambda e, o=rso[:, csl], a=tmpg[pb][:, hs], b=rgT[:, csl]: e.tensor_tensor(out=o, in0=a, in1=b, op=ALU.mult)),
                     reads=[("tmpg", pb, h), ("rgT", h, n // 4)], writes=[("rso", h)])
        for h in range(2):
            P.dma("sp", rs_d[h * 128:(h + 1) * 128, :], rso[:, h * S + (TH if lastm else 0):(h + 1) * S], "rsst", reads=[("rso", h)])
        P.barrier()
        ar.release()

        ar.mark()
        if "sb" not in PARTS:
            ar.release(); ar.release(); return
        sbo = ar.alloc(4 * S, BF16, parts=64)
        e32 = [ar.alloc(512, F32) for _ in range(4)]
        sp16 = [ar.alloc(512, BF16) for _ in range(4)]
        w32 = [ar.alloc(512, F32) for _ in range(2)]
        a16 = [ar.alloc(512, BF16) for _ in range(4)]
        sbm = cA[:, CA_SBM:CA_SBM + 2048]
        cgen = None
        if io.get("cache_io") is not None:
            cstg = [ar.alloc(1024, F32) for _ in range(2)]
            cc16 = [ar.alloc(1024, BF16) for _ in range(2)]
            cgen = self.cache_chunks(io["cache_io"], cstg, cc16)
        step_i = 0

        def emit_z(s, hd, T, kb):
            base, pr = (hd % 2) * 64, hd // 2
            c0 = max(0, kb - 4 * T) * 128
            P.op("pe", (lambda e, o=ps[s][:, c0:], a=skT[base:base + 64, pr * S + kb * 128: pr * S + (kb + 1) * 128],
                        b=sqT[base:base + 64, pr * S + T * 512 + c0: pr * S + (T + 1) * 512]: e.matmul(o, lhsT=a, rhs=b, start=True, stop=True)),
                 reads=[("skT", pr, kb // 4), ("sqT", pr, T)], writes=[("pz", s)])

        for pr in range(2):
            for T in (range(4, 8) if lastm else range(8)):
                kbs = list(range(4 * T + 3, -1, -1))
                for s in range(2):
                    emit_z(s, pr * 2 + s, T, kbs[0])
                for ki, kb in enumerate(kbs):
                    first, last = (ki == 0), (ki == len(kbs) - 1)
                    c0 = max(0, kb - 4 * T) * 128
                    step_i += 1
                    pj = step_i % 2
                    if cgen is not None and step_i % 3 == 0:
                        em = next(cgen, None)
                        if em is not None:
                            em()
                    for s in range(2):
                        P.op("act", (lambda e, o=e32[s + 2 * pj][:, c0:], i=ps[s][:, c0:]: e.activation(out=o, in_=i, func=AF.Exp)),
                             reads=[("pz", s)], writes=[("e32", s, pj)])
                    if kb >= 4 * T:
                        r = kb - 4 * T
                        for s in range(2):
                            P.op("dve", (lambda e, o=e32[s + 2 * pj][:, c0:], a=e32[s + 2 * pj][:, c0:], m=sbm[:, r * 512 + c0:(r + 1) * 512]: e.tensor_tensor(out=o, in0=a, in1=m, op=ALU.mult)),
                                 reads=[("e32", s, pj), "cA"], writes=[("e32", s, pj)])
                    for s in range(2):
                        P.op("act", (lambda e, o=sp16[s + 2 * pj][:, c0:], i=e32[s + 2 * pj][:, c0:]: e.activation(out=o, in_=i, func=AF.Ln, bias=1.0)),
                             reads=[("e32", s, pj)], writes=[("sp16", s, pj)])
                    for s in range(2):
                        P.op("pe", (lambda e, o=ps[2 + s][:, c0:], b=sp16[s + 2 * pj][:, c0:], st=first: e.matmul(o, lhsT=U16, rhs=b, start=st, stop=False, skip_group_check=True)),
                             reads=[("sp16", s, pj), "c16b"], writes=[("pR", s)])
                    if not last:
                        for s in range(2):
                            emit_z(s, pr * 2 + s, T, kbs[ki + 1])
                    for s in range(2):
                        P.op("act", (lambda e, o=w32[s][:, c0:], i=ps[2 + s][:, c0:]: e.activation(out=o, in_=i, func=AF.Exp, scale=-1.0)),
                             reads=[("pR", s)], writes=[("w32", s)])
                    for s in range(2):
                        P.op("pe", (lambda e, o=ps[2 + s][:, c0:], b=sp16[s + 2 * pj][:, c0:], sp_=last: e.matmul(o, lhsT=L16, rhs=b, start=False, stop=True if sp_ else False, skip_group_check=True)),
                             reads=[("sp16", s, pj), "c16b"], writes=[("pR", s)])
                    for s in range(2):
                        P.op("dve", (lambda e, o=a16[s + 2 * pj][:, c0:], a=e32[s + 2 * pj][:, c0:], b=w32[s][:, c0:]: e.tensor_tensor(out=o, in0=a, in1=b, op=ALU.mult)),
                             reads=[("e32", s, pj), ("w32", s)], writes=[("a16", s, pj)])
                    for s in range(2):
                        hd = pr * 2 + s
                        P.op("pe", (lambda e, o=ps[4 + s][0:64, c0:], a=vtk[:, kb * 512 + 256 + hd * 64: kb * 512 + 256 + (hd + 1) * 64], b=a16[s + 2 * pj][:, c0:], st=first, sp_=last:
                                    e.matmul(o, lhsT=a, rhs=b, start=st, stop=sp_, skip_group_check=True)),
                             reads=[("a16", s, pj), ("vtk", kb)], writes=[("po", s)])
                for s in range(2):
                    hd = pr * 2 + s
                    P.op("act" if s else "dve",
                         (lambda e, o=sbo[:, hd * S + T * 512: hd * S + (T + 1) * 512], i=ps[4 + s][0:64, :], s_=s:
                          (e.copy(out=o, in_=i) if s_ else e.tensor_copy(out=o, in_=i))),
                         reads=[("po", s)], writes=[("sbo", hd)])
        if cgen is not None:
            for em in cgen:
                em()
        for hd in range(4):
            P.dma("sp", rs_d[256 + hd * 64: 256 + (hd + 1) * 64, :], sbo[:, hd * S + (TH if lastm else 0):(hd + 1) * S], "rsst", reads=[("sbo", hd)])
        P.barrier()
        ar.release()
        ar.release()

    def stage_b(self, io):
        P, ar, ps, psb = self.P, self.ar, self.ps, self.psb
        xres_d, xb_d, rsa_d = io["xres_i"], io["xb_i"], io["rsall"]
        xres_o, xb_o = io["xres_o"], io["xb_o"]
        cB_d, lB_d = io["cB"], io["lB"]
        wGu_d, wGv_d, wGt_d = io["wGu"], io["wGv"], io["wGt"]
        pR_d, pS_d, pG_d = io["pR"], io["pS"], io["pG"]
        wO_d, wU_d, wD_d = io["wO"], io["wU"], io["wD"]
        wU16, wD16 = io["wU16"], io["wD16"]

        ar.mark()
        cB = ar.alloc(CB_N, F32)
        lV = ar.alloc(LV_N, F32)
        P.dma("sp", cB, cB_d, "cB", writes=["cB"])
        P.dma("sp", lV, io["lV"], "lV", writes=["lV"])
        onesF = cB[:, CB_ONES:CB_ONES + 128]
        xres = ar.alloc(8 * TH, F32)
        xb = ar.alloc(8 * TH, BF16)
        stg = [ar.alloc(1024, F32) for _ in range(2)]
        XR8 = [("xres", dc) for dc in range(8)]
        for dc in range(8):
            P.dma("sp", xb[:, dc * TH:(dc + 1) * TH], xb_d[dc * 128:(dc + 1) * 128, :], "bld", writes=[("xb", dc)])

        def load_xres():
            if io.get("dyn"):
                P.custom_dma("sp", (lambda e, o=xres.rearrange("p (dc t) -> p dc t", dc=8):
                                    e.dma_start(out=o, in_=xres_d[bass.ds(self.parity(e), 1)].rearrange("o (dc p) t -> p (o dc) t", p=128))),
                             "bld", writes=XR8)
            else:
                for dc in range(8):
                    P.dma("sp", xres[:, dc * TH:(dc + 1) * TH], xres_d[dc * 128:(dc + 1) * 128, :], "bld", writes=[("xres", dc)])
        XR = [("xres", dc) for dc in range(8)]
        XB = [("xb", dc) for dc in range(8)]

        ar.mark()
        c16 = [ar.alloc(1024, BF16) for _ in range(2)]
        k = 0
        for src, dst, nblk, per, wk in (((wU_d, wU16, 32, 1024, "wcU"), (wD_d, wD16, 8, 4096, "wcD")) if io["make_cache"] else ()):
            for blk in range(nblk):
                sflat = src[blk].rearrange("p a b -> p (a b)")
                for o in range(0, per, 1024):
                    w = min(1024, per - o)
                    bi = k % 2
                    k += 1
                    P.dma("sp", stg[bi][:, 0:w], sflat[:, o:o + w], f"stg{bi}", writes=[("stg", bi)])
                    P.op("pool" if bi else "dve", (lambda e, a=c16[bi][:, 0:w], b=stg[bi][:, 0:w]: e.tensor_copy(out=a, in_=b)),
                         reads=[("stg", bi)], writes=[("c16", bi)])
                    P.dma("sp", dst[blk][:, o:o + w], c16[bi][:, 0:w], f"wc{bi}", reads=[("c16", bi)], writes=[(wk, blk, o)])
        ar.release()
        if io["make_cache"]:
            P.barrier()

        ar.mark()
        rsf = ar.alloc(8 * TH, BF16)

        def load_rsf():
          for hh in range(2):
            for c4 in range(2):
                for base, slot in ((0, hh * 2 + c4), (256, 4 + hh * 2 + c4)):
                    dst = rsf[:, slot * TH:(slot + 1) * TH]
                    if io.get("dyn_rs"):
                        P.custom_dma("sp", (lambda e, o=dst, hh=hh, r0=base + c4 * 128:
                                            e.dma_start(out=o, in_=rsa_d.rearrange("h r (two t) -> h r two t", two=2)[hh, r0:r0 + 128, bass.ds(self.parity(e), 1), :]
                                                        .rearrange("p o t -> p (o t)"))),
                                     "bld", writes=[("rsf", slot)])
                    else:
                        P.dma("sp", dst, rsa_d[hh, base + c4 * 128: base + (c4 + 1) * 128, :], "bld", reads=["rsall_d"], writes=[("rsf", slot)])
        sgT = ar.alloc(4 * TH, BF16)
        ar.mark()
        lB = ar.alloc(LB_N, F32)
        P.dma("sp", lB, lB_d, "lB", writes=["lB"])
        wgu = ar.alloc(4 * 1024, BF16)
        wgv = ar.alloc(4096, BF16)
        wsg = ar.alloc(512, BF16)
        guT = [ar.alloc(4 * 512, BF16) for _ in range(2)]
        g32 = [ar.alloc(512, F32) for _ in range(2)]
        vn = [ar.alloc(512, BF16) for _ in range(2)]
        stt = [ar.alloc(32, F32) for _ in range(2)]
        t32 = [ar.alloc(512, F32) for _ in range(2)]
        for cb in range(4):
            self.load_cast(wgu[:, cb * 1024:(cb + 1) * 1024], wGu_d[cb].rearrange("p a b -> p (a b)"), 1024, ("wgu", cb), stg)
        self.load_cast(wgv, wGv_d[0].rearrange("p a b -> p (a b)"), 4096, "wgv", stg)
        if io.get("pre_rs") is not None:
            io["pre_rs"]()
        load_rsf()
        load_xres()
        causb = cB[:, CB_CAUS:CB_CAUS + 128].unsqueeze(1).to_broadcast([128, 4, 128])
        P.op("dve", (lambda e: e.tensor_tensor(out=wsg.rearrange("p (g i) -> p g i", g=4), in0=lB[:, LB_WT:LB_WT + 512].rearrange("p (g i) -> p g i", g=4),
                                               in1=causb, op=ALU.mult)), reads=["lB", "cB"], writes=["wsg"])
        for T in range(4):
            gb = guT[T % 2]
            for cb in range(4):
                bank = ps[cb % 2]
                for dc in range(8):
                    P.op("pe", (lambda e, o=bank[:, :], a=wgu[:, cb * 1024 + dc * 128: cb * 1024 + (dc + 1) * 128],
                                b=xb[:, dc * TH + T * 512: dc * TH + (T + 1) * 512], st=(dc == 0), sp=(dc == 7): e.matmul(o, lhsT=a, rhs=b, start=st, stop=sp)),
                         reads=[("wgu", cb), ("xb", dc)], writes=[("ps", cb % 2)])
                P.op("act", (lambda e, o=gb[:, cb * 512:(cb + 1) * 512], i=bank[:, :]: e.activation(out=o, in_=i, func=AF.Gelu_apprx_tanh)),
                     reads=[("ps", cb % 2)], writes=[("guT", T % 2, cb)])
            for q in range(4):
                n = T * 4 + q
                pb = n % 2
                pv, psv = ps[2 + pb], ps[4 + pb]
                for dc in range(8):
                    P.op("pe", (lambda e, o=pv[:, :], a=xb[:, dc * TH + n * 128: dc * TH + (n + 1) * 128], b=wgv[:, dc * 512:(dc + 1) * 512], st=(dc == 0), sp=(dc == 7):
                                e.matmul(o, lhsT=a, rhs=b, start=st, stop=sp)),
                         reads=["wgv", ("xb", dc)], writes=[("ps", 2 + pb)])
                P.op("act", (lambda e, o=g32[pb], i=pv[:, :]: e.activation(out=o, in_=i, func=AF.Gelu_apprx_tanh)),
                     reads=[("ps", 2 + pb)], writes=[("g32", pb)])
                sv = stt[pb]
                P.op("dve", (lambda e, o=sv[:, 0:6], i=g32[pb]: e.bn_stats(out=o, in_=i)), reads=[("g32", pb)], writes=[("stt", pb)])
                P.op("dve", (lambda e, o=sv[:, 8:10], i=sv[:, 0:6]: e.bn_aggr(out=o, in_=i)), reads=[("stt", pb)], writes=[("stt", pb)])
                P.op("act", (lambda e, o=sv[:, 11:12], i=sv[:, 9:10]: e.activation(out=o, in_=i, func=AF.Ln, bias=EPS)), reads=[("stt", pb)], writes=[("stt", pb)])
                P.op("act", (lambda e, o=sv[:, 10:11], i=sv[:, 11:12]: e.activation(out=o, in_=i, func=AF.Exp, scale=-0.5)), reads=[("stt", pb)], writes=[("stt", pb)])
                P.op("dve", (lambda e, o=t32[pb], a=g32[pb], m=sv[:, 8:9], r=sv[:, 10:11]: e.tensor_scalar(out=o, in0=a, scalar1=m, scalar2=r, op0=ALU.subtract, op1=ALU.mult)),
                     reads=[("g32", pb), ("stt", pb)], writes=[("t32", pb)])
                P.op("pool", (lambda e, o=t32[pb], a=t32[pb], b=lB[:, LB_LNG:LB_LNG + 512]: e.tensor_tensor(out=o, in0=a, in1=b, op=ALU.mult)),
                     reads=[("t32", pb), "lB"], writes=[("t32", pb)])
                P.op("pool", (lambda e, o=vn[pb], a=t32[pb], b=lB[:, LB_LNB:LB_LNB + 512]: e.tensor_tensor(out=o, in0=a, in1=b, op=ALU.add)),
                     reads=[("t32", pb), "lB"], writes=[("vn", pb)])
                for g in range(4):
                    P.op("pe", (lambda e, o=psv[:, g * 128:(g + 1) * 128], a=vn[pb][:, g * 128:(g + 1) * 128], b=wsg[:, g * 128:(g + 1) * 128]:
                                e.matmul(o, lhsT=a, rhs=b, start=True, stop=True)),
                         reads=[("vn", pb), "wsg"], writes=[("ps", 4 + pb)])
                P.op("dve", (lambda e, o=t32[pb], a=psv[:, :], b=lB[:, LB_SB:LB_SB + 512]: e.tensor_tensor(out=o, in0=a, in1=b, op=ALU.add)),
                     reads=[("ps", 4 + pb), ("t32", pb), "lB"], writes=[("t32", pb)])
                gview = gb.rearrange("p (g t) -> p g t", g=4)[:, :, q * 128:(q + 1) * 128]
                oview = sgT.rearrange("p (g t) -> p g t", g=4)[:, :, n * 128:(n + 1) * 128]
                P.op("pool", (lambda e, o=oview, a=t32[pb].rearrange("p (g i) -> p g i", g=4), b=gview: e.tensor_tensor(out=o, in0=a, in1=b, op=ALU.mult)),
                     reads=[("t32", pb)] + [("guT", T % 2, cb) for cb in range(4)], writes=[("sgT", n)])
        ar.release()
        P.barrier()
        SG = [("sgT", n) for n in range(16)]
        RS = [("rsf", i) for i in range(8)]

        mg = ar.alloc(8 * TH, BF16)
        ar.mark()
        wg = [ar.alloc(3 * 1024, BF16)] * 2
        wp = [ar.alloc(3 * 512, BF16)] * 2
        sg32 = [ar.alloc(512, F32) for _ in range(2)]
        m32 = [ar.alloc(512, F32) for _ in range(2)]
        srcs = [(pR_d, 0, "ret"), (pS_d, 4, "sb"), (pG_d, None, "sgu")]
        for cb in range(8):
            wb = 0
            for br in range(3):
                self.load_cast(wg[wb][:, br * 1024:(br + 1) * 1024], wGt_d[br * 8 + cb].rearrange("p a b -> p (a b)"), 1024, ("wg", wb, br), stg)
                self.load_cast(wp[wb][:, br * 512:(br + 1) * 512], srcs[br][0][cb].rearrange("p a b -> p (a b)"), 512, ("wp", wb, br), stg)
            for T in range(4):
                for br in range(3):
                    j = (T * 3 + br) % 2
                    pg, pp = ps[j], ps[2 + j]
                    for dc in range(8):
                        P.op("pe", (lambda e, o=pg[:, :], a=wg[wb][:, br * 1024 + dc * 128: br * 1024 + (dc + 1) * 128],
                                    b=xb[:, dc * TH + T * 512: dc * TH + (T + 1) * 512], st=(dc == 0), sp=(dc == 7): e.matmul(o, lhsT=a, rhs=b, start=st, stop=sp)),
                             reads=[("wg", wb, br), ("xb", dc)], writes=[("ps", j)])
                    P.op("act", (lambda e, o=sg32[j], i=pg[:, :]: e.activation(out=o, in_=i, func=AF.Sigmoid)),
                         reads=[("ps", j)], writes=[("sg32", j)])
                    for kc in range(4):
                        if br < 2:
                            rhs = rsf[:, (srcs[br][1] + kc) * TH + T * 512: (srcs[br][1] + kc) * TH + (T + 1) * 512]
                            rk = [("rsf", srcs[br][1] + kc)]
                        else:
                            rhs = sgT[:, kc * TH + T * 512: kc * TH + (T + 1) * 512]
                            rk = SG[T * 4:(T + 1) * 4]
                        P.op("pe", (lambda e, o=pp[:, :], a=wp[wb][:, br * 512 + kc * 128: br * 512 + (kc + 1) * 128], b=rhs, st=(kc == 0), sp=(kc == 3):
                                    e.matmul(o, lhsT=a, rhs=b, start=st, stop=sp)),
                             reads=[("wp", wb, br)] + rk, writes=[("ps", 2 + j)])
                    mt = m32[T % 2]
                    if br == 0:
                        P.op("dve", (lambda e, o=mt, a=pp[:, :], b=sg32[j]: e.tensor_tensor(out=o, in0=a, in1=b, op=ALU.mult)),
                             reads=[("ps", 2 + j), ("sg32", j)], writes=[("m32", T % 2)])
                    else:
                        P.op("dve", (lambda e, o=sg32[j], a=pp[:, :], b=sg32[j]: e.tensor_tensor(out=o, in0=a, in1=b, op=ALU.mult)),
                             reads=[("ps", 2 + j), ("sg32", j)], writes=[("sg32", j)])
                        dst = mt if br == 1 else mg[:, cb * TH + T * 512: cb * TH + (T + 1) * 512]
                        wk = [("m32", T % 2)] if br == 1 else [("mg", cb)]
                        P.op("pool", (lambda e, o=dst, a=mt, b=sg32[j]: e.tensor_tensor(out=o, in0=a, in1=b, op=ALU.add)),
                             reads=[("m32", T % 2), ("sg32", j)], writes=wk)
        ar.release()
        P.barrier()

        ar.mark()
        wo = [ar.alloc(1024, BF16) for _ in range(2)]
        for cb in range(8):
            wb = cb % 2
            self.load_cast(wo[wb], wO_d[cb].rearrange("p a b -> p (a b)"), 1024, ("wo", wb), stg)
            for T in range(4):
                bank = ps[T % 2]
                for kc in range(8):
                    P.op("pe", (lambda e, o=bank[:, :], a=wo[wb][:, kc * 128:(kc + 1) * 128], b=mg[:, kc * TH + T * 512: kc * TH + (T + 1) * 512], st=(kc == 0), sp=(kc == 7):
                                e.matmul(o, lhsT=a, rhs=b, start=st, stop=sp)),
                         reads=[("wo", wb), ("mg", kc)], writes=[("ps", T % 2)])
                xs = xres[:, cb * TH + T * 512: cb * TH + (T + 1) * 512]
                P.op("dve", (lambda e, o=xs, a=xs, b=bank[:, :]: e.scalar_tensor_tensor(out=o, in0=a, scalar=ALPHA, in1=b, op0=ALU.mult, op1=ALU.add)),
                     reads=[("ps", T % 2), ("xres", cb)], writes=[("xres", cb)])
        ar.release()
        ar.release()
        self.layer_norm_fm(xres, xb, lV, LB_L1G, LB_L1B, onesF)
        P.barrier()

        ar.mark()
        hT = ar.alloc(32 * 512, BF16)
        wu = [ar.alloc(4096, BF16) for _ in range(2)]
        wd = [ar.alloc(4096, BF16) for _ in range(2)]
        r32 = [ar.alloc(512, F32) for _ in range(2)]
        for T in range(4):
            for f4 in range(8):
                wb = f4 % 2
                for i in range(4):
                    P.dma("sp", wu[wb][:, i * 1024:(i + 1) * 1024], wU16[f4 * 4 + i], f"wu{wb}", writes=[("wu", wb)])
                for i in range(4):
                    fb = f4 * 4 + i
                    bank = ps[fb % 2]
                    for dc in range(8):
                        P.op("pe", (lambda e, o=bank[:, :], a=wu[wb][:, i * 1024 + dc * 128: i * 1024 + (dc + 1) * 128],
                                    b=xb[:, dc * TH + T * 512: dc * TH + (T + 1) * 512], st=(dc == 0), sp=(dc == 7): e.matmul(o, lhsT=a, rhs=b, start=st, stop=sp)),
                             reads=[("wu", wb), ("xb", dc)], writes=[("ps", fb % 2)])
                    P.op("act", (lambda e, o=r32[fb % 2], i_=bank[:, :]: e.activation(out=o, in_=i_, func=AF.Relu)),
                         reads=[("ps", fb % 2)], writes=[("r32", fb % 2)])
                    P.op("dve", (lambda e, o=hT[:, fb * 512:(fb + 1) * 512], a=r32[fb % 2]: e.tensor_tensor(out=o, in0=a, in1=a, op=ALU.mult)),
                         reads=[("r32", fb % 2)], writes=[("hT", fb)])
            for cb in range(8):
                wb = cb % 2
                P.dma("sp", wd[wb], wD16[cb], f"wd{wb}", writes=[("wd", wb)])
                bank = ps[2 + cb % 2]
                for fc in range(32):
                    P.op("pe", (lambda e, o=bank[:, :], a=wd[wb][:, fc * 128:(fc + 1) * 128], b=hT[:, fc * 512:(fc + 1) * 512], st=(fc == 0), sp=(fc == 31):
                                e.matmul(o, lhsT=a, rhs=b, start=st, stop=sp)),
                         reads=[("wd", wb), ("hT", fc)], writes=[("ps", 2 + cb % 2)])
                xs = xres[:, cb * TH + T * 512: cb * TH + (T + 1) * 512]
                P.op("dve", (lambda e, o=xs, a=xs, b=bank[:, :]: e.scalar_tensor_tensor(out=o, in0=a, scalar=ALPHA, in1=b, op0=ALU.mult, op1=ALU.add)),
                     reads=[("ps", 2 + cb % 2), ("xres", cb)], writes=[("xres", cb)])
        ar.release()
        P.barrier()
        self.layer_norm_fm(xres, xb, lV, LB_L2G, LB_L2B, onesF, store=(xres_o, xb_o))
        P.barrier()
        ar.release()

    def layer_norm_fm(self, xres, xb, lB, og, ob, onesF, store=None):
        P, ar, ps = self.P, self.ar, self.ps
        P.barrier()
        ar.mark()
        usq = ar.alloc(8 * 512, F32)
        mean = [ar.alloc(512, F32) for _ in range(2)]
        rstd = [ar.alloc(512, F32) for _ in range(2)]
        vtm = [ar.alloc(512, F32) for _ in range(2)]
        tmp = [ar.alloc(512, F32) for _ in range(3)]
        banks = [(ps[4], ps[5]), (ps[2], ps[3])]

        def xs_(cb, T):
            return xres[:, cb * TH + T * 512: cb * TH + (T + 1) * 512]

        def stats(T):
            j = T % 2
            p1, p2 = banks[j]
            for cb in range(8):
                P.op("act", (lambda e, o=usq[:, cb * 512:(cb + 1) * 512], i=xs_(cb, T): e.activation(out=o, in_=i, func=AF.Square)),
                     reads=[("xr", cb, T)], writes=[("usq", cb)])
                P.op("pe", (lambda e, o=p1[:, :], b=xs_(cb, T), st=(cb == 0), sp=(cb == 7): e.matmul(o, lhsT=onesF, rhs=b, start=st, stop=sp)),
                     reads=[("xr", cb, T), "cB"], writes=[("lnp1", j)])
                P.op("pe", (lambda e, o=p2[:, :], b=usq[:, cb * 512:(cb + 1) * 512], st=(cb == 0), sp=(cb == 7): e.matmul(o, lhsT=onesF, rhs=b, start=st, stop=sp)),
                     reads=[("usq", cb), "cB"], writes=[("lnp2", j)])
            P.op("act", (lambda e: e.copy(out=mean[j], in_=p1[:, :])), reads=[("lnp1", j)], writes=[("mean", j)])
            P.op("dve", (lambda e: e.tensor_tensor(out=vtm[j], in0=mean[j], in1=mean[j], op=ALU.mult)), reads=[("mean", j)], writes=[("vt", j)])
            P.op("dve", (lambda e: e.tensor_tensor(out=vtm[j], in0=p2[:, :], in1=vtm[j], op=ALU.subtract)), reads=[("lnp2", j), ("vt", j)], writes=[("vt", j)])
            P.op("act", (lambda e: e.activation(out=vtm[j], in_=vtm[j], func=AF.Ln, bias=EPS)), reads=[("vt", j)], writes=[("vt", j)])
            P.op("act", (lambda e: e.activation(out=rstd[j], in_=vtm[j], func=AF.Exp, scale=-0.5)), reads=[("vt", j)], writes=[("rstd", j)])

        def norm(T):
            j = T % 2
            for cb in range(8):
                xs = xs_(cb, T)
                tb = tmp[cb % 3]
                P.op("dve", (lambda e, o=tb, a=xs: e.tensor_tensor(out=o, in0=a, in1=mean[j], op=ALU.subtract)),
                     reads=[("xr", cb, T), ("mean", j)], writes=[("lt", cb % 3)])
                P.op("dve", (lambda e, o=tb: e.tensor_tensor(out=o, in0=o, in1=rstd[j], op=ALU.mult)),
                     reads=[("lt", cb % 3), ("rstd", j)], writes=[("lt", cb % 3)])
                P.op("act", (lambda e, o=xs, i=tb, g=lB[:, og + cb:og + cb + 1], b=lB[:, ob + cb:ob + cb + 1]: e.activation(out=o, in_=i, func=AF.Identity, scale=g, bias=b)),
                     reads=[("lt", cb % 3), "lV"], writes=[("xr", cb, T)])
                P.op("dve", (lambda e, o=xb[:, cb * TH + T * 512: cb * TH + (T + 1) * 512], i=xs: e.tensor_copy(out=o, in_=i)),
                     reads=[("xr", cb, T)], writes=[("xbk", cb, T)])
            if store is not None:
                xo, bo = store
                tsl = slice(T * 512, (T + 1) * 512)
                P.dma("sp", xo.rearrange("(dc p) t -> p dc t", p=128)[:, :, tsl], xres.rearrange("p (dc t) -> p dc t", dc=8)[:, :, tsl],
                      "bst", reads=[("xr", cb, T) for cb in range(8)])
                if bo is not None:
                    P.dma("sp", bo.rearrange("(dc p) t -> p dc t", p=128)[:, :, tsl], xb.rearrange("p (dc t) -> p dc t", dc=8)[:, :, tsl],
                          "bst", reads=[("xbk", cb, T) for cb in range(8)])

        stats(0)
        for T in range(4):
            if T + 1 < 4:
                stats(T + 1)
            norm(T)
        ar.release()


_CACHE = {}


def _prog(stages):
    key = tuple(stages)
    if key not in _CACHE:
        b = Builder(list(stages))
        b.build()
        _CACHE[key] = b
    return _CACHE[key]


def _run(stage, l, maps_all, state):
    b = _prog([stage])
    in_maps = []
    for c in range(8):
        m = {}
        for nm in b.ext_in:
            if nm + str(l) in maps_all[c]:
                m[nm] = maps_all[c][nm + str(l)]
            elif nm in maps_all[c]:
                m[nm] = maps_all[c][nm]
            else:
                m[nm] = state[c][nm]
        in_maps.append(m)
    res = run_bass_kernel_spmd(b.nc, in_maps, core_ids=list(range(8)))
    for c in range(8):
        for nm in b.ext_out:
            state[c][nm] = np.asarray(res.results[c][nm])


def kernel_unfused(**inputs):
    maps = _host_inputs(inputs)
    state = [dict() for _ in range(8)]
    _run("P0", 0, maps, state)
    for l in range(DEPTH):
        for c in range(8):
            pr = c // 2 * 2
            state[c]["xball"] = np.stack([state[pr]["xb_o"], state[pr + 1]["xb_o"]])
            state[c]["xres_i"] = state[c]["xres_o"]
            state[c]["xb_i"] = state[c]["xb_o"]
        _run("A0", l, maps, state)
        for c in range(8):
            pr, hh = c // 2 * 2, c % 2
            state[c]["rsall"] = np.ascontiguousarray(
                np.stack([state[pr]["rs"][:, hh * TH:(hh + 1) * TH], state[pr + 1]["rs"][:, hh * TH:(hh + 1) * TH]]))
        _run("B0", l, maps, state)
    out = np.empty((NB, S, D), np.float32)
    for c in range(8):
        b, hh = c // 2, c % 2
        out[b, hh * TH:(hh + 1) * TH, :] = state[c]["xres_o"].T
    return out


def kernel(**inputs):
    maps = _host_inputs(inputs)
    b = _prog(["FX"])
    nv = int(np.random.randint(1 << 10, 1 << 26))
    nonce = (nv * 8 + np.repeat(np.arange(8, dtype=np.int64), 16)[None, :]).astype(np.int32)
    in_maps = []
    for c in range(8):
        m = {}
        for nm in b.ext_in:
            m[nm] = nonce if nm == "nonce" else maps[c][nm]
        in_maps.append(m)
    res = run_bass_kernel_spmd(b.nc, in_maps, core_ids=list(range(8)))
    out = np.empty((NB, S, D), np.float32)
    for c in range(8):
        bb, hh = c // 2, c % 2
        out[bb, hh * TH:(hh + 1) * TH, :] = np.asarray(res.results[c]["outT"]).T
    return out


def kernel_dup(**inputs):
    maps = _host_inputs(inputs)
    b = _prog(["FUSED"])
    in_maps = []
    for c in range(8):
        pr = c // 2 * 2
        m = {}
        for nm in b.ext_in:
            if nm == "xT2":
                m[nm] = np.stack([maps[pr]["xT"], maps[pr + 1]["xT"]])
            elif nm.startswith("cA_"):
                m[nm] = maps[pr + int(nm[-1])]["cA"]
            elif nm[-2] == "_" and nm[:-2] in maps[c]:
                m[nm] = maps[pr + int(nm[-1])][nm[:-2]]
            else:
                m[nm] = maps[c][nm]
        in_maps.append(m)
    res = run_bass_kernel_spmd(b.nc, in_maps, core_ids=list(range(8)))
    out = np.empty((NB, S, D), np.float32)
    for c in range(8):
        bb, hh = c // 2, c % 2
        out[bb, hh * TH:(hh + 1) * TH, :] = np.asarray(res.results[c]["outT"]).T
    return out
```

```python
import contextlib
import numpy as np
import ml_dtypes
import concourse.bass as bass
import concourse.mybir as mybir
from concourse.bass_utils import run_bass_kernel_spmd

F32 = mybir.dt.float32
BF16 = mybir.dt.bfloat16
AF = mybir.ActivationFunctionType
ALU = mybir.AluOpType

D = 1024
S = 4096
NB = 4
DEPTH = 2
TH = 2048
DFF = 4096
ALPHA = (2 * DEPTH) ** 0.25
EPS = 1e-5
FUSED = False
import os as _os
PARTS = _os.environ.get("KPARTS", "proj,ret,sb").split(",")
SUB = _os.environ.get("KSUB", "fm,tm,rot,tr").split(",")

ENGS = ("pe", "act", "dve", "pool", "sp")
SAME_ENGINE_SYNC = True
SCHEDULE = True


class _Op:
    __slots__ = ("eng", "fn", "chan", "deps", "tick", "needs_inc", "kind", "waits_extra", "seg", "cost", "fin")

    def __init__(self, eng, fn, chan, kind):
        self.seg = 0
        self.cost = None
        self.fin = 0.0
        self.eng = eng
        self.fn = fn
        self.chan = chan
        self.deps = set()
        self.tick = None
        self.needs_inc = False
        self.kind = kind
        self.waits_extra = None


class Prog:
    def __init__(self, nc):
        self.nc = nc
        self.streams = {e: [] for e in ENGS}
        self.res = {}
        self.chan_count = {}
        self.last_op = {e: None for e in ENGS}
        self.seg = 0

    def _add(self, op, reads, writes):
        deps = set()
        for k in reads:
            st = self.res.get(k)
            if st is not None and st[0] is not None:
                deps.add(st[0])
        for k in writes:
            st = self.res.get(k)
            if st is not None:
                if st[0] is not None:
                    deps.add(st[0])
                deps.update(st[1])
        for k in writes:
            self.res[k] = [op, []]
        for k in reads:
            st = self.res.get(k)
            if st is None:
                st = self.res[k] = [None, []]
            if k not in writes:
                st[1].append(op)
        deps.discard(op)
        op.deps = deps
        op.seg = self.seg
        self.streams[op.eng].append(op)
        self.last_op[op.eng] = op
        return op

    def op(self, eng, fn, reads=(), writes=()):
        return self._add(_Op(eng, fn, None, "c"), tuple(reads), tuple(writes))

    def dma(self, queue, out, in_, chan, reads=(), writes=(), **kw):
        k = self.chan_count.get(chan, 0)
        self.chan_count[chan] = k + 1
        o = _Op(queue, (lambda e: e.dma_start(out=out, in_=in_, **kw)), (chan, k), "d")
        return self._add(o, tuple(reads), tuple(writes))

    def xop(self, queue, fn, reads=()):
        o = _Op(queue, fn, None, "x")
        o.cost = 2.0
        return self._add(o, tuple(reads), ())

    def custom_dma(self, queue, fn, chan, reads=(), writes=()):
        k = self.chan_count.get(chan, 0)
        self.chan_count[chan] = k + 1
        o = _Op(queue, fn, (chan, k), "d")
        return self._add(o, tuple(reads), tuple(writes))

    def barrier(self):
        lasts = [self.last_op[e] for e in ENGS
                 if self.last_op[e] is not None and self.last_op[e].kind == "c"]
        lasts = []
        for e in ENGS:
            for o in reversed(self.streams[e]):
                if o.kind == "c":
                    lasts.append(o)
                    break
        chans = dict(self.chan_count)
        for e in ENGS:
            o = _Op(e, None, None, "b")
            o.deps = set(lasts)
            o.waits_extra = chans
            o.seg = self.seg
            self.streams[e].append(o)
        self.res = {}
        self.seg += 1

    COST = {"pe": 0.22, "act": 0.5, "dve": 0.6, "pool": 1.0, "sp": 0.1}
    DMA_LAT = 3.0
    XLAT = 0.5
    SLAT = 0.35
    WINDOW = 32

    def schedule(self):
        nseg = self.seg + 1
        per = {e: [[] for _ in range(nseg + 1)] for e in ENGS}
        bar = {e: [None] * (nseg + 1) for e in ENGS}
        for e in ENGS:
            for o in self.streams[e]:
                if o.kind == "b":
                    bar[e][o.seg] = o
                else:
                    per[e][o.seg].append(o)
        new = {e: [] for e in ENGS}
        for sg in range(nseg + 1):
            lists = {e: per[e][sg] for e in ENGS}
            if any(lists[e] for e in ENGS):
                inseg = set()
                for e in ENGS:
                    inseg.update(lists[e])
                ptr = {e: 0 for e in ENGS}
                done = set()
                tfree = {e: 0.0 for e in ENGS}
                out = {e: [] for e in ENGS}
                pend = {e: list(lists[e]) for e in ENGS}
                t = 0.0
                remaining = sum(len(v) for v in pend.values())
                while remaining:
                    progressed = False
                    nxt = None
                    for e in ENGS:
                        if not pend[e]:
                            continue
                        if tfree[e] > t + 1e-9:
                            nxt = tfree[e] if nxt is None else min(nxt, tfree[e])
                            continue
                        win = pend[e][:1] if e == "sp" else pend[e][:self.WINDOW]
                        best = None
                        for o in win:
                            rdy = 0.0
                            ok = True
                            for d in o.deps:
                                if d not in inseg:
                                    continue
                                if d not in done:
                                    ok = False
                                    break
                                lat = 0.0 if (d.eng == e and e == "pe") else (self.SLAT if d.eng == e else self.XLAT)
                                rdy = max(rdy, d.fin + lat)
                            if not ok:
                                continue
                            if rdy <= t + 1e-9:
                                best = o
                                break
                            nxt = rdy if nxt is None else min(nxt, rdy)
                        if best is not None:
                            c = best.cost if best.cost is not None else self.COST[e]
                            if best.kind == "d":
                                best.fin = t + self.DMA_LAT
                                tfree[e] = t + c
                            else:
                                best.fin = t + c
                                tfree[e] = t + c
                            done.add(best)
                            pend[e].remove(best)
                            out[e].append(best)
                            remaining -= 1
                            progressed = True
                            nxt = tfree[e] if nxt is None else min(nxt, tfree[e])
                    if not progressed:
                        if nxt is None or nxt <= t + 1e-9:
                            for e in ENGS:
                                out[e].extend(pend[e])
                                pend[e] = []
                            break
                        t = nxt
                    else:
                        t = t if nxt is None else min(t + 0.05, nxt) if False else t
                for e in ENGS:
                    new[e].extend(out[e])
            for e in ENGS:
                if bar[e][sg] is not None:
                    new[e].append(bar[e][sg])
        for e in ENGS:
            assert len(new[e]) == len(self.streams[e]), (e, len(new[e]), len(self.streams[e]))
        self.streams = new

    def emit(self):
        nc = self.nc
        if SCHEDULE:
            self.schedule()
        for e in ENGS:
            for o in self.streams[e]:
                for d in o.deps:
                    if d.kind == "c":
                        if d.eng == o.eng and (d.eng == "pe" or not SAME_ENGINE_SYNC) and o.kind != "b":
                            continue
                        d.needs_inc = True
        for e in ENGS:
            t = 0
            for o in self.streams[e]:
                if o.kind == "c" and o.needs_inc:
                    t += 1
                    o.tick = t
        with contextlib.ExitStack() as es:
            esem = {e: es.enter_context(nc.semaphore("s_" + e)) for e in ENGS}
            csem = {c: es.enter_context(nc.semaphore("c_" + str(c))) for c in self.chan_count}
            self.flag_sem = es.enter_context(nc.semaphore("flag_sem"))
            block = es.enter_context(nc.Block())

            def run(e):
                def body(eng):
                    known = {}

                    def wait(sem, val):
                        if known.get(sem.name, 0) >= val:
                            return
                        known[sem.name] = val
                        eng.wait_ge(sem, val)

                    for o in self.streams[e]:
                        for d in o.deps:
                            if d.kind == "c":
                                if d.tick is None:
                                    continue
                                if d.eng == e and (e == "pe" or not SAME_ENGINE_SYNC) and o.kind != "b":
                                    continue
                                wait(esem[d.eng], d.tick)
                            elif d.kind == "d":
                                c, k = d.chan
                                wait(csem[c], 16 * (k + 1))
                        if o.kind == "b":
                            for c, n in o.waits_extra.items():
                                wait(csem[c], 16 * n)
                            continue
                        ins = o.fn(eng)
                        if o.kind == "x":
                            continue
                        if o.kind == "d":
                            ins.then_inc(csem[o.chan[0]], 16)
                        elif o.needs_inc:
                            ins.then_inc(esem[e], 1)
                    if e == "sp":
                        for c, n in self.chan_count.items():
                            wait(csem[c], 16 * n)
                        for e2 in ENGS:
                            lt = max([o.tick for o in self.streams[e2] if o.tick is not None] or [0])
                            if lt:
                                wait(esem[e2], lt)
                return body

            block.tensor(run("pe"))
            block.scalar(run("act"))
            block.vector(run("dve"))
            block.gpsimd(run("pool"))
            block.sync(run("sp"))


class Arena:
    def __init__(self, t32, nbytes):
        self.t = t32
        self.n = nbytes
        self.top = 0
        self.marks = []

    def alloc(self, nelem, dtype, parts=128):
        esz = 4 if dtype == F32 else 2
        nb = (nelem * esz + 63) // 64 * 64
        assert self.top + nb <= self.n, f"SBUF arena overflow {self.top}+{nb}>{self.n}"
        o = self.top // 4
        self.top += nb
        v = self.t[0:parts, o:o + nb // 4]
        if dtype != F32:
            v = v.bitcast(dtype)
        return v[:, 0:nelem]

    def mark(self):
        self.marks.append(self.top)

    def release(self):
        self.top = self.marks.pop()


CA_IDENT, CA_CAUS, CA_U, CA_L, CA_SBM, CA_DEC, CA_CH, CA_CD, CA_N = (
    0, 128, 256, 384, 512, 2560, 2564, 2566, 2568)
CR_COS, CR_SIN, CR_N = 0, 2048, 4096
CB_CAUS, CB_ONES, CB_N = 0, 128, 256
LB_LNG, LB_LNB, LB_WT, LB_SB, LB_N = 0, 512, 1024, 1536, 2048
LB_L1G, LB_L1B, LB_L2G, LB_L2B, LV_N = 0, 8, 16, 24, 32


def _consts_A(hh):
    c = np.zeros((128, CA_N), np.float32)
    p = np.arange(128)
    c[:, CA_IDENT:CA_IDENT + 128] = np.eye(128)
    c[:, CA_CAUS:CA_CAUS + 128] = (p[:, None] <= p[None, :])
    c[:, CA_U:CA_U + 128] = (p[:, None] >= p[None, :])
    c[:, CA_L:CA_L + 128] = (p[:, None] < p[None, :])
    t = np.arange(512)
    for r in range(4):
        c[:, CA_SBM + r * 512:CA_SBM + (r + 1) * 512] = ((r * 128 + p)[:, None] < t[None, :])
    for h in range(2):
        hg = hh * 2 + h
        g = 1.0 - 2.0 ** (-5.0 - hg)
        lg = np.log(g)
        c[:, CA_DEC + h] = (128.0 ** -0.5) * np.exp(lg * (p + 1.0))
        c[:, CA_DEC + 2 + h] = np.exp(lg * (127.0 - p))
        c[:, CA_CH + h] = np.exp(-lg * 128.0)
        c[:, CA_CD + h] = np.exp(lg * 128.0)
    return c


def _consts_R(shift=0):
    c = np.zeros((128, CR_N), np.float32)
    p = np.arange(128)
    half = 64
    inv_freq = (10000.0 ** (-np.arange(half, dtype=np.float32) / half)).astype(np.float32)
    pos = np.abs((np.arange(32)[None, :] - shift) * 128 + p[:, None]).astype(np.float32)
    ang = (pos[:, :, None] * inv_freq[None, None, :]).astype(np.float32)
    c[:, CR_COS:CR_COS + 2048] = np.cos(ang).astype(np.float32).reshape(128, 2048)
    c[:, CR_SIN:CR_SIN + 2048] = np.sin(ang).astype(np.float32).reshape(128, 2048)
    return c


def _consts_B():
    c = np.zeros((128, CB_N), np.float32)
    p = np.arange(128)
    c[:, CB_CAUS:CB_CAUS + 128] = (p[:, None] <= p[None, :])
    c[:, CB_ONES:CB_ONES + 128] = 1.0 / D
    return c


def _blk_lhsT(w, cw=128):
    K, N = w.shape
    return np.ascontiguousarray(w.reshape(K // 128, 128, N // cw, cw).transpose(2, 1, 0, 3))


def _host_inputs(inp):
    x = np.asarray(inp["x"], np.float32)
    maps = []
    for core in range(8):
        b, hh = core // 2, core % 2
        m = {}
        m["xT"] = np.ascontiguousarray(x[b, hh * TH:(hh + 1) * TH, :].T)
        m["cA"] = _consts_A(hh)
        m["cB"] = _consts_B()
        m["cR"] = _consts_R()
        m["cRL"] = _consts_R(16 if hh == 0 else 0)
        for l in range(DEPTH):
            w_in = np.asarray(inp["w_in"][l], np.float32)
            hs = slice(hh * 256, (hh + 1) * 256)
            blk = lambda i: w_in[:, i * 512:(i + 1) * 512]
            rq, rk, rv, rg, sq, sk, sv = [blk(i)[:, hs] for i in range(7)]
            m[f"wAf{l}"] = _blk_lhsT(np.concatenate([rg, sq, sk], axis=1))
            m[f"wAt{l}"] = _blk_lhsT(np.concatenate([rq, rk, rv, sv], axis=1), cw=512)
            gu, gv = w_in[:, 3584:4096], w_in[:, 4096:4608]
            gates = w_in[:, 4608:7680]
            m[f"wGu{l}"] = _blk_lhsT(gu)
            m[f"wGv{l}"] = _blk_lhsT(gv, cw=512)
            m[f"wGt{l}"] = _blk_lhsT(gates)
            m[f"pR{l}"] = _blk_lhsT(np.asarray(inp["p_ret"][l], np.float32))
            m[f"pS{l}"] = _blk_lhsT(np.asarray(inp["p_sb"][l], np.float32))
            m[f"pG{l}"] = _blk_lhsT(np.asarray(inp["p_sgu"][l], np.float32))
            m[f"wO{l}"] = _blk_lhsT(np.asarray(inp["w_out"][l], np.float32))
            m[f"wU{l}"] = _blk_lhsT(np.asarray(inp["w_up"][l], np.float32))
            m[f"wD{l}"] = _blk_lhsT(np.asarray(inp["w_down"][l], np.float32))
            la = np.zeros((128, 4), np.float32)
            la[:, 0:2] = np.asarray(inp["ret_gn_g"][l], np.float32)[hs].reshape(2, 128).T
            la[:, 2:4] = np.asarray(inp["ret_gn_b"][l], np.float32)[hs].reshape(2, 128).T
            m[f"lA{l}"] = la
            lb = np.zeros((128, LB_N), np.float32)
            lb[:, LB_LNG:LB_LNG + 512] = np.asarray(inp["sgu_ln_g"][l], np.float32)[None, :]
            lb[:, LB_LNB:LB_LNB + 512] = np.asarray(inp["sgu_ln_b"][l], np.float32)[None, :]
            sw = np.asarray(inp["sgu_w"][l], np.float32)
            lb[:, LB_WT:LB_WT + 512] = sw.transpose(2, 0, 1).reshape(128, 512)
            lb[:, LB_SB:LB_SB + 512] = np.asarray(inp["sgu_b"][l], np.float32).reshape(1, 512)
            lv = np.zeros((128, LV_N), np.float32)
            for nm, off in (("ln1_g", LB_L1G), ("ln1_b", LB_L1B), ("ln2_g", LB_L2G), ("ln2_b", LB_L2B)):
                lv[:, off:off + 8] = np.asarray(inp[nm][l], np.float32).reshape(8, 128).T
            m[f"lB{l}"] = lb
            m[f"lV{l}"] = lv
        maps.append(m)
    return maps


IN_SHAPES = {"xT": [D, TH], "cA": [128, CA_N], "cB": [128, CB_N], "cR": [128, CR_N], "cRL": [128, CR_N]}
for _l in range(DEPTH):
    IN_SHAPES.update({
        f"wAf{_l}": [6, 128, 8, 128], f"wAt{_l}": [2, 128, 8, 512],
        f"wGu{_l}": [4, 128, 8, 128], f"wGv{_l}": [1, 128, 8, 512], f"wGt{_l}": [24, 128, 8, 128],
        f"pR{_l}": [8, 128, 4, 128], f"pS{_l}": [8, 128, 4, 128], f"pG{_l}": [8, 128, 4, 128],
        f"wO{_l}": [8, 128, 8, 128], f"wU{_l}": [32, 128, 8, 128], f"wD{_l}": [8, 128, 32, 128],
        f"lA{_l}": [128, 4], f"lB{_l}": [128, LB_N], f"lV{_l}": [128, LV_N]})


class Builder:
    def __init__(self, stages):
        self.stages = stages
        self.nc = bass.Bass("TRN2", target_bir_lowering=False)
        self.dram = {}
        self.ext_in = []
        self.ext_out = []

    def dt(self, name, shape, dtype, kind):
        if name not in self.dram:
            self.dram[name] = self.nc.dram_tensor(name, list(shape), dtype, kind=kind).ap()
            if kind == "ExternalInput":
                self.ext_in.append(name)
            elif kind == "ExternalOutput":
                self.ext_out.append(name)
        return self.dram[name]

    def win(self, name):
        return self.dt(name, IN_SHAPES[name], F32, "ExternalInput")

    def winl(self, base, l):
        return self.dt(base + (str(l) if FUSED else ""), IN_SHAPES[base + str(l)], F32, "ExternalInput")

    def build(self):
        nc = self.nc
        with contextlib.ExitStack() as es:
            at = es.enter_context(nc.sbuf_tensor("arena", [128, 53200], F32))
            self.ar = Arena(at, 53200 * 4)
            self.ps = [es.enter_context(nc.psum_tensor(f"ps{i}", [128, 512], F32)) for i in range(6)]
            self.psb = es.enter_context(nc.psum_tensor("psb", [128, 1024], BF16))
            self.psb2 = es.enter_context(nc.psum_tensor("psb2", [128, 1024], BF16))
            self.P = Prog(nc)
            if self.stages == ["FX"]:
                self.wire_fx()
            elif self.stages == ["FUSED"]:
                self.wire_fused()
            else:
                for s in self.stages:
                    self.wire_unfused(s)
                    self.P.barrier()
            self.P.emit()
        return nc

    def wire_unfused(self, s):
        EI, EO = "ExternalInput", "ExternalOutput"
        w = lambda base: self.dt(base, IN_SHAPES[base + "0"] if base + "0" in IN_SHAPES else IN_SHAPES[base], F32, EI)
        if s == "P0":
            self.stage_p0(dict(xT=w("xT"), xres_o=self.dt("xres_o", [D, TH], F32, EO), xb_o=self.dt("xb_o", [D, TH], BF16, EO)))
        elif s[0] == "A":
            self.stage_a(dict(xall=self.dt("xball", [2, D, TH], BF16, EI), rs=self.dt("rs", [512, S], BF16, EO),
                              cA=w("cA"), cR=w("cR"), lA=w("lA"), wAf=w("wAf"), wAt=w("wAt")))
        elif s[0] == "B":
            io = dict(xres_i=self.dt("xres_i", [D, TH], F32, EI), xb_i=self.dt("xb_i", [D, TH], BF16, EI),
                      rsall=self.dt("rsall", [2, 512, TH], BF16, EI),
                      xres_o=self.dt("xres_o", [D, TH], F32, EO), xb_o=self.dt("xb_o", [D, TH], BF16, EO),
                      wU16=self.dt("wU16", [32, 128, 1024], BF16, "Internal"), wD16=self.dt("wD16", [8, 128, 4096], BF16, "Internal"),
                      make_cache=True)
            for nm in ("cB", "lB", "lV", "wGu", "wGv", "wGt", "pR", "pS", "pG", "wO", "wU", "wD"):
                io[nm] = w(nm)
            self.stage_b(io)

    def wire_fx(self):
        EI = "ExternalInput"
        P = self.P
        w = lambda base: self.dt(base, IN_SHAPES[base], F32, EI)
        wl = lambda base, l: self.dt(f"{base}{l}", IN_SHAPES[base + "0"], F32, EI)
        I32 = mybir.dt.int32
        nonce = self.dt("nonce", [1, 128], I32, EI)
        sh = lambda nm, shape, dtp: self.dram.setdefault(nm, self.nc.dram_tensor(nm, shape, dtp, kind="Internal", addr_space="Shared").ap())
        XB = [sh("EX0", [2, D, TH], BF16)] * DEPTH
        RS = [sh("EX1", [2, 512, S], BF16)] * DEPTH
        FL = sh("FL", [2, 16], I32)
        xres = [self.dt(f"xres_p{l}", [D, TH], F32, "Internal") for l in range(DEPTH)]
        xbp = [self.dt(f"xb_p{l}", [D, TH], BF16, "Internal") for l in range(DEPTH)]
        rsp = [self.dt(f"rs_p{l}", [512, S], BF16, "Internal") for l in range(DEPTH)]
        rsall = [self.dt(f"rsall_p{l}", [2, 512, TH], BF16, "Internal") for l in range(DEPTH)]
        outT = self.dt("outT", [D, TH], F32, "ExternalOutput")
        self.ar.mark()
        ntile = self.ar.alloc(128, F32, parts=1).bitcast(I32)
        P.dma("sp", ntile, nonce, "nonce", writes=["ntile"])
        phase = [0]

        def publish(dst_fn, src):
            phase[0] += 1
            k = phase[0]
            P.custom_dma("sp", (lambda e: e.dma_start(out=dst_fn(self.parity(e)), in_=src)), "xch", writes=[("xch", k)])

            def fn(e, k=k):
                par = self.parity(e)
                e.dma_start(out=FL[bass.ds(par, 1)], in_=ntile[0:1, k * 16:(k + 1) * 16]).then_inc(self.P.flag_sem, 16)
                e.wait_ge(self.P.flag_sem, 16 * k)
            P.xop("sp", fn, reads=[("xch", k), "ntile"])
            return k

        def wait_partner(k):
            def fn(e, k=k):
                par = self.parity(e)
                if getattr(self, "_nbase", None) is None:
                    self._nbase = e.alloc_register("nonce_base")
                    e.reg_load(self._nbase, nonce[0:1, 0:1])
                with e.register(f"want{k}") as want, e.register(f"got{k}") as got, e.register(f"r{k}") as r:
                    e.reg_add(want, self._nbase, k)
                    e.reg_mov(r, 1)
                    with e.While(r):
                        e.reg_load(got, FL[bass.ds(1 - par, 1), 0:1])
                        e.reg_sub(r, got, want)
                        e.reg_alu(r, r, -4, ALU.bitwise_and)
            P.xop("sp", fn, reads=["ntile"])

        def publish_and_wait(dst_fn, src, key):
            wait_partner(publish(dst_fn, src))

        self.stage_p0(dict(xT=w("xT"), xres_o=None, xb_o=xbp[0]))
        xres[0] = w("xT")
        P.barrier()
        kx = publish(lambda par: XB[0][bass.ds(par, 1)].rearrange("o d t -> (o d) t"), xbp[0])
        for l in range(DEPTH):
            wU16 = self.dt(f"wU16_{l}", [32, 128, 1024], BF16, "Internal")
            wD16 = self.dt(f"wD16_{l}", [8, 128, 4096], BF16, "Internal")
            cio = dict(wU=wl("wU", l), wD=wl("wD", l), wU16=wU16, wD16=wD16)
            self.stage_a(dict(xall=XB[l], rs=rsp[l], cA=w("cA"), cR=w("cR"), lA=wl("lA", l), wAf=wl("wAf", l), wAt=wl("wAt", l), cache_io=cio,
                              pre_x=(lambda kx=kx: wait_partner(kx))))
            P.barrier()
            kk = publish(lambda par, l=l: RS[l][bass.ds(par, 1)].rearrange("o r t -> (o r) t"), rsp[l])

            def pre_rs(l=l, kk=kk):
                wait_partner(kk)
                P.custom_dma("sp", (lambda e: e.dma_start(out=rsall[l], in_=RS[l].rearrange("h r (two t) -> h r two t", two=2)[:, :, bass.ds(self.parity(e), 1), :]
                                                          .rearrange("h r o t -> h r (o t)"))), "xch2", writes=["rsall_d"])
            lastl = (l == DEPTH - 1)
            io = dict(xres_i=xres[l], xb_i=xbp[l], rsall=rsall[l], wU16=wU16, wD16=wD16, make_cache=False, pre_rs=pre_rs,
                      xres_o=outT if lastl else xres[l + 1], xb_o=None if lastl else xbp[l + 1], cB=w("cB"))
            for nm in ("lB", "lV", "wGu", "wGv", "wGt", "pR", "pS", "pG", "wO", "wU", "wD"):
                io[nm] = wl(nm, l)
            self.stage_b(io)
            P.barrier()
            if not lastl:
                kx = publish(lambda par, l=l: XB[l + 1][bass.ds(par, 1)].rearrange("o d t -> (o d) t"), xbp[l + 1])
        self.ar.release()

    def wire_fused(self):
        EI = "ExternalInput"
        xT = self.dt("xT2", [2, D, TH], F32, EI)
        xres = [self.dt(f"xres_s{l}", [2, D, TH], F32, "Internal") for l in range(DEPTH)]
        xb = [self.dt(f"xb_s{l}", [3 if l == DEPTH - 1 else 2, D, TH], BF16, "Internal") for l in range(DEPTH)]
        rs = [self.dt(f"rs_s{l}", [2, 512, TH if l == DEPTH - 1 else S], BF16, "Internal") for l in range(DEPTH)]
        self.ar.mark()
        zt = self.ar.alloc(TH, BF16)
        self.P.op("pool", lambda e: e.memset(zt, 0.0), writes=["zt"])
        for dc in range(8):
            self.P.dma("sp", xb[DEPTH - 1][0, dc * 128:(dc + 1) * 128, :], zt, "zst", reads=["zt"])
        self.ar.release()
        self.P.barrier()
        outT = self.dt("outT", [D, TH], F32, "ExternalOutput")
        wl = lambda base, l, sfx="": self.dt(f"{base}{l}{sfx}", IN_SHAPES[base + "0"], F32, EI)
        for th in range(2):
            self.stage_p0(dict(xT=xT[th], xres_o=xres[0][th], xb_o=xb[0][th]))
            self.P.barrier()
        for l in range(DEPTH):
            wU16 = self.dt(f"wU16_{l}", [32, 128, 1024], BF16, "Internal")
            wD16 = self.dt(f"wD16_{l}", [8, 128, 4096], BF16, "Internal")
            lastl = (l == DEPTH - 1)
            xin = xb[l]
            if lastl:
                xin = self.dt("xb_shift", [2, D, TH], BF16, "Internal")
                for r in range(2):
                    self.P.custom_dma("sp", (lambda e, r=r: e.dma_start(out=xin[r], in_=xb[l][bass.ds(self.parity(e) + r, 1)].rearrange("o d t -> (o d) t"))),
                                      "xsh")
                self.P.barrier()
            for hh in range(2):
                cio = dict(wU=wl("wU", l), wD=wl("wD", l), wU16=wU16, wD16=wD16) if hh == 0 else None
                self.stage_a(dict(xall=xin, rs=rs[l][hh], cA=self.dt(f"cA_{hh}", IN_SHAPES["cA"], F32, EI),
                                  cR=self.win("cRL" if lastl else "cR"), last=lastl,
                                  lA=wl("lA", l, f"_{hh}"), wAf=wl("wAf", l, f"_{hh}"), wAt=wl("wAt", l, f"_{hh}"), cache_io=cio))
                self.P.barrier()
            for th in range(1 if lastl else 2):
                if lastl:
                    io = dict(xres_i=xres[l], xb_i=xin[1], rsall=rs[l], dyn=True, wU16=wU16, wD16=wD16, make_cache=False)
                    io["xres_o"], io["xb_o"] = outT, None
                else:
                    io = dict(xres_i=xres[l][th], xb_i=xb[l][th], rsall=rs[l][:, :, th * TH:(th + 1) * TH],
                              wU16=wU16, wD16=wD16, make_cache=False)
                    io["xres_o"], io["xb_o"] = xres[l + 1][th], xb[l + 1][(1 + th) if l + 1 == DEPTH - 1 else th]
                io["cB"] = self.win("cB")
                for nm in ("lB", "lV", "wGu", "wGv", "wGt", "pR", "pS", "pG", "wO", "wU", "wD"):
                    io[nm] = wl(nm, l)
                self.stage_b(io)
                self.P.barrier()

    def parity(self, e):
        if getattr(self, "_par", None) is None:
            self._par = e.snap(e.partition_id() % 2, min_val=0, max_val=1)
        return self._par

    def load_cast(self, dst16, src, n, tag, stg, nbuf=2):
        P = self.P
        CH = stg[0].shape[1]
        cnt = getattr(self, "_lc_cnt", 0)
        for o in range(0, n, CH):
            w = min(CH, n - o)
            bi = cnt % nbuf
            cnt += 1
            sb = stg[bi]
            P.dma("sp", sb[:, 0:w], src[:, o:o + w], f"stg{bi}", writes=[("stg", bi)])
            P.op("dve", (lambda e, a=dst16[:, o:o + w], b=sb[:, 0:w]: e.tensor_copy(out=a, in_=b)),
                 reads=[("stg", bi)], writes=[tag])
        self._lc_cnt = cnt

    def cache_chunks(self, cio, stg, c16):
        P = self.P
        k = 0
        for src, dst, nblk, per in ((cio["wU"], cio["wU16"], 32, 1024), (cio["wD"], cio["wD16"], 8, 4096)):
            for blk in range(nblk):
                sflat = src[blk].rearrange("p a b -> p (a b)")
                for o in range(0, per, 1024):
                    def emit(bi=k % len(stg), sflat=sflat, dst=dst, blk=blk, o=o):
                        P.dma("sp", stg[bi], sflat[:, o:o + 1024], f"cstg{bi}", writes=[("cstg", bi)])
                        P.op("dve", (lambda e, a=c16[bi], b=stg[bi]: e.tensor_copy(out=a, in_=b)),
                             reads=[("cstg", bi)], writes=[("cc16", bi)])
                        P.dma("sp", dst[blk][:, o:o + 1024], c16[bi], f"cwc{bi}", reads=[("cc16", bi)])
                    yield emit
                    k += 1

    def stage_p0(self, io):
        P, ar = self.P, self.ar
        xT, xres_d, xb_d = io["xT"], io["xres_o"], io["xb_o"]
        ar.mark()
        x32 = ar.alloc(8 * TH, F32)
        x16 = ar.alloc(8 * TH, BF16)
        for dc in range(8):
            sl = slice(dc * TH, (dc + 1) * TH)
            P.dma("sp", x32[:, sl], xT[dc * 128:(dc + 1) * 128, :], "p0l", writes=[("x32", dc)])
            P.op("dve", (lambda e, a=x16[:, sl], b=x32[:, sl]: e.tensor_copy(out=a, in_=b)),
                 reads=[("x32", dc)], writes=[("x16", dc)])
            if xres_d is not None:
                P.dma("sp", xres_d[dc * 128:(dc + 1) * 128, :], x32[:, sl], "p0s", reads=[("x32", dc)])
            P.dma("sp", xb_d[dc * 128:(dc + 1) * 128, :], x16[:, sl], "p0s", reads=[("x16", dc)])
        ar.release()

    def stage_a(self, io):
        P, ar, ps, psb, psb2 = self.P, self.ar, self.ps, self.psb, self.psb2
        xall, rs_d = io["xall"], io["rs"]
        lastm = io.get("last", False)
        cA_d, lA_d = io["cA"], io["lA"]
        wAf_d, wAt_d = io["wAf"], io["wAt"]
        ar.mark()
        cA = ar.alloc(CA_N, F32)
        lA = ar.alloc(4, F32)
        c16 = ar.alloc(384, BF16)
        P.dma("sp", cA, cA_d, "cA", writes=["cA"])
        P.dma("sp", lA, lA_d, "lA", writes=["lA"])
        P.op("dve", lambda e: e.tensor_copy(out=c16[:, 0:128], in_=cA[:, CA_IDENT:CA_IDENT + 128]), reads=["cA"], writes=["c16a"])
        P.op("dve", lambda e: e.tensor_copy(out=c16[:, 128:384], in_=cA[:, CA_U:CA_U + 256]), reads=["cA"], writes=["c16b"])
        ident, U16, L16 = c16[:, 0:128], c16[:, 128:256], c16[:, 256:384]
        caus = cA[:, CA_CAUS:CA_CAUS + 128]
        rgT = ar.alloc(2 * S, BF16)
        sqT = ar.alloc(2 * S, BF16)
        skT = ar.alloc(2 * S, BF16)
        qdT = ar.alloc(2 * S, BF16)
        kdT = ar.alloc(2 * S, BF16)
        kdk = ar.alloc(32 * 256, BF16)
        vtk = ar.alloc(32 * 512, BF16)
        ar.mark()
        wf = ar.alloc(6 * 1024, BF16)
        wt = ar.alloc(2 * 4096, BF16)
        stg = [ar.alloc(1024, F32) for _ in range(2)]
        cR_d = io["cR"]
        crt = [ar.alloc(512, F32) for _ in range(2)]
        xt = [ar.alloc(8 * 512, BF16) for _ in range(2)]
        qk32 = [ar.alloc(512, F32)] * 2
        qk16 = [ar.alloc(512, BF16) for _ in range(2)]
        tmpr = [ar.alloc(512, F32)] * 2
        for cb in range(6):
            self.load_cast(wf[:, cb * 1024:(cb + 1) * 1024], wAf_d[cb].rearrange("p a b -> p (a b)"), 1024, ("wf", cb), stg)
        for g in range(2):
            self.load_cast(wt[:, g * 4096:(g + 1) * 4096], wAt_d[g].rearrange("p a b -> p (a b)"), 4096, ("wt", g), stg)

        if io.get("pre_x") is not None:
            io["pre_x"]()
        for T in range(8):
            xb_ = xt[T % 2]
            r, t0 = T // 4, (T % 4) * 512
            P.dma("sp", xb_.rearrange("p (dc t) -> p dc t", dc=8),
                  xall[r].rearrange("(dc p) t -> p dc t", p=128)[:, :, t0:t0 + 512],
                  f"xt{T % 2}", writes=[("xt", T % 2)])
            P.dma("sp", crt[T % 2][:, 0:256], cR_d[:, CR_COS + T * 256: CR_COS + (T + 1) * 256], f"cr{T % 2}", writes=[("crt", T % 2)])
            P.dma("sp", crt[T % 2][:, 256:512], cR_d[:, CR_SIN + T * 256: CR_SIN + (T + 1) * 256], f"cr{T % 2}", writes=[("crt", T % 2)])
            for cb in (range(6) if "fm" in SUB else []):
                bank = ps[cb % 2]
                for dc in range(8):
                    P.op("pe", (lambda e, o=bank[:, :], a=wf[:, cb * 1024 + dc * 128: cb * 1024 + (dc + 1) * 128],
                                b=xb_[:, dc * 512:(dc + 1) * 512], st=(dc == 0), sp=(dc == 7):
                                e.matmul(o, lhsT=a, rhs=b, start=st, stop=sp)),
                         reads=[("wf", cb), ("xt", T % 2)], writes=[("ps", cb % 2)])
                if cb < 2:
                    dst = rgT[:, cb * S + T * 512: cb * S + (T + 1) * 512]
                    P.op("act", (lambda e, o=dst, i=bank[:, :]: e.activation(out=o, in_=i, func=AF.Silu)),
                         reads=[("ps", cb % 2)], writes=[("rgT", cb, T)])
                elif cb < 4:
                    dst = sqT[:, (cb - 2) * S + T * 512: (cb - 2) * S + (T + 1) * 512]
                    P.op("act", (lambda e, o=dst, i=bank[:, :]: e.activation(out=o, in_=i, func=AF.Copy, scale=0.125)),
                         reads=[("ps", cb % 2)], writes=[("sqT", cb - 2, T)])
                else:
                    dst = skT[:, (cb - 4) * S + T * 512: (cb - 4) * S + (T + 1) * 512]
                    P.op("dve", (lambda e, o=dst, i=bank[:, :]: e.tensor_copy(out=o, in_=i)),
                         reads=[("ps", cb % 2)], writes=[("skT", cb - 4, T)])
            for q in (range(4) if "tm" in SUB else []):
                n = T * 4 + q
                pq, pv = ps[2 + (n % 2)], ps[4 + (n % 2)]
                for g, bank in ((0, pq), (1, pv)):
                    for dc in range(8):
                        P.op("pe", (lambda e, o=bank[:, :], a=xb_[:, dc * 512 + q * 128: dc * 512 + (q + 1) * 128],
                                    b=wt[:, g * 4096 + dc * 512: g * 4096 + (dc + 1) * 512], st=(dc == 0), sp=(dc == 7):
                                    e.matmul(o, lhsT=a, rhs=b, start=st, stop=sp)),
                             reads=[("wt", g), ("xt", T % 2)], writes=[("ps", 2 + 2 * g + (n % 2))])
                P.op("act", (lambda e, o=vtk[:, n * 512:(n + 1) * 512], i=pv[:, :]: e.copy(out=o, in_=i)),
                     reads=[("ps", 4 + (n % 2))], writes=[("vtk", n)])
                if "rot" not in SUB:
                    continue
                A32, T32, O16 = qk32[n % 2], tmpr[n % 2], qk16[n % 2]
                X = pq[:, :].rearrange("p (g two f) -> p g two f", g=4, two=2)
                A4 = A32.rearrange("p (g two f) -> p g two f", g=4, two=2)
                T4 = T32.rearrange("p (g two f) -> p g two f", g=4, two=2)
                cosb = crt[T % 2][:, q * 64:(q + 1) * 64].unsqueeze(1).to_broadcast([128, 4, 64])
                sinb = crt[T % 2][:, 256 + q * 64: 256 + (q + 1) * 64].unsqueeze(1).to_broadcast([128, 4, 64])
                rk_ = [("ps", 2 + (n % 2)), ("crt", T % 2)]
                P.op("dve", (lambda e, o=A4[:, :, 0, :], a=X[:, :, 0, :], b=cosb: e.tensor_tensor(out=o, in0=a, in1=b, op=ALU.mult)),
                     reads=rk_, writes=[("A32a", 0)])
                P.op("dve", (lambda e, o=A4[:, :, 1, :], a=X[:, :, 1, :], b=cosb: e.tensor_tensor(out=o, in0=a, in1=b, op=ALU.mult)),
                     reads=rk_, writes=[("A32b", 0)])
                P.op("dve", (lambda e, o=T4[:, :, 0, :], a=X[:, :, 1, :], b=sinb: e.tensor_tensor(out=o, in0=a, in1=b, op=ALU.mult)),
                     reads=rk_, writes=[("T32a", 0)])
                P.op("dve", (lambda e, o=T4[:, :, 1, :], a=X[:, :, 0, :], b=sinb: e.tensor_tensor(out=o, in0=a, in1=b, op=ALU.mult)),
                     reads=rk_, writes=[("T32b", 0)])
                P.op("pool", (lambda e, o=A4[:, :, 0, :], a=A4[:, :, 0, :], b=T4[:, :, 0, :]: e.tensor_tensor(out=o, in0=a, in1=b, op=ALU.subtract)),
                     reads=[("A32a", 0), ("T32a", 0)], writes=[("A32a", 0)])
                P.op("pool", (lambda e, o=A4[:, :, 1, :], a=A4[:, :, 1, :], b=T4[:, :, 1, :]: e.tensor_tensor(out=o, in0=a, in1=b, op=ALU.add)),
                     reads=[("A32b", 0), ("T32b", 0)], writes=[("A32b", 0)])
                decb = cA[:, CA_DEC:CA_DEC + 4].unsqueeze(2).to_broadcast([128, 4, 128])
                P.op("pool", (lambda e, o=O16.rearrange("p (g f) -> p g f", g=4), a=A32.rearrange("p (g f) -> p g f", g=4), b=decb:
                              e.tensor_tensor(out=o, in0=a, in1=b, op=ALU.mult)),
                     reads=[("A32a", 0), ("A32b", 0), "cA"], writes=[("qk16", n % 2)])
                P.op("pool", (lambda e, o=kdk[:, n * 256:(n + 1) * 256], i=O16[:, 256:512]: e.tensor_copy(out=o, in_=i)),
                     reads=[("qk16", n % 2)], writes=[("kdk", n)])
                if "tr" not in SUB:
                    continue
                for g in range(4):
                    pT = psb if g < 2 else psb2
                    P.op("pe", (lambda e, o=pT[:, (g % 2) * 128:(g % 2 + 1) * 128], i=O16[:, g * 128:(g + 1) * 128]:
                                e.transpose(out=o, in_=i, identity=ident)),
                         reads=[("qk16", n % 2), "c16a"], writes=["psb" if g < 2 else "psb2"])
                for h in range(2):
                    P.op("act", (lambda e, o=qdT[:, h * S + n * 128: h * S + (n + 1) * 128], i=psb[:, h * 128:(h + 1) * 128]: e.copy(out=o, in_=i)),
                         reads=["psb"], writes=[("qdT", h, n)])
                    P.op("dve", (lambda e, o=kdT[:, h * S + n * 128: h * S + (n + 1) * 128], i=psb2[:, h * 128:(h + 1) * 128]: e.tensor_copy(out=o, in_=i)),
                         reads=["psb2"], writes=[("kdT", h, n)])
        ar.release()
        P.barrier()

        ar.mark()
        if "ret" not in PARTS:
            ar.release(); ar.release(); return
        rso = ar.alloc(2 * S, BF16)
        st32 = ar.alloc(256, F32)
        st16 = ar.alloc(256, BF16)
        std = [ar.alloc(256, BF16) for _ in range(2)]
        nrm = [ar.alloc(256, BF16) for _ in range(2)]
        stt = [ar.alloc(32, F32) for _ in range(2)]
        tmpg = [ar.alloc(256, F32) for _ in range(2)]
        for n in range(32):
            pb = n % 2
            pS, pO, pK = ps[0 + pb], ps[2 + pb], ps[4 + pb]
            H = [(h, slice(h * S + n * 128, h * S + (n + 1) * 128), slice(h * 128, (h + 1) * 128)) for h in range(2)]
            qry = not (lastm and n < 16)
            for h, csl, hs in (H if qry else []):
                P.op("pe", (lambda e, o=pS[:, hs], a=kdT[:, csl], b=qdT[:, csl]: e.matmul(o, lhsT=a, rhs=b, start=True, stop=True)),
                     reads=[("kdT", h, n), ("qdT", h, n)], writes=[("pS", pb)])
            for h, csl, hs in (H if qry else []):
                P.op("dve", (lambda e, o=std[pb][:, hs], a=pS[:, hs], s_=cA[:, CA_CH + h:CA_CH + h + 1], m=caus:
                             e.scalar_tensor_tensor(out=o, in0=a, scalar=s_, in1=m, op0=ALU.mult, op1=ALU.mult)),
                     reads=[("pS", pb), "cA"], writes=[("std", pb, h)])
            for h, csl, hs in (H if qry else []):
                vsl = vtk[:, n * 512 + h * 128: n * 512 + (h + 1) * 128]
                P.op("pe", (lambda e, o=pO[:, hs], a=std[pb][:, hs], b=vsl, sp=(n == 0): e.matmul(o, lhsT=a, rhs=b, start=True, stop=sp)),
                     reads=[("std", pb, h), ("vtk", n)], writes=[("pO", pb)])
                if n > 0:
                    P.op("pe", (lambda e, o=pO[:, hs], a=qdT[:, csl], b=st16[:, hs]: e.matmul(o, lhsT=a, rhs=b, start=False, stop=True)),
                         reads=[("qdT", h, n), ("st16", h)], writes=[("pO", pb)])
            for h, csl, hs in H:
                vsl = vtk[:, n * 512 + h * 128: n * 512 + (h + 1) * 128]
                P.op("pe", (lambda e, o=pK[:, hs], a=kdk[:, n * 256 + h * 128: n * 256 + (h + 1) * 128], b=vsl: e.matmul(o, lhsT=a, rhs=b, start=True, stop=True)),
                     reads=[("kdk", n), ("vtk", n)], writes=[("pK", pb)])
            for h, csl, hs in H:
                if n == 0:
                    P.op("dve", (lambda e, o=st32[:, hs], i=pK[:, hs]: e.tensor_copy(out=o, in_=i)),
                         reads=[("pK", pb)], writes=[("st32", h)])
                else:
                    P.op("dve", (lambda e, o=st32[:, hs], a=st32[:, hs], s_=cA[:, CA_CD + h:CA_CD + h + 1], b=pK[:, hs]:
                                 e.scalar_tensor_tensor(out=o, in0=a, scalar=s_, in1=b, op0=ALU.mult, op1=ALU.add)),
                         reads=[("pK", pb), ("st32", h), "cA"], writes=[("st32", h)])
                P.op("pool", (lambda e, o=st16[:, hs], i=st32[:, hs]: e.tensor_copy(out=o, in_=i)),
                     reads=[("st32", h)], writes=[("st16", h)])
            if not qry:
                continue
            sv = stt[pb]
            for h, csl, hs in H:
                b0 = h * 16
                P.op("dve", (lambda e, o=sv[:, b0:b0 + 6], i=pO[:, hs]: e.bn_stats(out=o, in_=i)),
                     reads=[("pO", pb)], writes=[("stt", pb, h)])
                P.op("dve", (lambda e, o=sv[:, b0 + 8:b0 + 10], i=sv[:, b0:b0 + 6]: e.bn_aggr(out=o, in_=i)),
                     reads=[("stt", pb, h)], writes=[("stt", pb, h)])
            for h, csl, hs in H:
                b0 = h * 16
                P.op("act", (lambda e, o=sv[:, b0 + 11:b0 + 12], i=sv[:, b0 + 9:b0 + 10]: e.activation(out=o, in_=i, func=AF.Ln, bias=EPS)),
                     reads=[("stt", pb, h)], writes=[("stt", pb, h)])
            for h, csl, hs in H:
                b0 = h * 16
                P.op("act", (lambda e, o=sv[:, b0 + 10:b0 + 11], i=sv[:, b0 + 11:b0 + 12]: e.activation(out=o, in_=i, func=AF.Exp, scale=-0.5)),
                     reads=[("stt", pb, h)], writes=[("stt", pb, h)])
            for h, csl, hs in H:
                b0 = h * 16
                P.op("dve", (lambda e, o=nrm[pb][:, hs], a=pO[:, hs], m=sv[:, b0 + 8:b0 + 9], r=sv[:, b0 + 10:b0 + 11]:
                             e.tensor_scalar(out=o, in0=a, scalar1=m, scalar2=r, op0=ALU.subtract, op1=ALU.mult)),
                     reads=[("pO", pb), ("stt", pb, h)], writes=[("nrm", pb, h)])
            pT = psb if pb == 0 else psb2
            for h, csl, hs in H:
                P.op("pe", (lambda e, o=pT[:, hs], i=nrm[pb][:, hs]: e.transpose(out=o, in_=i, identity=ident)),
                     reads=[("nrm", pb, h), "c16a"], writes=[("psbr", pb)])
            for h, csl, hs in H:
                P.op("dve", (lambda e, o=tmpg[pb][:, hs], a=pT[:, hs], g=lA[:, h:h + 1], b=lA[:, 2 + h:3 + h]:
                             e.tensor_scalar(out=o, in0=a, scalar1=g, scalar2=b, op0=ALU.mult, op1=ALU.add)),
                     reads=[("psbr", pb), "lA"], writes=[("tmpg", pb, h)])
                P.op("pool", (lambda e, o=rso[:, csl], a=tmpg[pb][:, hs], b=rgT[:, csl]: e.tensor_tensor(out=o, in0=a, in1=b, op=ALU.mult)),
                     reads=[("tmpg", pb, h), ("rgT", h, n // 4)], writes=[("rso", h)])
        for h in range(2):
            P.dma("sp", rs_d[h * 128:(h + 1) * 128, :], rso[:, h * S + (TH if lastm else 0):(h + 1) * S], "rsst", reads=[("rso", h)])
        P.barrier()
        ar.release()

        ar.mark()
        if "sb" not in PARTS:
            ar.release(); ar.release(); return
        sbo = ar.alloc(4 * S, BF16, parts=64)
        e32 = [ar.alloc(512, F32) for _ in range(4)]
        sp16 = [ar.alloc(512, BF16) for _ in range(4)]
        w32 = [ar.alloc(512, F32) for _ in range(2)]
        a16 = [ar.alloc(512, BF16) for _ in range(4)]
        sbm = cA[:, CA_SBM:CA_SBM + 2048]
        cgen = None
        if io.get("cache_io") is not None:
            cstg = [ar.alloc(1024, F32) for _ in range(2)]
            cc16 = [ar.alloc(1024, BF16) for _ in range(2)]
            cgen = self.cache_chunks(io["cache_io"], cstg, cc16)
        step_i = 0

        def emit_z(s, hd, T, kb):
            base, pr = (hd % 2) * 64, hd // 2
            c0 = max(0, kb - 4 * T) * 128
            P.op("pe", (lambda e, o=ps[s][:, c0:], a=skT[base:base + 64, pr * S + kb * 128: pr * S + (kb + 1) * 128],
                        b=sqT[base:base + 64, pr * S + T * 512 + c0: pr * S + (T + 1) * 512]: e.matmul(o, lhsT=a, rhs=b, start=True, stop=True)),
                 reads=[("skT", pr, kb // 4), ("sqT", pr, T)], writes=[("pz", s)])

        for pr in range(2):
            for T in (range(4, 8) if lastm else range(8)):
                kbs = list(range(4 * T + 3, -1, -1))
                for s in range(2):
                    emit_z(s, pr * 2 + s, T, kbs[0])
                for ki, kb in enumerate(kbs):
                    first, last = (ki == 0), (ki == len(kbs) - 1)
                    c0 = max(0, kb - 4 * T) * 128
                    step_i += 1
                    pj = step_i % 2
                    if cgen is not None and step_i % 3 == 0:
                        em = next(cgen, None)
                        if em is not None:
                            em()
                    for s in range(2):
                        P.op("act", (lambda e, o=e32[s + 2 * pj][:, c0:], i=ps[s][:, c0:]: e.activation(out=o, in_=i, func=AF.Exp)),
                             reads=[("pz", s)], writes=[("e32", s, pj)])
                    if kb >= 4 * T:
                        r = kb - 4 * T
                        for s in range(2):
                            P.op("dve", (lambda e, o=e32[s + 2 * pj][:, c0:], a=e32[s + 2 * pj][:, c0:], m=sbm[:, r * 512 + c0:(r + 1) * 512]: e.tensor_tensor(out=o, in0=a, in1=m, op=ALU.mult)),
                                 reads=[("e32", s, pj), "cA"], writes=[("e32", s, pj)])
                    for s in range(2):
                        P.op("act", (lambda e, o=sp16[s + 2 * pj][:, c0:], i=e32[s + 2 * pj][:, c0:]: e.activation(out=o, in_=i, func=AF.Ln, bias=1.0)),
                             reads=[("e32", s, pj)], writes=[("sp16", s, pj)])
                    for s in range(2):
                        P.op("pe", (lambda e, o=ps[2 + s][:, c0:], b=sp16[s + 2 * pj][:, c0:], st=first: e.matmul(o, lhsT=U16, rhs=b, start=st, stop=False, skip_group_check=True)),
                             reads=[("sp16", s, pj), "c16b"], writes=[("pR", s)])
                    if not last:
                        for s in range(2):
                            emit_z(s, pr * 2 + s, T, kbs[ki + 1])
                    for s in range(2):
                        P.op("act", (lambda e, o=w32[s][:, c0:], i=ps[2 + s][:, c0:]: e.activation(out=o, in_=i, func=AF.Exp, scale=-1.0)),
                             reads=[("pR", s)], writes=[("w32", s)])
                    for s in range(2):
                        P.op("pe", (lambda e, o=ps[2 + s][:, c0:], b=sp16[s + 2 * pj][:, c0:], sp_=last: e.matmul(o, lhsT=L16, rhs=b, start=False, stop=True if sp_ else False, skip_group_check=True)),
                             reads=[("sp16", s, pj), "c16b"], writes=[("pR", s)])
                    for s in range(2):
                        P.op("dve", (lambda e, o=a16[s + 2 * pj][:, c0:], a=e32[s + 2 * pj][:, c0:], b=w32[s][:, c0:]: e.tensor_tensor(out=o, in0=a, in1=b, op=ALU.mult)),
                             reads=[("e32", s, pj), ("w32", s)], writes=[("a16", s, pj)])
                    for s in range(2):
                        hd = pr * 2 + s
                        P.op("pe", (lambda e, o=ps[4 + s][0:64, c0:], a=vtk[:, kb * 512 + 256 + hd * 64: kb * 512 + 256 + (hd + 1) * 64], b=a16[s + 2 * pj][:, c0:], st=first, sp_=last:
                                    e.matmul(o, lhsT=a, rhs=b, start=st, stop=sp_, skip_group_check=True)),
                             reads=[("a16", s, pj), ("vtk", kb)], writes=[("po", s)])
                for s in range(2):
                    hd = pr * 2 + s
                    P.op("act" if s else "dve",
                         (lambda e, o=sbo[:, hd * S + T * 512: hd * S + (T + 1) * 512], i=ps[4 + s][0:64, :], s_=s:
                          (e.copy(out=o, in_=i) if s_ else e.tensor_copy(out=o, in_=i))),
                         reads=[("po", s)], writes=[("sbo", hd)])
        if cgen is not None:
            for em in cgen:
                em()
        for hd in range(4):
            P.dma("sp", rs_d[256 + hd * 64: 256 + (hd + 1) * 64, :], sbo[:, hd * S + (TH if lastm else 0):(hd + 1) * S], "rsst", reads=[("sbo", hd)])
        P.barrier()
        ar.release()
        ar.release()

    def stage_b(self, io):
        P, ar, ps, psb = self.P, self.ar, self.ps, self.psb
        xres_d, xb_d, rsa_d = io["xres_i"], io["xb_i"], io["rsall"]
        xres_o, xb_o = io["xres_o"], io["xb_o"]
        cB_d, lB_d = io["cB"], io["lB"]
        wGu_d, wGv_d, wGt_d = io["wGu"], io["wGv"], io["wGt"]
        pR_d, pS_d, pG_d = io["pR"], io["pS"], io["pG"]
        wO_d, wU_d, wD_d = io["wO"], io["wU"], io["wD"]
        wU16, wD16 = io["wU16"], io["wD16"]

        ar.mark()
        cB = ar.alloc(CB_N, F32)
        lV = ar.alloc(LV_N, F32)
        P.dma("sp", cB, cB_d, "cB", writes=["cB"])
        P.dma("sp", lV, io["lV"], "lV", writes=["lV"])
        onesF = cB[:, CB_ONES:CB_ONES + 128]
        xres = ar.alloc(8 * TH, F32)
        xb = ar.alloc(8 * TH, BF16)
        stg = [ar.alloc(1024, F32) for _ in range(2)]
        XR8 = [("xres", dc) for dc in range(8)]
        for dc in range(8):
            P.dma("sp", xb[:, dc * TH:(dc + 1) * TH], xb_d[dc * 128:(dc + 1) * 128, :], "bld", writes=[("xb", dc)])

        def load_xres():
            if io.get("dyn"):
                P.custom_dma("sp", (lambda e, o=xres.rearrange("p (dc t) -> p dc t", dc=8):
                                    e.dma_start(out=o, in_=xres_d[bass.ds(self.parity(e), 1)].rearrange("o (dc p) t -> p (o dc) t", p=128))),
                             "bld", writes=XR8)
            else:
                for dc in range(8):
                    P.dma("sp", xres[:, dc * TH:(dc + 1) * TH], xres_d[dc * 128:(dc + 1) * 128, :], "bld", writes=[("xres", dc)])
        XR = [("xres", dc) for dc in range(8)]
        XB = [("xb", dc) for dc in range(8)]

        ar.mark()
        c16 = [ar.alloc(1024, BF16) for _ in range(2)]
        k = 0
        for src, dst, nblk, per, wk in (((wU_d, wU16, 32, 1024, "wcU"), (wD_d, wD16, 8, 4096, "wcD")) if io["make_cache"] else ()):
            for blk in range(nblk):
                sflat = src[blk].rearrange("p a b -> p (a b)")
                for o in range(0, per, 1024):
                    w = min(1024, per - o)
                    bi = k % 2
                    k += 1
                    P.dma("sp", stg[bi][:, 0:w], sflat[:, o:o + w], f"stg{bi}", writes=[("stg", bi)])
                    P.op("pool" if bi else "dve", (lambda e, a=c16[bi][:, 0:w], b=stg[bi][:, 0:w]: e.tensor_copy(out=a, in_=b)),
                         reads=[("stg", bi)], writes=[("c16", bi)])
                    P.dma("sp", dst[blk][:, o:o + w], c16[bi][:, 0:w], f"wc{bi}", reads=[("c16", bi)], writes=[(wk, blk, o)])
        ar.release()
        if io["make_cache"]:
            P.barrier()

        ar.mark()
        rsf = ar.alloc(8 * TH, BF16)

        def load_rsf():
          for hh in range(2):
            for c4 in range(2):
                for base, slot in ((0, hh * 2 + c4), (256, 4 + hh * 2 + c4)):
                    dst = rsf[:, slot * TH:(slot + 1) * TH]
                    if io.get("dyn_rs"):
                        P.custom_dma("sp", (lambda e, o=dst, hh=hh, r0=base + c4 * 128:
                                            e.dma_start(out=o, in_=rsa_d.rearrange("h r (two t) -> h r two t", two=2)[hh, r0:r0 + 128, bass.ds(self.parity(e), 1), :]
                                                        .rearrange("p o t -> p (o t)"))),
                                     "bld", writes=[("rsf", slot)])
                    else:
                        P.dma("sp", dst, rsa_d[hh, base + c4 * 128: base + (c4 + 1) * 128, :], "bld", reads=["rsall_d"], writes=[("rsf", slot)])
        sgT = ar.alloc(4 * TH, BF16)
        ar.mark()
        lB = ar.alloc(LB_N, F32)
        P.dma("sp", lB, lB_d, "lB", writes=["lB"])
        wgu = ar.alloc(4 * 1024, BF16)
        wgv = ar.alloc(4096, BF16)
        wsg = ar.alloc(512, BF16)
        guT = [ar.alloc(4 * 512, BF16) for _ in range(2)]
        g32 = [ar.alloc(512, F32) for _ in range(2)]
        vn = [ar.alloc(512, BF16) for _ in range(2)]
        stt = [ar.alloc(32, F32) for _ in range(2)]
        t32 = [ar.alloc(512, F32) for _ in range(2)]
        for cb in range(4):
            self.load_cast(wgu[:, cb * 1024:(cb + 1) * 1024], wGu_d[cb].rearrange("p a b -> p (a b)"), 1024, ("wgu", cb), stg)
        self.load_cast(wgv, wGv_d[0].rearrange("p a b -> p (a b)"), 4096, "wgv", stg)
        if io.get("pre_rs") is not None:
            io["pre_rs"]()
        load_rsf()
        load_xres()
        causb = cB[:, CB_CAUS:CB_CAUS + 128].unsqueeze(1).to_broadcast([128, 4, 128])
        P.op("dve", (lambda e: e.tensor_tensor(out=wsg.rearrange("p (g i) -> p g i", g=4), in0=lB[:, LB_WT:LB_WT + 512].rearrange("p (g i) -> p g i", g=4),
                                               in1=causb, op=ALU.mult)), reads=["lB", "cB"], writes=["wsg"])
        for T in range(4):
            gb = guT[T % 2]
            for cb in range(4):
                bank = ps[cb % 2]
                for dc in range(8):
                    P.op("pe", (lambda e, o=bank[:, :], a=wgu[:, cb * 1024 + dc * 128: cb * 1024 + (dc + 1) * 128],
                                b=xb[:, dc * TH + T * 512: dc * TH + (T + 1) * 512], st=(dc == 0), sp=(dc == 7): e.matmul(o, lhsT=a, rhs=b, start=st, stop=sp)),
                         reads=[("wgu", cb), ("xb", dc)], writes=[("ps", cb % 2)])
                P.op("act", (lambda e, o=gb[:, cb * 512:(cb + 1) * 512], i=bank[:, :]: e.activation(out=o, in_=i, func=AF.Gelu_apprx_tanh)),
                     reads=[("ps", cb % 2)], writes=[("guT", T % 2, cb)])
            for q in range(4):
                n = T * 4 + q
                pb = n % 2
                pv, psv = ps[2 + pb], ps[4 + pb]
                for dc in range(8):
                    P.op("pe", (lambda e, o=pv[:, :], a=xb[:, dc * TH + n * 128: dc * TH + (n + 1) * 128], b=wgv[:, dc * 512:(dc + 1) * 512], st=(dc == 0), sp=(dc == 7):
                                e.matmul(o, lhsT=a, rhs=b, start=st, stop=sp)),
                         reads=["wgv", ("xb", dc)], writes=[("ps", 2 + pb)])
                P.op("act", (lambda e, o=g32[pb], i=pv[:, :]: e.activation(out=o, in_=i, func=AF.Gelu_apprx_tanh)),
                     reads=[("ps", 2 + pb)], writes=[("g32", pb)])
                sv = stt[pb]
                P.op("dve", (lambda e, o=sv[:, 0:6], i=g32[pb]: e.bn_stats(out=o, in_=i)), reads=[("g32", pb)], writes=[("stt", pb)])
                P.op("dve", (lambda e, o=sv[:, 8:10], i=sv[:, 0:6]: e.bn_aggr(out=o, in_=i)), reads=[("stt", pb)], writes=[("stt", pb)])
                P.op("act", (lambda e, o=sv[:, 11:12], i=sv[:, 9:10]: e.activation(out=o, in_=i, func=AF.Ln, bias=EPS)), reads=[("stt", pb)], writes=[("stt", pb)])
                P.op("act", (lambda e, o=sv[:, 10:11], i=sv[:, 11:12]: e.activation(out=o, in_=i, func=AF.Exp, scale=-0.5)), reads=[("stt", pb)], writes=[("stt", pb)])
                P.op("dve", (lambda e, o=t32[pb], a=g32[pb], m=sv[:, 8:9], r=sv[:, 10:11]: e.tensor_scalar(out=o, in0=a, scalar1=m, scalar2=r, op0=ALU.subtract, op1=ALU.mult)),
                     reads=[("g32", pb), ("stt", pb)], writes=[("t32", pb)])
                P.op("dve", (lambda e, o=t32[pb], a=t32[pb], b=lB[:, LB_LNG:LB_LNG + 512]: e.tensor_tensor(out=o, in0=a, in1=b, op=ALU.mult)),
                     reads=[("t32", pb), "lB"], writes=[("t32", pb)])
                P.op("dve", (lambda e, o=vn[pb], a=t32[pb], b=lB[:, LB_LNB:LB_LNB + 512]: e.tensor_tensor(out=o, in0=a, in1=b, op=ALU.add)),
                     reads=[("t32", pb), "lB"], writes=[("vn", pb)])
                for g in range(4):
                    P.op("pe", (lambda e, o=psv[:, g * 128:(g + 1) * 128], a=vn[pb][:, g * 128:(g + 1) * 128], b=wsg[:, g * 128:(g + 1) * 128]:
                                e.matmul(o, lhsT=a, rhs=b, start=True, stop=True)),
                         reads=[("vn", pb), "wsg"], writes=[("ps", 4 + pb)])
                P.op("dve", (lambda e, o=t32[pb], a=psv[:, :], b=lB[:, LB_SB:LB_SB + 512]: e.tensor_tensor(out=o, in0=a, in1=b, op=ALU.add)),
                     reads=[("ps", 4 + pb), ("t32", pb), "lB"], writes=[("t32", pb)])
                gview = gb.rearrange("p (g t) -> p g t", g=4)[:, :, q * 128:(q + 1) * 128]
                oview = sgT.rearrange("p (g t) -> p g t", g=4)[:, :, n * 128:(n + 1) * 128]
                P.op("dve", (lambda e, o=oview, a=t32[pb].rearrange("p (g i) -> p g i", g=4), b=gview: e.tensor_tensor(out=o, in0=a, in1=b, op=ALU.mult)),
                     reads=[("t32", pb)] + [("guT", T % 2, cb) for cb in range(4)], writes=[("sgT", n)])
        ar.release()
        P.barrier()
        SG = [("sgT", n) for n in range(16)]
        RS = [("rsf", i) for i in range(8)]

        mg = ar.alloc(8 * TH, BF16)
        ar.mark()
        wg = [ar.alloc(3 * 1024, BF16)] * 2
        wp = [ar.alloc(3 * 512, BF16)] * 2
        sg32 = [ar.alloc(512, F32) for _ in range(2)]
        m32 = [ar.alloc(512, F32) for _ in range(2)]
        srcs = [(pR_d, 0, "ret"), (pS_d, 4, "sb"), (pG_d, None, "sgu")]
        for cb in range(8):
            wb = 0
            for br in range(3):
                self.load_cast(wg[wb][:, br * 1024:(br + 1) * 1024], wGt_d[br * 8 + cb].rearrange("p a b -> p (a b)"), 1024, ("wg", wb, br), stg)
                self.load_cast(wp[wb][:, br * 512:(br + 1) * 512], srcs[br][0][cb].rearrange("p a b -> p (a b)"), 512, ("wp", wb, br), stg)
            for T in range(4):
                for br in range(3):
                    j = (T * 3 + br) % 2
                    pg, pp = ps[j], ps[2 + j]
                    for dc in range(8):
                        P.op("pe", (lambda e, o=pg[:, :], a=wg[wb][:, br * 1024 + dc * 128: br * 1024 + (dc + 1) * 128],
                                    b=xb[:, dc * TH + T * 512: dc * TH + (T + 1) * 512], st=(dc == 0), sp=(dc == 7): e.matmul(o, lhsT=a, rhs=b, start=st, stop=sp)),
                             reads=[("wg", wb, br), ("xb", dc)], writes=[("ps", j)])
                    P.op("act", (lambda e, o=sg32[j], i=pg[:, :]: e.activation(out=o, in_=i, func=AF.Sigmoid)),
                         reads=[("ps", j)], writes=[("sg32", j)])
                    for kc in range(4):
                        if br < 2:
                            rhs = rsf[:, (srcs[br][1] + kc) * TH + T * 512: (srcs[br][1] + kc) * TH + (T + 1) * 512]
                            rk = [("rsf", srcs[br][1] + kc)]
                        else:
                            rhs = sgT[:, kc * TH + T * 512: kc * TH + (T + 1) * 512]
                            rk = SG[T * 4:(T + 1) * 4]
                        P.op("pe", (lambda e, o=pp[:, :], a=wp[wb][:, br * 512 + kc * 128: br * 512 + (kc + 1) * 128], b=rhs, st=(kc == 0), sp=(kc == 3):
                                    e.matmul(o, lhsT=a, rhs=b, start=st, stop=sp)),
                             reads=[("wp", wb, br)] + rk, writes=[("ps", 2 + j)])
                    mt = m32[T % 2]
                    if br == 0:
                        P.op("dve", (lambda e, o=mt, a=pp[:, :], b=sg32[j]: e.tensor_tensor(out=o, in0=a, in1=b, op=ALU.mult)),
                             reads=[("ps", 2 + j), ("sg32", j)], writes=[("m32", T % 2)])
                    else:
                        P.op("dve", (lambda e, o=sg32[j], a=pp[:, :], b=sg32[j]: e.tensor_tensor(out=o, in0=a, in1=b, op=ALU.mult)),
                             reads=[("ps", 2 + j), ("sg32", j)], writes=[("sg32", j)])
                        dst = mt if br == 1 else mg[:, cb * TH + T * 512: cb * TH + (T + 1) * 512]
                        wk = [("m32", T % 2)] if br == 1 else [("mg", cb)]
                        P.op("dve", (lambda e, o=dst, a=mt, b=sg32[j]: e.tensor_tensor(out=o, in0=a, in1=b, op=ALU.add)),
                             reads=[("m32", T % 2), ("sg32", j)], writes=wk)
        ar.release()
        P.barrier()

        ar.mark()
        wo = [ar.alloc(1024, BF16) for _ in range(2)]
        for cb in range(8):
            wb = cb % 2
            self.load_cast(wo[wb], wO_d[cb].rearrange("p a b -> p (a b)"), 1024, ("wo", wb), stg)
            for T in range(4):
                bank = ps[T % 2]
                for kc in range(8):
                    P.op("pe", (lambda e, o=bank[:, :], a=wo[wb][:, kc * 128:(kc + 1) * 128], b=mg[:, kc * TH + T * 512: kc * TH + (T + 1) * 512], st=(kc == 0), sp=(kc == 7):
                                e.matmul(o, lhsT=a, rhs=b, start=st, stop=sp)),
                         reads=[("wo", wb), ("mg", kc)], writes=[("ps", T % 2)])
                xs = xres[:, cb * TH + T * 512: cb * TH + (T + 1) * 512]
                P.op("dve", (lambda e, o=xs, a=xs, b=bank[:, :]: e.scalar_tensor_tensor(out=o, in0=a, scalar=ALPHA, in1=b, op0=ALU.mult, op1=ALU.add)),
                     reads=[("ps", T % 2), ("xres", cb)], writes=[("xres", cb)])
        ar.release()
        ar.release()
        self.layer_norm_fm(xres, xb, lV, LB_L1G, LB_L1B, onesF)
        P.barrier()

        ar.mark()
        hT = ar.alloc(32 * 512, BF16)
        wu = [ar.alloc(4096, BF16) for _ in range(2)]
        wd = [ar.alloc(4096, BF16) for _ in range(2)]
        r32 = [ar.alloc(512, F32) for _ in range(2)]
        for T in range(4):
            for f4 in range(8):
                wb = f4 % 2
                for i in range(4):
                    P.dma("sp", wu[wb][:, i * 1024:(i + 1) * 1024], wU16[f4 * 4 + i], f"wu{wb}", writes=[("wu", wb)])
                for i in range(4):
                    fb = f4 * 4 + i
                    bank = ps[fb % 2]
                    for dc in range(8):
                        P.op("pe", (lambda e, o=bank[:, :], a=wu[wb][:, i * 1024 + dc * 128: i * 1024 + (dc + 1) * 128],
                                    b=xb[:, dc * TH + T * 512: dc * TH + (T + 1) * 512], st=(dc == 0), sp=(dc == 7): e.matmul(o, lhsT=a, rhs=b, start=st, stop=sp)),
                             reads=[("wu", wb), ("xb", dc)], writes=[("ps", fb % 2)])
                    P.op("act", (lambda e, o=r32[fb % 2], i_=bank[:, :]: e.activation(out=o, in_=i_, func=AF.Relu)),
                         reads=[("ps", fb % 2)], writes=[("r32", fb % 2)])
                    P.op("dve", (lambda e, o=hT[:, fb * 512:(fb + 1) * 512], a=r32[fb % 2]: e.tensor_tensor(out=o, in0=a, in1=a, op=ALU.mult)),
                         reads=[("r32", fb % 2)], writes=[("hT", fb)])
            for cb in range(8):
                wb = cb % 2
                P.dma("sp", wd[wb], wD16[cb], f"wd{wb}", writes=[("wd", wb)])
                bank = ps[2 + cb % 2]
                for fc in range(32):
                    P.op("pe", (lambda e, o=bank[:, :], a=wd[wb][:, fc * 128:(fc + 1) * 128], b=hT[:, fc * 512:(fc + 1) * 512], st=(fc == 0), sp=(fc == 31):
                                e.matmul(o, lhsT=a, rhs=b, start=st, stop=sp)),
                         reads=[("wd", wb), ("hT", fc)], writes=[("ps", 2 + cb % 2)])
                xs = xres[:, cb * TH + T * 512: cb * TH + (T + 1) * 512]
                P.op("dve", (lambda e, o=xs, a=xs, b=bank[:, :]: e.scalar_tensor_tensor(out=o, in0=a, scalar=ALPHA, in1=b, op0=ALU.mult, op1=ALU.add)),
                     reads=[("ps", 2 + cb % 2), ("xres", cb)], writes=[("xres", cb)])
        ar.release()
        P.barrier()
        self.layer_norm_fm(xres, xb, lV, LB_L2G, LB_L2B, onesF, store=(xres_o, xb_o))
        P.barrier()
        ar.release()

    def layer_norm_fm(self, xres, xb, lB, og, ob, onesF, store=None):
        P, ar, ps = self.P, self.ar, self.ps
        P.barrier()
        ar.mark()
        usq = ar.alloc(8 * 512, F32)
        mean = [ar.alloc(512, F32) for _ in range(2)]
        rstd = [ar.alloc(512, F32) for _ in range(2)]
        vtm = [ar.alloc(512, F32) for _ in range(2)]
        tmp = [ar.alloc(512, F32) for _ in range(3)]
        banks = [(ps[4], ps[5]), (ps[2], ps[3])]

        def xs_(cb, T):
            return xres[:, cb * TH + T * 512: cb * TH + (T + 1) * 512]

        def stats(T):
            j = T % 2
            p1, p2 = banks[j]
            for cb in range(8):
                P.op("act", (lambda e, o=usq[:, cb * 512:(cb + 1) * 512], i=xs_(cb, T): e.activation(out=o, in_=i, func=AF.Square)),
                     reads=[("xr", cb, T)], writes=[("usq", cb)])
                P.op("pe", (lambda e, o=p1[:, :], b=xs_(cb, T), st=(cb == 0), sp=(cb == 7): e.matmul(o, lhsT=onesF, rhs=b, start=st, stop=sp)),
                     reads=[("xr", cb, T), "cB"], writes=[("lnp1", j)])
                P.op("pe", (lambda e, o=p2[:, :], b=usq[:, cb * 512:(cb + 1) * 512], st=(cb == 0), sp=(cb == 7): e.matmul(o, lhsT=onesF, rhs=b, start=st, stop=sp)),
                     reads=[("usq", cb), "cB"], writes=[("lnp2", j)])
            P.op("act", (lambda e: e.copy(out=mean[j], in_=p1[:, :])), reads=[("lnp1", j)], writes=[("mean", j)])
            P.op("dve", (lambda e: e.tensor_tensor(out=vtm[j], in0=mean[j], in1=mean[j], op=ALU.mult)), reads=[("mean", j)], writes=[("vt", j)])
            P.op("dve", (lambda e: e.tensor_tensor(out=vtm[j], in0=p2[:, :], in1=vtm[j], op=ALU.subtract)), reads=[("lnp2", j), ("vt", j)], writes=[("vt", j)])
            P.op("act", (lambda e: e.activation(out=vtm[j], in_=vtm[j], func=AF.Ln, bias=EPS)), reads=[("vt", j)], writes=[("vt", j)])
            P.op("act", (lambda e: e.activation(out=rstd[j], in_=vtm[j], func=AF.Exp, scale=-0.5)), reads=[("vt", j)], writes=[("rstd", j)])

        def norm(T):
            j = T % 2
            for cb in range(8):
                xs = xs_(cb, T)
                tb = tmp[cb % 3]
                P.op("dve", (lambda e, o=tb, a=xs: e.tensor_tensor(out=o, in0=a, in1=mean[j], op=ALU.subtract)),
                     reads=[("xr", cb, T), ("mean", j)], writes=[("lt", cb % 3)])
                P.op("dve", (lambda e, o=tb: e.tensor_tensor(out=o, in0=o, in1=rstd[j], op=ALU.mult)),
                     reads=[("lt", cb % 3), ("rstd", j)], writes=[("lt", cb % 3)])
                P.op("act", (lambda e, o=xs, i=tb, g=lB[:, og + cb:og + cb + 1], b=lB[:, ob + cb:ob + cb + 1]: e.activation(out=o, in_=i, func=AF.Identity, scale=g, bias=b)),
                     reads=[("lt", cb % 3), "lV"], writes=[("xr", cb, T)])
                P.op("dve", (lambda e, o=xb[:, cb * TH + T * 512: cb * TH + (T + 1) * 512], i=xs: e.tensor_copy(out=o, in_=i)),
                     reads=[("xr", cb, T)], writes=[("xbk", cb, T)])
            if store is not None:
                xo, bo = store
                tsl = slice(T * 512, (T + 1) * 512)
                P.dma("sp", xo.rearrange("(dc p) t -> p dc t", p=128)[:, :, tsl], xres.rearrange("p (dc t) -> p dc t", dc=8)[:, :, tsl],
                      "bst", reads=[("xr", cb, T) for cb in range(8)])
                if bo is not None:
                    P.dma("sp", bo.rearrange("(dc p) t -> p dc t", p=128)[:, :, tsl], xb.rearrange("p (dc t) -> p dc t", dc=8)[:, :, tsl],
                          "bst", reads=[("xbk", cb, T) for cb in range(8)])

        stats(0)
        for T in range(4):
            if T + 1 < 4:
                stats(T + 1)
            norm(T)
        ar.release()


_CACHE = {}


def _prog(stages):
    key = tuple(stages)
    if key not in _CACHE:
        b = Builder(list(stages))
        b.build()
        _CACHE[key] = b
    return _CACHE[key]


def _run(stage, l, maps_all, state):
    b = _prog([stage])
    in_maps = []
    for c in range(8):
        m = {}
        for nm in b.ext_in:
            if nm + str(l) in maps_all[c]:
                m[nm] = maps_all[c][nm + str(l)]
            elif nm in maps_all[c]:
                m[nm] = maps_all[c][nm]
            else:
                m[nm] = state[c][nm]
        in_maps.append(m)
    res = run_bass_kernel_spmd(b.nc, in_maps, core_ids=list(range(8)))
    for c in range(8):
        for nm in b.ext_out:
            state[c][nm] = np.asarray(res.results[c][nm])


def kernel_unfused(**inputs):
    maps = _host_inputs(inputs)
    state = [dict() for _ in range(8)]
    _run("P0", 0, maps, state)
    for l in range(DEPTH):
        for c in range(8):
            pr = c // 2 * 2
            state[c]["xball"] = np.stack([state[pr]["xb_o"], state[pr + 1]["xb_o"]])
            state[c]["xres_i"] = state[c]["xres_o"]
            state[c]["xb_i"] = state[c]["xb_o"]
        _run("A0", l, maps, state)
        for c in range(8):
            pr, hh = c // 2 * 2, c % 2
            state[c]["rsall"] = np.ascontiguousarray(
                np.stack([state[pr]["rs"][:, hh * TH:(hh + 1) * TH], state[pr + 1]["rs"][:, hh * TH:(hh + 1) * TH]]))
        _run("B0", l, maps, state)
    out = np.empty((NB, S, D), np.float32)
    for c in range(8):
        b, hh = c // 2, c % 2
        out[b, hh * TH:(hh + 1) * TH, :] = state[c]["xres_o"].T
    return out


def kernel(**inputs):
    maps = _host_inputs(inputs)
    b = _prog(["FX"])
    nv = int(np.random.randint(1 << 10, 1 << 26))
    nonce = (nv * 8 + np.repeat(np.arange(8, dtype=np.int64), 16)[None, :]).astype(np.int32)
    in_maps = []
    for c in range(8):
        m = {}
        for nm in b.ext_in:
            m[nm] = nonce if nm == "nonce" else maps[c][nm]
        in_maps.append(m)
    res = run_bass_kernel_spmd(b.nc, in_maps, core_ids=list(range(8)))
    out = np.empty((NB, S, D), np.float32)
    for c in range(8):
        bb, hh = c // 2, c % 2
        out[bb, hh * TH:(hh + 1) * TH, :] = np.asarray(res.results[c]["outT"]).T
    return out


def kernel_dup(**inputs):
    maps = _host_inputs(inputs)
    b = _prog(["FUSED"])
    in_maps = []
    for c in range(8):
        pr = c // 2 * 2
        m = {}
        for nm in b.ext_in:
            if nm == "xT2":
                m[nm] = np.stack([maps[pr]["xT"], maps[pr + 1]["xT"]])
            elif nm.startswith("cA_"):
                m[nm] = maps[pr + int(nm[-1])]["cA"]
            elif nm[-2] == "_" and nm[:-2] in maps[c]:
                m[nm] = maps[pr + int(nm[-1])][nm[:-2]]
            else:
                m[nm] = maps[c][nm]
        in_maps.append(m)
    res = run_bass_kernel_spmd(b.nc, in_maps, core_ids=list(range(8)))
    out = np.empty((NB, S, D), np.float32)
    for c in range(8):
        bb, hh = c // 2, c % 2
        out[bb, hh * TH:(hh + 1) * TH, :] = np.asarray(res.results[c]["outT"]).T
    return out
```

```python
import contextlib
import numpy as np
import ml_dtypes
import concourse.bass as bass
import concourse.mybir as mybir
from concourse.bass_utils import run_bass_kernel_spmd

F32 = mybir.dt.float32
BF16 = mybir.dt.bfloat16
AF = mybir.ActivationFunctionType
ALU = mybir.AluOpType

D = 1024
S = 4096
NB = 4
DEPTH = 2
TH = 2048
DFF = 4096
ALPHA = (2 * DEPTH) ** 0.25
EPS = 1e-5
FUSED = False
import os as _os
PARTS = _os.environ.get("KPARTS", "proj,ret,sb").split(",")
SUB = _os.environ.get("KSUB", "fm,tm,rot,tr").split(",")

ENGS = ("pe", "act", "dve", "pool", "sp")
SAME_ENGINE_SYNC = True
SCHEDULE = True


class _Op:
    __slots__ = ("eng", "fn", "chan", "deps", "tick", "needs_inc", "kind", "waits_extra", "seg", "cost", "fin")

    def __init__(self, eng, fn, chan, kind):
        self.seg = 0
        self.cost = None
        self.fin = 0.0
        self.eng = eng
        self.fn = fn
        self.chan = chan
        self.deps = set()
        self.tick = None
        self.needs_inc = False
        self.kind = kind
        self.waits_extra = None


class Prog:
    def __init__(self, nc):
        self.nc = nc
        self.streams = {e: [] for e in ENGS}
        self.res = {}
        self.chan_count = {}
        self.last_op = {e: None for e in ENGS}
        self.seg = 0

    def _add(self, op, reads, writes):
        deps = set()
        for k in reads:
            st = self.res.get(k)
            if st is not None and st[0] is not None:
                deps.add(st[0])
        for k in writes:
            st = self.res.get(k)
            if st is not None:
                if st[0] is not None:
                    deps.add(st[0])
                deps.update(st[1])
        for k in writes:
            self.res[k] = [op, []]
        for k in reads:
            st = self.res.get(k)
            if st is None:
                st = self.res[k] = [None, []]
            if k not in writes:
                st[1].append(op)
        deps.discard(op)
        op.deps = deps
        op.seg = self.seg
        self.streams[op.eng].append(op)
        self.last_op[op.eng] = op
        return op

    def op(self, eng, fn, reads=(), writes=()):
        return self._add(_Op(eng, fn, None, "c"), tuple(reads), tuple(writes))

    def dma(self, queue, out, in_, chan, reads=(), writes=(), **kw):
        k = self.chan_count.get(chan, 0)
        self.chan_count[chan] = k + 1
        o = _Op(queue, (lambda e: e.dma_start(out=out, in_=in_, **kw)), (chan, k), "d")
        return self._add(o, tuple(reads), tuple(writes))

    def xop(self, queue, fn, reads=()):
        o = _Op(queue, fn, None, "x")
        o.cost = 2.0
        return self._add(o, tuple(reads), ())

    def custom_dma(self, queue, fn, chan, reads=(), writes=()):
        k = self.chan_count.get(chan, 0)
        self.chan_count[chan] = k + 1
        o = _Op(queue, fn, (chan, k), "d")
        return self._add(o, tuple(reads), tuple(writes))

    def barrier(self):
        lasts = [self.last_op[e] for e in ENGS
                 if self.last_op[e] is not None and self.last_op[e].kind == "c"]
        lasts = []
        for e in ENGS:
            for o in reversed(self.streams[e]):
                if o.kind == "c":
                    lasts.append(o)
                    break
        chans = dict(self.chan_count)
        for e in ENGS:
            o = _Op(e, None, None, "b")
            o.deps = set(lasts)
            o.waits_extra = chans
            o.seg = self.seg
            self.streams[e].append(o)
        self.res = {}
        self.seg += 1

    COST = {"pe": 0.22, "act": 0.5, "dve": 0.6, "pool": 1.0, "sp": 0.1}
    DMA_LAT = 3.0
    XLAT = 0.5
    SLAT = 0.35
    WINDOW = 48

    def schedule(self):
        nseg = self.seg + 1
        per = {e: [[] for _ in range(nseg + 1)] for e in ENGS}
        bar = {e: [None] * (nseg + 1) for e in ENGS}
        for e in ENGS:
            for o in self.streams[e]:
                if o.kind == "b":
                    bar[e][o.seg] = o
                else:
                    per[e][o.seg].append(o)
        new = {e: [] for e in ENGS}
        for sg in range(nseg + 1):
            lists = {e: per[e][sg] for e in ENGS}
            if any(lists[e] for e in ENGS):
                inseg = set()
                for e in ENGS:
                    inseg.update(lists[e])
                ptr = {e: 0 for e in ENGS}
                done = set()
                tfree = {e: 0.0 for e in ENGS}
                out = {e: [] for e in ENGS}
                pend = {e: list(lists[e]) for e in ENGS}
                t = 0.0
                remaining = sum(len(v) for v in pend.values())
                while remaining:
                    progressed = False
                    nxt = None
                    for e in ENGS:
                        if not pend[e]:
                            continue
                        if tfree[e] > t + 1e-9:
                            nxt = tfree[e] if nxt is None else min(nxt, tfree[e])
                            continue
                        win = pend[e][:1] if e == "sp" else pend[e][:self.WINDOW]
                        best = None
                        for o in win:
                            rdy = 0.0
                            ok = True
                            for d in o.deps:
                                if d not in inseg:
                                    continue
                                if d not in done:
                                    ok = False
                                    break
                                lat = 0.0 if (d.eng == e and e == "pe") else (self.SLAT if d.eng == e else self.XLAT)
                                rdy = max(rdy, d.fin + lat)
                            if not ok:
                                continue
                            if rdy <= t + 1e-9:
                                best = o
                                break
                            nxt = rdy if nxt is None else min(nxt, rdy)
                        if best is not None:
                            c = best.cost if best.cost is not None else self.COST[e]
                            if best.kind == "d":
                                best.fin = t + self.DMA_LAT
                                tfree[e] = t + c
                            else:
                                best.fin = t + c
                                tfree[e] = t + c
                            done.add(best)
                            pend[e].remove(best)
                            out[e].append(best)
                            remaining -= 1
                            progressed = True
                            nxt = tfree[e] if nxt is None else min(nxt, tfree[e])
                    if not progressed:
                        if nxt is None or nxt <= t + 1e-9:
                            for e in ENGS:
                                out[e].extend(pend[e])
                                pend[e] = []
                            break
                        t = nxt
                    else:
                        t = t if nxt is None else min(t + 0.05, nxt) if False else t
                for e in ENGS:
                    new[e].extend(out[e])
            for e in ENGS:
                if bar[e][sg] is not None:
                    new[e].append(bar[e][sg])
        for e in ENGS:
            assert len(new[e]) == len(self.streams[e]), (e, len(new[e]), len(self.streams[e]))
        self.streams = new

    def emit(self):
        nc = self.nc
        if SCHEDULE:
            self.schedule()
        for e in ENGS:
            for o in self.streams[e]:
                for d in o.deps:
                    if d.kind == "c":
                        if d.eng == o.eng and (d.eng == "pe" or not SAME_ENGINE_SYNC) and o.kind != "b":
                            continue
                        d.needs_inc = True
        for e in ENGS:
            t = 0
            for o in self.streams[e]:
                if o.kind == "c" and o.needs_inc:
                    t += 1
                    o.tick = t
        with contextlib.ExitStack() as es:
            esem = {e: es.enter_context(nc.semaphore("s_" + e)) for e in ENGS}
            csem = {c: es.enter_context(nc.semaphore("c_" + str(c))) for c in self.chan_count}
            self.flag_sem = es.enter_context(nc.semaphore("flag_sem"))
            block = es.enter_context(nc.Block())

            def run(e):
                def body(eng):
                    known = {}

                    def wait(sem, val):
                        if known.get(sem.name, 0) >= val:
                            return
                        known[sem.name] = val
                        eng.wait_ge(sem, val)

                    for o in self.streams[e]:
                        for d in o.deps:
                            if d.kind == "c":
                                if d.tick is None:
                                    continue
                                if d.eng == e and (e == "pe" or not SAME_ENGINE_SYNC) and o.kind != "b":
                                    continue
                                wait(esem[d.eng], d.tick)
                            elif d.kind == "d":
                                c, k = d.chan
                                wait(csem[c], 16 * (k + 1))
                        if o.kind == "b":
                            for c, n in o.waits_extra.items():
                                wait(csem[c], 16 * n)
                            continue
                        ins = o.fn(eng)
                        if o.kind == "x":
                            continue
                        if o.kind == "d":
                            ins.then_inc(csem[o.chan[0]], 16)
                        elif o.needs_inc:
                            ins.then_inc(esem[e], 1)
                    if e == "sp":
                        for c, n in self.chan_count.items():
                            wait(csem[c], 16 * n)
                        for e2 in ENGS:
                            lt = max([o.tick for o in self.streams[e2] if o.tick is not None] or [0])
                            if lt:
                                wait(esem[e2], lt)
                return body

            block.tensor(run("pe"))
            block.scalar(run("act"))
            block.vector(run("dve"))
            block.gpsimd(run("pool"))
            block.sync(run("sp"))


class Arena:
    def __init__(self, t32, nbytes):
        self.t = t32
        self.n = nbytes
        self.top = 0
        self.marks = []

    def alloc(self, nelem, dtype, parts=128):
        esz = 4 if dtype == F32 else 2
        nb = (nelem * esz + 63) // 64 * 64
        assert self.top + nb <= self.n, f"SBUF arena overflow {self.top}+{nb}>{self.n}"
        o = self.top // 4
        self.top += nb
        v = self.t[0:parts, o:o + nb // 4]
        if dtype != F32:
            v = v.bitcast(dtype)
        return v[:, 0:nelem]

    def mark(self):
        self.marks.append(self.top)

    def release(self):
        self.top = self.marks.pop()


CA_IDENT, CA_CAUS, CA_U, CA_L, CA_SBM, CA_DEC, CA_CH, CA_CD, CA_N = (
    0, 128, 256, 384, 512, 2560, 2564, 2566, 2568)
CR_COS, CR_SIN, CR_N = 0, 2048, 4096
CB_CAUS, CB_ONES, CB_N = 0, 128, 256
LB_LNG, LB_LNB, LB_WT, LB_SB, LB_N = 0, 512, 1024, 1536, 2048
LB_L1G, LB_L1B, LB_L2G, LB_L2B, LV_N = 0, 8, 16, 24, 32


def _consts_A(hh):
    c = np.zeros((128, CA_N), np.float32)
    p = np.arange(128)
    c[:, CA_IDENT:CA_IDENT + 128] = np.eye(128)
    c[:, CA_CAUS:CA_CAUS + 128] = (p[:, None] <= p[None, :])
    c[:, CA_U:CA_U + 128] = (p[:, None] >= p[None, :])
    c[:, CA_L:CA_L + 128] = (p[:, None] < p[None, :])
    t = np.arange(512)
    for r in range(4):
        c[:, CA_SBM + r * 512:CA_SBM + (r + 1) * 512] = ((r * 128 + p)[:, None] < t[None, :])
    for h in range(2):
        hg = hh * 2 + h
        g = 1.0 - 2.0 ** (-5.0 - hg)
        lg = np.log(g)
        c[:, CA_DEC + h] = (128.0 ** -0.5) * np.exp(lg * (p + 1.0))
        c[:, CA_DEC + 2 + h] = np.exp(lg * (127.0 - p))
        c[:, CA_CH + h] = np.exp(-lg * 128.0)
        c[:, CA_CD + h] = np.exp(lg * 128.0)
    return c


def _consts_R(shift=0):
    c = np.zeros((128, CR_N), np.float32)
    p = np.arange(128)
    half = 64
    inv_freq = (10000.0 ** (-np.arange(half, dtype=np.float32) / half)).astype(np.float32)
    pos = np.abs((np.arange(32)[None, :] - shift) * 128 + p[:, None]).astype(np.float32)
    ang = (pos[:, :, None] * inv_freq[None, None, :]).astype(np.float32)
    c[:, CR_COS:CR_COS + 2048] = np.cos(ang).astype(np.float32).reshape(128, 2048)
    c[:, CR_SIN:CR_SIN + 2048] = np.sin(ang).astype(np.float32).reshape(128, 2048)
    return c


def _consts_B():
    c = np.zeros((128, CB_N), np.float32)
    p = np.arange(128)
    c[:, CB_CAUS:CB_CAUS + 128] = (p[:, None] <= p[None, :])
    c[:, CB_ONES:CB_ONES + 128] = 1.0 / D
    return c


def _blk_lhsT(w, cw=128):
    K, N = w.shape
    return np.ascontiguousarray(w.reshape(K // 128, 128, N // cw, cw).transpose(2, 1, 0, 3))


def _host_inputs(inp):
    x = np.asarray(inp["x"], np.float32)
    maps = []
    for core in range(8):
        b, hh = core // 2, core % 2
        m = {}
        m["xT"] = np.ascontiguousarray(x[b, hh * TH:(hh + 1) * TH, :].T)
        m["cA"] = _consts_A(hh)
        m["cB"] = _consts_B()
        m["cR"] = _consts_R()
        m["cRL"] = _consts_R(16 if hh == 0 else 0)
        for l in range(DEPTH):
            w_in = np.asarray(inp["w_in"][l], np.float32)
            hs = slice(hh * 256, (hh + 1) * 256)
            blk = lambda i: w_in[:, i * 512:(i + 1) * 512]
            rq, rk, rv, rg, sq, sk, sv = [blk(i)[:, hs] for i in range(7)]
            m[f"wAf{l}"] = _blk_lhsT(np.concatenate([rg, sq, sk], axis=1))
            m[f"wAt{l}"] = _blk_lhsT(np.concatenate([rq, rk, rv, sv], axis=1), cw=512)
            gu, gv = w_in[:, 3584:4096], w_in[:, 4096:4608]
            gates = w_in[:, 4608:7680]
            m[f"wGu{l}"] = _blk_lhsT(gu)
            m[f"wGv{l}"] = _blk_lhsT(gv, cw=512)
            m[f"wGt{l}"] = _blk_lhsT(gates)
            m[f"pR{l}"] = _blk_lhsT(np.asarray(inp["p_ret"][l], np.float32))
            m[f"pS{l}"] = _blk_lhsT(np.asarray(inp["p_sb"][l], np.float32))
            m[f"pG{l}"] = _blk_lhsT(np.asarray(inp["p_sgu"][l], np.float32))
            m[f"wO{l}"] = _blk_lhsT(np.asarray(inp["w_out"][l], np.float32))
            m[f"wU{l}"] = _blk_lhsT(np.asarray(inp["w_up"][l], np.float32))
            m[f"wD{l}"] = _blk_lhsT(np.asarray(inp["w_down"][l], np.float32))
            la = np.zeros((128, 4), np.float32)
            la[:, 0:2] = np.asarray(inp["ret_gn_g"][l], np.float32)[hs].reshape(2, 128).T
            la[:, 2:4] = np.asarray(inp["ret_gn_b"][l], np.float32)[hs].reshape(2, 128).T
            m[f"lA{l}"] = la
            lb = np.zeros((128, LB_N), np.float32)
            lb[:, LB_LNG:LB_LNG + 512] = np.asarray(inp["sgu_ln_g"][l], np.float32)[None, :]
            lb[:, LB_LNB:LB_LNB + 512] = np.asarray(inp["sgu_ln_b"][l], np.float32)[None, :]
            sw = np.asarray(inp["sgu_w"][l], np.float32)
            lb[:, LB_WT:LB_WT + 512] = sw.transpose(2, 0, 1).reshape(128, 512)
            lb[:, LB_SB:LB_SB + 512] = np.asarray(inp["sgu_b"][l], np.float32).reshape(1, 512)
            lv = np.zeros((128, LV_N), np.float32)
            for nm, off in (("ln1_g", LB_L1G), ("ln1_b", LB_L1B), ("ln2_g", LB_L2G), ("ln2_b", LB_L2B)):
                lv[:, off:off + 8] = np.asarray(inp[nm][l], np.float32).reshape(8, 128).T
            m[f"lB{l}"] = lb
            m[f"lV{l}"] = lv
        maps.append(m)
    return maps


IN_SHAPES = {"xT": [D, TH], "cA": [128, CA_N], "cB": [128, CB_N], "cR": [128, CR_N], "cRL": [128, CR_N]}
for _l in range(DEPTH):
    IN_SHAPES.update({
        f"wAf{_l}": [6, 128, 8, 128], f"wAt{_l}": [2, 128, 8, 512],
        f"wGu{_l}": [4, 128, 8, 128], f"wGv{_l}": [1, 128, 8, 512], f"wGt{_l}": [24, 128, 8, 128],
        f"pR{_l}": [8, 128, 4, 128], f"pS{_l}": [8, 128, 4, 128], f"pG{_l}": [8, 128, 4, 128],
        f"wO{_l}": [8, 128, 8, 128], f"wU{_l}": [32, 128, 8, 128], f"wD{_l}": [8, 128, 32, 128],
        f"lA{_l}": [128, 4], f"lB{_l}": [128, LB_N], f"lV{_l}": [128, LV_N]})


class Builder:
    def __init__(self, stages):
        self.stages = stages
        self.nc = bass.Bass("TRN2", target_bir_lowering=False)
        self.dram = {}
        self.ext_in = []
        self.ext_out = []

    def dt(self, name, shape, dtype, kind):
        if name not in self.dram:
            self.dram[name] = self.nc.dram_tensor(name, list(shape), dtype, kind=kind).ap()
            if kind == "ExternalInput":
                self.ext_in.append(name)
            elif kind == "ExternalOutput":
                self.ext_out.append(name)
        return self.dram[name]

    def win(self, name):
        return self.dt(name, IN_SHAPES[name], F32, "ExternalInput")

    def winl(self, base, l):
        return self.dt(base + (str(l) if FUSED else ""), IN_SHAPES[base + str(l)], F32, "ExternalInput")

    def build(self):
        nc = self.nc
        with contextlib.ExitStack() as es:
            at = es.enter_context(nc.sbuf_tensor("arena", [128, 53200], F32))
            self.ar = Arena(at, 53200 * 4)
            self.ps = [es.enter_context(nc.psum_tensor(f"ps{i}", [128, 512], F32)) for i in range(6)]
            self.psb = es.enter_context(nc.psum_tensor("psb", [128, 1024], BF16))
            self.psb2 = es.enter_context(nc.psum_tensor("psb2", [128, 1024], BF16))
            self.P = Prog(nc)
            if self.stages == ["FX"]:
                self.wire_fx()
            elif self.stages == ["FUSED"]:
                self.wire_fused()
            else:
                for s in self.stages:
                    self.wire_unfused(s)
                    self.P.barrier()
            self.P.emit()
        return nc

    def wire_unfused(self, s):
        EI, EO = "ExternalInput", "ExternalOutput"
        w = lambda base: self.dt(base, IN_SHAPES[base + "0"] if base + "0" in IN_SHAPES else IN_SHAPES[base], F32, EI)
        if s == "P0":
            self.stage_p0(dict(xT=w("xT"), xres_o=self.dt("xres_o", [D, TH], F32, EO), xb_o=self.dt("xb_o", [D, TH], BF16, EO)))
        elif s[0] == "A":
            self.stage_a(dict(xall=self.dt("xball", [2, D, TH], BF16, EI), rs=self.dt("rs", [512, S], BF16, EO),
                              cA=w("cA"), cR=w("cR"), lA=w("lA"), wAf=w("wAf"), wAt=w("wAt")))
        elif s[0] == "B":
            io = dict(xres_i=self.dt("xres_i", [D, TH], F32, EI), xb_i=self.dt("xb_i", [D, TH], BF16, EI),
                      rsall=self.dt("rsall", [2, 512, TH], BF16, EI),
                      xres_o=self.dt("xres_o", [D, TH], F32, EO), xb_o=self.dt("xb_o", [D, TH], BF16, EO),
                      wU16=self.dt("wU16", [32, 128, 1024], BF16, "Internal"), wD16=self.dt("wD16", [8, 128, 4096], BF16, "Internal"),
                      make_cache=True)
            for nm in ("cB", "lB", "lV", "wGu", "wGv", "wGt", "pR", "pS", "pG", "wO", "wU", "wD"):
                io[nm] = w(nm)
            self.stage_b(io)

    def wire_fx(self):
        EI = "ExternalInput"
        P = self.P
        w = lambda base: self.dt(base, IN_SHAPES[base], F32, EI)
        wl = lambda base, l: self.dt(f"{base}{l}", IN_SHAPES[base + "0"], F32, EI)
        I32 = mybir.dt.int32
        nonce = self.dt("nonce", [1, 128], I32, EI)
        sh = lambda nm, shape, dtp: self.dram.setdefault(nm, self.nc.dram_tensor(nm, shape, dtp, kind="Internal", addr_space="Shared").ap())
        XB = [sh("EX0", [2, D, TH], BF16)] * DEPTH
        RS = [sh("EX1", [2, 512, S], BF16)] * DEPTH
        FL = sh("FL", [2, 16], I32)
        xres = [self.dt(f"xres_p{l}", [D, TH], F32, "Internal") for l in range(DEPTH)]
        xbp = [self.dt(f"xb_p{l}", [D, TH], BF16, "Internal") for l in range(DEPTH)]
        rsp = [self.dt(f"rs_p{l}", [512, S], BF16, "Internal") for l in range(DEPTH)]
        rsall = [self.dt(f"rsall_p{l}", [2, 512, TH], BF16, "Internal") for l in range(DEPTH)]
        outT = self.dt("outT", [D, TH], F32, "ExternalOutput")
        self.ar.mark()
        ntile = self.ar.alloc(128, F32, parts=1).bitcast(I32)
        P.dma("sp", ntile, nonce, "nonce", writes=["ntile"])
        phase = [0]

        def publish(dst_fn, src):
            phase[0] += 1
            k = phase[0]
            P.custom_dma("sp", (lambda e: e.dma_start(out=dst_fn(self.parity(e)), in_=src)), "xch", writes=[("xch", k)])

            def fn(e, k=k):
                par = self.parity(e)
                e.dma_start(out=FL[bass.ds(par, 1)], in_=ntile[0:1, k * 16:(k + 1) * 16]).then_inc(self.P.flag_sem, 16)
                e.wait_ge(self.P.flag_sem, 16 * k)
            P.xop("sp", fn, reads=[("xch", k), "ntile"])
            return k

        def wait_partner(k):
            def fn(e, k=k):
                par = self.parity(e)
                if getattr(self, "_nbase", None) is None:
                    self._nbase = e.alloc_register("nonce_base")
                    e.reg_load(self._nbase, nonce[0:1, 0:1])
                with e.register(f"want{k}") as want, e.register(f"got{k}") as got, e.register(f"r{k}") as r:
                    e.reg_add(want, self._nbase, k)
                    e.reg_mov(r, 1)
                    with e.While(r):
                        e.reg_load(got, FL[bass.ds(1 - par, 1), 0:1])
                        e.reg_sub(r, got, want)
                        e.reg_alu(r, r, -4, ALU.bitwise_and)
            P.xop("sp", fn, reads=["ntile"])

        def publish_and_wait(dst_fn, src, key):
            wait_partner(publish(dst_fn, src))

        self.stage_p0(dict(xT=w("xT"), xres_o=None, xb_o=xbp[0]))
        xres[0] = w("xT")
        P.barrier()
        kx = publish(lambda par: XB[0][bass.ds(par, 1)].rearrange("o d t -> (o d) t"), xbp[0])
        for l in range(DEPTH):
            wU16 = self.dt(f"wU16_{l}", [32, 128, 1024], BF16, "Internal")
            wD16 = self.dt(f"wD16_{l}", [8, 128, 4096], BF16, "Internal")
            cio = dict(wU=wl("wU", l), wD=wl("wD", l), wU16=wU16, wD16=wD16)
            self.stage_a(dict(xall=XB[l], rs=rsp[l], cA=w("cA"), cR=w("cR"), lA=wl("lA", l), wAf=wl("wAf", l), wAt=wl("wAt", l), cache_io=cio,
                              pre_x=(lambda kx=kx: wait_partner(kx))))
            P.barrier()
            kk = publish(lambda par, l=l: RS[l][bass.ds(par, 1)].rearrange("o r t -> (o r) t"), rsp[l])

            def pre_rs(l=l, kk=kk):
                wait_partner(kk)
                P.custom_dma("sp", (lambda e: e.dma_start(out=rsall[l], in_=RS[l].rearrange("h r (two t) -> h r two t", two=2)[:, :, bass.ds(self.parity(e), 1), :]
                                                          .rearrange("h r o t -> h r (o t)"))), "xch2", writes=["rsall_d"])
            lastl = (l == DEPTH - 1)
            io = dict(xres_i=xres[l], xb_i=xbp[l], rsall=rsall[l], wU16=wU16, wD16=wD16, make_cache=False, pre_rs=pre_rs,
                      xres_o=outT if lastl else xres[l + 1], xb_o=None if lastl else xbp[l + 1], cB=w("cB"))
            for nm in ("lB", "lV", "wGu", "wGv", "wGt", "pR", "pS", "pG", "wO", "wU", "wD"):
                io[nm] = wl(nm, l)
            self.stage_b(io)
            P.barrier()
            if not lastl:
                kx = publish(lambda par, l=l: XB[l + 1][bass.ds(par, 1)].rearrange("o d t -> (o d) t"), xbp[l + 1])
        self.ar.release()

    def wire_fused(self):
        EI = "ExternalInput"
        xT = self.dt("xT2", [2, D, TH], F32, EI)
        xres = [self.dt(f"xres_s{l}", [2, D, TH], F32, "Internal") for l in range(DEPTH)]
        xb = [self.dt(f"xb_s{l}", [3 if l == DEPTH - 1 else 2, D, TH], BF16, "Internal") for l in range(DEPTH)]
        rs = [self.dt(f"rs_s{l}", [2, 512, TH if l == DEPTH - 1 else S], BF16, "Internal") for l in range(DEPTH)]
        self.ar.mark()
        zt = self.ar.alloc(TH, BF16)
        self.P.op("pool", lambda e: e.memset(zt, 0.0), writes=["zt"])
        for dc in range(8):
            self.P.dma("sp", xb[DEPTH - 1][0, dc * 128:(dc + 1) * 128, :], zt, "zst", reads=["zt"])
        self.ar.release()
        self.P.barrier()
        outT = self.dt("outT", [D, TH], F32, "ExternalOutput")
        wl = lambda base, l, sfx="": self.dt(f"{base}{l}{sfx}", IN_SHAPES[base + "0"], F32, EI)
        for th in range(2):
            self.stage_p0(dict(xT=xT[th], xres_o=xres[0][th], xb_o=xb[0][th]))
            self.P.barrier()
        for l in range(DEPTH):
            wU16 = self.dt(f"wU16_{l}", [32, 128, 1024], BF16, "Internal")
            wD16 = self.dt(f"wD16_{l}", [8, 128, 4096], BF16, "Internal")
            lastl = (l == DEPTH - 1)
            xin = xb[l]
            if lastl:
                xin = self.dt("xb_shift", [2, D, TH], BF16, "Internal")
                for r in range(2):
                    self.P.custom_dma("sp", (lambda e, r=r: e.dma_start(out=xin[r], in_=xb[l][bass.ds(self.parity(e) + r, 1)].rearrange("o d t -> (o d) t"))),
                                      "xsh")
                self.P.barrier()
            for hh in range(2):
                cio = dict(wU=wl("wU", l), wD=wl("wD", l), wU16=wU16, wD16=wD16) if hh == 0 else None
                self.stage_a(dict(xall=xin, rs=rs[l][hh], cA=self.dt(f"cA_{hh}", IN_SHAPES["cA"], F32, EI),
                                  cR=self.win("cRL" if lastl else "cR"), last=lastl,
                                  lA=wl("lA", l, f"_{hh}"), wAf=wl("wAf", l, f"_{hh}"), wAt=wl("wAt", l, f"_{hh}"), cache_io=cio))
                self.P.barrier()
            for th in range(1 if lastl else 2):
                if lastl:
                    io = dict(xres_i=xres[l], xb_i=xin[1], rsall=rs[l], dyn=True, wU16=wU16, wD16=wD16, make_cache=False)
                    io["xres_o"], io["xb_o"] = outT, None
                else:
                    io = dict(xres_i=xres[l][th], xb_i=xb[l][th], rsall=rs[l][:, :, th * TH:(th + 1) * TH],
                              wU16=wU16, wD16=wD16, make_cache=False)
                    io["xres_o"], io["xb_o"] = xres[l + 1][th], xb[l + 1][(1 + th) if l + 1 == DEPTH - 1 else th]
                io["cB"] = self.win("cB")
                for nm in ("lB", "lV", "wGu", "wGv", "wGt", "pR", "pS", "pG", "wO", "wU", "wD"):
                    io[nm] = wl(nm, l)
                self.stage_b(io)
                self.P.barrier()

    def parity(self, e):
        if getattr(self, "_par", None) is None:
            self._par = e.snap(e.partition_id() % 2, min_val=0, max_val=1)
        return self._par

    def load_cast(self, dst16, src, n, tag, stg, nbuf=2):
        P = self.P
        CH = stg[0].shape[1]
        cnt = getattr(self, "_lc_cnt", 0)
        for o in range(0, n, CH):
            w = min(CH, n - o)
            bi = cnt % nbuf
            cnt += 1
            sb = stg[bi]
            P.dma("sp", sb[:, 0:w], src[:, o:o + w], f"stg{bi}", writes=[("stg", bi)])
            P.op("dve", (lambda e, a=dst16[:, o:o + w], b=sb[:, 0:w]: e.tensor_copy(out=a, in_=b)),
                 reads=[("stg", bi)], writes=[tag])
        self._lc_cnt = cnt

    def cache_chunks(self, cio, stg, c16):
        P = self.P
        k = 0
        for src, dst, nblk, per in ((cio["wU"], cio["wU16"], 32, 1024), (cio["wD"], cio["wD16"], 8, 4096)):
            for blk in range(nblk):
                sflat = src[blk].rearrange("p a b -> p (a b)")
                for o in range(0, per, 1024):
                    def emit(bi=k % len(stg), sflat=sflat, dst=dst, blk=blk, o=o):
                        P.dma("sp", stg[bi], sflat[:, o:o + 1024], f"cstg{bi}", writes=[("cstg", bi)])
                        P.op("dve", (lambda e, a=c16[bi], b=stg[bi]: e.tensor_copy(out=a, in_=b)),
                             reads=[("cstg", bi)], writes=[("cc16", bi)])
                        P.dma("sp", dst[blk][:, o:o + 1024], c16[bi], f"cwc{bi}", reads=[("cc16", bi)])
                    yield emit
                    k += 1

    def stage_p0(self, io):
        P, ar = self.P, self.ar
        xT, xres_d, xb_d = io["xT"], io["xres_o"], io["xb_o"]
        ar.mark()
        x32 = ar.alloc(8 * TH, F32)
        x16 = ar.alloc(8 * TH, BF16)
        for dc in range(8):
            sl = slice(dc * TH, (dc + 1) * TH)
            P.dma("sp", x32[:, sl], xT[dc * 128:(dc + 1) * 128, :], "p0l", writes=[("x32", dc)])
            P.op("dve", (lambda e, a=x16[:, sl], b=x32[:, sl]: e.tensor_copy(out=a, in_=b)),
                 reads=[("x32", dc)], writes=[("x16", dc)])
            if xres_d is not None:
                P.dma("sp", xres_d[dc * 128:(dc + 1) * 128, :], x32[:, sl], "p0s", reads=[("x32", dc)])
            P.dma("sp", xb_d[dc * 128:(dc + 1) * 128, :], x16[:, sl], "p0s", reads=[("x16", dc)])
        ar.release()

    def stage_a(self, io):
        P, ar, ps, psb, psb2 = self.P, self.ar, self.ps, self.psb, self.psb2
        xall, rs_d = io["xall"], io["rs"]
        lastm = io.get("last", False)
        cA_d, lA_d = io["cA"], io["lA"]
        wAf_d, wAt_d = io["wAf"], io["wAt"]
        ar.mark()
        cA = ar.alloc(CA_N, F32)
        lA = ar.alloc(4, F32)
        c16 = ar.alloc(384, BF16)
        P.dma("sp", cA, cA_d, "cA", writes=["cA"])
        P.dma("sp", lA, lA_d, "lA", writes=["lA"])
        P.op("dve", lambda e: e.tensor_copy(out=c16[:, 0:128], in_=cA[:, CA_IDENT:CA_IDENT + 128]), reads=["cA"], writes=["c16a"])
        P.op("dve", lambda e: e.tensor_copy(out=c16[:, 128:384], in_=cA[:, CA_U:CA_U + 256]), reads=["cA"], writes=["c16b"])
        ident, U16, L16 = c16[:, 0:128], c16[:, 128:256], c16[:, 256:384]
        caus = cA[:, CA_CAUS:CA_CAUS + 128]
        rgT = ar.alloc(2 * S, BF16)
        sqT = ar.alloc(2 * S, BF16)
        skT = ar.alloc(2 * S, BF16)
        qdT = ar.alloc(2 * S, BF16)
        kdT = ar.alloc(2 * S, BF16)
        kdk = ar.alloc(32 * 256, BF16)
        vtk = ar.alloc(32 * 512, BF16)
        ar.mark()
        wf = ar.alloc(6 * 1024, BF16)
        wt = ar.alloc(2 * 4096, BF16)
        stg = [ar.alloc(1024, F32) for _ in range(2)]
        cR_d = io["cR"]
        crt = [ar.alloc(512, F32) for _ in range(2)]
        xt = [ar.alloc(8 * 512, BF16) for _ in range(2)]
        qk32 = [ar.alloc(512, F32)] * 2
        qk16 = [ar.alloc(512, BF16) for _ in range(2)]
        tmpr = [ar.alloc(512, F32)] * 2
        for cb in range(6):
            self.load_cast(wf[:, cb * 1024:(cb + 1) * 1024], wAf_d[cb].rearrange("p a b -> p (a b)"), 1024, ("wf", cb), stg)
        for g in range(2):
            self.load_cast(wt[:, g * 4096:(g + 1) * 4096], wAt_d[g].rearrange("p a b -> p (a b)"), 4096, ("wt", g), stg)

        if io.get("pre_x") is not None:
            io["pre_x"]()
        for T in range(8):
            xb_ = xt[T % 2]
            r, t0 = T // 4, (T % 4) * 512
            P.dma("sp", xb_.rearrange("p (dc t) -> p dc t", dc=8),
                  xall[r].rearrange("(dc p) t -> p dc t", p=128)[:, :, t0:t0 + 512],
                  f"xt{T % 2}", writes=[("xt", T % 2)])
            P.dma("sp", crt[T % 2][:, 0:256], cR_d[:, CR_COS + T * 256: CR_COS + (T + 1) * 256], f"cr{T % 2}", writes=[("crt", T % 2)])
            P.dma("sp", crt[T % 2][:, 256:512], cR_d[:, CR_SIN + T * 256: CR_SIN + (T + 1) * 256], f"cr{T % 2}", writes=[("crt", T % 2)])
            for cb in (range(6) if "fm" in SUB else []):
                bank = ps[cb % 2]
                for dc in range(8):
                    P.op("pe", (lambda e, o=bank[:, :], a=wf[:, cb * 1024 + dc * 128: cb * 1024 + (dc + 1) * 128],
                                b=xb_[:, dc * 512:(dc + 1) * 512], st=(dc == 0), sp=(dc == 7):
                                e.matmul(o, lhsT=a, rhs=b, start=st, stop=sp)),
                         reads=[("wf", cb), ("xt", T % 2)], writes=[("ps", cb % 2)])
                if cb < 2:
                    dst = rgT[:, cb * S + T * 512: cb * S + (T + 1) * 512]
                    P.op("act", (lambda e, o=dst, i=bank[:, :]: e.activation(out=o, in_=i, func=AF.Silu)),
                         reads=[("ps", cb % 2)], writes=[("rgT", cb, T)])
                elif cb < 4:
                    dst = sqT[:, (cb - 2) * S + T * 512: (cb - 2) * S + (T + 1) * 512]
                    P.op("act", (lambda e, o=dst, i=bank[:, :]: e.activation(out=o, in_=i, func=AF.Copy, scale=0.125)),
                         reads=[("ps", cb % 2)], writes=[("sqT", cb - 2, T)])
                else:
                    dst = skT[:, (cb - 4) * S + T * 512: (cb - 4) * S + (T + 1) * 512]
                    P.op("dve", (lambda e, o=dst, i=bank[:, :]: e.tensor_copy(out=o, in_=i)),
                         reads=[("ps", cb % 2)], writes=[("skT", cb - 4, T)])
            for q in (range(4) if "tm" in SUB else []):
                n = T * 4 + q
                pq, pv = ps[2 + (n % 2)], ps[4 + (n % 2)]
                for g, bank in ((0, pq), (1, pv)):
                    for dc in range(8):
                        P.op("pe", (lambda e, o=bank[:, :], a=xb_[:, dc * 512 + q * 128: dc * 512 + (q + 1) * 128],
                                    b=wt[:, g * 4096 + dc * 512: g * 4096 + (dc + 1) * 512], st=(dc == 0), sp=(dc == 7):
                                    e.matmul(o, lhsT=a, rhs=b, start=st, stop=sp)),
                             reads=[("wt", g), ("xt", T % 2)], writes=[("ps", 2 + 2 * g + (n % 2))])
                P.op("act", (lambda e, o=vtk[:, n * 512:(n + 1) * 512], i=pv[:, :]: e.copy(out=o, in_=i)),
                     reads=[("ps", 4 + (n % 2))], writes=[("vtk", n)])
                if "rot" not in SUB:
                    continue
                A32, T32, O16 = qk32[n % 2], tmpr[n % 2], qk16[n % 2]
                X = pq[:, :].rearrange("p (g two f) -> p g two f", g=4, two=2)
                A4 = A32.rearrange("p (g two f) -> p g two f", g=4, two=2)
                T4 = T32.rearrange("p (g two f) -> p g two f", g=4, two=2)
                cosb = crt[T % 2][:, q * 64:(q + 1) * 64].unsqueeze(1).to_broadcast([128, 4, 64])
                sinb = crt[T % 2][:, 256 + q * 64: 256 + (q + 1) * 64].unsqueeze(1).to_broadcast([128, 4, 64])
                rk_ = [("ps", 2 + (n % 2)), ("crt", T % 2)]
                P.op("dve", (lambda e, o=A4[:, :, 0, :], a=X[:, :, 0, :], b=cosb: e.tensor_tensor(out=o, in0=a, in1=b, op=ALU.mult)),
                     reads=rk_, writes=[("A32a", 0)])
                P.op("dve", (lambda e, o=A4[:, :, 1, :], a=X[:, :, 1, :], b=cosb: e.tensor_tensor(out=o, in0=a, in1=b, op=ALU.mult)),
                     reads=rk_, writes=[("A32b", 0)])
                P.op("dve", (lambda e, o=T4[:, :, 0, :], a=X[:, :, 1, :], b=sinb: e.tensor_tensor(out=o, in0=a, in1=b, op=ALU.mult)),
                     reads=rk_, writes=[("T32a", 0)])
                P.op("dve", (lambda e, o=T4[:, :, 1, :], a=X[:, :, 0, :], b=sinb: e.tensor_tensor(out=o, in0=a, in1=b, op=ALU.mult)),
                     reads=rk_, writes=[("T32b", 0)])
                P.op("pool", (lambda e, o=A4[:, :, 0, :], a=A4[:, :, 0, :], b=T4[:, :, 0, :]: e.tensor_tensor(out=o, in0=a, in1=b, op=ALU.subtract)),
                     reads=[("A32a", 0), ("T32a", 0)], writes=[("A32a", 0)])
                P.op("pool", (lambda e, o=A4[:, :, 1, :], a=A4[:, :, 1, :], b=T4[:, :, 1, :]: e.tensor_tensor(out=o, in0=a, in1=b, op=ALU.add)),
                     reads=[("A32b", 0), ("T32b", 0)], writes=[("A32b", 0)])
                decb = cA[:, CA_DEC:CA_DEC + 4].unsqueeze(2).to_broadcast([128, 4, 128])
                P.op("pool", (lambda e, o=O16.rearrange("p (g f) -> p g f", g=4), a=A32.rearrange("p (g f) -> p g f", g=4), b=decb:
                              e.tensor_tensor(out=o, in0=a, in1=b, op=ALU.mult)),
                     reads=[("A32a", 0), ("A32b", 0), "cA"], writes=[("qk16", n % 2)])
                P.op("pool", (lambda e, o=kdk[:, n * 256:(n + 1) * 256], i=O16[:, 256:512]: e.tensor_copy(out=o, in_=i)),
                     reads=[("qk16", n % 2)], writes=[("kdk", n)])
                if "tr" not in SUB:
                    continue
                for g in range(4):
                    pT = psb if g < 2 else psb2
                    P.op("pe", (lambda e, o=pT[:, (g % 2) * 128:(g % 2 + 1) * 128], i=O16[:, g * 128:(g + 1) * 128]:
                                e.transpose(out=o, in_=i, identity=ident)),
                         reads=[("qk16", n % 2), "c16a"], writes=["psb" if g < 2 else "psb2"])
                for h in range(2):
                    P.op("act", (lambda e, o=qdT[:, h * S + n * 128: h * S + (n + 1) * 128], i=psb[:, h * 128:(h + 1) * 128]: e.copy(out=o, in_=i)),
                         reads=["psb"], writes=[("qdT", h, n)])
                    P.op("dve", (lambda e, o=kdT[:, h * S + n * 128: h * S + (n + 1) * 128], i=psb2[:, h * 128:(h + 1) * 128]: e.tensor_copy(out=o, in_=i)),
                         reads=["psb2"], writes=[("kdT", h, n)])
        ar.release()
        P.barrier()

        ar.mark()
        if "ret" not in PARTS:
            ar.release(); ar.release(); return
        rso = ar.alloc(2 * S, BF16)
        st32 = ar.alloc(256, F32)
        st16 = ar.alloc(256, BF16)
        std = [ar.alloc(256, BF16) for _ in range(2)]
        nrm = [ar.alloc(256, BF16) for _ in range(2)]
        stt = [ar.alloc(32, F32) for _ in range(2)]
        tmpg = [ar.alloc(256, F32) for _ in range(2)]
        for n in range(32):
            pb = n % 2
            pS, pO, pK = ps[0 + pb], ps[2 + pb], ps[4 + pb]
            H = [(h, slice(h * S + n * 128, h * S + (n + 1) * 128), slice(h * 128, (h + 1) * 128)) for h in range(2)]
            qry = not (lastm and n < 16)
            for h, csl, hs in (H if qry else []):
                P.op("pe", (lambda e, o=pS[:, hs], a=kdT[:, csl], b=qdT[:, csl]: e.matmul(o, lhsT=a, rhs=b, start=True, stop=True)),
                     reads=[("kdT", h, n), ("qdT", h, n)], writes=[("pS", pb)])
            for h, csl, hs in (H if qry else []):
                P.op("dve", (lambda e, o=std[pb][:, hs], a=pS[:, hs], s_=cA[:, CA_CH + h:CA_CH + h + 1], m=caus:
                             e.scalar_tensor_tensor(out=o, in0=a, scalar=s_, in1=m, op0=ALU.mult, op1=ALU.mult)),
                     reads=[("pS", pb), "cA"], writes=[("std", pb, h)])
            for h, csl, hs in (H if qry else []):
                vsl = vtk[:, n * 512 + h * 128: n * 512 + (h + 1) * 128]
                P.op("pe", (lambda e, o=pO[:, hs], a=std[pb][:, hs], b=vsl, sp=(n == 0): e.matmul(o, lhsT=a, rhs=b, start=True, stop=sp)),
                     reads=[("std", pb, h), ("vtk", n)], writes=[("pO", pb)])
                if n > 0:
                    P.op("pe", (lambda e, o=pO[:, hs], a=qdT[:, csl], b=st16[:, hs]: e.matmul(o, lhsT=a, rhs=b, start=False, stop=True)),
                         reads=[("qdT", h, n), ("st16", h)], writes=[("pO", pb)])
            for h, csl, hs in H:
                vsl = vtk[:, n * 512 + h * 128: n * 512 + (h + 1) * 128]
                P.op("pe", (lambda e, o=pK[:, hs], a=kdk[:, n * 256 + h * 128: n * 256 + (h + 1) * 128], b=vsl: e.matmul(o, lhsT=a, rhs=b, start=True, stop=True)),
                     reads=[("kdk", n), ("vtk", n)], writes=[("pK", pb)])
            for h, csl, hs in H:
                if n == 0:
                    P.op("dve", (lambda e, o=st32[:, hs], i=pK[:, hs]: e.tensor_copy(out=o, in_=i)),
                         reads=[("pK", pb)], writes=[("st32", h)])
                else:
                    P.op("dve", (lambda e, o=st32[:, hs], a=st32[:, hs], s_=cA[:, CA_CD + h:CA_CD + h + 1], b=pK[:, hs]:
                                 e.scalar_tensor_tensor(out=o, in0=a, scalar=s_, in1=b, op0=ALU.mult, op1=ALU.add)),
                         reads=[("pK", pb), ("st32", h), "cA"], writes=[("st32", h)])
                P.op("pool", (lambda e, o=st16[:, hs], i=st32[:, hs]: e.tensor_copy(out=o, in_=i)),
                     reads=[("st32", h)], writes=[("st16", h)])
            if not qry:
                continue
            sv = stt[pb]
            for h, csl, hs in H:
                b0 = h * 16
                P.op("dve", (lambda e, o=sv[:, b0:b0 + 6], i=pO[:, hs]: e.bn_stats(out=o, in_=i)),
                     reads=[("pO", pb)], writes=[("stt", pb, h)])
                P.op("dve", (lambda e, o=sv[:, b0 + 8:b0 + 10], i=sv[:, b0:b0 + 6]: e.bn_aggr(out=o, in_=i)),
                     reads=[("stt", pb, h)], writes=[("stt", pb, h)])
            for h, csl, hs in H:
                b0 = h * 16
                P.op("act", (lambda e, o=sv[:, b0 + 11:b0 + 12], i=sv[:, b0 + 9:b0 + 10]: e.activation(out=o, in_=i, func=AF.Ln, bias=EPS)),
                     reads=[("stt", pb, h)], writes=[("stt", pb, h)])
            for h, csl, hs in H:
                b0 = h * 16
                P.op("act", (lambda e, o=sv[:, b0 + 10:b0 + 11], i=sv[:, b0 + 11:b0 + 12]: e.activation(out=o, in_=i, func=AF.Exp, scale=-0.5)),
                     reads=[("stt", pb, h)], writes=[("stt", pb, h)])
            for h, csl, hs in H:
                b0 = h * 16
                P.op("dve", (lambda e, o=nrm[pb][:, hs], a=pO[:, hs], m=sv[:, b0 + 8:b0 + 9], r=sv[:, b0 + 10:b0 + 11]:
                             e.tensor_scalar(out=o, in0=a, scalar1=m, scalar2=r, op0=ALU.subtract, op1=ALU.mult)),
                     reads=[("pO", pb), ("stt", pb, h)], writes=[("nrm", pb, h)])
            pT = psb if pb == 0 else psb2
            for h, csl, hs in H:
                P.op("pe", (lambda e, o=pT[:, hs], i=nrm[pb][:, hs]: e.transpose(out=o, in_=i, identity=ident)),
                     reads=[("nrm", pb, h), "c16a"], writes=[("psbr", pb)])
            for h, csl, hs in H:
                P.op("dve", (lambda e, o=tmpg[pb][:, hs], a=pT[:, hs], g=lA[:, h:h + 1], b=lA[:, 2 + h:3 + h]:
                             e.tensor_scalar(out=o, in0=a, scalar1=g, scalar2=b, op0=ALU.mult, op1=ALU.add)),
                     reads=[("psbr", pb), "lA"], writes=[("tmpg", pb, h)])
                P.op("pool", (lambda e, o=rso[:, csl], a=tmpg[pb][:, hs], b=rgT[:, csl]: e.tensor_tensor(out=o, in0=a, in1=b, op=ALU.mult)),
                     reads=[("tmpg", pb, h), ("rgT", h, n // 4)], writes=[("rso", h)])
        for h in range(2):
            P.dma("sp", rs_d[h * 128:(h + 1) * 128, :], rso[:, h * S + (TH if lastm else 0):(h + 1) * S], "rsst", reads=[("rso", h)])
        P.barrier()
        ar.release()

        ar.mark()
        if "sb" not in PARTS:
            ar.release(); ar.release(); return
        sbo = ar.alloc(4 * S, BF16, parts=64)
        e32 = [ar.alloc(512, F32) for _ in range(4)]
        sp16 = [ar.alloc(512, BF16) for _ in range(4)]
        w32 = [ar.alloc(512, F32) for _ in range(2)]
        a16 = [ar.alloc(512, BF16) for _ in range(4)]
        sbm = cA[:, CA_SBM:CA_SBM + 2048]
        cgen = None
        if io.get("cache_io") is not None:
            cstg = [ar.alloc(1024, F32) for _ in range(2)]
            cc16 = [ar.alloc(1024, BF16) for _ in range(2)]
            cgen = self.cache_chunks(io["cache_io"], cstg, cc16)
        step_i = 0

        def emit_z(s, hd, T, kb):
            base, pr = (hd % 2) * 64, hd // 2
            c0 = max(0, kb - 4 * T) * 128
            P.op("pe", (lambda e, o=ps[s][:, c0:], a=skT[base:base + 64, pr * S + kb * 128: pr * S + (kb + 1) * 128],
                        b=sqT[base:base + 64, pr * S + T * 512 + c0: pr * S + (T + 1) * 512]: e.matmul(o, lhsT=a, rhs=b, start=True, stop=True)),
                 reads=[("skT", pr, kb // 4), ("sqT", pr, T)], writes=[("pz", s)])

        for pr in range(2):
            for T in (range(4, 8) if lastm else range(8)):
                kbs = list(range(4 * T + 3, -1, -1))
                for s in range(2):
                    emit_z(s, pr * 2 + s, T, kbs[0])
                for ki, kb in enumerate(kbs):
                    first, last = (ki == 0), (ki == len(kbs) - 1)
                    c0 = max(0, kb - 4 * T) * 128
                    step_i += 1
                    pj = step_i % 2
                    if cgen is not None and step_i % 3 == 0:
                        em = next(cgen, None)
                        if em is not None:
                            em()
                    for s in range(2):
                        P.op("act", (lambda e, o=e32[s + 2 * pj][:, c0:], i=ps[s][:, c0:]: e.activation(out=o, in_=i, func=AF.Exp)),
                             reads=[("pz", s)], writes=[("e32", s, pj)])
                    if kb >= 4 * T:
                        r = kb - 4 * T
                        for s in range(2):
                            P.op("dve", (lambda e, o=e32[s + 2 * pj][:, c0:], a=e32[s + 2 * pj][:, c0:], m=sbm[:, r * 512 + c0:(r + 1) * 512]: e.tensor_tensor(out=o, in0=a, in1=m, op=ALU.mult)),
                                 reads=[("e32", s, pj), "cA"], writes=[("e32", s, pj)])
                    for s in range(2):
                        P.op("act", (lambda e, o=sp16[s + 2 * pj][:, c0:], i=e32[s + 2 * pj][:, c0:]: e.activation(out=o, in_=i, func=AF.Ln, bias=1.0)),
                             reads=[("e32", s, pj)], writes=[("sp16", s, pj)])
                    for s in range(2):
                        P.op("pe", (lambda e, o=ps[2 + s][:, c0:], b=sp16[s + 2 * pj][:, c0:], st=first: e.matmul(o, lhsT=U16, rhs=b, start=st, stop=False, skip_group_check=True)),
                             reads=[("sp16", s, pj), "c16b"], writes=[("pR", s)])
                    if not last:
                        for s in range(2):
                            emit_z(s, pr * 2 + s, T, kbs[ki + 1])
                    for s in range(2):
                        P.op("act", (lambda e, o=w32[s][:, c0:], i=ps[2 + s][:, c0:]: e.activation(out=o, in_=i, func=AF.Exp, scale=-1.0)),
                             reads=[("pR", s)], writes=[("w32", s)])
                    for s in range(2):
                        P.op("pe", (lambda e, o=ps[2 + s][:, c0:], b=sp16[s + 2 * pj][:, c0:], sp_=last: e.matmul(o, lhsT=L16, rhs=b, start=False, stop=True if sp_ else False, skip_group_check=True)),
                             reads=[("sp16", s, pj), "c16b"], writes=[("pR", s)])
                    for s in range(2):
                        P.op("dve", (lambda e, o=a16[s + 2 * pj][:, c0:], a=e32[s + 2 * pj][:, c0:], b=w32[s][:, c0:]: e.tensor_tensor(out=o, in0=a, in1=b, op=ALU.mult)),
                             reads=[("e32", s, pj), ("w32", s)], writes=[("a16", s, pj)])
                    for s in range(2):
                        hd = pr * 2 + s
                        P.op("pe", (lambda e, o=ps[4 + s][0:64, c0:], a=vtk[:, kb * 512 + 256 + hd * 64: kb * 512 + 256 + (hd + 1) * 64], b=a16[s + 2 * pj][:, c0:], st=first, sp_=last:
                                    e.matmul(o, lhsT=a, rhs=b, start=st, stop=sp_, skip_group_check=True)),
                             reads=[("a16", s, pj), ("vtk", kb)], writes=[("po", s)])
                for s in range(2):
                    hd = pr * 2 + s
                    P.op("act" if s else "dve",
                         (lambda e, o=sbo[:, hd * S + T * 512: hd * S + (T + 1) * 512], i=ps[4 + s][0:64, :], s_=s:
                          (e.copy(out=o, in_=i) if s_ else e.tensor_copy(out=o, in_=i))),
                         reads=[("po", s)], writes=[("sbo", hd)])
        if cgen is not None:
            for em in cgen:
                em()
        for hd in range(4):
            P.dma("sp", rs_d[256 + hd * 64: 256 + (hd + 1) * 64, :], sbo[:, hd * S + (TH if lastm else 0):(hd + 1) * S], "rsst", reads=[("sbo", hd)])
        P.barrier()
        ar.release()
        ar.release()

    def stage_b(self, io):
        P, ar, ps, psb = self.P, self.ar, self.ps, self.psb
        xres_d, xb_d, rsa_d = io["xres_i"], io["xb_i"], io["rsall"]
        xres_o, xb_o = io["xres_o"], io["xb_o"]
        cB_d, lB_d = io["cB"], io["lB"]
        wGu_d, wGv_d, wGt_d = io["wGu"], io["wGv"], io["wGt"]
        pR_d, pS_d, pG_d = io["pR"], io["pS"], io["pG"]
        wO_d, wU_d, wD_d = io["wO"], io["wU"], io["wD"]
        wU16, wD16 = io["wU16"], io["wD16"]

        ar.mark()
        cB = ar.alloc(CB_N, F32)
        lV = ar.alloc(LV_N, F32)
        P.dma("sp", cB, cB_d, "cB", writes=["cB"])
        P.dma("sp", lV, io["lV"], "lV", writes=["lV"])
        onesF = cB[:, CB_ONES:CB_ONES + 128]
        xres = ar.alloc(8 * TH, F32)
        xb = ar.alloc(8 * TH, BF16)
        stg = [ar.alloc(1024, F32) for _ in range(2)]
        XR8 = [("xres", dc) for dc in range(8)]
        for dc in range(8):
            P.dma("sp", xb[:, dc * TH:(dc + 1) * TH], xb_d[dc * 128:(dc + 1) * 128, :], "bld", writes=[("xb", dc)])

        def load_xres():
            if io.get("dyn"):
                P.custom_dma("sp", (lambda e, o=xres.rearrange("p (dc t) -> p dc t", dc=8):
                                    e.dma_start(out=o, in_=xres_d[bass.ds(self.parity(e), 1)].rearrange("o (dc p) t -> p (o dc) t", p=128))),
                             "bld", writes=XR8)
            else:
                for dc in range(8):
                    P.dma("sp", xres[:, dc * TH:(dc + 1) * TH], xres_d[dc * 128:(dc + 1) * 128, :], "bld", writes=[("xres", dc)])
        XR = [("xres", dc) for dc in range(8)]
        XB = [("xb", dc) for dc in range(8)]

        ar.mark()
        c16 = [ar.alloc(1024, BF16) for _ in range(2)]
        k = 0
        for src, dst, nblk, per, wk in (((wU_d, wU16, 32, 1024, "wcU"), (wD_d, wD16, 8, 4096, "wcD")) if io["make_cache"] else ()):
            for blk in range(nblk):
                sflat = src[blk].rearrange("p a b -> p (a b)")
                for o in range(0, per, 1024):
                    w = min(1024, per - o)
                    bi = k % 2
                    k += 1
                    P.dma("sp", stg[bi][:, 0:w], sflat[:, o:o + w], f"stg{bi}", writes=[("stg", bi)])
                    P.op("pool" if bi else "dve", (lambda e, a=c16[bi][:, 0:w], b=stg[bi][:, 0:w]: e.tensor_copy(out=a, in_=b)),
                         reads=[("stg", bi)], writes=[("c16", bi)])
                    P.dma("sp", dst[blk][:, o:o + w], c16[bi][:, 0:w], f"wc{bi}", reads=[("c16", bi)], writes=[(wk, blk, o)])
        ar.release()
        if io["make_cache"]:
            P.barrier()

        ar.mark()
        rsf = ar.alloc(8 * TH, BF16)

        def load_rsf():
          for hh in range(2):
            for c4 in range(2):
                for base, slot in ((0, hh * 2 + c4), (256, 4 + hh * 2 + c4)):
                    dst = rsf[:, slot * TH:(slot + 1) * TH]
                    if io.get("dyn_rs"):
                        P.custom_dma("sp", (lambda e, o=dst, hh=hh, r0=base + c4 * 128:
                                            e.dma_start(out=o, in_=rsa_d.rearrange("h r (two t) -> h r two t", two=2)[hh, r0:r0 + 128, bass.ds(self.parity(e), 1), :]
                                                        .rearrange("p o t -> p (o t)"))),
                                     "bld", writes=[("rsf", slot)])
                    else:
                        P.dma("sp", dst, rsa_d[hh, base + c4 * 128: base + (c4 + 1) * 128, :], "bld", reads=["rsall_d"], writes=[("rsf", slot)])
        sgT = ar.alloc(4 * TH, BF16)
        ar.mark()
        lB = ar.alloc(LB_N, F32)
        P.dma("sp", lB, lB_d, "lB", writes=["lB"])
        wgu = ar.alloc(4 * 1024, BF16)
        wgv = ar.alloc(4096, BF16)
        wsg = ar.alloc(512, BF16)
        guT = [ar.alloc(4 * 512, BF16) for _ in range(2)]
        g32 = [ar.alloc(512, F32) for _ in range(2)]
        vn = [ar.alloc(512, BF16) for _ in range(2)]
        stt = [ar.alloc(32, F32) for _ in range(2)]
        t32 = [ar.alloc(512, F32) for _ in range(2)]
        for cb in range(4):
            self.load_cast(wgu[:, cb * 1024:(cb + 1) * 1024], wGu_d[cb].rearrange("p a b -> p (a b)"), 1024, ("wgu", cb), stg)
        self.load_cast(wgv, wGv_d[0].rearrange("p a b -> p (a b)"), 4096, "wgv", stg)
        if io.get("pre_rs") is not None:
            io["pre_rs"]()
        load_rsf()
        load_xres()
        causb = cB[:, CB_CAUS:CB_CAUS + 128].unsqueeze(1).to_broadcast([128, 4, 128])
        P.op("dve", (lambda e: e.tensor_tensor(out=wsg.rearrange("p (g i) -> p g i", g=4), in0=lB[:, LB_WT:LB_WT + 512].rearrange("p (g i) -> p g i", g=4),
                                               in1=causb, op=ALU.mult)), reads=["lB", "cB"], writes=["wsg"])
        for T in range(4):
            gb = guT[T % 2]
            for cb in range(4):
                bank = ps[cb % 2]
                for dc in range(8):
                    P.op("pe", (lambda e, o=bank[:, :], a=wgu[:, cb * 1024 + dc * 128: cb * 1024 + (dc + 1) * 128],
                                b=xb[:, dc * TH + T * 512: dc * TH + (T + 1) * 512], st=(dc == 0), sp=(dc == 7): e.matmul(o, lhsT=a, rhs=b, start=st, stop=sp)),
                         reads=[("wgu", cb), ("xb", dc)], writes=[("ps", cb % 2)])
                P.op("act", (lambda e, o=gb[:, cb * 512:(cb + 1) * 512], i=bank[:, :]: e.activation(out=o, in_=i, func=AF.Gelu_apprx_tanh)),
                     reads=[("ps", cb % 2)], writes=[("guT", T % 2, cb)])
            for q in range(4):
                n = T * 4 + q
                pb = n % 2
                pv, psv = ps[2 + pb], ps[4 + pb]
                for dc in range(8):
                    P.op("pe", (lambda e, o=pv[:, :], a=xb[:, dc * TH + n * 128: dc * TH + (n + 1) * 128], b=wgv[:, dc * 512:(dc + 1) * 512], st=(dc == 0), sp=(dc == 7):
                                e.matmul(o, lhsT=a, rhs=b, start=st, stop=sp)),
                         reads=["wgv", ("xb", dc)], writes=[("ps", 2 + pb)])
                P.op("act", (lambda e, o=g32[pb], i=pv[:, :]: e.activation(out=o, in_=i, func=AF.Gelu_apprx_tanh)),
                     reads=[("ps", 2 + pb)], writes=[("g32", pb)])
                sv = stt[pb]
                P.op("dve", (lambda e, o=sv[:, 0:6], i=g32[pb]: e.bn_stats(out=o, in_=i)), reads=[("g32", pb)], writes=[("stt", pb)])
                P.op("dve", (lambda e, o=sv[:, 8:10], i=sv[:, 0:6]: e.bn_aggr(out=o, in_=i)), reads=[("stt", pb)], writes=[("stt", pb)])
                P.op("act", (lambda e, o=sv[:, 11:12], i=sv[:, 9:10]: e.activation(out=o, in_=i, func=AF.Ln, bias=EPS)), reads=[("stt", pb)], writes=[("stt", pb)])
                P.op("act", (lambda e, o=sv[:, 10:11], i=sv[:, 11:12]: e.activation(out=o, in_=i, func=AF.Exp, scale=-0.5)), reads=[("stt", pb)], writes=[("stt", pb)])
                P.op("dve", (lambda e, o=t32[pb], a=g32[pb], m=sv[:, 8:9], r=sv[:, 10:11]: e.tensor_scalar(out=o, in0=a, scalar1=m, scalar2=r, op0=ALU.subtract, op1=ALU.mult)),
                     reads=[("g32", pb), ("stt", pb)], writes=[("t32", pb)])
                P.op("pool", (lambda e, o=t32[pb], a=t32[pb], b=lB[:, LB_LNG:LB_LNG + 512]: e.tensor_tensor(out=o, in0=a, in1=b, op=ALU.mult)),
                     reads=[("t32", pb), "lB"], writes=[("t32", pb)])
                P.op("pool", (lambda e, o=vn[pb], a=t32[pb], b=lB[:, LB_LNB:LB_LNB + 512]: e.tensor_tensor(out=o, in0=a, in1=b, op=ALU.add)),
                     reads=[("t32", pb), "lB"], writes=[("vn", pb)])
                for g in range(4):
                    P.op("pe", (lambda e, o=psv[:, g * 128:(g + 1) * 128], a=vn[pb][:, g * 128:(g + 1) * 128], b=wsg[:, g * 128:(g + 1) * 128]:
                                e.matmul(o, lhsT=a, rhs=b, start=True, stop=True)),
                         reads=[("vn", pb), "wsg"], writes=[("ps", 4 + pb)])
                P.op("dve", (lambda e, o=t32[pb], a=psv[:, :], b=lB[:, LB_SB:LB_SB + 512]: e.tensor_tensor(out=o, in0=a, in1=b, op=ALU.add)),
                     reads=[("ps", 4 + pb), ("t32", pb), "lB"], writes=[("t32", pb)])
                gview = gb.rearrange("p (g t) -> p g t", g=4)[:, :, q * 128:(q + 1) * 128]
                oview = sgT.rearrange("p (g t) -> p g t", g=4)[:, :, n * 128:(n + 1) * 128]
                P.op("pool", (lambda e, o=oview, a=t32[pb].rearrange("p (g i) -> p g i", g=4), b=gview: e.tensor_tensor(out=o, in0=a, in1=b, op=ALU.mult)),
                     reads=[("t32", pb)] + [("guT", T % 2, cb) for cb in range(4)], writes=[("sgT", n)])
        ar.release()
        P.barrier()
        SG = [("sgT", n) for n in range(16)]
        RS = [("rsf", i) for i in range(8)]

        mg = ar.alloc(8 * TH, BF16)
        ar.mark()
        wg = [ar.alloc(3 * 1024, BF16)] * 2
        wp = [ar.alloc(3 * 512, BF16)] * 2
        sg32 = [ar.alloc(512, F32) for _ in range(2)]
        m32 = [ar.alloc(512, F32) for _ in range(2)]
        srcs = [(pR_d, 0, "ret"), (pS_d, 4, "sb"), (pG_d, None, "sgu")]
        for cb in range(8):
            wb = 0
            for br in range(3):
                self.load_cast(wg[wb][:, br * 1024:(br + 1) * 1024], wGt_d[br * 8 + cb].rearrange("p a b -> p (a b)"), 1024, ("wg", wb, br), stg)
                self.load_cast(wp[wb][:, br * 512:(br + 1) * 512], srcs[br][0][cb].rearrange("p a b -> p (a b)"), 512, ("wp", wb, br), stg)
            for T in range(4):
                for br in range(3):
                    j = (T * 3 + br) % 2
                    pg, pp = ps[j], ps[2 + j]
                    for dc in range(8):
                        P.op("pe", (lambda e, o=pg[:, :], a=wg[wb][:, br * 1024 + dc * 128: br * 1024 + (dc + 1) * 128],
                                    b=xb[:, dc * TH + T * 512: dc * TH + (T + 1) * 512], st=(dc == 0), sp=(dc == 7): e.matmul(o, lhsT=a, rhs=b, start=st, stop=sp)),
                             reads=[("wg", wb, br), ("xb", dc)], writes=[("ps", j)])
                    P.op("act", (lambda e, o=sg32[j], i=pg[:, :]: e.activation(out=o, in_=i, func=AF.Sigmoid)),
                         reads=[("ps", j)], writes=[("sg32", j)])
                    for kc in range(4):
                        if br < 2:
                            rhs = rsf[:, (srcs[br][1] + kc) * TH + T * 512: (srcs[br][1] + kc) * TH + (T + 1) * 512]
                            rk = [("rsf", srcs[br][1] + kc)]
                        else:
                            rhs = sgT[:, kc * TH + T * 512: kc * TH + (T + 1) * 512]
                            rk = SG[T * 4:(T + 1) * 4]
                        P.op("pe", (lambda e, o=pp[:, :], a=wp[wb][:, br * 512 + kc * 128: br * 512 + (kc + 1) * 128], b=rhs, st=(kc == 0), sp=(kc == 3):
                                    e.matmul(o, lhsT=a, rhs=b, start=st, stop=sp)),
                             reads=[("wp", wb, br)] + rk, writes=[("ps", 2 + j)])
                    mt = m32[T % 2]
                    if br == 0:
                        P.op("dve", (lambda e, o=mt, a=pp[:, :], b=sg32[j]: e.tensor_tensor(out=o, in0=a, in1=b, op=ALU.mult)),
                             reads=[("ps", 2 + j), ("sg32", j)], writes=[("m32", T % 2)])
                    else:
                        P.op("dve", (lambda e, o=sg32[j], a=pp[:, :], b=sg32[j]: e.tensor_tensor(out=o, in0=a, in1=b, op=ALU.mult)),
                             reads=[("ps", 2 + j), ("sg32", j)], writes=[("sg32", j)])
                        dst = mt if br == 1 else mg[:, cb * TH + T * 512: cb * TH + (T + 1) * 512]
                        wk = [("m32", T % 2)] if br == 1 else [("mg", cb)]
                        P.op("pool", (lambda e, o=dst, a=mt, b=sg32[j]: e.tensor_tensor(out=o, in0=a, in1=b, op=ALU.add)),
                             reads=[("m32", T % 2), ("sg32", j)], writes=wk)
        ar.release()
        P.barrier()

        ar.mark()
        wo = [ar.alloc(1024, BF16) for _ in range(2)]
        for cb in range(8):
            wb = cb % 2
            self.load_cast(wo[wb], wO_d[cb].rearrange("p a b -> p (a b)"), 1024, ("wo", wb), stg)
            for T in range(4):
                bank = ps[T % 2]
                for kc in range(8):
                    P.op("pe", (lambda e, o=bank[:, :], a=wo[wb][:, kc * 128:(kc + 1) * 128], b=mg[:, kc * TH + T * 512: kc * TH + (T + 1) * 512], st=(kc == 0), sp=(kc == 7):
                                e.matmul(o, lhsT=a, rhs=b, start=st, stop=sp)),
                         reads=[("wo", wb), ("mg", kc)], writes=[("ps", T % 2)])
                xs = xres[:, cb * TH + T * 512: cb * TH + (T + 1) * 512]
                P.op("dve", (lambda e, o=xs, a=xs, b=bank[:, :]: e.scalar_tensor_tensor(out=o, in0=a, scalar=ALPHA, in1=b, op0=ALU.mult, op1=ALU.add)),
                     reads=[("ps", T % 2), ("xres", cb)], writes=[("xres", cb)])
        ar.release()
        ar.release()
        self.layer_norm_fm(xres, xb, lV, LB_L1G, LB_L1B, onesF)
        P.barrier()

        ar.mark()
        hT = ar.alloc(32 * 512, BF16)
        wu = [ar.alloc(4096, BF16) for _ in range(2)]
        wd = [ar.alloc(4096, BF16) for _ in range(2)]
        r32 = [ar.alloc(512, F32) for _ in range(2)]
        for T in range(4):
            for f4 in range(8):
                wb = f4 % 2
                for i in range(4):
                    P.dma("sp", wu[wb][:, i * 1024:(i + 1) * 1024], wU16[f4 * 4 + i], f"wu{wb}", writes=[("wu", wb)])
                for i in range(4):
                    fb = f4 * 4 + i
                    bank = ps[fb % 2]
                    for dc in range(8):
                        P.op("pe", (lambda e, o=bank[:, :], a=wu[wb][:, i * 1024 + dc * 128: i * 1024 + (dc + 1) * 128],
                                    b=xb[:, dc * TH + T * 512: dc * TH + (T + 1) * 512], st=(dc == 0), sp=(dc == 7): e.matmul(o, lhsT=a, rhs=b, start=st, stop=sp)),
                             reads=[("wu", wb), ("xb", dc)], writes=[("ps", fb % 2)])
                    P.op("act", (lambda e, o=r32[fb % 2], i_=bank[:, :]: e.activation(out=o, in_=i_, func=AF.Relu)),
                         reads=[("ps", fb % 2)], writes=[("r32", fb % 2)])
                    P.op("dve", (lambda e, o=hT[:, fb * 512:(fb + 1) * 512], a=r32[fb % 2]: e.tensor_tensor(out=o, in0=a, in1=a, op=ALU.mult)),
                         reads=[("r32", fb % 2)], writes=[("hT", fb)])
            for cb in range(8):
                wb = cb % 2
                P.dma("sp", wd[wb], wD16[cb], f"wd{wb}", writes=[("wd", wb)])
                bank = ps[2 + cb % 2]
                for fc in range(32):
                    P.op("pe", (lambda e, o=bank[:, :], a=wd[wb][:, fc * 128:(fc + 1) * 128], b=hT[:, fc * 512:(fc + 1) * 512], st=(fc == 0), sp=(fc == 31):
                                e.matmul(o, lhsT=a, rhs=b, start=st, stop=sp)),
                         reads=[("wd", wb), ("hT", fc)], writes=[("ps", 2 + cb % 2)])
                xs = xres[:, cb * TH + T * 512: cb * TH + (T + 1) * 512]
                P.op("dve", (lambda e, o=xs, a=xs, b=bank[:, :]: e.scalar_tensor_tensor(out=o, in0=a, scalar=ALPHA, in1=b, op0=ALU.mult, op1=ALU.add)),
                     reads=[("ps", 2 + cb % 2), ("xres", cb)], writes=[("xres", cb)])
        ar.release()
        P.barrier()
        self.layer_norm_fm(xres, xb, lV, LB_L2G, LB_L2B, onesF, store=(xres_o, xb_o))
        P.barrier()
        ar.release()

    def layer_norm_fm(self, xres, xb, lB, og, ob, onesF, store=None):
        P, ar, ps = self.P, self.ar, self.ps
        P.barrier()
        ar.mark()
        usq = ar.alloc(8 * 512, F32)
        mean = [ar.alloc(512, F32) for _ in range(2)]
        rstd = [ar.alloc(512, F32) for _ in range(2)]
        vtm = [ar.alloc(512, F32) for _ in range(2)]
        tmp = [ar.alloc(512, F32) for _ in range(3)]
        banks = [(ps[4], ps[5]), (ps[2], ps[3])]

        def xs_(cb, T):
            return xres[:, cb * TH + T * 512: cb * TH + (T + 1) * 512]

        def stats(T):
            j = T % 2
            p1, p2 = banks[j]
            for cb in range(8):
                P.op("act", (lambda e, o=usq[:, cb * 512:(cb + 1) * 512], i=xs_(cb, T): e.activation(out=o, in_=i, func=AF.Square)),
                     reads=[("xr", cb, T)], writes=[("usq", cb)])
                P.op("pe", (lambda e, o=p1[:, :], b=xs_(cb, T), st=(cb == 0), sp=(cb == 7): e.matmul(o, lhsT=onesF, rhs=b, start=st, stop=sp)),
                     reads=[("xr", cb, T), "cB"], writes=[("lnp1", j)])
                P.op("pe", (lambda e, o=p2[:, :], b=usq[:, cb * 512:(cb + 1) * 512], st=(cb == 0), sp=(cb == 7): e.matmul(o, lhsT=onesF, rhs=b, start=st, stop=sp)),
                     reads=[("usq", cb), "cB"], writes=[("lnp2", j)])
            P.op("act", (lambda e: e.copy(out=mean[j], in_=p1[:, :])), reads=[("lnp1", j)], writes=[("mean", j)])
            P.op("dve", (lambda e: e.tensor_tensor(out=vtm[j], in0=mean[j], in1=mean[j], op=ALU.mult)), reads=[("mean", j)], writes=[("vt", j)])
            P.op("dve", (lambda e: e.tensor_tensor(out=vtm[j], in0=p2[:, :], in1=vtm[j], op=ALU.subtract)), reads=[("lnp2", j), ("vt", j)], writes=[("vt", j)])
            P.op("act", (lambda e: e.activation(out=vtm[j], in_=vtm[j], func=AF.Ln, bias=EPS)), reads=[("vt", j)], writes=[("vt", j)])
            P.op("act", (lambda e: e.activation(out=rstd[j], in_=vtm[j], func=AF.Exp, scale=-0.5)), reads=[("vt", j)], writes=[("rstd", j)])

        def norm(T):
            j = T % 2
            for cb in range(8):
                xs = xs_(cb, T)
                tb = tmp[cb % 3]
                P.op("dve", (lambda e, o=tb, a=xs: e.tensor_tensor(out=o, in0=a, in1=mean[j], op=ALU.subtract)),
                     reads=[("xr", cb, T), ("mean", j)], writes=[("lt", cb % 3)])
                P.op("dve", (lambda e, o=tb: e.tensor_tensor(out=o, in0=o, in1=rstd[j], op=ALU.mult)),
                     reads=[("lt", cb % 3), ("rstd", j)], writes=[("lt", cb % 3)])
                P.op("act", (lambda e, o=xs, i=tb, g=lB[:, og + cb:og + cb + 1], b=lB[:, ob + cb:ob + cb + 1]: e.activation(out=o, in_=i, func=AF.Identity, scale=g, bias=b)),
                     reads=[("lt", cb % 3), "lV"], writes=[("xr", cb, T)])
                P.op("dve", (lambda e, o=xb[:, cb * TH + T * 512: cb * TH + (T + 1) * 512], i=xs: e.tensor_copy(out=o, in_=i)),
                     reads=[("xr", cb, T)], writes=[("xbk", cb, T)])
            if store is not None:
                xo, bo = store
                tsl = slice(T * 512, (T + 1) * 512)
                P.dma("sp", xo.rearrange("(dc p) t -> p dc t", p=128)[:, :, tsl], xres.rearrange("p (dc t) -> p dc t", dc=8)[:, :, tsl],
                      "bst", reads=[("xr", cb, T) for cb in range(8)])
                if bo is not None:
                    P.dma("sp", bo.rearrange("(dc p) t -> p dc t", p=128)[:, :, tsl], xb.rearrange("p (dc t) -> p dc t", dc=8)[:, :, tsl],
                          "bst", reads=[("xbk", cb, T) for cb in range(8)])

        stats(0)
        for T in range(4):
            if T + 1 < 4:
                stats(T + 1)
            norm(T)
        ar.release()


_CACHE = {}


def _prog(stages):
    key = tuple(stages)
    if key not in _CACHE:
        b = Builder(list(stages))
        b.build()
        _CACHE[key] = b
    return _CACHE[key]


def _run(stage, l, maps_all, state):
    b = _prog([stage])
    in_maps = []
    for c in range(8):
        m = {}
        for nm in b.ext_in:
            if nm + str(l) in maps_all[c]:
                m[nm] = maps_all[c][nm + str(l)]
            elif nm in maps_all[c]:
                m[nm] = maps_all[c][nm]
            else:
                m[nm] = state[c][nm]
        in_maps.append(m)
    res = run_bass_kernel_spmd(b.nc, in_maps, core_ids=list(range(8)))
    for c in range(8):
        for nm in b.ext_out:
            state[c][nm] = np.asarray(res.results[c][nm])


def kernel_unfused(**inputs):
    maps = _host_inputs(inputs)
    state = [dict() for _ in range(8)]
    _run("P0", 0, maps, state)
    for l in range(DEPTH):
        for c in range(8):
            pr = c // 2 * 2
            state[c]["xball"] = np.stack([state[pr]["xb_o"], state[pr + 1]["xb_o"]])
            state[c]["xres_i"] = state[c]["xres_o"]
            state[c]["xb_i"] = state[c]["xb_o"]
        _run("A0", l, maps, state)
        for c in range(8):
            pr, hh = c // 2 * 2, c % 2
            state[c]["rsall"] = np.ascontiguousarray(
                np.stack([state[pr]["rs"][:, hh * TH:(hh + 1) * TH], state[pr + 1]["rs"][:, hh * TH:(hh + 1) * TH]]))
        _run("B0", l, maps, state)
    out = np.empty((NB, S, D), np.float32)
    for c in range(8):
        b, hh = c // 2, c % 2
        out[b, hh * TH:(hh + 1) * TH, :] = state[c]["xres_o"].T
    return out


def kernel(**inputs):
    maps = _host_inputs(inputs)
    b = _prog(["FX"])
    nv = int(np.random.randint(1 << 10, 1 << 26))
    nonce = (nv * 8 + np.repeat(np.arange(8, dtype=np.int64), 16)[None, :]).astype(np.int32)
    in_maps = []
    for c in range(8):
        m = {}
        for nm in b.ext_in:
            m[nm] = nonce if nm == "nonce" else maps[c][nm]
        in_maps.append(m)
    res = run_bass_kernel_spmd(b.nc, in_maps, core_ids=list(range(8)))
    out = np.empty((NB, S, D), np.float32)
    for c in range(8):
        bb, hh = c // 2, c % 2
        out[bb, hh * TH:(hh + 1) * TH, :] = np.asarray(res.results[c]["outT"]).T
    return out


def kernel_dup(**inputs):
    maps = _host_inputs(inputs)
    b = _prog(["FUSED"])
    in_maps = []
    for c in range(8):
        pr = c // 2 * 2
        m = {}
        for nm in b.ext_in:
            if nm == "xT2":
                m[nm] = np.stack([maps[pr]["xT"], maps[pr + 1]["xT"]])
            elif nm.startswith("cA_"):
                m[nm] = maps[pr + int(nm[-1])]["cA"]
            elif nm[-2] == "_" and nm[:-2] in maps[c]:
                m[nm] = maps[pr + int(nm[-1])][nm[:-2]]
            else:
                m[nm] = maps[c][nm]
        in_maps.append(m)
    res = run_bass_kernel_spmd(b.nc, in_maps, core_ids=list(range(8)))
    out = np.empty((NB, S, D), np.float32)
    for c in range(8):
        bb, hh = c // 2, c % 2
        out[bb, hh * TH:(hh + 1) * TH, :] = np.asarray(res.results[c]["outT"]).T
    return out
```
